# Optimizing a Trainium2 kernel written in Bass

```python
import math
import jax
import jax.numpy as jnp
from jax import lax
import numpy as np

D_MODEL = 1024
BATCH = 8
SEQ = 2048
DEPTH = 2
DEC_BATCH = 32
DEC_SEQ = 1
PAST_LEN = 8192
PAGE_SIZE = 128

HEAD_DIM = 64
ATT_WIDTH = D_MODEL // 2
N_HEADS = ATT_WIDTH // HEAD_DIM
N_KV = 2
HG = N_HEADS // N_KV
CMP_BLOCK = 32
CMP_STRIDE = 16
CMP_HID = 128
SLC_BLOCK = 64
N_SEL = 16
WINDOW = 512
Q_BLOCK = 128
FORCE_BONUS = 1e6
SSM_INNER = D_MODEL - ATT_WIDTH
SSM_HEAD_DIM = 64
SSM_HEADS = SSM_INNER // SSM_HEAD_DIM
SSM_GROUPS = 2
D_STATE = 64
SSM_CONV = 4
SSM_CHUNK = 128
CONV_DIM = SSM_INNER + 2 * SSM_GROUPS * D_STATE
D_FF = 2816
FFN_CONV = 3

IN_DIM = ATT_WIDTH + 6 * N_KV * HEAD_DIM + 3 * N_HEADS + SSM_INNER + CONV_DIM + SSM_HEADS
EPS = 1e-6
NEG = -1e30
BIG_POS = 2 ** 30
SCALE = HEAD_DIM ** -0.5

kernel_name = 'nsa_mamba2_hymba_convffn_decode_step'


def _rmsnorm(x, g):
    xf = x.astype(jnp.float32)
    y = xf * lax.rsqrt(jnp.mean(xf * xf, axis=-1, keepdims=True) + EPS)
    return (y * g.astype(jnp.float32)).astype(x.dtype)


def _alibi_slopes():
    h = jnp.arange(1, N_HEADS + 1, dtype=jnp.float32)
    return jnp.exp2(-8.0 * h / N_HEADS).reshape(N_KV, HG)


def _masked_softmax(s, mask):
    p = jax.nn.softmax(jnp.where(mask, s, NEG), axis=-1)
    return jnp.where(mask, p, 0.0)


def _causal_dwconv(u, prev, w, b):
    width, T = w.shape[0], u.shape[1]
    ucat = jnp.concatenate([prev.astype(u.dtype), u], axis=1)
    out = b + sum(ucat[:, k:k + T] * w[k] for k in range(width))
    return out, ucat[:, ucat.shape[1] - (width - 1):]


def _compress(k, w1, b1, w2):
    bsz, L = k.shape[:2]
    nh = -(-L // CMP_STRIDE)
    k = jnp.pad(k, ((0, 0), (0, nh * CMP_STRIDE - L), (0, 0), (0, 0)))
    halves = k.reshape(bsz, nh, CMP_STRIDE, N_KV, HEAD_DIM)
    pa = jnp.einsum('bnjgd,jdf->bngf', halves[:, :-1], w1[:CMP_STRIDE])
    pb = jnp.einsum('bnjgd,jdf->bngf', halves[:, 1:], w1[CMP_STRIDE:])
    hid = jax.nn.gelu(pa + pb + b1)
    return jnp.einsum('bngf,fd->bngd', hid, w2)


def _cmp_branch(q, q_pos, kcmp, vcmp, slopes):
    n = kcmp.shape[1]
    ends = jnp.arange(n, dtype=jnp.int32) * CMP_STRIDE + (CMP_BLOCK - 1)
    s = jnp.einsum('bqgjd,bngd->bqgjn', q, kcmp).astype(jnp.float32) * SCALE
    dist = (q_pos[:, None] - ends[None, :]).astype(jnp.float32)
    s = s - slopes[None, None, :, :, None] * dist[None, :, None, None, :]
    p = _masked_softmax(s, (dist >= 0)[None, :, None, None, :])
    o = jnp.einsum('bqgjn,bngd->bqgjd', p.astype(vcmp.dtype), vcmp)
    return o, p.sum(axis=3)


def _select(p_grp, q_pos, n_slc):
    n_cmp = p_grp.shape[-1]
    cs = jnp.arange(n_cmp) * CMP_STRIDE
    ss = jnp.arange(n_slc) * SLC_BLOCK
    overlap = ((cs[:, None] < ss[None, :] + SLC_BLOCK) &
               (cs[:, None] + CMP_BLOCK > ss[None, :])).astype(jnp.float32)
    imp = jnp.einsum('bqgn,nj->bqgj', p_grp, overlap)
    cur = (q_pos // SLC_BLOCK)[:, None]
    j = jnp.arange(n_slc)[None, :]
    forced = ((j == 0) | (j == cur) | (j == cur - 1))[None, :, None, :]
    valid = (ss[None, :] <= q_pos[:, None])[None, :, None, :]
    score = jnp.where(valid, imp + jnp.where(forced, FORCE_BONUS, 0.0), NEG)
    _, idx = lax.top_k(score, min(N_SEL, n_slc))
    return idx


def _slc_block(q, q_pos, idx, kblk, vblk, slopes):
    s = jnp.einsum('bqgjd,bqgnsd->bqgjns', q, kblk).astype(jnp.float32) * SCALE
    kpos = idx[..., None] * SLC_BLOCK + jnp.arange(SLC_BLOCK)
    dist = q_pos[None, :, None, None, None] - kpos
    s = s - slopes[None, None, :, :, None, None] * dist[:, :, :, None].astype(jnp.float32)
    shp = s.shape
    mask = jnp.broadcast_to((dist >= 0)[:, :, :, None], shp)
    p = _masked_softmax(s.reshape(shp[:4] + (-1,)), mask.reshape(shp[:4] + (-1,))).reshape(shp)
    return jnp.einsum('bqgjns,bqgnsd->bqgjd', p.astype(vblk.dtype), vblk)


def _blocked(fn, q, q_pos, idx):
    bsz, T = q.shape[:2]
    qb = min(Q_BLOCK, T)
    nb = -(-T // qb)
    pad = nb * qb - T

    def split(a):
        a = jnp.pad(a, ((0, 0), (0, pad)) + ((0, 0),) * (a.ndim - 2), mode='edge')
        return jnp.moveaxis(a.reshape((bsz, nb, qb) + a.shape[2:]), 1, 0)

    pos = jnp.pad(q_pos, (0, pad), mode='edge').reshape(nb, qb)
    out = lax.map(lambda a: fn(a[0], a[1], a[2]), (split(q), pos, split(idx)))
    out = jnp.moveaxis(out, 0, 1)
    return out.reshape((bsz, nb * qb) + out.shape[3:])[:, :T]


def _win_branch(q, q_pos, k, v, k_pos, slopes):
    bsz, T = q.shape[:2]
    Lk = k.shape[1]
    q_off = Lk - T
    qb = min(Q_BLOCK, T)
    nb = -(-T // qb)
    pad = nb * qb - T
    qp = jnp.pad(q, ((0, 0), (0, pad), (0, 0), (0, 0), (0, 0)))
    qpos = jnp.pad(q_pos, (0, pad), mode='edge').reshape(nb, qb)
    kv_pad = ((0, 0), (WINDOW, pad), (0, 0), (0, 0))
    kpad, vpad = jnp.pad(k, kv_pad), jnp.pad(v, kv_pad)
    kpos = jnp.concatenate([jnp.full((WINDOW,), -WINDOW - 1, jnp.int32), k_pos.astype(jnp.int32),
                            jnp.full((pad,), BIG_POS, jnp.int32)])
    band = q_off + jnp.arange(nb)[:, None] * qb + jnp.arange(qb + WINDOW)[None, :]
    kb, vb, kposb = kpad[:, band], vpad[:, band], kpos[band]
    qr = qp.reshape(bsz, nb, qb, N_KV, HG, HEAD_DIM)
    s = jnp.einsum('bnqgjd,bnkgd->bnqgjk', qr, kb).astype(jnp.float32) * SCALE
    dist = qpos[:, :, None] - kposb[:, None, :]
    s = s - slopes[None, None, None, :, :, None] * dist[None, :, :, None, None, :].astype(jnp.float32)
    mask = ((dist >= 0) & (dist <= WINDOW))[None, :, :, None, None, :]
    p = _masked_softmax(s, mask)
    o = jnp.einsum('bnqgjk,bnkgd->bnqgjd', p.astype(vb.dtype), vb)
    return o.reshape(bsz, nb * qb, N_KV, HG, HEAD_DIM)[:, :T]


def _ssd(x, dt, A, B, C, h0):
    bsz, T = x.shape[:2]
    l = min(SSM_CHUNK, T)
    nc = -(-T // l)
    pad = nc * l - T
    hg = SSM_HEADS // SSM_GROUPS

    def padt(a):
        return jnp.pad(a.astype(jnp.float32), ((0, 0), (0, pad)) + ((0, 0),) * (a.ndim - 2))

    xr = padt(x).reshape(bsz, nc, l, SSM_GROUPS, hg, SSM_HEAD_DIM)
    dtr = padt(dt).reshape(bsz, nc, l, SSM_GROUPS, hg)
    Br = padt(B).reshape(bsz, nc, l, SSM_GROUPS, D_STATE)
    Cr = padt(C).reshape(bsz, nc, l, SSM_GROUPS, D_STATE)
    cs = jnp.cumsum(dtr * A.reshape(SSM_GROUPS, hg), axis=2)
    tril = jnp.tril(jnp.ones((l, l), dtype=bool))
    seg = cs[:, :, :, None] - cs[:, :, None, :]
    Lm = jnp.exp(jnp.where(tril[None, None, :, :, None, None], seg, NEG))
    CB = jnp.einsum('bclgn,bcsgn->bclsg', Cr, Br)
    y_diag = jnp.einsum('bclsg,bclsgh,bcsgh,bcsghp->bclghp', CB, Lm, dtr, xr)
    st = jnp.einsum('bclgn,bclgh,bclghp->bcghpn', Br, jnp.exp(cs[:, :, -1:] - cs) * dtr, xr)
    dec = jnp.exp(cs[:, :, -1])

    def step(h, inp):
        s_c, d_c = inp
        return h * d_c[..., None, None] + s_c, h

    h_init = h0.astype(jnp.float32).reshape(bsz, SSM_GROUPS, hg, SSM_HEAD_DIM, D_STATE)
    h_fin, h_start = lax.scan(step, h_init, (jnp.moveaxis(st, 1, 0), jnp.moveaxis(dec, 1, 0)))
    h_start = jnp.moveaxis(h_start, 0, 1)
    y_off = jnp.einsum('bclgn,bcghpn,bclgh->bclghp', Cr, h_start, jnp.exp(cs))
    y = (y_diag + y_off).reshape(bsz, nc * l, SSM_HEADS, SSM_HEAD_DIM)[:, :T]
    return y, h_fin.reshape(bsz, SSM_HEADS, SSM_HEAD_DIM, D_STATE)


def _layer(x, c, q_pos, lp, slopes, past):
    bsz, T, _ = x.shape
    mod = jnp.einsum('bd,de->be', jax.nn.silu(c), lp['ada_w']) + lp['ada_b']
    sh1, sc1, g1, sh2, sc2, g2 = [m[:, None, :] for m in jnp.split(mod, 6, axis=-1)]
    h = _rmsnorm(x, lp['norm1_g']) * (1 + sc1) + sh1
    proj = jnp.einsum('btd,de->bte', h, lp['w_in'])
    kvw = N_KV * HEAD_DIM
    sizes = [ATT_WIDTH] + [kvw] * 6 + [3 * N_HEADS, SSM_INNER, CONV_DIM]
    cuts = [sum(sizes[:i + 1]) for i in range(len(sizes))]
    q, kc, vc, ks, vs, kw, vw, gt, z, xbc, dt_raw = jnp.split(proj, cuts, axis=-1)
    q = q.reshape(bsz, T, N_KV, HG, HEAD_DIM)
    kc, vc, ks, vs, kw, vw = [a.reshape(bsz, T, N_KV, HEAD_DIM) for a in (kc, vc, ks, vs, kw, vw)]
    bi = jnp.arange(bsz)[:, None, None, None]
    gi = jnp.arange(N_KV)[None, None, :, None]
    n_new = -(-T // SLC_BLOCK)
    padb = ((0, 0), (0, n_new * SLC_BLOCK - T), (0, 0), (0, 0))
    ksb = jnp.pad(ks, padb).reshape(bsz, n_new, SLC_BLOCK, N_KV, HEAD_DIM)
    vsb = jnp.pad(vs, padb).reshape(bsz, n_new, SLC_BLOCK, N_KV, HEAD_DIM)
    if past is None:
        kc_all, vc_all = kc, vc
        n_slc = n_new

        def gather(ib):
            return ksb[bi, ib, :, gi, :], vsb[bi, ib, :, gi, :]

        kw_all, vw_all, kw_pos = kw, vw, q_pos
        h0 = jnp.zeros((bsz, SSM_HEADS, SSM_HEAD_DIM, D_STATE), jnp.float32)
        conv_prev = jnp.zeros((bsz, SSM_CONV - 1, CONV_DIM), x.dtype)
        ffn_prev = jnp.zeros((bsz, FFN_CONV - 1, D_FF), x.dtype)
    else:
        (pc_k, pc_v, pool_k, pool_v, page_table, win_k, win_v, h0, conv_prev, ffn_prev) = past
        kc_all = jnp.concatenate([pc_k.astype(kc.dtype), kc], axis=1)
        vc_all = jnp.concatenate([pc_v.astype(vc.dtype), vc], axis=1)
        bpp = PAGE_SIZE // SLC_BLOCK
        n_past = page_table.shape[1] * bpp
        n_slc = n_past + n_new

        def gather(ib):
            pi = jnp.minimum(ib, n_past - 1)
            phys = page_table[bi, pi // bpp] * bpp + pi % bpp
            ni = jnp.clip(ib - n_past, 0, n_new - 1)
            is_past = (ib < n_past)[..., None, None]
            kg = jnp.where(is_past, pool_k[phys, :, gi, :].astype(ks.dtype), ksb[bi, ni, :, gi, :])
            vg = jnp.where(is_past, pool_v[phys, :, gi, :].astype(vs.dtype), vsb[bi, ni, :, gi, :])
            return kg, vg

        wb = win_k.shape[1]
        past_len = page_table.shape[1] * PAGE_SIZE
        kw_all = jnp.concatenate([win_k.astype(kw.dtype), kw], axis=1)
        vw_all = jnp.concatenate([win_v.astype(vw.dtype), vw], axis=1)
        kw_pos = jnp.concatenate([past_len - wb + jnp.arange(wb, dtype=jnp.int32), q_pos])

    kcmp = _compress(kc_all, lp['cmpk_w1'], lp['cmpk_b1'], lp['cmpk_w2'])
    vcmp = _compress(vc_all, lp['cmpv_w1'], lp['cmpv_b1'], lp['cmpv_w2'])
    o_cmp, p_grp = _cmp_branch(q, q_pos, kcmp, vcmp, slopes)
    idx = _select(p_grp, q_pos, n_slc)

    def slc_fn(qb_, pb_, ib_):
        kg, vg = gather(ib_)
        return _slc_block(qb_, pb_, ib_, kg, vg, slopes)

    o_slc = _blocked(slc_fn, q, q_pos, idx)
    o_win = _win_branch(q, q_pos, kw_all, vw_all, kw_pos, slopes)
    gates = jax.nn.sigmoid(gt.astype(jnp.float32)).reshape(bsz, T, 3, N_KV, HG, 1).astype(x.dtype)
    o_att = (gates[:, :, 0] * o_cmp + gates[:, :, 1] * o_slc + gates[:, :, 2] * o_win).reshape(bsz, T, ATT_WIDTH)

    xbc, conv_new = _causal_dwconv(xbc, conv_prev, lp['ssm_conv_w'], lp['ssm_conv_b'])
    xbc = jax.nn.silu(xbc)
    xm, Bm, Cm = jnp.split(xbc, [SSM_INNER, SSM_INNER + SSM_GROUPS * D_STATE], axis=-1)
    xm = xm.reshape(bsz, T, SSM_HEADS, SSM_HEAD_DIM)
    Bm = Bm.reshape(bsz, T, SSM_GROUPS, D_STATE)
    Cm = Cm.reshape(bsz, T, SSM_GROUPS, D_STATE)
    dt = jax.nn.softplus(dt_raw.astype(jnp.float32) + lp['dt_bias'].astype(jnp.float32))
    A = -jnp.exp(lp['a_log'].astype(jnp.float32))
    y, h_new = _ssd(xm, dt, A, Bm, Cm, h0)
    y = y + lp['d_skip'].astype(jnp.float32)[:, None] * xm.astype(jnp.float32)
    y = y.reshape(bsz, T, SSM_GROUPS, SSM_INNER // SSM_GROUPS) * \
        jax.nn.silu(z.astype(jnp.float32)).reshape(bsz, T, SSM_GROUPS, SSM_INNER // SSM_GROUPS)
    y = y * lax.rsqrt(jnp.mean(y * y, axis=-1, keepdims=True) + EPS)
    y = (y.reshape(bsz, T, SSM_INNER) * lp['ssm_norm_g'].astype(jnp.float32)).astype(x.dtype)

    mix = jnp.einsum('bte,ed->btd', jnp.concatenate([o_att, y], axis=-1), lp['w_out'])
    x = x + g1 * mix

    h2 = _rmsnorm(x, lp['norm2_g']) * (1 + sc2) + sh2
    up = jnp.einsum('btd,df->btf', h2, lp['ffn_w_up'])
    ug, uv = jnp.split(up, 2, axis=-1)
    ugc, ffn_new = _causal_dwconv(ug, ffn_prev, lp['ffn_conv_w'], lp['ffn_conv_b'])
    x = x + g2 * jnp.einsum('btf,fd->btd', jax.nn.silu(ugc) * uv, lp['ffn_w_down'])

    nk = min(WINDOW, kw_all.shape[1])
    new_state = (kc, vc, ks, vs, kw_all[:, kw_all.shape[1] - nk:], vw_all[:, vw_all.shape[1] - nk:],
                 h_new, conv_new, ffn_new)
    return x, new_state


def setup_inputs(seed: int = 0) -> dict:
    key = jax.random.key(seed)
    keys = iter(jax.random.split(key, 48))

    def nrm(shape, scale):
        return scale * jax.random.normal(next(keys), shape, jnp.float32)

    n_pages = PAST_LEN // PAGE_SIZE
    n_used = DEC_BATCH * n_pages
    n_pool = n_used + -(-n_used // 4)
    wb = min(WINDOW, PAST_LEN)
    page_table = jax.random.permutation(next(keys), n_pool)[:n_used].reshape(DEC_BATCH, n_pages).astype(jnp.int32)
    dt0 = jnp.exp(jax.random.uniform(next(keys), (DEPTH, SSM_HEADS), jnp.float32,
                                     math.log(1e-3), math.log(1e-1)))
    dt_bias = dt0 + jnp.log(-jnp.expm1(-dt0))
    a_log = jnp.log(jax.random.uniform(next(keys), (DEPTH, SSM_HEADS), jnp.float32, 1.0, 16.0))
    cache_shape = (DEPTH, n_pool, PAGE_SIZE, N_KV, HEAD_DIM)
    win_shape = (DEPTH, DEC_BATCH, wb, N_KV, HEAD_DIM)
    return {
        'x_prompt': nrm((BATCH, SEQ, D_MODEL), 1.0),
        'x_sample': nrm((DEC_BATCH, DEC_SEQ, D_MODEL), 1.0),
        'cache_cmp_k': nrm(cache_shape, 1.0),
        'cache_cmp_v': nrm(cache_shape, 1.0),
        'cache_slc_k': nrm(cache_shape, 1.0),
        'cache_slc_v': nrm(cache_shape, 1.0),
        'state_win_k': nrm(win_shape, 1.0),
        'state_win_v': nrm(win_shape, 1.0),
        'state_ssm': nrm((DEPTH, DEC_BATCH, SSM_HEADS, SSM_HEAD_DIM, D_STATE), 0.1),
        'state_ssm_conv': nrm((DEPTH, DEC_BATCH, SSM_CONV - 1, CONV_DIM), 1.0),
        'state_ffn_conv': nrm((DEPTH, DEC_BATCH, FFN_CONV - 1, D_FF), 1.0),
        'page_table': page_table,
        'c_prompt': nrm((BATCH, D_MODEL), 1.0),
        'c_sample': nrm((DEC_BATCH, D_MODEL), 1.0),
        'ada_w': nrm((DEPTH, D_MODEL, 6 * D_MODEL), 0.3 * D_MODEL ** -0.5),
        'ada_b': nrm((DEPTH, 6 * D_MODEL), 0.01),
        'norm1_g': 1.0 + nrm((DEPTH, D_MODEL), 0.01),
        'norm2_g': 1.0 + nrm((DEPTH, D_MODEL), 0.01),
        'w_in': nrm((DEPTH, D_MODEL, IN_DIM), D_MODEL ** -0.5),
        'cmpk_w1': nrm((DEPTH, CMP_BLOCK, HEAD_DIM, CMP_HID), (CMP_BLOCK * HEAD_DIM) ** -0.5),
        'cmpk_b1': nrm((DEPTH, CMP_HID), 0.01),
        'cmpk_w2': nrm((DEPTH, CMP_HID, HEAD_DIM), CMP_HID ** -0.5),
        'cmpv_w1': nrm((DEPTH, CMP_BLOCK, HEAD_DIM, CMP_HID), (CMP_BLOCK * HEAD_DIM) ** -0.5),
        'cmpv_b1': nrm((DEPTH, CMP_HID), 0.01),
        'cmpv_w2': nrm((DEPTH, CMP_HID, HEAD_DIM), CMP_HID ** -0.5),
        'ssm_conv_w': nrm((DEPTH, SSM_CONV, CONV_DIM), SSM_CONV ** -0.5),
        'ssm_conv_b': nrm((DEPTH, CONV_DIM), 0.01),
        'dt_bias': dt_bias,
        'a_log': a_log,
        'd_skip': 1.0 + nrm((DEPTH, SSM_HEADS), 0.1),
        'ssm_norm_g': 1.0 + nrm((DEPTH, SSM_INNER), 0.01),
        'w_out': nrm((DEPTH, ATT_WIDTH + SSM_INNER, D_MODEL), (ATT_WIDTH + SSM_INNER) ** -0.5),
        'ffn_w_up': nrm((DEPTH, D_MODEL, 2 * D_FF), D_MODEL ** -0.5),
        'ffn_conv_w': nrm((DEPTH, FFN_CONV, D_FF), FFN_CONV ** -0.5),
        'ffn_conv_b': nrm((DEPTH, D_FF), 0.01),
        'ffn_w_down': nrm((DEPTH, D_FF, D_MODEL), D_FF ** -0.5),
        'final_g': 1.0 + nrm((D_MODEL,), 0.01),
    }


def reference(x_prompt, x_sample, cache_cmp_k, cache_cmp_v, cache_slc_k, cache_slc_v, state_win_k,
              state_win_v, state_ssm, state_ssm_conv, state_ffn_conv, page_table, c_prompt, c_sample,
              ada_w, ada_b, norm1_g, norm2_g, w_in, cmpk_w1, cmpk_b1, cmpk_w2, cmpv_w1, cmpv_b1, cmpv_w2,
              ssm_conv_w, ssm_conv_b, dt_bias, a_log, d_skip, ssm_norm_g, w_out, ffn_w_up, ffn_conv_w,
              ffn_conv_b, ffn_w_down, final_g):
    slopes = _alibi_slopes()
    ns = x_sample.shape[0]
    past_len = page_table.shape[1] * PAGE_SIZE
    pos_p = jnp.arange(x_prompt.shape[1], dtype=jnp.int32)
    pos_s = past_len + jnp.arange(x_sample.shape[1], dtype=jnp.int32)
    xp, xs = x_prompt, x_sample
    new_p, new_s = [], []
    for l in range(DEPTH):
        lp = dict(ada_w=ada_w[l], ada_b=ada_b[l], norm1_g=norm1_g[l], norm2_g=norm2_g[l], w_in=w_in[l],
                  cmpk_w1=cmpk_w1[l], cmpk_b1=cmpk_b1[l], cmpk_w2=cmpk_w2[l],
                  cmpv_w1=cmpv_w1[l], cmpv_b1=cmpv_b1[l], cmpv_w2=cmpv_w2[l],
                  ssm_conv_w=ssm_conv_w[l], ssm_conv_b=ssm_conv_b[l], dt_bias=dt_bias[l], a_log=a_log[l],
                  d_skip=d_skip[l], ssm_norm_g=ssm_norm_g[l], w_out=w_out[l], ffn_w_up=ffn_w_up[l],
                  ffn_conv_w=ffn_conv_w[l], ffn_conv_b=ffn_conv_b[l], ffn_w_down=ffn_w_down[l])
        xp, st_p = _layer(xp, c_prompt, pos_p, lp, slopes, None)
        past = (cache_cmp_k[l][page_table].reshape(ns, -1, N_KV, HEAD_DIM),
                cache_cmp_v[l][page_table].reshape(ns, -1, N_KV, HEAD_DIM),
                cache_slc_k[l].reshape(-1, SLC_BLOCK, N_KV, HEAD_DIM),
                cache_slc_v[l].reshape(-1, SLC_BLOCK, N_KV, HEAD_DIM),
                page_table, state_win_k[l], state_win_v[l], state_ssm[l], state_ssm_conv[l],
                state_ffn_conv[l])
        xs, st_s = _layer(xs, c_sample, pos_s, lp, slopes, past)
        new_p.append(st_p)
        new_s.append(st_s)
    y_prompt = _rmsnorm(xp, final_g)
    y_sample = _rmsnorm(xs, final_g)
    (cmp_k_p, cmp_v_p, slc_k_p, slc_v_p, win_k_p, win_v_p, ssm_p, ssm_conv_p, ffn_conv_p) = \
        [jnp.stack([st[i] for st in new_p]) for i in range(9)]
    (cmp_k_s, cmp_v_s, slc_k_s, slc_v_s, win_k_s, win_v_s, ssm_s, ssm_conv_s, ffn_conv_s) = \
        [jnp.stack([st[i] for st in new_s]) for i in range(9)]
    return (y_prompt, y_sample, cmp_k_p, cmp_k_s, cmp_v_p, cmp_v_s, slc_k_p, slc_k_s, slc_v_p, slc_v_s,
            win_k_p, win_k_s, win_v_p, win_v_s, ssm_p, ssm_s, ssm_conv_p, ssm_conv_s, ffn_conv_p, ffn_conv_s)
```

```python
import numpy as np
import concourse.bass as bass
import concourse.mybir as mybir
from concourse.bass_utils import run_bass_kernel_spmd

F32 = mybir.dt.float32
BF16 = mybir.dt.bfloat16
I32 = mybir.dt.int32
ALU = mybir.AluOpType
AF = mybir.ActivationFunctionType
AX = mybir.AxisListType

NCORES = 8
D = 1024
KD = 8
T = 2048
NT = 16
DEPTH = 2
IN_DIM = 2592
D_FF = 2816
NFC = 22
EPS = 1e-6
NEGB = -30000.0
STAGE = 99

SAME_ENGINE_SYNC = True


class Tk:
    __slots__ = ("name", "w", "r", "sem", "cnt", "excl")

    def __init__(self, name, excl=False):
        self.name = name
        self.excl = excl
        self.w = {}
        self.r = {}
        self.sem = None
        self.cnt = 0


class Prog:
    ENG = ("pe", "act", "dve", "pool", "sp")

    def __init__(self, nc):
        self.nc = nc
        self.ops = {e: [] for e in self.ENG}
        self.cnt = {e: 0 for e in self.ENG}
        self.known = {e: {} for e in self.ENG}
        self.esem = {e: nc.alloc_semaphore("sem_" + e) for e in self.ENG}
        self.needed = {e: set() for e in self.ENG}
        self.dsems = []

    def _dsem(self, t):
        if t.sem is None:
            t.sem = self.nc.alloc_semaphore("dsem_%d" % len(self.dsems))
            self.dsems.append(t)
        return t.sem

    def _waits(self, eng, reads, writes, extra=()):
        need = {}
        for t in reads:
            for k, v in t.w.items():
                if need.get(k, 0) < v:
                    need[k] = v
        for t in writes:
            for k, v in t.w.items():
                if need.get(k, 0) < v:
                    need[k] = v
            for k, v in t.r.items():
                if need.get(k, 0) < v:
                    need[k] = v
        for k, v in extra:
            if need.get(k, 0) < v:
                need[k] = v
        out = []
        kn = self.known[eng]
        for k, v in need.items():
            if isinstance(k, str) and k == eng and (eng == "pe" or not SAME_ENGINE_SYNC):
                continue
            if kn.get(k, 0) >= v:
                continue
            kn[k] = v
            if isinstance(k, str):
                self.needed[k].add(v)
            out.append((k, v))
        return out

    def _mark(self, ev, reads, writes, partial):
        k, v = ev
        for t in reads:
            if t.r.get(k, 0) < v:
                t.r[k] = v
        for t in writes:
            if partial:
                if t.w.get(k, 0) < v:
                    t.w[k] = v
            else:
                t.w = {k: v}
                t.r = {}

    def op(self, eng, fn, reads=(), writes=(), partial=False):
        ex = [t for t in reads if t.excl]
        waits = self._waits(eng, reads, list(writes) + ex)
        self.cnt[eng] += 1
        ev = (eng, self.cnt[eng])
        self.ops[eng].append((waits, fn, ("E", eng, self.cnt[eng])))
        self._mark(ev, [t for t in reads if not t.excl], writes, partial)
        if ex:
            self._mark(ev, (), ex, True)

    def dma(self, eng, fn, semt, reads=(), writes=(), n=1, partial=False):
        sem = self._dsem(semt)
        extra = [(sem, semt.cnt)] if semt.cnt else []
        waits = self._waits(eng, reads, writes, extra)
        semt.cnt += 16 * n
        self.ops[eng].append((waits, fn, ("D", sem, 16)))
        self._mark((sem, semt.cnt), reads, writes, partial)

    def barrier(self):
        for e in self.ENG:
            waits = []
            kn = self.known[e]
            for k in self.ENG:
                if k != e and self.cnt[k] > kn.get(k, 0):
                    kn[k] = self.cnt[k]
                    self.needed[k].add(self.cnt[k])
                    waits.append((k, self.cnt[k]))
            for t in self.dsems:
                if t.cnt > kn.get(t.sem, 0):
                    kn[t.sem] = t.cnt
                    waits.append((t.sem, t.cnt))
            if waits:
                self.ops[e].append((waits, None, None))

    def emit(self):
        nc = self.nc
        engobj = {"pe": "tensor", "act": "scalar", "dve": "vector", "pool": "gpsimd", "sp": "sync"}
        rank = {e: {raw: i + 1 for i, raw in enumerate(sorted(self.needed[e]))} for e in self.ENG}
        esem, needed = self.esem, self.needed
        with nc.Block() as block:
            for e in self.ENG:
                ops = self.ops[e]
                if not ops:
                    continue

                def body(eng, ops=ops):
                    for waits, fn, inc in ops:
                        for k, v in waits:
                            if isinstance(k, str):
                                eng.wait_ge(esem[k], rank[k][v])
                            else:
                                eng.wait_ge(k, v)
                        if fn is None:
                            continue
                        ins = fn(eng)
                        if inc[0] == "E":
                            if inc[2] in needed[inc[1]]:
                                ins.then_inc(esem[inc[1]], 1)
                        elif isinstance(ins, (list, tuple)):
                            for i in ins:
                                i.then_inc(inc[1], inc[2])
                        else:
                            ins.then_inc(inc[1], inc[2])
                getattr(block, engobj[e])(body)


class Arena:
    def __init__(self, nc, nbytes):
        self.base = nc.alloc_sbuf_tensor("arena", [128, nbytes // 4], F32).ap()
        self.nbytes = nbytes
        self.off = 0
        self.marks = []

    def alloc(self, shape, dtype, parts=128):
        esz = 4 if dtype in (F32, I32) else 2
        n = 1
        for s in shape:
            n *= s
        nb = (n * esz + 63) // 64 * 64
        assert self.off + nb <= self.nbytes, ("arena overflow", self.off, nb, self.nbytes)
        v = self.base[0:parts, self.off // 4:(self.off + nb) // 4]
        if dtype != F32:
            v = v.bitcast(dtype)
        v = v[:, 0:n]
        self.off += nb
        if len(shape) == 2:
            v = v.rearrange("p (a b) -> p a b", b=shape[1])
        elif len(shape) == 3:
            v = v.rearrange("p (a b c) -> p a b c", b=shape[1], c=shape[2])
        return v

    def mark(self):
        return self.off

    def reset(self, m):
        self.off = m


def _lay_kc(w, kc=KD):
    n = w.shape[1]
    return np.ascontiguousarray(w.reshape(kc, 128, n).transpose(1, 0, 2))


def _col(v, nch):
    return np.ascontiguousarray(v.reshape(nch, 128).T)


def build_program(stage=STAGE):
    nc = bass.Bass("TRN2", target_bir_lowering=False)
    P = Prog(nc)
    dr = {}

    def din(name, shape, dt=F32):
        dr[name] = nc.dram_tensor(name, list(shape), dt, kind="ExternalInput").ap()
        return dr[name]

    def dout(name, shape, dt=F32):
        dr[name] = nc.dram_tensor(name, list(shape), dt, kind="ExternalOutput").ap()
        return dr[name]

    xT_in = din("xT_in", [128, KD, T])
    cT_in = din("cT_in", [128, KD, 5])
    adaw_in = din("adaw", [DEPTH, 6, 128, KD, 1024])
    adab_in = din("adab", [DEPTH, 128, 48])
    n1g_in = din("n1g", [DEPTH, 128, KD])
    n2g_in = din("n2g", [DEPTH, 128, KD])
    fing_in = din("fing", [128, KD])
    wtm_in = din("wtm", [DEPTH, 128, KD, 1312])
    wfm_in = din("wfm", [DEPTH, 20, 128, KD, 128])
    scw_in = din("scw", [DEPTH, 128, 6, 4])
    scb_in = din("scb", [DEPTH, 128, 6])
    ident_in = din("ident", [128, 128])
    dtb_in = din("dtb", [DEPTH, 128, 8])
    alog_in = din("alog", [DEPTH, 128, 8])
    dsk_in = din("dsk", [DEPTH, 128, 8])
    sng_in = din("sng", [DEPTH, 128, 512])
    utri_in = din("utri", [128, 128])
    mtrif_in = din("mtrif", [128, 128])
    ssm_out = dout("ssm_out", [DEPTH, 128, 4, 64])
    NPOOL = 2560
    ptb_in = din("ptb", [128, 256], I32)
    pidx_in = din("pidx", [128, 256], I32)
    cmpkT_pool = [din("cmpkT_pool%d" % l, [NPOOL * 128, 128]) for l in range(DEPTH)]
    cmpvT_pool = [din("cmpvT_pool%d" % l, [NPOOL * 128, 128]) for l in range(DEPTH)]
    slckT_pool = [din("slckT_pool%d" % l, [NPOOL * 128, 128]) for l in range(DEPTH)]
    slcv_pool = [din("slcv_pool%d" % l, [NPOOL * 128, 128]) for l in range(DEPTH)]
    wkT_in = din("wkT_in", [DEPTH, 4, 128, 512])
    augL_in = din("augL", [128, 2, 128])
    augR_in = din("augR", [128, 2, 296])
    mcs_in = din("mcs", [128, 296])
    ovs_in = din("ovs", [128, 4, 130])
    bonus_s_in = din("bonus_s", [128, 129])
    idx_s_in = din("idx_s", [128, 129], I32)
    oneo_in = din("oneo", [128, 2, 128])
    fst_in = din("fst_in", [DEPTH, 128, 24, 2, 4])
    fcs_out = dout("fcs_out", [DEPTH, 128, 24, 2, 4])
    ysT_out = dout("ysT_out", [128, KD, 4])
    xsT_in = din("xsT_in", [128, KD, 4])
    cst_in = din("cst_in", [DEPTH, 128, 6, 3, 4])
    hs0_in = din("hs0_in", [DEPTH, 128, 4, 4, 64])
    hcol_in = din("hcol", [DEPTH, 8, 3])
    dskc_in = din("dskc", [DEPTH, 128, 4])
    sngc_in = din("sngc", [DEPTH, 128, 4])
    ex_in = din("ex", [8, 4, 128])
    wk_in = din("wk_in", [DEPTH, 4, 512, 128])
    wv_in = din("wv_in", [DEPTH, 4, 512, 128])
    wk_out = dout("wk_out", [DEPTH, 4, 512, 128])
    wv_out = dout("wv_out", [DEPTH, 4, 512, 128])
    kvn_out = dout("kvn_out", [DEPTH, 128, 6, 4])
    sconvs_out = dout("sconvs_out", [DEPTH, 128, 6, 3, 4])
    ssms_out = dout("ssms_out", [DEPTH, 128, 4, 4, 64])
    wo_in = din("wo", [DEPTH, 128, KD, D])
    wup_in = din("wup", [DEPTH, 6, 128, KD, 2, 512])
    wdn_in = din("wdn", [DEPTH, 128, 24, D])
    fcw_in = din("fcw", [DEPTH, 128, 24, 3])
    fcb_in = din("fcb", [DEPTH, 128, 24])
    fconv_out = dout("fconv_out", [DEPTH, 128, NFC, 2])
    w1d_in = din("w1d", [DEPTH, 2, 128, 32, 128])
    w2k_in = din("w2k", [DEPTH, 128, 128])
    w2v_in = din("w2v", [DEPTH, 128, 64])
    cb1_in = din("cb1", [DEPTH, 128, 2])
    mtri_in = din("mtri", [128, 128])
    manti_in = din("manti", [128, 128])
    cmpmask_in = din("cmpmask", [128, T])
    eall_in = din("eall", [128, T])
    kaug_in = din("kaug", [128, 32, 128])
    qaug_in = din("qaug", [128, 8, 128])
    ov_in = din("ov", [128, 33])
    bonus_in = din("bonus", [128, NT, 32])
    cap_in = din("cap", [128, NT, 32])
    idxb_in = din("idxb", [128, 32], I32)

    kv_out = dout("kv_out", [DEPTH, T, 768])
    sconv_out = dout("sconv_out", [DEPTH, 128, 6, 3])
    yT_out = dout("yT_out", [128, KD, T])

    banks = [nc.alloc_psum_tensor("bank%d" % i, [128, 512], F32).ap() for i in range(8)]
    tb = [Tk("bank%d" % i, excl=True) for i in range(8)]

    AR = Arena(nc, 190 * 1024)
    identf = AR.alloc([128], F32)
    identb = AR.alloc([128], BF16)
    onesb = AR.alloc([128], BF16)
    t_const = Tk("const")
    P.dma("sp", lambda e: e.dma_start(out=identf, in_=ident_in), t_const, writes=[t_const])
    P.op("dve", lambda e: e.tensor_copy(out=identb, in_=identf), reads=[t_const], writes=[t_const], partial=True)
    P.op("dve", lambda e: e.memset(onesb, 1.0), writes=[t_const], partial=True)

    utri = AR.alloc([128], F32)
    mtrif = AR.alloc([128], F32)
    onesf = AR.alloc([128], F32)
    P.dma("sp", lambda e: e.dma_start(out=utri, in_=utri_in), t_const, writes=[t_const], partial=True)
    P.dma("sp", lambda e: e.dma_start(out=mtrif, in_=mtrif_in), t_const, writes=[t_const], partial=True)
    P.op("dve", lambda e: e.memset(onesf, 1.0), writes=[t_const], partial=True)
    mods = AR.alloc([DEPTH * 48, 5], F32)
    t_mods = Tk("mods")
    smallv = AR.alloc([64], F32)
    t_small = Tk("small")
    n1g = [smallv[:, 0 + 8 * l: 8 + 8 * l] for l in range(DEPTH)]
    n2g = [smallv[:, 16 + 8 * l: 24 + 8 * l] for l in range(DEPTH)]
    fing = smallv[:, 32:40]
    for l in range(DEPTH):
        P.dma("sp", lambda e, l=l: e.dma_start(out=n1g[l], in_=n1g_in[l]), t_small, writes=[t_small], partial=True)
        P.dma("sp", lambda e, l=l: e.dma_start(out=n2g[l], in_=n2g_in[l]), t_small, writes=[t_small], partial=True)
    P.dma("sp", lambda e: e.dma_start(out=fing, in_=fing_in), t_small, writes=[t_small], partial=True)
    AB = AR.alloc([DEPTH * 2 * 2, 8], F32)
    t_AB = Tk("AB")
    base_mark = AR.mark()

    m0 = AR.mark()
    cT = AR.alloc([KD, 5], F32)
    cTb = AR.alloc([KD, 5], BF16)
    adab = AR.alloc([DEPTH, 48], F32)
    t_c = Tk("c")
    t_adab = Tk("adab")
    P.dma("sp", lambda e: e.dma_start(out=cT, in_=cT_in), t_c, writes=[t_c])
    P.op("act", lambda e: e.activation(out=cTb, in_=cT, func=AF.Silu), reads=[t_c], writes=[t_c], partial=True)
    for l in range(DEPTH):
        P.dma("sp", lambda e, l=l: e.dma_start(out=adab[:, l, :], in_=adab_in[l]), t_adab, writes=[t_adab], partial=True)
    wa = [AR.alloc([KD, 1024], BF16) for _ in range(2)]
    t_wa = [Tk("wa0"), Tk("wa1")]
    it = 0
    for l in range(DEPTH):
        bk = l % 2
        for s in range(6):
            sl = it % 2
            it += 1
            P.dma("pool", lambda e, l=l, s=s, sl=sl: e.dma_start(out=wa[sl], in_=adaw_in[l, s]), t_wa[sl], writes=[t_wa[sl]])
            for c8 in range(8):
                c = s * 8 + c8
                for kc in range(KD):
                    P.op("pe", lambda e, bk=bk, c=c, c8=c8, kc=kc, sl=sl: e.matmul(
                        banks[bk][:, c * 5:(c + 1) * 5], lhsT=wa[sl][:, kc, c8 * 128:(c8 + 1) * 128], rhs=cTb[:, kc, :],
                        start=(kc == 0), stop=(kc == KD - 1)), reads=[t_wa[sl], t_c], writes=[tb[bk]], partial=True)
        for r in range(5):
            P.op("dve", lambda e, l=l, r=r, bk=bk: e.tensor_tensor(
                out=mods[:, l * 48:(l + 1) * 48, r], in0=banks[bk][:, 0:240].rearrange("p (c r) -> p c r", r=5)[:, :, r],
                in1=adab[:, l, :], op=ALU.add), reads=[tb[bk], t_adab], writes=[t_mods], partial=True)
        for j, (gv, sci, shi) in enumerate(((n1g[l], 1, 0), (n2g[l], 4, 3))):
            P.op("dve", lambda e, l=l, j=j, gv=gv, sci=sci: e.scalar_tensor_tensor(
                out=AB[:, l * 4 + 2 * j, :], in0=mods[:, l * 48 + sci * 8: l * 48 + sci * 8 + 8, 0], scalar=1.0, in1=gv,
                op0=ALU.add, op1=ALU.mult), reads=[t_mods, t_small], writes=[t_AB], partial=True)
            P.op("dve", lambda e, l=l, j=j, shi=shi: e.tensor_copy(
                out=AB[:, l * 4 + 2 * j + 1, :], in_=mods[:, l * 48 + shi * 8: l * 48 + shi * 8 + 8, 0]),
                reads=[t_mods], writes=[t_AB], partial=True)
    P.barrier()
    AR.reset(m0)

    xsT = AR.alloc([KD, 4], F32)
    hsT = AR.alloc([KD, 4], BF16)
    osT = AR.alloc([KD, 4], BF16)
    kvnT = AR.alloc([6, 4], F32)
    QgT = AR.alloc([4, 4], BF16)
    xbcs = AR.alloc([6, 4], F32)
    zsT = AR.alloc([4, 4], F32)
    dd = AR.alloc([8], F32)
    gsT = AR.alloc([24], F32)
    VN = AR.alloc([4, 2, 130], BF16)
    exm = AR.alloc([4, 128], F32)
    t_xs, t_hs, t_os, t_kvn, t_Qg, t_xbcs, t_zsT, t_dd, t_gs, t_VN, t_ex = (Tk(n) for n in
        ("xs", "hs", "os", "kvn", "Qg", "xbcs", "zsT", "dd", "gs", "VN", "ex"))
    P.dma("sp", lambda e: e.dma_start(out=xsT, in_=xsT_in), t_xs, writes=[t_xs])
    P.dma("sp", lambda e: e.dma_start(out=exm[0:8, :, :], in_=ex_in), t_ex, writes=[t_ex])
    t_win = Tk("win")

    def sample_norm(l, sci, shi, gvec):
        m = AR.mark()
        t1 = AR.alloc([KD, 4], F32)
        As = AR.alloc([KD, 4], F32)
        rs4 = AR.alloc([4], F32)
        t_t1, t_As, t_rs4 = Tk("st1"), Tk("sAs"), Tk("srs")
        P.op("dve", lambda e: e.tensor_tensor(out=t1, in0=xsT, in1=xsT, op=ALU.mult), reads=[t_xs], writes=[t_t1])
        for kc in range(KD):
            P.op("pe", lambda e, kc=kc: e.matmul(banks[0][:, 0:4], lhsT=onesf, rhs=t1[:, kc, :], start=(kc == 0), stop=(kc == KD - 1)),
                 reads=[t_t1, t_const], writes=[tb[0]], partial=(kc > 0))
        P.op("dve", lambda e: e.tensor_scalar(out=rs4, in0=banks[0][:, 0:4], scalar1=1.0 / D, scalar2=EPS, op0=ALU.mult, op1=ALU.add),
             reads=[tb[0]], writes=[t_rs4])
        P.op("act", lambda e: e.activation(out=rs4, in_=rs4, func=AF.Sqrt), reads=[t_rs4], writes=[t_rs4])
        P.op("dve", lambda e: e.reciprocal(out=rs4, in_=rs4), reads=[t_rs4], writes=[t_rs4])
        for r in range(4):
            P.op("dve", lambda e, r=r: e.scalar_tensor_tensor(
                out=As[:, :, r], in0=mods[:, l * 48 + sci * 8: l * 48 + sci * 8 + 8, 1 + r], scalar=1.0, in1=gvec, op0=ALU.add, op1=ALU.mult),
                reads=[t_mods, t_small], writes=[t_As], partial=(r > 0))
        P.op("dve", lambda e: e.tensor_tensor(out=t1, in0=xsT, in1=As, op=ALU.mult), reads=[t_xs, t_As], writes=[t_t1])
        for kc in range(KD):
            P.op("dve", lambda e, kc=kc: e.tensor_tensor(out=t1[:, kc, :], in0=t1[:, kc, :], in1=rs4, op=ALU.mult), reads=[t_t1, t_rs4], writes=[t_t1], partial=True)
        P.op("dve", lambda e: e.tensor_tensor(out=hsT, in0=t1, in1=mods[:, l * 48 + shi * 8: l * 48 + shi * 8 + 8, 1:5], op=ALU.add),
             reads=[t_t1, t_mods], writes=[t_hs])
        AR.reset(m)

    hT = AR.alloc([KD, T], BF16)
    t_h = Tk("hT")
    layer_mark = AR.mark()
    xT = AR.alloc([KD, T], F32)
    t_x = Tk("xT")
    xscr = nc.dram_tensor("xscr", [128, KD, T], F32, kind="Internal").ap()
    t_xscr = Tk("xscr")

    def load_x(src):
        for kc in range(KD):
            P.dma("sp", lambda e, kc=kc, src=src: e.dma_start(out=xT[:, kc, :], in_=src[:, kc, :]), t_x,
                  reads=[t_xscr], writes=[t_x], partial=True)

    def norm_mod(Acol, Bcol, scratch_banks):
        m = AR.mark()
        sq = AR.alloc([KD, 512], BF16)
        rs = AR.alloc([512], F32)
        tmp = AR.alloc([2, 512], F32)
        t_sq, t_rs, t_tmp = Tk("sq"), Tk("rs"), [Tk("tmp0"), Tk("tmp1")]
        for tg in range(4):
            ts = slice(tg * 512, (tg + 1) * 512)
            bk = scratch_banks[tg % len(scratch_banks)]
            P.op("act", lambda e, ts=ts: e.activation(out=sq, in_=xT[:, :, ts], func=AF.Square), reads=[t_x], writes=[t_sq])
            for kc in range(KD):
                P.op("pe", lambda e, kc=kc, bk=bk: e.matmul(banks[bk], lhsT=onesb, rhs=sq[:, kc, :], start=(kc == 0), stop=(kc == KD - 1)),
                     reads=[t_sq, t_const], writes=[tb[bk]], partial=(kc > 0))
            P.op("dve", lambda e, bk=bk: e.tensor_scalar(out=rs, in0=banks[bk], scalar1=1.0 / D, scalar2=EPS, op0=ALU.mult, op1=ALU.add),
                 reads=[tb[bk]], writes=[t_rs])
            P.op("act", lambda e: e.activation(out=rs, in_=rs, func=AF.Sqrt), reads=[t_rs], writes=[t_rs])
            P.op("dve", lambda e: e.reciprocal(out=rs, in_=rs), reads=[t_rs], writes=[t_rs])
            for kc in range(KD):
                tt = kc % 2
                P.op("dve", lambda e, kc=kc, ts=ts, tt=tt: e.scalar_tensor_tensor(
                    out=tmp[:, tt, :], in0=xT[:, kc, ts], scalar=Acol[:, kc:kc + 1], in1=rs, op0=ALU.mult, op1=ALU.mult),
                    reads=[t_x, t_rs, t_AB], writes=[t_tmp[tt]])
                P.op("act", lambda e, kc=kc, ts=ts, tt=tt: e.activation(
                    out=hT[:, kc, ts], in_=tmp[:, tt, :], func=AF.Identity, bias=Bcol[:, kc:kc + 1], scale=1.0),
                    reads=[t_tmp[tt], t_AB], writes=[t_h], partial=True)
        AR.reset(m)

    for l in range(DEPTH):
        AR.reset(layer_mark)
        xsrc = xT_in if l == 0 else xscr
        AR.alloc([KD, T], F32)
        A1, B1, A2, B2 = (AB[:, l * 4 + j, :] for j in range(4))
        load_x(xsrc)
        norm_mod(A1, B1, [0, 1])
        P.barrier()
        sample_norm(l, 1, 0, n1g[l])
        P.barrier()
        AR.reset(layer_mark)
        Vall = AR.alloc([NT, 4, 65], BF16)
        zs = AR.alloc([NT, 512], BF16)
        gat = AR.alloc([NT, 24], F32)
        dtt = AR.alloc([NT, 8], F32)
        aa = AR.alloc([NT, 8], F32)
        lay = AR.alloc([4, 8], F32)
        sng = AR.alloc([512], F32)
        QT = AR.alloc([4, T], BF16)
        kvcT = AR.alloc([2, T], BF16)
        KsT = AR.alloc([2, T], BF16)
        KwT = AR.alloc([2, T], BF16)
        xbcT = AR.alloc([6, T], BF16)
        mA = AR.mark()
        wtm = AR.alloc([KD, 1312], BF16)
        t_wtm = Tk("wtm")
        for kc in range(KD):
            P.dma("pool", lambda e, l=l, kc=kc: e.dma_start(out=wtm[:, kc, :], in_=wtm_in[l, :, kc, :]), t_wtm, writes=[t_wtm], partial=True)
        kvst = [AR.alloc([768], F32) for _ in range(2)]
        t_kvst = [Tk("kvst0"), Tk("kvst1")]
        for tt in range(NT):
            tsl = slice(tt * 128, (tt + 1) * 128)
            sl = tt % 2
            for half in range(2):
                bk = 2 + (tt * 2 + half) % 4
                ncol = 512 if half == 0 else 256
                c0 = half * 512
                for kc in range(KD):
                    P.op("pe", lambda e, bk=bk, kc=kc, tsl=tsl, c0=c0, ncol=ncol: e.matmul(
                        banks[bk][:, 0:ncol], lhsT=hT[:, kc, tsl], rhs=wtm[:, kc, c0:c0 + ncol], start=(kc == 0), stop=(kc == KD - 1)),
                        reads=[t_h, t_wtm], writes=[tb[bk]], partial=(kc > 0))
                P.op("act", lambda e, bk=bk, sl=sl, c0=c0, ncol=ncol: e.copy(out=kvst[sl][:, c0:c0 + ncol], in_=banks[bk][:, 0:ncol]),
                     reads=[tb[bk]], writes=[t_kvst[sl]], partial=(half > 0))
            P.dma("sp", lambda e, l=l, tsl=tsl, sl=sl: e.dma_start(out=kv_out[l, tsl, :], in_=kvst[sl]), t_kvst[sl], reads=[t_kvst[sl]])
        t_V, t_zs, t_gat, t_dt, t_lay, t_Q, t_kvc, t_Ks, t_Kw, t_xbc = (Tk(n) for n in
            ("V", "zs", "gat", "dt", "lay", "Q", "kvc", "Ks", "Kw", "xbc"))
        mA2 = AR.mark()
        P.op("dve", lambda e: e.memset(Vall, 1.0), writes=[t_V])
        P.dma("sp", lambda e, l=l: e.dma_start(out=lay[:, 0, :], in_=dtb_in[l]), t_lay, writes=[t_lay], partial=True)
        P.dma("sp", lambda e, l=l: e.dma_start(out=lay[:, 1, :], in_=alog_in[l]), t_lay, writes=[t_lay], partial=True)
        P.dma("sp", lambda e, l=l: e.dma_start(out=lay[:, 2, :], in_=dsk_in[l]), t_lay, writes=[t_lay], partial=True)
        P.dma("sp", lambda e, l=l: e.dma_start(out=sng, in_=sng_in[l]), t_lay, writes=[t_lay], partial=True)
        P.op("act", lambda e: e.activation(out=lay[:, 1, :], in_=lay[:, 1, :], func=AF.Exp), reads=[t_lay], writes=[t_lay], partial=True)
        P.op("dve", lambda e: e.tensor_scalar(out=lay[:, 1, :], in0=lay[:, 1, :], scalar1=-1.0, scalar2=None, op0=ALU.mult),
             reads=[t_lay], writes=[t_lay], partial=True)
        for tt in range(NT):
            tsl = slice(tt * 128, (tt + 1) * 128)
            b1 = 2 + (tt % 2) * 2
            b2 = b1 + 1
            for kc in range(KD):
                P.op("pe", lambda e, b1=b1, kc=kc, tsl=tsl: e.matmul(
                    banks[b1][:, 0:128], lhsT=hT[:, kc, tsl], rhs=wtm[:, kc, 384:512], start=(kc == 0), stop=(kc == KD - 1)),
                    reads=[t_h, t_wtm], writes=[tb[b1]], partial=(kc > 0))
            for kc in range(KD):
                P.op("pe", lambda e, b1=b1, kc=kc, tsl=tsl: e.matmul(
                    banks[b1][:, 128:256], lhsT=hT[:, kc, tsl], rhs=wtm[:, kc, 640:768], start=(kc == 0), stop=(kc == KD - 1)),
                    reads=[t_h, t_wtm], writes=[tb[b1]], partial=True)
            for kc in range(KD):
                P.op("pe", lambda e, b1=b1, kc=kc, tsl=tsl: e.matmul(
                    banks[b1][:, 256:280], lhsT=hT[:, kc, tsl], rhs=wtm[:, kc, 768:792], start=(kc == 0), stop=(kc == KD - 1)),
                    reads=[t_h, t_wtm], writes=[tb[b1]], partial=True)
            for kc in range(KD):
                P.op("pe", lambda e, b1=b1, kc=kc, tsl=tsl: e.matmul(
                    banks[b1][:, 280:288], lhsT=hT[:, kc, tsl], rhs=wtm[:, kc, 1304:1312], start=(kc == 0), stop=(kc == KD - 1)),
                    reads=[t_h, t_wtm], writes=[tb[b1]], partial=True)
            P.op("act", lambda e, b1=b1, tt=tt: e.copy(out=Vall[:, tt, :, 0:64], in_=banks[b1][:, 0:256].rearrange("p (a b) -> p a b", b=64)),
                 reads=[tb[b1]], writes=[t_V], partial=True)
            P.op("act", lambda e, b1=b1, tt=tt: e.activation(out=gat[:, tt, :], in_=banks[b1][:, 256:280], func=AF.Sigmoid),
                 reads=[tb[b1]], writes=[t_gat], partial=True)
            P.op("dve", lambda e, b1=b1, tt=tt: e.tensor_tensor(out=dtt[:, tt, :], in0=banks[b1][:, 280:288], in1=lay[:, 0, :], op=ALU.add),
                 reads=[tb[b1], t_lay], writes=[t_dt], partial=True)
            for kc in range(KD):
                P.op("pe", lambda e, b2=b2, kc=kc, tsl=tsl: e.matmul(
                    banks[b2], lhsT=hT[:, kc, tsl], rhs=wtm[:, kc, 792:1304], start=(kc == 0), stop=(kc == KD - 1)),
                    reads=[t_h, t_wtm], writes=[tb[b2]], partial=(kc > 0))
            P.op("act", lambda e, b2=b2, tt=tt: e.activation(out=zs[:, tt, :], in_=banks[b2], func=AF.Silu),
                 reads=[tb[b2]], writes=[t_zs], partial=True)
        def smm(bk, prt, cols, lhs_fn, rhs_fn, reads):
            for kc in range(KD):
                P.op("pe", lambda e, kc=kc: e.matmul(banks[bk][prt, cols], lhsT=lhs_fn(kc), rhs=rhs_fn(kc), start=(kc == 0), stop=(kc == KD - 1)),
                     reads=reads, writes=[tb[bk]], partial=True)
        for c6 in range(6):
            smm(0, slice(0, 128), slice(c6 * 4, c6 * 4 + 4), lambda kc, c6=c6: wtm[:, kc, c6 * 128:(c6 + 1) * 128], lambda kc: hsT[:, kc, :], [t_wtm, t_hs])
        for c4 in range(4):
            smm(0, slice(0, 128), slice(24 + c4 * 4, 28 + c4 * 4), lambda kc, c4=c4: wtm[:, kc, 792 + c4 * 128:792 + (c4 + 1) * 128], lambda kc: hsT[:, kc, :], [t_wtm, t_hs])
        smm(0, slice(0, 8), slice(40, 44), lambda kc: wtm[:, kc, 1304:1312], lambda kc: hsT[:, kc, :], [t_wtm, t_hs])
        for bg in range(6):
            smm(0, slice(0, 4), slice(44 + bg * 4, 48 + bg * 4), lambda kc, bg=bg: wtm[:, kc, 768 + bg * 4:772 + bg * 4], lambda kc: hsT[:, kc, :], [t_wtm, t_hs])
        for b in range(4):
            smm(1, slice(0, 1), slice(b * 128, (b + 1) * 128), lambda kc, b=b: hsT[:, kc, b:b + 1], lambda kc: wtm[:, kc, 384:512], [t_wtm, t_hs])
        for b in range(4):
            smm(6, slice(0, 1), slice(b * 128, (b + 1) * 128), lambda kc, b=b: hsT[:, kc, b:b + 1], lambda kc: wtm[:, kc, 640:768], [t_wtm, t_hs])
        P.op("act", lambda e: e.copy(out=kvnT, in_=banks[0][:, 0:24].rearrange("p (a b) -> p a b", b=4)), reads=[tb[0]], writes=[t_kvn])
        P.op("act", lambda e: e.activation(out=zsT, in_=banks[0][:, 24:40].rearrange("p (a b) -> p a b", b=4), func=AF.Silu), reads=[tb[0]], writes=[t_zsT])
        P.op("act", lambda e: e.activation(out=gsT[0:4, :], in_=banks[0][0:4, 44:68], func=AF.Sigmoid), reads=[tb[0]], writes=[t_gs])
        hcol = AR.alloc([3], F32)
        t_hcol = Tk("hcol")
        P.dma("sp", lambda e, l=l: e.dma_start(out=hcol[0:8, :], in_=hcol_in[l]), t_hcol, writes=[t_hcol])
        P.op("act", lambda e: e.activation(out=dd[0:8, 0:4], in_=banks[0][0:8, 40:44], func=AF.Exp, bias=hcol[0:8, 0:1], scale=1.0), reads=[tb[0], t_hcol], writes=[t_dd])
        P.op("dve", lambda e: e.tensor_scalar(out=dd[0:8, 0:4], in0=dd[0:8, 0:4], scalar1=1.0, scalar2=None, op0=ALU.add), reads=[t_dd], writes=[t_dd])
        P.op("act", lambda e: e.activation(out=dd[0:8, 0:4], in_=dd[0:8, 0:4], func=AF.Ln), reads=[t_dd], writes=[t_dd])
        P.op("act", lambda e: e.activation(out=hcol[0:8, 1:2], in_=hcol[0:8, 1:2], func=AF.Exp), reads=[t_hcol], writes=[t_hcol])
        P.op("dve", lambda e: e.tensor_scalar(out=dd[0:8, 4:8], in0=dd[0:8, 0:4], scalar1=hcol[0:8, 1:2], scalar2=-1.0, op0=ALU.mult, op1=ALU.mult), reads=[t_dd, t_hcol], writes=[t_dd])
        P.op("act", lambda e: e.activation(out=dd[0:8, 4:8], in_=dd[0:8, 4:8], func=AF.Exp), reads=[t_dd], writes=[t_dd])
        P.op("dve", lambda e: e.memset(VN[0:1], 1.0), writes=[t_VN])
        P.op("act", lambda e: e.copy(out=VN[0:1, :, 0, :].rearrange("p b (g x) -> p b g x", g=2)[:, :, :, 0:64], in_=banks[1][0:1, 0:512].rearrange("p (b g d) -> p b g d", b=4, g=2)),
             reads=[tb[1]], writes=[t_VN], partial=True)
        P.op("act", lambda e: e.copy(out=VN[0:1, :, 1, :].rearrange("p b (g x) -> p b g x", g=2)[:, :, :, 0:64], in_=banks[6][0:1, 0:512].rearrange("p (b g d) -> p b g d", b=4, g=2)),
             reads=[tb[6]], writes=[t_VN], partial=True)
        P.dma("sp", lambda e, l=l: e.dma_start(out=kvn_out[l], in_=kvnT), t_kvn, reads=[t_kvn])
        for b in range(4):
            P.dma("sp", lambda e, l=l, b=b: e.dma_start(out=wk_out[l, b, 0:511, :], in_=wk_in[l, b, 1:512, :]), t_win, writes=[t_win], partial=True)
            P.dma("sp", lambda e, l=l, b=b: e.dma_start(out=wv_out[l, b, 0:511, :], in_=wv_in[l, b, 1:512, :]), t_win, writes=[t_win], partial=True)
            P.dma("sp", lambda e, l=l, b=b: e.dma_start(out=wk_out[l, b, 511:512, :].rearrange("o f -> f o"), in_=kvnT[:, 4, b:b + 1]), t_kvn, reads=[t_kvn])
            P.dma("sp", lambda e, l=l, b=b: e.dma_start(out=wv_out[l, b, 511:512, :].rearrange("o f -> f o"), in_=kvnT[:, 5, b:b + 1]), t_kvn, reads=[t_kvn])
        P.op("act", lambda e: e.activation(out=dtt, in_=dtt, func=AF.Exp), reads=[t_dt], writes=[t_dt])
        P.op("dve", lambda e: e.tensor_scalar(out=dtt, in0=dtt, scalar1=1.0, scalar2=None, op0=ALU.add), reads=[t_dt], writes=[t_dt])
        P.op("act", lambda e: e.activation(out=dtt, in_=dtt, func=AF.Ln), reads=[t_dt], writes=[t_dt])
        for tt in range(NT):
            P.op("dve", lambda e, tt=tt: e.tensor_tensor(out=aa[:, tt, :], in0=dtt[:, tt, :], in1=lay[:, 1, :], op=ALU.mult),
                 reads=[t_dt, t_lay], writes=[t_dt], partial=True)
        wch = [AR.alloc([KD, 128], BF16) for _ in range(3)]
        t_wch = [Tk("wch%d" % i) for i in range(3)]
        stg = [AR.alloc([515], F32) for _ in range(2)]
        t_stg = [Tk("stg0"), Tk("stg1")]
        cacc = [AR.alloc([512], F32) for _ in range(2)]
        t_cacc = [Tk("cacc0"), Tk("cacc1")]
        scw = AR.alloc([6, 4], F32)
        scb = AR.alloc([6], F32)
        t_sc = Tk("sc")
        P.dma("sp", lambda e, l=l: e.dma_start(out=scw, in_=scw_in[l]), t_sc, writes=[t_sc], partial=True)
        P.dma("sp", lambda e, l=l: e.dma_start(out=scb, in_=scb_in[l]), t_sc, writes=[t_sc], partial=True)
        nb = 0
        for c in range(20):
            sl = c % 3
            P.dma("pool", lambda e, l=l, c=c, sl=sl: e.dma_start(out=wch[sl], in_=wfm_in[l, c]), t_wch[sl], writes=[t_wch[sl]])
            if c >= 10:
                scol = slice((c - 10) * 4, (c - 10) * 4 + 4)
                sbk = 0 if c < 16 else 1
                if c >= 16:
                    scol = slice((c - 16) * 4, (c - 16) * 4 + 4)
                for kc in range(KD):
                    P.op("pe", lambda e, kc=kc, sl=sl, sbk=sbk, scol=scol: e.matmul(banks[sbk][:, scol], lhsT=wch[sl][:, kc, :], rhs=hsT[:, kc, :], start=(kc == 0), stop=(kc == KD - 1)),
                         reads=[t_wch[sl], t_hs], writes=[tb[sbk]], partial=True)
            if c >= 16:
                continue
            for tg in range(4):
                ts = slice(tg * 512, (tg + 1) * 512)
                bk = 4 + nb % 4
                nb += 1
                for kc in range(KD):
                    P.op("pe", lambda e, bk=bk, kc=kc, sl=sl, ts=ts: e.matmul(
                        banks[bk], lhsT=wch[sl][:, kc, :], rhs=hT[:, kc, ts], start=(kc == 0), stop=(kc == KD - 1)),
                        reads=[t_h, t_wch[sl]], writes=[tb[bk]], partial=(kc > 0))
                if c < 4:
                    P.op("act", lambda e, bk=bk, c=c, ts=ts: e.activation(out=QT[:, c, ts], in_=banks[bk], func=AF.Copy, scale=0.125),
                         reads=[tb[bk]], writes=[t_Q], partial=True)
                elif c < 6:
                    P.op("act", lambda e, bk=bk, c=c, ts=ts: e.copy(out=kvcT[:, c - 4, ts], in_=banks[bk]),
                         reads=[tb[bk]], writes=[t_kvc], partial=True)
                elif c < 8:
                    P.op("act", lambda e, bk=bk, c=c, ts=ts: e.copy(out=KsT[:, c - 6, ts], in_=banks[bk]),
                         reads=[tb[bk]], writes=[t_Ks], partial=True)
                elif c < 10:
                    P.op("act", lambda e, bk=bk, c=c, ts=ts: e.copy(out=KwT[:, c - 8, ts], in_=banks[bk]),
                         reads=[tb[bk]], writes=[t_Kw], partial=True)
                else:
                    cc = c - 10
                    si = tg % 2
                    if tg == 0:
                        P.op("dve", lambda e, si=si: e.memset(stg[si][:, 0:3], 0.0), writes=[t_stg[si]], partial=True)
                    P.op("act", lambda e, bk=bk, si=si: e.copy(out=stg[si][:, 3:515], in_=banks[bk]),
                         reads=[tb[bk]], writes=[t_stg[si]], partial=True)
                    P.op("dve", lambda e, si=si, cc=cc: e.tensor_scalar(
                        out=cacc[si], in0=stg[si][:, 3:515], scalar1=scw[:, cc, 3:4], scalar2=scb[:, cc:cc + 1], op0=ALU.mult, op1=ALU.add),
                        reads=[t_stg[si], t_sc], writes=[t_cacc[si]])
                    for k in range(3):
                        P.op("dve", lambda e, si=si, cc=cc, k=k: e.scalar_tensor_tensor(
                            out=cacc[si], in0=stg[si][:, k:k + 512], scalar=scw[:, cc, k:k + 1], in1=cacc[si], op0=ALU.mult, op1=ALU.add),
                            reads=[t_stg[si], t_sc], writes=[t_cacc[si]])
                    P.op("act", lambda e, si=si, cc=cc, ts=ts: e.activation(out=xbcT[:, cc, ts], in_=cacc[si], func=AF.Silu),
                         reads=[t_cacc[si]], writes=[t_xbc], partial=True)
                    if tg < 3:
                        P.op("dve", lambda e, si=si: e.tensor_copy(out=stg[1 - si][:, 0:3], in_=stg[si][:, 512:515]),
                             reads=[t_stg[si]], writes=[t_stg[1 - si]], partial=True)
                    else:
                        P.dma("sp", lambda e, l=l, si=si, cc=cc: e.dma_start(out=sconv_out[l, :, cc, :], in_=stg[si][:, 512:515]),
                              t_stg[si], reads=[t_stg[si]])
        P.op("act", lambda e: e.activation(out=QgT, in_=banks[1][:, 0:16].rearrange("p (a b) -> p a b", b=4), func=AF.Copy, scale=0.125), reads=[tb[1]], writes=[t_Qg])
        cst = AR.alloc([6, 4, 4], F32)
        cst_o = AR.alloc([6, 3, 4], F32)
        cac = AR.alloc([6, 4], F32)
        t_cst, t_cso, t_cac = Tk("cst"), Tk("cso"), Tk("cac")
        P.dma("sp", lambda e, l=l: e.dma_start(out=cst[:, :, 0:3, :], in_=cst_in[l]), t_cst, writes=[t_cst])
        P.op("act", lambda e: e.copy(out=cst[:, :, 3, :], in_=banks[0][:, 0:24].rearrange("p (a b) -> p a b", b=4)), reads=[tb[0]], writes=[t_cst], partial=True)
        P.op("dve", lambda e: e.tensor_copy(out=cst_o, in_=cst[:, :, 1:4, :]), reads=[t_cst], writes=[t_cso])
        P.dma("sp", lambda e, l=l: e.dma_start(out=sconvs_out[l], in_=cst_o), t_cso, reads=[t_cso])
        for cc in range(6):
            P.op("dve", lambda e, cc=cc: e.tensor_scalar(out=cac[:, cc, :], in0=cst[:, cc, 3, :], scalar1=scw[:, cc, 3:4], scalar2=scb[:, cc:cc + 1], op0=ALU.mult, op1=ALU.add),
                 reads=[t_cst, t_sc], writes=[t_cac], partial=(cc > 0))
            for k in range(3):
                P.op("dve", lambda e, cc=cc, k=k: e.scalar_tensor_tensor(out=cac[:, cc, :], in0=cst[:, cc, k, :], scalar=scw[:, cc, k:k + 1], in1=cac[:, cc, :], op0=ALU.mult, op1=ALU.add),
                     reads=[t_cst, t_sc, t_cac], writes=[t_cac], partial=True)
        P.op("act", lambda e: e.activation(out=xbcs, in_=cac, func=AF.Silu), reads=[t_cac], writes=[t_xbcs])
        P.barrier()
        AR.reset(mA)
        if stage <= 2:
            break
        mSS = AR.mark()
        hSs = AR.alloc([4, 4, 64], F32)
        dx = AR.alloc([4, 8], F32)
        BCtm = AR.alloc([256], F32)
        BCbd = AR.alloc([4, 256], F32)
        BC = AR.alloc([4, 2, 128], F32)
        us = AR.alloc([4, 4], F32)
        ys = AR.alloc([4, 4], F32)
        y2s = AR.alloc([4, 4], F32)
        stmp = AR.alloc([2, 64], F32)
        ssg = AR.alloc([2, 4], F32)
        colv = AR.alloc([8], F32)
        t_hSs, t_dx, t_BCtm, t_BCbd, t_BC, t_us, t_ys, t_y2s, t_stmp, t_ssg, t_colv = (Tk(n) for n in
            ("hSs", "dx", "BCtm", "BCbd", "BC", "us", "ys", "y2s", "stmp", "ssg", "colv"))
        P.dma("sp", lambda e, l=l: e.dma_start(out=hSs, in_=hs0_in[l]), t_hSs, writes=[t_hSs])
        P.dma("sp", lambda e, l=l: e.dma_start(out=colv[:, 0:4], in_=dskc_in[l]), t_colv, writes=[t_colv], partial=True)
        P.dma("sp", lambda e, l=l: e.dma_start(out=colv[:, 4:8], in_=sngc_in[l]), t_colv, writes=[t_colv], partial=True)
        for c in range(4):
            P.op("pe", lambda e, c=c: e.matmul(banks[0][:, c * 8:(c + 1) * 8], lhsT=exm[0:8, c, :], rhs=dd[0:8, :], start=True, stop=True),
                 reads=[t_ex, t_dd], writes=[tb[0]], partial=(c > 0))
        P.op("act", lambda e: e.copy(out=dx, in_=banks[0][:, 0:32].rearrange("p (a b) -> p a b", b=8)), reads=[tb[0]], writes=[t_dx])
        for i2 in range(2):
            P.op("pe", lambda e, i2=i2: e.transpose(out=banks[1][0:4, i2 * 128:(i2 + 1) * 128], in_=xbcs[:, 4 + i2, :], identity=identf),
                 reads=[t_xbcs, t_const], writes=[tb[1]], partial=(i2 > 0))
        P.op("act", lambda e: e.copy(out=BCtm[0:4, :], in_=banks[1][0:4, 0:256]), reads=[tb[1]], writes=[t_BCtm])
        for b in range(4):
            P.op("dve", lambda e, b=b: e.tensor_scalar(out=BCbd[0:4, b, :], in0=BCtm[0:4, :], scalar1=identf[0:4, b:b + 1], scalar2=None, op0=ALU.mult),
                 reads=[t_BCtm, t_const], writes=[t_BCbd], partial=(b > 0))
        for b in range(4):
            bk = 2 + b // 2
            P.op("pe", lambda e, b=b, bk=bk: e.matmul(banks[bk][:, (b % 2) * 256:(b % 2 + 1) * 256], lhsT=onesf[0:4, :], rhs=BCbd[0:4, b, :], start=True, stop=True),
                 reads=[t_BCbd, t_const], writes=[tb[bk]], partial=(b % 2 > 0))
        for i2 in range(2):
            P.op("act", lambda e, i2=i2: e.copy(out=BC[:, 2 * i2:2 * i2 + 2, :, :], in_=banks[2 + i2].rearrange("p (b t n) -> p b t n", b=2, t=2)),
                 reads=[tb[2 + i2]], writes=[t_BC], partial=(i2 > 0))
        P.op("dve", lambda e: e.tensor_tensor(out=us, in0=xbcs[:, 0:4, :], in1=dx[:, :, 0:4], op=ALU.mult), reads=[t_xbcs, t_dx], writes=[t_us])
        first = True
        for c in range(4):
            g = c // 2
            for b in range(4):
                P.op("dve", lambda e, c=c, b=b, g=g: e.tensor_scalar(out=stmp[:, 0, :], in0=BC[:, b, 0, g * 64:(g + 1) * 64], scalar1=us[:, c, b:b + 1], scalar2=None, op0=ALU.mult),
                     reads=[t_BC, t_us], writes=[t_stmp])
                P.op("dve", lambda e, c=c, b=b: e.scalar_tensor_tensor(out=hSs[:, c, b, :], in0=hSs[:, c, b, :], scalar=dx[:, c, 4 + b:5 + b], in1=stmp[:, 0, :], op0=ALU.mult, op1=ALU.add),
                     reads=[t_hSs, t_dx, t_stmp], writes=[t_hSs], partial=True)
                P.op("dve", lambda e, c=c, b=b, g=g: e.tensor_tensor(out=stmp[:, 1, :], in0=hSs[:, c, b, :], in1=BC[:, b, 1, g * 64:(g + 1) * 64], op=ALU.mult),
                     reads=[t_hSs, t_BC], writes=[t_stmp], partial=True)
                P.op("dve", lambda e, c=c, b=b: e.reduce_sum(out=ys[:, c, b:b + 1], in_=stmp[:, 1, :], axis=AX.X),
                     reads=[t_stmp], writes=[t_ys], partial=(not first))
                first = False
        P.dma("sp", lambda e, l=l: e.dma_start(out=ssms_out[l], in_=hSs), t_hSs, reads=[t_hSs])
        for c in range(4):
            P.op("dve", lambda e, c=c: e.scalar_tensor_tensor(out=ys[:, c, :], in0=xbcs[:, c, :], scalar=colv[:, c:c + 1], in1=ys[:, c, :], op0=ALU.mult, op1=ALU.add),
                 reads=[t_xbcs, t_colv, t_ys], writes=[t_ys], partial=True)
        P.op("dve", lambda e: e.tensor_tensor(out=ys, in0=ys, in1=zsT, op=ALU.mult), reads=[t_ys, t_zsT], writes=[t_ys])
        P.op("dve", lambda e: e.tensor_tensor(out=y2s, in0=ys, in1=ys, op=ALU.mult), reads=[t_ys], writes=[t_y2s])
        P.op("pe", lambda e: e.matmul(banks[0][:, 0:16], lhsT=onesf, rhs=y2s.rearrange("p a b -> p (a b)"), start=True, stop=True), reads=[t_y2s, t_const], writes=[tb[0]])
        P.op("act", lambda e: e.copy(out=y2s.rearrange("p a b -> p (a b)"), in_=banks[0][:, 0:16]), reads=[tb[0]], writes=[t_y2s])
        s4 = y2s.rearrange("p (g c) b -> p g c b", g=2)
        P.op("dve", lambda e, s4=s4: e.tensor_tensor(out=ssg, in0=s4[:, :, 0, :], in1=s4[:, :, 1, :], op=ALU.add), reads=[t_y2s], writes=[t_ssg])
        P.op("dve", lambda e: e.tensor_scalar(out=ssg, in0=ssg, scalar1=1.0 / 256, scalar2=EPS, op0=ALU.mult, op1=ALU.add), reads=[t_ssg], writes=[t_ssg])
        P.op("act", lambda e: e.activation(out=ssg, in_=ssg, func=AF.Sqrt), reads=[t_ssg], writes=[t_ssg])
        P.op("dve", lambda e: e.reciprocal(out=ssg, in_=ssg), reads=[t_ssg], writes=[t_ssg])
        for c in range(4):
            P.op("dve", lambda e, c=c: e.scalar_tensor_tensor(out=osT[:, 4 + c, :], in0=ys[:, c, :], scalar=colv[:, 4 + c:5 + c], in1=ssg[:, c // 2, :], op0=ALU.mult, op1=ALU.mult),
                 reads=[t_ys, t_colv, t_ssg], writes=[t_os], partial=True)
        P.barrier()
        AR.reset(mSS)
        oT = hT
        t_o = t_h
        mS = AR.mark()
        xB = AR.alloc([NT, 640], BF16)
        t_xB = Tk("xB")
        for tt in range(NT):
            tsl = slice(tt * 128, (tt + 1) * 128)
            bk = tt % 2
            bv = banks[bk].bitcast(BF16)
            for cc in range(5):
                P.op("pe", lambda e, bv=bv, cc=cc, tsl=tsl: e.transpose(out=bv[:, cc * 128:(cc + 1) * 128], in_=xbcT[:, cc, tsl], identity=identb),
                     reads=[t_xbc, t_const], writes=[tb[bk]], partial=(cc > 0))
            P.op("act", lambda e, bv=bv, tt=tt: e.copy(out=xB[:, tt, :], in_=bv[:, 0:640]), reads=[tb[bk]], writes=[t_xB], partial=True)
        hS = AR.alloc([4, 64], F32)
        hSb = AR.alloc([4, 64], BF16)
        t_hS = Tk("hS")
        P.op("dve", lambda e: e.memset(hS, 0.0), writes=[t_hS])
        P.op("dve", lambda e: e.memset(hSb, 0.0), writes=[t_hS], partial=True)
        ncs = AR.alloc([8], F32)
        ecs = AR.alloc([8], F32)
        decc = AR.alloc([8], F32)
        wcol = AR.alloc([8], F32)
        abc = [AR.alloc([128], F32) for _ in range(2)]
        LT = [AR.alloc([128], F32) for _ in range(2)]
        WT = [AR.alloc([128], BF16) for _ in range(2)]
        GT = AR.alloc([2, 128], F32)
        xw = AR.alloc([512], BF16)
        ydsb = AR.alloc([512], F32)
        yy = AR.alloc([512], F32)
        yf = AR.alloc([512], BF16)
        ssq = AR.alloc([4], F32)
        t_ncs, t_ecs, t_dec, t_wcol, t_GT, t_xw, t_yd, t_yy, t_yf, t_ssq = (Tk(n) for n in
            ("ncs", "ecs", "dec", "wcol", "GT", "xw", "yd", "yy", "yf", "ssq"))
        t_abc = [Tk("abc0"), Tk("abc1")]
        t_LT = [Tk("LT0"), Tk("LT1")]
        t_WT = [Tk("WT0"), Tk("WT1")]
        GB = (1, 7)
        for c in range(NT):
            csl = slice(c * 128, (c + 1) * 128)
            P.op("pe", lambda e, c=c: e.matmul(banks[0][:, 0:8], lhsT=utri, rhs=aa[:, c, :], start=True, stop=True),
                 reads=[t_const, t_dt], writes=[tb[0]])
            P.op("dve", lambda e: e.tensor_scalar(out=ncs, in0=banks[0][:, 0:8], scalar1=-1.0, scalar2=None, op0=ALU.mult),
                 reads=[tb[0]], writes=[t_ncs])
            P.op("act", lambda e: e.activation(out=ecs, in_=banks[0][:, 0:8], func=AF.Exp), reads=[tb[0]], writes=[t_ecs])
            for g in range(2):
                ps = slice(g * 64, (g + 1) * 64)
                gb = GB[g]
                P.op("pe", lambda e, g=g, ps=ps, csl=csl, gb=gb: e.matmul(
                    banks[gb][:, 0:128], lhsT=xbcT[ps, 4, csl], rhs=xbcT[ps, 5, csl], start=True, stop=True),
                    reads=[t_xbc], writes=[tb[gb]])
                P.op("act", lambda e, g=g, gb=gb: e.copy(out=GT[:, g, :], in_=banks[gb][:, 0:128]), reads=[tb[gb]], writes=[t_GT], partial=(g > 0))
            for h in range(8):
                g = h // 4
                hb = h % 2
                sb = 2 + (h // 4)
                col = slice((h % 4) * 128, (h % 4 + 1) * 128)
                P.op("dve", lambda e, c=c, h=h, hb=hb: e.tensor_scalar(out=abc[hb], in0=onesf, scalar1=aa[:, c, h:h + 1], scalar2=None, op0=ALU.mult),
                     reads=[t_dt, t_const], writes=[t_abc[hb]])
                P.op("pe", lambda e, sb=sb, col=col, hb=hb: e.matmul(banks[sb][:, col], lhsT=abc[hb], rhs=utri, start=True, stop=False),
                     reads=[t_abc[hb], t_const], writes=[tb[sb]], partial=(h % 4 > 0))
                P.op("pe", lambda e, sb=sb, col=col: e.matmul(banks[sb][:, col], lhsT=identf, rhs=mtrif, start=False, stop=True),
                     reads=[t_const], writes=[tb[sb]], partial=True)
            for h in range(8):
                g = h // 4
                hb = h % 2
                sb = 2 + (h // 4)
                col = slice((h % 4) * 128, (h % 4 + 1) * 128)
                P.op("act", lambda e, sb=sb, col=col, hb=hb, h=h: e.activation(out=LT[hb], in_=banks[sb][:, col], func=AF.Exp, bias=ncs[:, h:h + 1], scale=1.0),
                     reads=[tb[sb], t_ncs], writes=[t_LT[hb]])
                P.op("act", lambda e, sb=sb, h=h: e.activation(out=decc[:, h:h + 1], in_=banks[sb][:, (h % 4) * 128 + 127:(h % 4) * 128 + 128], func=AF.Exp),
                     reads=[tb[sb]], writes=[t_dec], partial=True)
                P.op("dve", lambda e, hb=hb, h=h, g=g, c=c: e.scalar_tensor_tensor(
                    out=WT[hb], in0=LT[hb], scalar=dtt[:, c, h:h + 1], in1=GT[:, g, :], op0=ALU.mult, op1=ALU.mult),
                    reads=[t_LT[hb], t_dt, t_GT], writes=[t_WT[hb]])
                P.op("dve", lambda e, hb=hb, h=h, c=c: e.tensor_tensor(out=wcol[:, h:h + 1], in0=LT[hb][:, 127:128], in1=dtt[:, c, h:h + 1], op=ALU.mult),
                     reads=[t_LT[hb], t_dt], writes=[t_wcol], partial=True)
                P.op("dve", lambda e, h=h, c=c: e.tensor_scalar(out=xw[:, h * 64:(h + 1) * 64], in0=xB[:, c, h * 64:(h + 1) * 64], scalar1=wcol[:, h:h + 1], scalar2=None, op0=ALU.mult),
                     reads=[t_xB, t_wcol], writes=[t_xw], partial=True)
                P.op("pe", lambda e, hb=hb, h=h, c=c: e.matmul(banks[4][:, h * 64:(h + 1) * 64], lhsT=WT[hb], rhs=xB[:, c, h * 64:(h + 1) * 64], start=True, stop=True),
                     reads=[t_WT[hb], t_xB], writes=[tb[4]], partial=(h > 0))
                ps = slice(g * 64, (g + 1) * 64)
                gb = GB[g]
                P.op("pe", lambda e, h=h, ps=ps, csl=csl, gb=gb: e.matmul(banks[gb][:, 128 + (h % 4) * 64:128 + (h % 4 + 1) * 64], lhsT=xbcT[ps, 5, csl], rhs=hSb[ps, h % 4, :], start=True, stop=True),
                     reads=[t_xbc, t_hS], writes=[tb[gb]], partial=True)
            P.op("act", lambda e: e.copy(out=ydsb, in_=banks[4]), reads=[tb[4]], writes=[t_yd])
            for h in range(8):
                hs = slice(h * 64, (h + 1) * 64)
                gb = GB[h // 4]
                P.op("dve", lambda e, h=h, hs=hs, gb=gb: e.scalar_tensor_tensor(out=yy[:, hs], in0=banks[gb][:, 128 + (h % 4) * 64:128 + (h % 4 + 1) * 64], scalar=ecs[:, h:h + 1], in1=ydsb[:, hs], op0=ALU.mult, op1=ALU.add),
                     reads=[tb[gb], t_ecs, t_yd], writes=[t_yy], partial=(h > 0))
            for h in range(8):
                hs = slice(h * 64, (h + 1) * 64)
                P.op("dve", lambda e, h=h, hs=hs, c=c: e.scalar_tensor_tensor(out=yy[:, hs], in0=xB[:, c, hs], scalar=lay[:, 2, h:h + 1], in1=yy[:, hs], op0=ALU.mult, op1=ALU.add),
                     reads=[t_xB, t_lay, t_yy], writes=[t_yy], partial=True)
            P.op("pe", lambda e, c=c: e.matmul(banks[6], lhsT=xB[:, c, 512:640], rhs=xw, start=True, stop=True),
                 reads=[t_xB, t_xw], writes=[tb[6]])
            for h in range(8):
                g = h // 4
                ps = slice(g * 64, (g + 1) * 64)
                P.op("dve", lambda e, h=h, ps=ps: e.scalar_tensor_tensor(out=hS[ps, h % 4, :], in0=hS[ps, h % 4, :], scalar=decc[ps, h:h + 1], in1=banks[6][ps, h * 64:(h + 1) * 64], op0=ALU.mult, op1=ALU.add),
                     reads=[t_hS, t_dec, tb[6]], writes=[t_hS], partial=(h > 0))
            P.op("act", lambda e: e.copy(out=hSb, in_=hS), reads=[t_hS], writes=[t_hS], partial=True)
            P.op("dve", lambda e, c=c: e.tensor_tensor(out=yy, in0=yy, in1=zs[:, c, :], op=ALU.mult), reads=[t_yy, t_zs], writes=[t_yy])
            for g in range(2):
                P.op("act", lambda e, g=g: e.activation(out=ydsb[:, g * 256:(g + 1) * 256], in_=yy[:, g * 256:(g + 1) * 256], func=AF.Square, accum_out=ssq[:, g:g + 1]),
                     reads=[t_yy], writes=[t_ssq, t_yd], partial=(g > 0))
            P.op("dve", lambda e: e.tensor_scalar(out=ssq[:, 2:4], in0=ssq[:, 0:2], scalar1=1.0 / 256, scalar2=EPS, op0=ALU.mult, op1=ALU.add),
                 reads=[t_ssq], writes=[t_ssq], partial=True)
            P.op("act", lambda e: e.activation(out=ssq[:, 2:4], in_=ssq[:, 2:4], func=AF.Sqrt), reads=[t_ssq], writes=[t_ssq], partial=True)
            P.op("dve", lambda e: e.reciprocal(out=ssq[:, 2:4], in_=ssq[:, 2:4]), reads=[t_ssq], writes=[t_ssq], partial=True)
            for g in range(2):
                gs = slice(g * 256, (g + 1) * 256)
                P.op("dve", lambda e, g=g, gs=gs: e.scalar_tensor_tensor(out=yf[:, gs], in0=yy[:, gs], scalar=ssq[:, 2 + g:3 + g], in1=sng[:, gs], op0=ALU.mult, op1=ALU.mult),
                     reads=[t_yy, t_ssq, t_lay], writes=[t_yf], partial=(g > 0))
            bv7 = banks[0].bitcast(BF16)
            for e4 in range(4):
                P.op("pe", lambda e, e4=e4, bv7=bv7: e.transpose(out=bv7[:, e4 * 128:(e4 + 1) * 128], in_=yf[:, e4 * 128:(e4 + 1) * 128], identity=identb),
                     reads=[t_yf, t_const], writes=[tb[0]], partial=(e4 > 0))
            P.op("act", lambda e, bv7=bv7, csl=csl: e.copy(out=oT[:, 4:8, csl], in_=bv7[:, 0:512].rearrange("p (a b) -> p a b", b=128)),
                 reads=[tb[0]], writes=[t_o], partial=True)
        P.dma("sp", lambda e, l=l: e.dma_start(out=ssm_out[l], in_=hS), t_hS, reads=[t_hS])
        P.barrier()
        AR.reset(mS)
        if stage <= 3:
            break
        mT = AR.mark()
        mtri = AR.alloc([128], BF16)
        manti = AR.alloc([128], BF16)
        cmpmask = AR.alloc([T], BF16)
        eall = AR.alloc([T], BF16)
        kaug = AR.alloc([32, 128], BF16)
        qaug = AR.alloc([8, 128], BF16)
        ovc = AR.alloc([33], BF16)
        bonus = AR.alloc([NT, 32], F32)
        cap = AR.alloc([NT, 32], F32)
        idxb = AR.alloc([32], I32)
        t_ac = Tk("attconst")
        for dst, src in ((mtri, mtri_in), (manti, manti_in), (cmpmask, cmpmask_in), (eall, eall_in), (kaug, kaug_in),
                         (qaug, qaug_in), (ovc, ov_in)):
            P.dma("pool", lambda e, dst=dst, src=src: e.dma_start(out=dst, in_=src), t_ac, writes=[t_ac], partial=True)
        for dst, src in ((bonus, bonus_in), (cap, cap_in), (idxb, idxb_in)):
            P.dma("sp", lambda e, dst=dst, src=src: e.dma_start(out=dst, in_=src), t_ac, writes=[t_ac], partial=True)
        kcmpT = AR.alloc([2, 128], BF16)
        VC = AR.alloc([2, 97], BF16)
        t_kcmp, t_VC = Tk("kcmp"), Tk("VC")
        mC = AR.mark()
        w1d = AR.alloc([2, 32, 128], BF16)
        w2k = AR.alloc([128], BF16)
        w2v = AR.alloc([64], BF16)
        cb1 = AR.alloc([2], F32)
        t_cw = Tk("cmpw")
        for kv in range(2):
            P.dma("pool", lambda e, l=l, kv=kv: e.dma_start(out=w1d[:, kv, :, :], in_=w1d_in[l, kv]), t_cw, writes=[t_cw], partial=True)
        P.dma("pool", lambda e, l=l: e.dma_start(out=w2k, in_=w2k_in[l]), t_cw, writes=[t_cw], partial=True)
        P.dma("pool", lambda e, l=l: e.dma_start(out=w2v, in_=w2v_in[l]), t_cw, writes=[t_cw], partial=True)
        P.dma("sp", lambda e, l=l: e.dma_start(out=cb1, in_=cb1_in[l]), t_cw, writes=[t_cw], partial=True)
        P.op("dve", lambda e: e.memset(kcmpT, 0.0), writes=[t_kcmp])
        P.op("dve", lambda e: e.memset(VC, 0.0), writes=[t_VC])
        for g in range(2):
            P.op("dve", lambda e, g=g: e.tensor_copy(out=VC[:, g, 64:97], in_=ovc), reads=[t_ac], writes=[t_VC], partial=True)
        gx = AR.alloc([4, 128], F32)
        gu = AR.alloc([4, 128], F32)
        hid = AR.alloc([4, 128], BF16)
        t_gx, t_gu, t_hid = Tk("gx"), Tk("gu"), Tk("hid")
        RGB = ((0, 1), (2, 3))
        for kv in range(2):
            for g in range(2):
                i4 = kv * 2 + g
                ps = slice(g * 64, (g + 1) * 64)
                bk = RGB[g][kv]
                for j in range(32):
                    P.op("pe", lambda e, kv=kv, ps=ps, j=j, bk=bk: e.matmul(
                        banks[bk][:, 0:127], lhsT=w1d[ps, kv, j, :], rhs=kvcT[ps, kv, j:j + 16 * 126 + 1:16], start=(j == 0), stop=(j == 31)),
                        reads=[t_cw, t_kvc], writes=[tb[bk]], partial=(j > 0))
                n = slice(0, 127)
                P.op("act", lambda e, i4=i4, kv=kv, bk=bk: e.activation(out=gx[:, i4, 0:127], in_=banks[bk][:, 0:127], func=AF.Identity, bias=cb1[:, kv:kv + 1], scale=1.0),
                     reads=[tb[bk], t_cw], writes=[t_gx], partial=True)
        P.op("dve", lambda e: e.memset(gx[:, :, 127:128], 0.0), writes=[t_gx], partial=True)
        P.op("dve", lambda e: e.tensor_tensor(out=gu, in0=gx, in1=gx, op=ALU.mult), reads=[t_gx], writes=[t_gu])
        P.op("dve", lambda e: e.tensor_scalar(out=gu, in0=gu, scalar1=0.044715, scalar2=1.0, op0=ALU.mult, op1=ALU.add), reads=[t_gu], writes=[t_gu])
        P.op("dve", lambda e: e.tensor_tensor(out=gu, in0=gu, in1=gx, op=ALU.mult), reads=[t_gu, t_gx], writes=[t_gu])
        P.op("act", lambda e: e.activation(out=gu, in_=gu, func=AF.Sigmoid, scale=1.5957691216), reads=[t_gu], writes=[t_gu])
        P.op("dve", lambda e: e.tensor_tensor(out=hid, in0=gu, in1=gx, op=ALU.mult), reads=[t_gu, t_gx], writes=[t_hid])
        for g in range(2):
            P.op("pe", lambda e, g=g: e.matmul(banks[4][:, g * 128:g * 128 + 127], lhsT=w2k, rhs=hid[:, g, 0:127], start=True, stop=True),
                 reads=[t_cw, t_hid], writes=[tb[4]], partial=(g > 0))
        P.op("act", lambda e: e.copy(out=kcmpT[:, :, 0:127], in_=banks[4][:, 0:256].rearrange("p (a b) -> p a b", b=128)[:, :, 0:127]),
             reads=[tb[4]], writes=[t_kcmp], partial=True)
        for g in range(2):
            P.op("pe", lambda e, g=g: e.matmul(banks[5][0:127, g * 64:(g + 1) * 64], lhsT=hid[:, 2 + g, 0:127], rhs=w2v, start=True, stop=True),
                 reads=[t_cw, t_hid], writes=[tb[5]], partial=(g > 0))
        P.op("act", lambda e: e.copy(out=VC[0:127, :, 0:64], in_=banks[5][0:127, 0:128].rearrange("p (a b) -> p a b", b=64)),
             reads=[tb[5]], writes=[t_VC], partial=True)
        P.barrier()
        AR.reset(mC)
        PT = [AR.alloc([512], BF16) for _ in range(4)]
        t_PT = [Tk("PT%d" % i) for i in range(4)]
        otile = AR.alloc([512], F32)
        otb = AR.alloc([512], BF16)
        imp = AR.alloc([32], F32)
        sc = AR.alloc([32], F32)
        w2t = AR.alloc([32], F32)
        m8 = AR.alloc([16], F32)
        nsd = AR.alloc([96], F32)
        rr = AR.alloc([8], F32)
        nsT = [AR.alloc([2, 128], BF16) for _ in range(2)]
        t_ot, t_otb, t_imp, t_sc, t_m8, t_nsd, t_rr = (Tk(n) for n in ("ot", "otb", "imp", "sc", "m8", "nsd", "rr"))
        t_nsT = [Tk("nsT0"), Tk("nsT1")]
        P.op("dve", lambda e: e.memset(nsd, 0.0), writes=[t_nsd])
        sbank_next = [0, 0]
        accb = 0
        for qt in range(NT):
            qsl = slice(qt * 128, (qt + 1) * 128)
            nsq = nsT[qt % 2]
            t_nsq = t_nsT[qt % 2]
            for g in range(2):
                for br in (0, 1, 2):
                    ab = 4 + accb % 2
                    accb += 1
                    vw = 97 if br == 0 else 65
                    chunks = []
                    for j in range(4):
                        h = 4 * g + j
                        rg = h % 2
                        if br == 0:
                            blocks = [(0, "c")]
                        elif br == 1:
                            blocks = [(kt, "d" if kt == qt else "s") for kt in range(qt + 1)]
                        else:
                            blocks = []
                            for kt in range(max(0, qt - 4), qt + 1):
                                blocks.append((kt, "d" if kt == qt else ("a" if kt == qt - 4 else "n")))
                        for c0 in range(0, len(blocks), 4):
                            chunks.append((j, h, rg, blocks[c0:c0 + 4], c0 == 0))
                    pend = None

                    def emit_pv(ch, sbk, ab=ab, vw=vw, br=br, g=g):
                        j, h, rg, blks, first = ch
                        for bi, (kt, kind) in enumerate(blks):
                            if br == 0:
                                rhs = VC[:, g, :]
                            elif br == 1:
                                rhs = Vall[:, kt, g, :]
                            else:
                                rhs = Vall[:, kt, 2 + g, :]
                            P.op("pe", lambda e, ab=ab, j=j, vw=vw, sbk=sbk, bi=bi, rhs=rhs, st=(first and bi == 0): e.matmul(
                                banks[ab][:, j * 97:j * 97 + vw], lhsT=PT[sbk][:, bi * 128:(bi + 1) * 128], rhs=rhs, start=st, stop=False,
                                skip_group_check=True),
                                reads=[t_PT[sbk], t_V, t_VC], writes=[tb[ab]], partial=True)
                    for ch in chunks:
                        j, h, rg, blks, first = ch
                        sbk = RGB[rg][sbank_next[rg] % 2]
                        sbank_next[rg] += 1
                        rows = slice(rg * 64, (rg + 1) * 64)
                        arow = slice(rg * 64, rg * 64 + 4)
                        for bi, (kt, kind) in enumerate(blks):
                            cs_ = slice(bi * 128, (bi + 1) * 128)
                            ksl = slice(kt * 128, (kt + 1) * 128)
                            if br == 0:
                                lk = kcmpT[rows, g, :]
                                la = kaug[arow, 16 + qt, :]
                            elif br == 1:
                                lk = KsT[rows, g, ksl]
                                la = kaug[arow, qt - kt, :]
                            else:
                                lk = KwT[rows, g, ksl]
                                la = kaug[arow, qt - kt, :]
                            P.op("pe", lambda e, sbk=sbk, cs_=cs_, lk=lk, rows=rows, h=h, qsl=qsl: e.matmul(
                                banks[sbk][:, cs_], lhsT=lk, rhs=QT[rows, h // 2, qsl], start=True, stop=False, skip_group_check=True),
                                reads=[t_Q, t_Ks, t_Kw, t_kcmp], writes=[tb[sbk]], partial=(bi > 0))
                            extra = []
                            if kind == "c":
                                extra.append((identb, cmpmask[:, qsl], [t_const, t_ac]))
                            if br == 1:
                                erow = slice(rg * 64, rg * 64 + 32)
                                extra.append((eall[erow, ksl], nsq[erow, g, :], [t_ac, t_nsq]))
                            if kind == "d":
                                extra.append((identb, mtri, [t_const, t_ac]))
                            if kind == "a":
                                extra.append((identb, manti, [t_const, t_ac]))
                            P.op("pe", lambda e, sbk=sbk, cs_=cs_, la=la, arow=arow, h=h, last=(not extra): e.matmul(
                                banks[sbk][:, cs_], lhsT=la, rhs=qaug[arow, h, :], start=False, stop=last, skip_group_check=True),
                                reads=[t_ac], writes=[tb[sbk]], partial=True)
                            for xi, (lt_, rh_, rd_) in enumerate(extra):
                                P.op("pe", lambda e, sbk=sbk, cs_=cs_, lt_=lt_, rh_=rh_, last=(xi == len(extra) - 1): e.matmul(
                                    banks[sbk][:, cs_], lhsT=lt_, rhs=rh_, start=False, stop=last, skip_group_check=True),
                                    reads=rd_, writes=[tb[sbk]], partial=True)
                        nb_ = len(blks)
                        P.op("act", lambda e, sbk=sbk, nb_=nb_: e.activation(out=PT[sbk][:, 0:nb_ * 128], in_=banks[sbk][:, 0:nb_ * 128], func=AF.Exp),
                             reads=[tb[sbk]], writes=[t_PT[sbk]])
                        if pend is not None:
                            emit_pv(*pend)
                        pend = (ch, sbk)
                    emit_pv(*pend)
                    accv = banks[ab][:, 0:388].rearrange("p (a b) -> p a b", b=97)
                    P.op("dve", lambda e, accv=accv: e.tensor_scalar(out=rr[:, 0:4], in0=accv[:, :, 64], scalar1=1e-30, scalar2=None, op0=ALU.max),
                         reads=[tb[ab]], writes=[t_rr])
                    P.op("dve", lambda e: e.reciprocal(out=rr[:, 0:4], in_=rr[:, 0:4]), reads=[t_rr], writes=[t_rr])
                    P.op("dve", lambda e, qt=qt, br=br, g=g: e.tensor_tensor(out=rr[:, 4:8], in0=rr[:, 0:4], in1=gat[:, qt, br * 8 + g * 4: br * 8 + g * 4 + 4], op=ALU.mult),
                         reads=[t_rr, t_gat], writes=[t_rr], partial=True)
                    for j in range(4):
                        h = 4 * g + j
                        hs = slice(h * 64, (h + 1) * 64)
                        if br == 0:
                            P.op("dve", lambda e, accv=accv, j=j, hs=hs: e.tensor_scalar(out=otile[:, hs], in0=accv[:, j, 0:64], scalar1=rr[:, 4 + j:5 + j], scalar2=None, op0=ALU.mult),
                                 reads=[tb[ab], t_rr], writes=[t_ot], partial=True)
                            if j == 0:
                                P.op("dve", lambda e, accv=accv, j=j: e.tensor_scalar(out=imp, in0=accv[:, j, 65:97], scalar1=rr[:, j:j + 1], scalar2=None, op0=ALU.mult),
                                     reads=[tb[ab], t_rr], writes=[t_imp])
                            else:
                                P.op("dve", lambda e, accv=accv, j=j: e.scalar_tensor_tensor(out=imp, in0=accv[:, j, 65:97], scalar=rr[:, j:j + 1], in1=imp, op0=ALU.mult, op1=ALU.add),
                                     reads=[tb[ab], t_rr, t_imp], writes=[t_imp])
                        else:
                            P.op("dve", lambda e, accv=accv, j=j, hs=hs: e.scalar_tensor_tensor(out=otile[:, hs], in0=accv[:, j, 0:64], scalar=rr[:, 4 + j:5 + j], in1=otile[:, hs], op0=ALU.mult, op1=ALU.add),
                                 reads=[tb[ab], t_rr, t_ot], writes=[t_ot], partial=True)
                    if br == 0:
                        P.op("dve", lambda e: e.tensor_scalar(out=sc, in0=imp, scalar1=float(2.0 ** 64), scalar2=float(2.0 ** -60), op0=ALU.mult, op1=ALU.max),
                             reads=[t_imp], writes=[t_sc])
                        P.op("dve", lambda e, qt=qt: e.tensor_tensor(out=sc, in0=sc, in1=bonus[:, qt, :], op=ALU.add), reads=[t_sc, t_ac], writes=[t_sc])
                        sci = sc.bitcast(I32)
                        P.op("dve", lambda e, sci=sci: e.tensor_single_scalar(out=sci, in_=sci, scalar=-32, op=ALU.bitwise_and), reads=[t_sc], writes=[t_sc])
                        P.op("dve", lambda e, sci=sci: e.tensor_tensor(out=sci, in0=sci, in1=idxb, op=ALU.bitwise_or), reads=[t_sc, t_ac], writes=[t_sc])
                        P.op("dve", lambda e, qt=qt: e.tensor_tensor(out=sc, in0=sc, in1=cap[:, qt, :], op=ALU.min), reads=[t_sc, t_ac], writes=[t_sc])
                        P.op("dve", lambda e: e.max(out=m8[:, 0:8], in_=sc), reads=[t_sc], writes=[t_m8])
                        P.op("dve", lambda e: e.match_replace(out=w2t, in_to_replace=m8[:, 0:8], in_values=sc, imm_value=-3e38), reads=[t_sc, t_m8], writes=[t_m8], partial=True)
                        P.op("dve", lambda e: e.max(out=m8[:, 8:16], in_=w2t), reads=[t_m8], writes=[t_m8], partial=True)
                        P.op("dve", lambda e: e.tensor_scalar(out=w2t, in0=sc, scalar1=m8[:, 15:16], scalar2=None, op0=ALU.is_ge), reads=[t_sc, t_m8], writes=[t_m8], partial=True)
                        P.op("dve", lambda e: e.tensor_scalar(out=nsd[:, 0:32], in0=w2t, scalar1=1.0, scalar2=-NEGB, op0=ALU.subtract, op1=ALU.mult), reads=[t_m8], writes=[t_nsd], partial=True)
                        P.op("dve", lambda e: e.tensor_copy(out=nsd[:, 64:96], in_=nsd[:, 0:32]), reads=[t_nsd], writes=[t_nsd], partial=True)
                        P.op("pe", lambda e: e.transpose(out=banks[6][0:96, 0:128], in_=nsd, identity=identf), reads=[t_nsd, t_const], writes=[tb[6]])
                        P.op("act", lambda e, nsq=nsq, g=g: e.copy(out=nsq[0:96, g, :], in_=banks[6][0:96, 0:128]), reads=[tb[6]], writes=[t_nsq], partial=True)
            P.op("act", lambda e: e.copy(out=otb, in_=otile), reads=[t_ot], writes=[t_otb])
            bv6 = banks[7].bitcast(BF16)
            for e4 in range(4):
                P.op("pe", lambda e, e4=e4, bv6=bv6: e.transpose(out=bv6[:, e4 * 128:(e4 + 1) * 128], in_=otb[:, e4 * 128:(e4 + 1) * 128], identity=identb),
                     reads=[t_otb, t_const], writes=[tb[7]], partial=(e4 > 0))
            P.op("act", lambda e, bv6=bv6, qsl=qsl: e.copy(out=oT[:, 0:4, qsl], in_=bv6[:, 0:512].rearrange("p (a b) -> p a b", b=128)),
                 reads=[tb[7]], writes=[t_o], partial=True)
        P.barrier()
        AR.reset(mT)
        if stage <= 4:
            break
        AR.reset(layer_mark)
        idxall = AR.alloc([256], I32)
        augL = AR.alloc([2, 128], BF16)
        augR = AR.alloc([2, 296], BF16)
        mcs = AR.alloc([296], BF16)
        ovs = AR.alloc([4, 130], BF16)
        bonus_s = AR.alloc([129], F32)
        idx_s = AR.alloc([129], I32)
        oneo = AR.alloc([2, 128], BF16)
        t_sc2 = Tk("sconst")
        P.dma("sp", lambda e: e.dma_start(out=idxall, in_=ptb_in), t_sc2, writes=[t_sc2])
        pidx = AR.alloc([256], I32)
        P.dma("sp", lambda e: e.dma_start(out=pidx, in_=pidx_in), t_sc2, writes=[t_sc2], partial=True)
        P.op("dve", lambda e: e.tensor_single_scalar(out=idxall, in_=idxall, scalar=7, op=ALU.logical_shift_left), reads=[t_sc2], writes=[t_sc2], partial=True)
        P.op("dve", lambda e: e.tensor_tensor(out=idxall, in0=idxall, in1=pidx, op=ALU.bitwise_or), reads=[t_sc2], writes=[t_sc2], partial=True)
        for dst, src in ((augL, augL_in), (augR, augR_in), (mcs, mcs_in), (ovs, ovs_in), (oneo, oneo_in)):
            P.dma("pool", lambda e, dst=dst, src=src: e.dma_start(out=dst, in_=src), t_sc2, writes=[t_sc2], partial=True)
        P.dma("sp", lambda e: e.dma_start(out=bonus_s, in_=bonus_s_in), t_sc2, writes=[t_sc2], partial=True)
        P.dma("sp", lambda e: e.dma_start(out=idx_s, in_=idx_s_in), t_sc2, writes=[t_sc2], partial=True)
        w1ds = AR.alloc([2, 32, 128], BF16)
        w2ks = AR.alloc([128], BF16)
        w2vs = AR.alloc([64], BF16)
        cb1s = AR.alloc([2], F32)
        t_cws = Tk("cmpw_s")
        for kv in range(2):
            P.dma("pool", lambda e, l=l, kv=kv: e.dma_start(out=w1ds[:, kv, :, :], in_=w1d_in[l, kv]), t_cws, writes=[t_cws], partial=True)
        P.dma("pool", lambda e, l=l: e.dma_start(out=w2ks, in_=w2k_in[l]), t_cws, writes=[t_cws], partial=True)
        P.dma("pool", lambda e, l=l: e.dma_start(out=w2vs, in_=w2v_in[l]), t_cws, writes=[t_cws], partial=True)
        P.dma("sp", lambda e, l=l: e.dma_start(out=cb1s, in_=cb1_in[l]), t_cws, writes=[t_cws], partial=True)
        KTa = AR.alloc([8192], BF16)
        KTb = AR.alloc([8192], BF16)
        Vsl = AR.alloc([64, 2, 65], BF16)
        gst = [AR.alloc([8, 128], F32) for _ in range(4)]
        t_KTa, t_KTb, t_Vsl = Tk("KTa"), Tk("KTb"), Tk("Vsl")
        t_gst = [Tk("gst%d" % i) for i in range(4)]
        gxs = AR.alloc([4, 512], F32)
        gus = AR.alloc([4, 512], F32)
        hids = AR.alloc([4, 512], BF16)
        t_gxs, t_gus, t_hids = Tk("gxs"), Tk("gus"), Tk("hids")
        kcs = AR.alloc([512], BF16)
        VCs = AR.alloc([4, 2, 194], BF16)
        KTn = AR.alloc([2, 4, 128], BF16)
        KwTs = AR.alloc([512], BF16)
        kwst = AR.alloc([512], F32)
        Vw = AR.alloc([4, 2, 65], BF16)
        vwst = AR.alloc([4, 128], F32)
        PTc = AR.alloc([16], BF16)
        PT2 = AR.alloc([280], BF16)
        accs = AR.alloc([194], F32)
        r65 = AR.alloc([66], F32)
        rg2 = AR.alloc([4], F32)
        scs = AR.alloc([129], F32)
        wts = AR.alloc([129], F32)
        m8s = AR.alloc([16], F32)
        nss = AR.alloc([130], F32)
        nseo2 = [AR.alloc([2, 64, 4], BF16) for _ in range(2)]
        osum = AR.alloc([8, 128], F32)
        t_kcs, t_VCs, t_KTn, t_KwTs, t_kwst, t_Vw, t_vwst, t_PTc, t_PT2, t_accs, t_r65, t_scs, t_m8s, t_nss, t_nseo, t_osum = (Tk(n) for n in
            ("kcs", "VCs", "KTn", "KwTs", "kwst", "Vw", "vwst", "PTc", "PT2", "accs", "r65", "scs", "m8s", "nss", "nseo", "osum"))
        P.op("dve", lambda e: e.memset(KTn, 0.0), writes=[t_KTn])
        for b in range(4):
            P.op("dve", lambda e, b=b: e.tensor_copy(out=KTn[:, 0, b, 0:1], in_=kvnT[:, 2, b:b + 1]), reads=[t_kvn], writes=[t_KTn], partial=True)
            P.op("dve", lambda e, b=b: e.tensor_copy(out=KTn[:, 1, b, 0:1], in_=kvnT[:, 4, b:b + 1]), reads=[t_kvn], writes=[t_KTn], partial=True)
        P.op("dve", lambda e: e.memset(VCs, 0.0), writes=[t_VCs])
        for g in range(2):
            P.op("dve", lambda e, g=g: e.tensor_copy(out=VCs[:, :, g, 64:194], in_=ovs), reads=[t_sc2], writes=[t_VCs], partial=True)
        P.op("dve", lambda e: e.memset(Vsl, 1.0), writes=[t_Vsl])
        P.op("dve", lambda e: e.memset(Vw, 1.0), writes=[t_Vw])
        P.op("dve", lambda e: e.memset(gxs, 0.0), writes=[t_gxs])
        P.op("dve", lambda e: e.memset(r65, 0.0), writes=[t_r65])
        gi = [0]

        def gather(pool_ap, b, dstfn, t_dst, first_partial):
            for a8 in range(8):
                si = gi[0] % 4
                gi[0] += 1

                def issue(e, a8=a8, si=si):
                    return [e.indirect_dma_start(out=gst[si][:, a, :], out_offset=None, in_=pool_ap,
                                                 in_offset=bass.IndirectOffsetOnAxis(ap=idxall[:, b * 64 + a8 * 8 + a: b * 64 + a8 * 8 + a + 1], axis=0))
                            for a in range(8)]
                P.dma("pool", issue, t_gst[si], reads=[t_sc2], writes=[t_gst[si]], n=8)
                dstfn(a8, si)

        for b in range(4):
            gather(cmpkT_pool[l], b, lambda a8, si: P.op("act", lambda e, a8=a8, si=si: e.copy(out=KTa[:, a8 * 1024:(a8 + 1) * 1024], in_=gst[si].rearrange("p a t -> p (a t)")),
                                                         reads=[t_gst[si]], writes=[t_KTa], partial=True), t_KTa, False)
            gather(cmpvT_pool[l], b, lambda a8, si: P.op("act", lambda e, a8=a8, si=si: e.copy(out=KTb[:, a8 * 1024:(a8 + 1) * 1024], in_=gst[si].rearrange("p a t -> p (a t)")),
                                                         reads=[t_gst[si]], writes=[t_KTb], partial=True), t_KTb, False)
            for kv in range(2):
                KT = KTa if kv == 0 else KTb
                t_KT = t_KTa if kv == 0 else t_KTb
                for g in range(2):
                    i4 = kv * 2 + g
                    ps = slice(g * 64, (g + 1) * 64)
                    bk = RGB[g][kv]
                    for j in range(32):
                        P.op("pe", lambda e, kv=kv, ps=ps, j=j, bk=bk, KT=KT: e.matmul(
                            banks[bk][:, 0:511], lhsT=w1ds[ps, kv, j, :], rhs=KT[ps, j:j + 16 * 510 + 1:16], start=(j == 0), stop=(j == 31)),
                            reads=[t_cws, t_KT], writes=[tb[bk]], partial=(j > 0))
                    P.op("act", lambda e, i4=i4, kv=kv, bk=bk: e.activation(out=gxs[:, i4, 0:511], in_=banks[bk][:, 0:511], func=AF.Identity, bias=cb1s[:, kv:kv + 1], scale=1.0),
                         reads=[tb[bk], t_cws], writes=[t_gxs], partial=True)
            P.op("dve", lambda e: e.tensor_tensor(out=gus, in0=gxs, in1=gxs, op=ALU.mult), reads=[t_gxs], writes=[t_gus])
            P.op("dve", lambda e: e.tensor_scalar(out=gus, in0=gus, scalar1=0.044715, scalar2=1.0, op0=ALU.mult, op1=ALU.add), reads=[t_gus], writes=[t_gus])
            P.op("dve", lambda e: e.tensor_tensor(out=gus, in0=gus, in1=gxs, op=ALU.mult), reads=[t_gus, t_gxs], writes=[t_gus])
            P.op("act", lambda e: e.activation(out=gus, in_=gus, func=AF.Sigmoid, scale=1.5957691216), reads=[t_gus], writes=[t_gus])
            P.op("dve", lambda e: e.tensor_tensor(out=hids, in0=gus, in1=gxs, op=ALU.mult), reads=[t_gus, t_gxs], writes=[t_hids])
            for g in range(2):
                P.op("pe", lambda e, g=g: e.matmul(banks[4 + g][:, 0:512], lhsT=w2ks, rhs=hids[:, g, :], start=True, stop=True), reads=[t_cws, t_hids], writes=[tb[4 + g]])
                ps = slice(g * 64, (g + 1) * 64)
                P.op("act", lambda e, g=g, ps=ps: e.copy(out=kcs[ps, :], in_=banks[4 + g][ps, 0:512]), reads=[tb[4 + g]], writes=[t_kcs], partial=(g > 0))
            for g in range(2):
                for nt in range(4):
                    P.op("pe", lambda e, g=g, nt=nt: e.matmul(banks[6][:, (g * 4 + nt) * 64:(g * 4 + nt + 1) * 64], lhsT=hids[:, 2 + g, nt * 128:(nt + 1) * 128], rhs=w2vs, start=True, stop=True),
                         reads=[t_cws, t_hids], writes=[tb[6]], partial=(g + nt > 0))
            P.op("act", lambda e: e.copy(out=VCs[:, :, :, 0:64].rearrange("p t g d -> p g t d"), in_=banks[6].rearrange("p (g t d) -> p g t d", g=2, t=4)),
                 reads=[tb[6]], writes=[t_VCs], partial=True)
            for g in range(2):
                rows = slice(g * 64, (g + 1) * 64)
                tr = g * 64
                sb1 = RGB[g][0]
                P.op("pe", lambda e, tr=tr, g=g, sb1=sb1: e.matmul(banks[sb1][:, 0:16], lhsT=augL[tr:tr + 3, 1, :], rhs=augR[tr:tr + 3, g, 280:296], start=True, stop=False, skip_group_check=True),
                     reads=[t_sc2], writes=[tb[sb1]])
                P.op("pe", lambda e, sb1=sb1: e.matmul(banks[sb1][:, 0:16], lhsT=identb, rhs=mcs[:, 280:296], start=False, stop=False, skip_group_check=True),
                     reads=[t_sc2, t_const], writes=[tb[sb1]], partial=True)
                for nt in range(4):
                    P.op("pe", lambda e, rows=rows, nt=nt, sb1=sb1, b=b: e.matmul(banks[sb1][:, nt * 4:(nt + 1) * 4], lhsT=kcs[rows, nt * 128:(nt + 1) * 128], rhs=QgT[rows, :, b],
                                                                           start=False, stop=(nt == 3), skip_group_check=True), reads=[t_kcs, t_Qg], writes=[tb[sb1]], partial=True)
                P.op("act", lambda e, sb1=sb1: e.activation(out=PTc, in_=banks[sb1][:, 0:16], func=AF.Exp), reads=[tb[sb1]], writes=[t_PTc])
                for nt in range(4):
                    P.op("pe", lambda e, nt=nt, g=g: e.matmul(banks[7][0:4, 0:194], lhsT=PTc[:, nt * 4:(nt + 1) * 4], rhs=VCs[:, nt, g, :], start=(nt == 0), stop=(nt == 3)),
                         reads=[t_PTc, t_VCs], writes=[tb[7]], partial=(nt > 0))
                P.op("act", lambda e: e.copy(out=accs[0:4, :], in_=banks[7][0:4, 0:194]), reads=[tb[7]], writes=[t_accs])
                P.op("dve", lambda e: e.tensor_scalar(out=r65[0:4, 64:65], in0=accs[0:4, 64:65], scalar1=1e-30, scalar2=None, op0=ALU.max), reads=[t_accs], writes=[t_r65], partial=True)
                P.op("dve", lambda e: e.reciprocal(out=r65[0:4, 64:65], in_=r65[0:4, 64:65]), reads=[t_r65], writes=[t_r65], partial=True)
                P.op("dve", lambda e, g=g, b=b: e.tensor_tensor(out=r65[0:4, 65:66], in0=r65[0:4, 64:65], in1=gsT[0:4, (0 * 2 + g) * 4 + b:(0 * 2 + g) * 4 + b + 1], op=ALU.mult),
                     reads=[t_r65, t_gs], writes=[t_r65], partial=True)
                bg = b * 2 + g
                P.op("dve", lambda e, bg=bg: e.tensor_scalar(out=osum[0:4, bg, 0:64], in0=accs[0:4, 0:64], scalar1=r65[0:4, 65:66], scalar2=None, op0=ALU.mult),
                     reads=[t_accs, t_r65], writes=[t_osum], partial=True)
                lw = r65[0:4, 64:65] if g == 0 else r65[0:4, 0:65]
                npart = 1 if g == 0 else 65
                P.op("pe", lambda e, lw=lw, npart=npart: e.matmul(banks[6][0:npart, 0:129], lhsT=lw, rhs=accs[0:4, 65:194], start=True, stop=True),
                     reads=[t_r65, t_accs], writes=[tb[6]])
                tp = slice(tr, tr + 1)
                P.op("dve", lambda e, tp=tp: e.tensor_scalar(out=scs[tp, :], in0=banks[6][tp, 0:129], scalar1=float(2.0 ** 64), scalar2=float(2.0 ** -60), op0=ALU.mult, op1=ALU.max),
                     reads=[tb[6]], writes=[t_scs])
                P.op("dve", lambda e, tp=tp: e.tensor_tensor(out=scs[tp, :], in0=scs[tp, :], in1=bonus_s[tp, :], op=ALU.add), reads=[t_scs, t_sc2], writes=[t_scs])
                sci2 = scs.bitcast(I32)
                P.op("dve", lambda e, tp=tp, sci2=sci2: e.tensor_single_scalar(out=sci2[tp, :], in_=sci2[tp, :], scalar=-256, op=ALU.bitwise_and), reads=[t_scs], writes=[t_scs])
                P.op("dve", lambda e, tp=tp, sci2=sci2: e.tensor_tensor(out=sci2[tp, :], in0=sci2[tp, :], in1=idx_s[tp, :], op=ALU.bitwise_or), reads=[t_scs, t_sc2], writes=[t_scs])
                P.op("dve", lambda e, tp=tp: e.max(out=m8s[tp, 0:8], in_=scs[tp, :]), reads=[t_scs], writes=[t_m8s])
                P.op("dve", lambda e, tp=tp: e.match_replace(out=wts[tp, :], in_to_replace=m8s[tp, 0:8], in_values=scs[tp, :], imm_value=-3e38), reads=[t_scs, t_m8s], writes=[t_m8s], partial=True)
                P.op("dve", lambda e, tp=tp: e.max(out=m8s[tp, 8:16], in_=wts[tp, :]), reads=[t_m8s], writes=[t_m8s], partial=True)
                P.op("dve", lambda e, tp=tp: e.tensor_scalar(out=wts[tp, :], in0=scs[tp, :], scalar1=m8s[tp, 15:16], scalar2=None, op0=ALU.is_ge), reads=[t_scs, t_m8s], writes=[t_m8s], partial=True)
                P.op("dve", lambda e, tp=tp: e.tensor_scalar(out=nss[tp, 0:129], in0=wts[tp, :], scalar1=1.0, scalar2=-NEGB, op0=ALU.subtract, op1=ALU.mult), reads=[t_m8s], writes=[t_nss])
                nv = nss[tp, 0:128].rearrange("p (k two) -> p k two", two=2)
                for eo in range(2):
                    for j in range(4):
                        P.op("dve", lambda e, tp=tp, eo=eo, j=j, nv=nv, g=g: e.tensor_copy(out=nseo2[g][tp, eo, :, j], in_=nv[:, :, eo]), reads=[t_nss], writes=[t_nseo], partial=True)
                if g == 0:
                    gather(slckT_pool[l], b, lambda a8, si: P.op("act", lambda e, a8=a8, si=si: e.copy(out=KTa[:, a8 * 1024:(a8 + 1) * 1024], in_=gst[si].rearrange("p a t -> p (a t)")),
                                                                 reads=[t_gst[si]], writes=[t_KTa], partial=True), t_KTa, False)
                    gather(slcv_pool[l], b, lambda a8, si: P.op("act", lambda e, a8=a8, si=si: e.copy(out=Vsl[:, a8 * 8:(a8 + 1) * 8, :, 0:64], in_=gst[si].rearrange("p a (g d) -> p a g d", g=2)),
                                                                reads=[t_gst[si]], writes=[t_Vsl], partial=True), t_Vsl, False)
                    P.dma("sp", lambda e, l=l, b=b: e.dma_start(out=kwst, in_=wkT_in[l, b]), t_kwst, writes=[t_kwst])
                    P.op("act", lambda e: e.copy(out=KwTs, in_=kwst), reads=[t_kwst], writes=[t_KwTs])
                    P.dma("sp", lambda e, l=l, b=b: e.dma_start(out=vwst, in_=wv_in[l, b].rearrange("(t p) f -> p t f", p=128)), t_vwst, writes=[t_vwst])
                    P.op("act", lambda e: e.copy(out=Vw[:, :, :, 0:64], in_=vwst.rearrange("p t (g d) -> p t g d", g=2)), reads=[t_vwst], writes=[t_Vw], partial=True)
            for g in range(2):
                rows = slice(g * 64, (g + 1) * 64)
                tr = g * 64
                tp = slice(tr, tr + 1)
                sb2 = RGB[g][1]
                bg = b * 2 + g
                P.op("pe", lambda e, tr=tr, g=g, sb2=sb2: e.matmul(banks[sb2][:, 0:280], lhsT=augL[tr:tr + 3, 0, :], rhs=augR[tr:tr + 3, g, 0:280], start=True, stop=False, skip_group_check=True),
                     reads=[t_sc2], writes=[tb[sb2]])
                for eo in range(2):
                    P.op("pe", lambda e, tp=tp, eo=eo, sb2=sb2, g=g: e.matmul(banks[sb2][:, 0:256], lhsT=oneo[tp, eo, :], rhs=nseo2[g][tp, eo, :, :].rearrange("p k j -> p (k j)"), start=False, stop=False, skip_group_check=True),
                         reads=[t_sc2, t_nseo], writes=[tb[sb2]], partial=True)
                P.op("pe", lambda e, sb2=sb2: e.matmul(banks[sb2][:, 0:280], lhsT=identb, rhs=mcs[:, 0:280], start=False, stop=False, skip_group_check=True),
                     reads=[t_sc2, t_const], writes=[tb[sb2]], partial=True)
                for kt in range(70):
                    if kt < 64:
                        lk, rd = KTa[rows, kt * 128:(kt + 1) * 128], [t_KTa]
                    elif kt == 64:
                        lk, rd = KTn[rows, 0, b, :], [t_KTn]
                    elif kt < 69:
                        lk, rd = KwTs[rows, (kt - 65) * 128:(kt - 64) * 128], [t_KwTs]
                    else:
                        lk, rd = KTn[rows, 1, b, :], [t_KTn]
                    P.op("pe", lambda e, lk=lk, rows=rows, kt=kt, sb2=sb2, b=b: e.matmul(banks[sb2][:, kt * 4:(kt + 1) * 4], lhsT=lk, rhs=QgT[rows, :, b], start=False, stop=(kt == 69), skip_group_check=True),
                         reads=rd + [t_Qg], writes=[tb[sb2]], partial=True)
                P.op("act", lambda e, sb2=sb2: e.activation(out=PT2, in_=banks[sb2][:, 0:280], func=AF.Exp), reads=[tb[sb2]], writes=[t_PT2])
                for kt in range(65):
                    if kt < 64:
                        P.op("pe", lambda e, kt=kt, g=g: e.matmul(banks[7][0:4, 0:65], lhsT=PT2[:, kt * 4:(kt + 1) * 4], rhs=Vsl[:, kt, g, :], start=(kt == 0), stop=False, skip_group_check=True),
                             reads=[t_PT2, t_Vsl], writes=[tb[7]], partial=(kt > 0))
                    else:
                        P.op("pe", lambda e, g=g, b=b: e.matmul(banks[7][0:4, 0:65], lhsT=PT2[0:1, 256:260], rhs=VN[0:1, b, 0, g * 65:(g + 1) * 65], start=False, stop=True, skip_group_check=True),
                             reads=[t_PT2, t_VN], writes=[tb[7]], partial=True)
                for kt in range(5):
                    if kt < 4:
                        P.op("pe", lambda e, kt=kt, g=g: e.matmul(banks[7][0:4, 65:130], lhsT=PT2[:, 260 + kt * 4:264 + kt * 4], rhs=Vw[:, kt, g, :], start=False, stop=False, skip_group_check=True),
                             reads=[t_PT2, t_Vw], writes=[tb[7]], partial=True)
                    else:
                        P.op("pe", lambda e, g=g, b=b: e.matmul(banks[7][0:4, 65:130], lhsT=PT2[0:1, 276:280], rhs=VN[0:1, b, 1, g * 65:(g + 1) * 65], start=False, stop=True, skip_group_check=True),
                             reads=[t_PT2, t_VN], writes=[tb[7]], partial=True)
                P.op("act", lambda e: e.copy(out=accs[0:4, 0:130], in_=banks[7][0:4, 0:130]), reads=[tb[7]], writes=[t_accs])
                for br in (1, 2):
                    c0 = (br - 1) * 65
                    P.op("dve", lambda e, c0=c0: e.tensor_scalar(out=rg2[0:4, 0:1], in0=accs[0:4, c0 + 64:c0 + 65], scalar1=1e-30, scalar2=None, op0=ALU.max), reads=[t_accs], writes=[t_r65])
                    P.op("dve", lambda e: e.reciprocal(out=rg2[0:4, 0:1], in_=rg2[0:4, 0:1]), reads=[t_r65], writes=[t_r65])
                    P.op("dve", lambda e, br=br, g=g, b=b: e.tensor_tensor(out=rg2[0:4, 1:2], in0=rg2[0:4, 0:1], in1=gsT[0:4, (br * 2 + g) * 4 + b:(br * 2 + g) * 4 + b + 1], op=ALU.mult),
                         reads=[t_r65, t_gs], writes=[t_r65])
                    P.op("dve", lambda e, c0=c0, bg=bg: e.scalar_tensor_tensor(out=osum[0:4, bg, 0:64], in0=accs[0:4, c0:c0 + 64], scalar=rg2[0:4, 1:2], in1=osum[0:4, bg, 0:64], op0=ALU.mult, op1=ALU.add),
                         reads=[t_accs, t_r65, t_osum], writes=[t_osum], partial=True)
        P.op("dve", lambda e: e.tensor_copy(out=osum[0:4, :, 64:128], in_=osum[0:4, :, 0:64]), reads=[t_osum], writes=[t_osum], partial=True)
        for bg in range(8):
            b, g = bg // 2, bg % 2
            P.op("pe", lambda e, bg=bg: e.transpose(out=banks[6][:, bg * 4:(bg + 1) * 4], in_=osum[0:4, bg, :], identity=identf[0:4, 0:4]), reads=[t_osum, t_const], writes=[tb[6]], partial=(bg > 0))
        ot4 = banks[6][:, 0:32].rearrange("p (b g a r) -> p b g a r", b=4, g=2, a=2)
        for r in range(2):
            rs_ = slice(r * 64, (r + 1) * 64)
            P.op("act", lambda e, r=r, rs_=rs_, ot4=ot4: e.copy(out=osT[rs_, 0:4, :].rearrange("p (g a) b -> p b g a", g=2), in_=ot4[rs_, :, :, :, r]),
                 reads=[tb[6]], writes=[t_os], partial=True)
        P.barrier()
        AR.reset(layer_mark)
        AR.alloc([KD, T], F32)
        load_x(xsrc)
        mO = AR.mark()
        wo = AR.alloc([KD, D], BF16)
        t_wo = Tk("wo")
        for ec in range(KD):
            P.dma("pool", lambda e, l=l, ec=ec: e.dma_start(out=wo[:, ec, :], in_=wo_in[l, :, ec, :]), t_wo, writes=[t_wo], partial=True)
        nb = 0
        for dc in range(KD):
            for tg in range(4):
                ts = slice(tg * 512, (tg + 1) * 512)
                bk = nb % 4
                nb += 1
                for ec in range(KD):
                    P.op("pe", lambda e, bk=bk, ec=ec, dc=dc, ts=ts: e.matmul(
                        banks[bk], lhsT=wo[:, ec, dc * 128:(dc + 1) * 128], rhs=oT[:, ec, ts], start=(ec == 0), stop=(ec == KD - 1)),
                        reads=[t_wo, t_o], writes=[tb[bk]], partial=(ec > 0))
                P.op("dve", lambda e, bk=bk, dc=dc, ts=ts, l=l: e.scalar_tensor_tensor(
                    out=xT[:, dc, ts], in0=banks[bk], scalar=mods[:, l * 48 + 16 + dc, 0:1], in1=xT[:, dc, ts], op0=ALU.mult, op1=ALU.add),
                    reads=[tb[bk], t_mods, t_x], writes=[t_x], partial=True)
        smx = AR.alloc([4], F32)
        t_smx = Tk("smx")
        for dc in range(KD):
            bk = 4 + dc % 2
            for ec in range(KD):
                P.op("pe", lambda e, bk=bk, ec=ec, dc=dc: e.matmul(banks[bk][:, 0:4], lhsT=wo[:, ec, dc * 128:(dc + 1) * 128], rhs=osT[:, ec, :], start=(ec == 0), stop=(ec == KD - 1)),
                     reads=[t_wo, t_os], writes=[tb[bk]], partial=(ec > 0))
            P.op("dve", lambda e, bk=bk, dc=dc, l=l: e.tensor_tensor(out=smx, in0=banks[bk][:, 0:4], in1=mods[:, l * 48 + 16 + dc, 1:5], op=ALU.mult), reads=[tb[bk], t_mods], writes=[t_smx])
            P.op("dve", lambda e, dc=dc: e.tensor_tensor(out=xsT[:, dc, :], in0=xsT[:, dc, :], in1=smx, op=ALU.add), reads=[t_smx, t_xs], writes=[t_xs], partial=True)
        P.barrier()
        AR.reset(mO)
        norm_mod(A2, B2, [0, 1])
        P.barrier()
        sample_norm(l, 4, 3, n2g[l])
        P.barrier()
        fsp = AR.alloc([24, 3, 4], F32)
        fso = AR.alloc([24, 2, 4], F32)
        sact = AR.alloc([4, 4], BF16)
        sfa = AR.alloc([4], F32)
        t_fsp, t_fso, t_sact, t_sfa = Tk("fsp"), Tk("fso"), Tk("sact"), Tk("sfa")
        P.dma("sp", lambda e, l=l: e.dma_start(out=fsp[:, :, 0:2, :], in_=fst_in[l]), t_fsp, writes=[t_fsp])
        P.op("dve", lambda e: e.memset(fso, 0.0), writes=[t_fso])
        wup = [AR.alloc([KD, 2, 512], BF16) for _ in range(2)]
        wdn = [AR.alloc([4, D], BF16) for _ in range(2)]
        t_wup = [Tk("wup0"), Tk("wup1")]
        t_wdn = [Tk("wdn0"), Tk("wdn1")]
        fcw = AR.alloc([24, 3], F32)
        fcb = AR.alloc([24], F32)
        t_fc = Tk("fc")
        P.dma("sp", lambda e, l=l: e.dma_start(out=fcw, in_=fcw_in[l]), t_fc, writes=[t_fc], partial=True)
        P.dma("sp", lambda e, l=l: e.dma_start(out=fcb, in_=fcb_in[l]), t_fc, writes=[t_fc], partial=True)
        fst = [AR.alloc([514], F32) for _ in range(2)]
        t_fst = [Tk("fst0"), Tk("fst1")]
        hal = AR.alloc([4, 2], F32)
        t_hal = Tk("hal")
        fac = [AR.alloc([512], F32) for _ in range(2)]
        t_fac = [Tk("fac0"), Tk("fac1")]
        actT = [AR.alloc([4, 512], BF16) for _ in range(2)]
        t_act = [Tk("act0"), Tk("act1")]
        nbk = 0
        it = 0
        for fg in range(6):
            nf = 4 if fg < 5 else 2
            ws = fg % 2
            for kc in range(KD):
                P.dma("pool", lambda e, l=l, fg=fg, ws=ws, kc=kc: e.dma_start(out=wup[ws][:, kc, :, :], in_=wup_in[l, fg, :, kc, :, :]),
                      t_wup[ws], writes=[t_wup[ws]], partial=(kc > 0))
            for fcl in range(nf):
                P.dma("pool", lambda e, l=l, fg=fg, ws=ws, fcl=fcl: e.dma_start(out=wdn[ws][:, fcl, :], in_=wdn_in[l, :, fg * 4 + fcl, :]),
                      t_wdn[ws], writes=[t_wdn[ws]], partial=(fcl > 0))
            for tg in range(4):
                ts = slice(tg * 512, (tg + 1) * 512)
                asl = it % 2
                it += 1
                for fcl in range(nf):
                    fc = fg * 4 + fcl
                    bu = 2 + (nbk % 2) * 2
                    bv = bu + 1
                    nbk += 1
                    si = fcl % 2
                    for kc in range(KD):
                        P.op("pe", lambda e, bu=bu, kc=kc, ws=ws, fcl=fcl, ts=ts: e.matmul(
                            banks[bu], lhsT=wup[ws][:, kc, 0, fcl * 128:(fcl + 1) * 128], rhs=hT[:, kc, ts], start=(kc == 0), stop=(kc == KD - 1)),
                            reads=[t_wup[ws], t_h], writes=[tb[bu]], partial=(kc > 0))
                    for kc in range(KD):
                        P.op("pe", lambda e, bv=bv, kc=kc, ws=ws, fcl=fcl, ts=ts: e.matmul(
                            banks[bv], lhsT=wup[ws][:, kc, 1, fcl * 128:(fcl + 1) * 128], rhs=hT[:, kc, ts], start=(kc == 0), stop=(kc == KD - 1)),
                            reads=[t_wup[ws], t_h], writes=[tb[bv]], partial=(kc > 0))
                    if tg == 0:
                        P.op("dve", lambda e, si=si: e.memset(fst[si][:, 0:2], 0.0), writes=[t_fst[si]], partial=True)
                    else:
                        P.op("dve", lambda e, si=si, fcl=fcl: e.tensor_copy(out=fst[si][:, 0:2], in_=hal[:, fcl, :]), reads=[t_hal], writes=[t_fst[si]], partial=True)
                    P.op("act", lambda e, bu=bu, si=si: e.copy(out=fst[si][:, 2:514], in_=banks[bu]), reads=[tb[bu]], writes=[t_fst[si]], partial=True)
                    if tg < 3:
                        P.op("dve", lambda e, si=si, fcl=fcl: e.tensor_copy(out=hal[:, fcl, :], in_=fst[si][:, 512:514]), reads=[t_fst[si]], writes=[t_hal], partial=True)
                    else:
                        P.dma("sp", lambda e, l=l, si=si, fc=fc: e.dma_start(out=fconv_out[l, :, fc, :], in_=fst[si][:, 512:514]), t_fst[si], reads=[t_fst[si]])
                    P.op("dve", lambda e, si=si, fc=fc: e.tensor_scalar(out=fac[si], in0=fst[si][:, 2:514], scalar1=fcw[:, fc, 2:3], scalar2=fcb[:, fc:fc + 1], op0=ALU.mult, op1=ALU.add),
                         reads=[t_fst[si], t_fc], writes=[t_fac[si]])
                    for k in range(2):
                        P.op("dve", lambda e, si=si, fc=fc, k=k: e.scalar_tensor_tensor(out=fac[si], in0=fst[si][:, k:k + 512], scalar=fcw[:, fc, k:k + 1], in1=fac[si], op0=ALU.mult, op1=ALU.add),
                             reads=[t_fst[si], t_fc], writes=[t_fac[si]])
                    P.op("act", lambda e, si=si: e.activation(out=fac[si], in_=fac[si], func=AF.Silu), reads=[t_fac[si]], writes=[t_fac[si]])
                    P.op("dve", lambda e, si=si, bv=bv, asl=asl, fcl=fcl: e.tensor_tensor(out=actT[asl][:, fcl, :], in0=fac[si], in1=banks[bv], op=ALU.mult),
                         reads=[t_fac[si], tb[bv]], writes=[t_act[asl]], partial=(fcl > 0))
                for dc in range(KD):
                    bd = dc % 2
                    for fcl in range(nf):
                        P.op("pe", lambda e, bd=bd, ws=ws, fcl=fcl, dc=dc, asl=asl, nf=nf: e.matmul(
                            banks[bd], lhsT=wdn[ws][:, fcl, dc * 128:(dc + 1) * 128], rhs=actT[asl][:, fcl, :], start=(fcl == 0), stop=(fcl == nf - 1)),
                            reads=[t_wdn[ws], t_act[asl]], writes=[tb[bd]], partial=(fcl > 0))
                    P.op("dve", lambda e, bd=bd, dc=dc, ts=ts, l=l: e.scalar_tensor_tensor(
                        out=xT[:, dc, ts], in0=banks[bd], scalar=mods[:, l * 48 + 40 + dc, 0:1], in1=xT[:, dc, ts], op0=ALU.mult, op1=ALU.add),
                        reads=[tb[bd], t_mods, t_x], writes=[t_x], partial=True)
            for fcl in range(nf):
                fc = fg * 4 + fcl
                for half in range(2):
                    for kc in range(KD):
                        P.op("pe", lambda e, kc=kc, ws=ws, fcl=fcl, half=half: e.matmul(banks[6][:, half * 4:(half + 1) * 4], lhsT=wup[ws][:, kc, half, fcl * 128:(fcl + 1) * 128], rhs=hsT[:, kc, :],
                                                                                    start=(kc == 0), stop=(kc == KD - 1)), reads=[t_wup[ws], t_hs], writes=[tb[6]], partial=(half + kc > 0))
                P.op("act", lambda e, fc=fc: e.copy(out=fsp[:, fc, 2, :], in_=banks[6][:, 0:4]), reads=[tb[6]], writes=[t_fsp], partial=True)
                P.op("dve", lambda e, fc=fc: e.tensor_scalar(out=sfa, in0=fsp[:, fc, 2, :], scalar1=fcw[:, fc, 2:3], scalar2=fcb[:, fc:fc + 1], op0=ALU.mult, op1=ALU.add),
                     reads=[t_fsp, t_fc], writes=[t_sfa])
                for k in range(2):
                    P.op("dve", lambda e, fc=fc, k=k: e.scalar_tensor_tensor(out=sfa, in0=fsp[:, fc, k, :], scalar=fcw[:, fc, k:k + 1], in1=sfa, op0=ALU.mult, op1=ALU.add),
                         reads=[t_fsp, t_fc, t_sfa], writes=[t_sfa])
                P.op("act", lambda e: e.activation(out=sfa, in_=sfa, func=AF.Silu), reads=[t_sfa], writes=[t_sfa])
                P.op("dve", lambda e, fcl=fcl: e.tensor_tensor(out=sact[:, fcl, :], in0=sfa, in1=banks[6][:, 4:8], op=ALU.mult), reads=[t_sfa, tb[6]], writes=[t_sact], partial=(fcl > 0))
            for dc in range(KD):
                for fcl in range(nf):
                    P.op("pe", lambda e, ws=ws, fcl=fcl, dc=dc, nf=nf: e.matmul(banks[7][:, 0:4], lhsT=wdn[ws][:, fcl, dc * 128:(dc + 1) * 128], rhs=sact[:, fcl, :], start=(fcl == 0), stop=(fcl == nf - 1)),
                         reads=[t_wdn[ws], t_sact], writes=[tb[7]], partial=(fcl > 0))
                P.op("dve", lambda e, dc=dc, l=l: e.tensor_tensor(out=sfa, in0=banks[7][:, 0:4], in1=mods[:, l * 48 + 40 + dc, 1:5], op=ALU.mult), reads=[tb[7], t_mods], writes=[t_sfa])
                P.op("dve", lambda e, dc=dc: e.tensor_tensor(out=xsT[:, dc, :], in0=xsT[:, dc, :], in1=sfa, op=ALU.add), reads=[t_sfa, t_xs], writes=[t_xs], partial=True)
        P.barrier()
        P.op("dve", lambda e: e.tensor_copy(out=fso[:, 0:NFC, 0, :], in_=fsp[:, 0:NFC, 1, :]), reads=[t_fsp], writes=[t_fso], partial=True)
        P.op("dve", lambda e: e.tensor_copy(out=fso[:, 0:NFC, 1, :], in_=fsp[:, 0:NFC, 2, :]), reads=[t_fsp], writes=[t_fso], partial=True)
        P.dma("sp", lambda e, l=l: e.dma_start(out=fcs_out[l], in_=fso), t_fso, reads=[t_fso])
        if l < DEPTH - 1:
            for kc in range(KD):
                P.dma("sp", lambda e, kc=kc: e.dma_start(out=xscr[:, kc, :], in_=xT[:, kc, :]), t_x, reads=[t_x], writes=[t_xscr], partial=True)
        P.barrier()
    if stage >= 99:
        AR.reset(layer_mark)
        AR.alloc([KD, T], F32)
        sq = AR.alloc([KD, 512], BF16)
        rs = AR.alloc([512], F32)
        yst = [AR.alloc([512], F32) for _ in range(2)]
        t_sq, t_rs, t_yst = Tk("fsq"), Tk("frs"), [Tk("yst0"), Tk("yst1")]
        for tg in range(4):
            ts = slice(tg * 512, (tg + 1) * 512)
            bk = tg % 2
            P.op("act", lambda e, ts=ts: e.activation(out=sq, in_=xT[:, :, ts], func=AF.Square), reads=[t_x], writes=[t_sq])
            for kc in range(KD):
                P.op("pe", lambda e, kc=kc, bk=bk: e.matmul(banks[bk], lhsT=onesb, rhs=sq[:, kc, :], start=(kc == 0), stop=(kc == KD - 1)),
                     reads=[t_sq, t_const], writes=[tb[bk]], partial=(kc > 0))
            P.op("dve", lambda e, bk=bk: e.tensor_scalar(out=rs, in0=banks[bk], scalar1=1.0 / D, scalar2=EPS, op0=ALU.mult, op1=ALU.add),
                 reads=[tb[bk]], writes=[t_rs])
            P.op("act", lambda e: e.activation(out=rs, in_=rs, func=AF.Sqrt), reads=[t_rs], writes=[t_rs])
            P.op("dve", lambda e: e.reciprocal(out=rs, in_=rs), reads=[t_rs], writes=[t_rs])
            for kc in range(KD):
                yi = kc % 2
                P.op("dve", lambda e, kc=kc, ts=ts, yi=yi: e.scalar_tensor_tensor(
                    out=yst[yi], in0=xT[:, kc, ts], scalar=fing[:, kc:kc + 1], in1=rs, op0=ALU.mult, op1=ALU.mult),
                    reads=[t_x, t_rs, t_small], writes=[t_yst[yi]])
                P.dma("sp", lambda e, kc=kc, ts=ts, yi=yi: e.dma_start(out=yT_out[:, kc, ts], in_=yst[yi]), t_yst[yi], reads=[t_yst[yi]])
        st1 = AR.alloc([KD, 4], F32)
        srs = AR.alloc([4], F32)
        t_st1, t_srs = Tk("fst1"), Tk("fsrs")
        P.op("dve", lambda e: e.tensor_tensor(out=st1, in0=xsT, in1=xsT, op=ALU.mult), reads=[t_xs], writes=[t_st1])
        for kc in range(KD):
            P.op("pe", lambda e, kc=kc: e.matmul(banks[2][:, 0:4], lhsT=onesf, rhs=st1[:, kc, :], start=(kc == 0), stop=(kc == KD - 1)), reads=[t_st1, t_const], writes=[tb[2]], partial=(kc > 0))
        P.op("dve", lambda e: e.tensor_scalar(out=srs, in0=banks[2][:, 0:4], scalar1=1.0 / D, scalar2=EPS, op0=ALU.mult, op1=ALU.add), reads=[tb[2]], writes=[t_srs])
        P.op("act", lambda e: e.activation(out=srs, in_=srs, func=AF.Sqrt), reads=[t_srs], writes=[t_srs])
        P.op("dve", lambda e: e.reciprocal(out=srs, in_=srs), reads=[t_srs], writes=[t_srs])
        for kc in range(KD):
            P.op("dve", lambda e, kc=kc: e.scalar_tensor_tensor(out=st1[:, kc, :], in0=xsT[:, kc, :], scalar=fing[:, kc:kc + 1], in1=srs, op0=ALU.mult, op1=ALU.mult),
                 reads=[t_xs, t_srs, t_small], writes=[t_st1], partial=True)
        P.dma("sp", lambda e: e.dma_start(out=ysT_out, in_=st1), t_st1, reads=[t_st1])
    P.barrier()
    P.emit()
    return nc


_PROG_CACHE = {}


def _host_consts():
    r = np.arange(128)
    f32 = np.float32
    tok = np.arange(T)
    ncmp = np.arange(128)
    ends = 16 * ncmp + 31
    cmpmask = np.where((ends[:, None] > tok[None, :]) | (ncmp[:, None] >= 127), NEGB, 0.0).astype(f32)
    eall = np.zeros((128, T), f32)
    for base in (0, 64):
        eall[base + (tok // 64), tok] = 1.0
    slopes = 2.0 ** (-(np.arange(8) + 1.0))
    kaug = np.zeros((128, 32, 128), f32)
    qaug = np.zeros((128, 8, 128), f32)
    for base in (0, 64):
        for dlt in range(16):
            kaug[base + 0, dlt, :] = -dlt
            kaug[base + 1, dlt, :] = r
            kaug[base + 2, dlt, :] = 1.0
        for qt in range(16):
            kaug[base + 0, 16 + qt, :] = ends // 128 - qt
            kaug[base + 1, 16 + qt, :] = ends % 128
            kaug[base + 2, 16 + qt, :] = 1.0
        for h in range(8):
            qaug[base + 0, h, :] = 128.0 * slopes[h]
            qaug[base + 1, h, :] = slopes[h]
            qaug[base + 2, h, :] = -slopes[h] * r
    ov = np.zeros((128, 33), f32)
    ov[:, 0] = 1.0
    cs_ = 16 * ncmp
    ss_ = 64 * np.arange(32)
    ov[:, 1:] = ((cs_[:, None] < ss_[None, :] + 64) & (cs_[:, None] + 32 > ss_[None, :])).astype(f32)
    ov[127, 1:] = 0.0
    qpos = (np.arange(NT)[None, :] * 128 + r[:, None])
    cur = qpos // 64
    jj = np.arange(32)[None, None, :]
    forced = (jj == 0) | (jj == cur[:, :, None]) | (jj == cur[:, :, None] - 1)
    valid = (jj * 64) <= qpos[:, :, None]
    bonus = np.where(forced, 1e6 * 2.0 ** 64, 0.0).astype(f32)
    cap = np.where(valid, 3e38, -1e30).astype(f32)
    idxb = np.broadcast_to((31 - np.arange(32)).astype(np.int32)[None, :], (128, 32)).copy()
    augL = np.zeros((128, 2, 128), f32)
    augR = np.zeros((128, 2, 296), f32)
    A_t = np.array([kt - 64 for kt in range(64)] + [0] + [kt - 4 for kt in range(4)] + [0], f32)
    for base in (0, 64):
        augL[base + 0, 0, :] = r
        augL[base + 1, 0, :] = 1.0
        augL[base + 2, 0, :] = 1.0
        augL[base + 0, 1, :] = 16.0 * r
        augL[base + 1, 1, :] = 1.0
        augL[base + 2, 1, :] = 1.0
        for g in range(2):
            for j in range(4):
                sj = slopes[4 * g + j]
                augR[base + 0, g, j:280:4] = sj
                augR[base + 1, g, j:280:4] = 128.0 * sj * A_t
                augR[base + 0, g, 280 + j:296:4] = sj
                augR[base + 1, g, 280 + j:296:4] = 31.0 * sj
                augR[base + 2, g, 280 + j:296:4] = 2048.0 * sj * (np.arange(4) - 4)
    mcs = np.zeros((128, 296), f32)
    mcs[1:, 64 * 4:65 * 4] = NEGB
    mcs[1:, 69 * 4:70 * 4] = NEGB
    mcs[127, 280 + 12:296] = NEGB
    ns = np.arange(512)
    jb = np.arange(129)
    ovl = ((16 * ns[:, None] < 64 * jb[None, :] + 64) & (16 * ns[:, None] + 32 > 64 * jb[None, :])).astype(f32)
    ovl[511, :] = 0.0
    ovs = np.zeros((128, 4, 130), f32)
    ovs[:, :, 0] = 1.0
    ovs[:, :, 1:] = ovl.reshape(4, 128, 129).transpose(1, 0, 2)
    bonus_s = np.zeros((128, 129), f32)
    bonus_s[:, [0, 127, 128]] = 1e6 * 2.0 ** 64
    idx_s = np.broadcast_to((255 - jb).astype(np.int32)[None, :], (128, 129)).copy()
    oneo = np.zeros((128, 2, 128), f32)
    oneo[:, 0, :64] = 1.0
    oneo[:, 1, 64:] = 1.0
    pidx = np.broadcast_to(np.arange(128, dtype=np.int32)[:, None], (128, 256)).copy()
    return {"augL": augL, "augR": augR, "mcs": mcs, "ovs": ovs, "bonus_s": bonus_s, "idx_s": idx_s, "oneo": oneo, "pidx": pidx,
            "ident": np.eye(128, dtype=np.float32),
            "mtri": np.where(r[:, None] > r[None, :], NEGB, 0.0).astype(f32),
            "manti": np.where(r[None, :] > r[:, None], NEGB, 0.0).astype(f32),
            "cmpmask": cmpmask, "eall": eall, "kaug": kaug, "qaug": qaug, "ov": ov,
            "bonus": bonus, "cap": cap, "idxb": idxb,
            "utri": (r[:, None] <= r[None, :]).astype(np.float32),
            "mtrif": np.where(r[:, None] > r[None, :], NEGB, 0.0).astype(np.float32)}


def kernel(**inp):
    f32 = np.float32
    nc = _PROG_CACHE.get("nc")
    if nc is None:
        nc = build_program()
        _PROG_CACHE["nc"] = nc
    shared = dict(_host_consts())
    ada_w = np.asarray(inp["ada_w"], f32)
    shared["adaw"] = np.ascontiguousarray(
        ada_w.reshape(DEPTH, KD, 128, 6, 1024).transpose(0, 3, 2, 1, 4))
    shared["adab"] = np.ascontiguousarray(np.asarray(inp["ada_b"], f32).reshape(DEPTH, 48, 128).transpose(0, 2, 1))
    shared["n1g"] = np.stack([_col(np.asarray(inp["norm1_g"], f32)[l], KD) for l in range(DEPTH)])
    shared["n2g"] = np.stack([_col(np.asarray(inp["norm2_g"], f32)[l], KD) for l in range(DEPTH)])
    shared["fing"] = _col(np.asarray(inp["final_g"], f32), KD)
    w_in = np.asarray(inp["w_in"], f32)
    tm_cols = np.concatenate([np.arange(512, 1816), np.arange(2584, 2592)])
    shared["wtm"] = np.stack([_lay_kc(w_in[l][:, tm_cols]) for l in range(DEPTH)])
    fm_chunks = [np.arange(128 * i, 128 * i + 128) for i in range(4)]
    fm_chunks += [np.arange(512, 640), np.arange(640, 768)]
    fm_chunks += [np.concatenate([np.arange(768 + 64 * g, 832 + 64 * g)] * 2) for g in range(2)]
    fm_chunks += [np.concatenate([np.arange(1024 + 64 * g, 1088 + 64 * g)] * 2) for g in range(2)]
    fm_chunks += [np.arange(1816 + 128 * c, 1816 + 128 * c + 128) for c in range(6)]
    fm_chunks += [np.concatenate([np.arange(64 * j, 64 * j + 64), np.arange(256 + 64 * j, 320 + 64 * j)]) for j in range(4)]
    shared["wfm"] = np.stack([np.stack([_lay_kc(w_in[l][:, ch]) for ch in fm_chunks]) for l in range(DEPTH)])
    scw = np.asarray(inp["ssm_conv_w"], f32)
    shared["scw"] = np.ascontiguousarray(scw.reshape(DEPTH, 4, 6, 128).transpose(0, 3, 2, 1))
    rep = lambda v: np.ascontiguousarray(np.broadcast_to(np.asarray(v, f32)[:, None, :], (DEPTH, 128, np.asarray(v).shape[-1])))
    shared["dtb"] = rep(inp["dt_bias"])
    shared["alog"] = rep(inp["a_log"])
    shared["dsk"] = rep(inp["d_skip"])
    shared["sng"] = rep(inp["ssm_norm_g"])
    shared["scb"] = np.ascontiguousarray(np.asarray(inp["ssm_conv_b"], f32).reshape(DEPTH, 6, 128).transpose(0, 2, 1))

    shared["wo"] = np.stack([_lay_kc(np.asarray(inp["w_out"], f32)[l]) for l in range(DEPTH)])
    wu = np.asarray(inp["ffn_w_up"], f32)
    wu = np.pad(wu.reshape(DEPTH, KD, 128, 2, D_FF), ((0, 0), (0, 0), (0, 0), (0, 0), (0, 3072 - D_FF)))
    shared["wup"] = np.ascontiguousarray(wu.reshape(DEPTH, KD, 128, 2, 6, 512).transpose(0, 4, 2, 1, 3, 5))
    wd = np.pad(np.asarray(inp["ffn_w_down"], f32), ((0, 0), (0, 3072 - D_FF), (0, 0)))
    shared["wdn"] = np.ascontiguousarray(wd.reshape(DEPTH, 24, 128, D).transpose(0, 2, 1, 3))
    fw = np.pad(np.asarray(inp["ffn_conv_w"], f32), ((0, 0), (0, 0), (0, 3072 - D_FF)))
    shared["fcw"] = np.ascontiguousarray(fw.reshape(DEPTH, 3, 24, 128).transpose(0, 3, 2, 1))
    fb = np.pad(np.asarray(inp["ffn_conv_b"], f32), ((0, 0), (0, 3072 - D_FF)))
    shared["fcb"] = np.ascontiguousarray(fb.reshape(DEPTH, 24, 128).transpose(0, 2, 1))
    w1 = np.stack([np.asarray(inp["cmpk_w1"], f32), np.asarray(inp["cmpv_w1"], f32)], axis=1)
    w1 = w1.transpose(0, 1, 3, 2, 4)
    shared["w1d"] = np.ascontiguousarray(np.concatenate([w1, w1], axis=2))
    w2k = np.asarray(inp["cmpk_w2"], f32)
    shared["w2k"] = np.ascontiguousarray(np.concatenate([w2k, w2k], axis=2))
    shared["w2v"] = np.ascontiguousarray(np.asarray(inp["cmpv_w2"], f32))
    shared["cb1"] = np.ascontiguousarray(np.stack([np.asarray(inp["cmpk_b1"], f32), np.asarray(inp["cmpv_b1"], f32)], axis=2))
    hc = np.stack([np.asarray(inp["dt_bias"], f32), np.asarray(inp["a_log"], f32), np.asarray(inp["d_skip"], f32)], axis=2)
    shared["hcol"] = np.ascontiguousarray(hc)
    dsk = np.asarray(inp["d_skip"], f32)
    shared["dskc"] = np.ascontiguousarray(np.repeat(dsk.reshape(DEPTH, 4, 2, 1), 64, axis=3).transpose(0, 2, 3, 1).reshape(DEPTH, 128, 4))
    shared["sngc"] = np.ascontiguousarray(np.asarray(inp["ssm_norm_g"], f32).reshape(DEPTH, 4, 128).transpose(0, 2, 1))
    ex = np.zeros((8, 4, 128), f32)
    for c in range(4):
        for hh in range(2):
            ex[2 * c + hh, c, hh * 64:(hh + 1) * 64] = 1.0
    shared["ex"] = ex
    def poolT(a):
        a = np.asarray(a, f32).reshape(DEPTH, -1, 128, 128)
        return np.ascontiguousarray(a.transpose(0, 1, 3, 2)).reshape(DEPTH, -1, 128)
    for nm, key in (("cmpkT_pool", "cache_cmp_k"), ("cmpvT_pool", "cache_cmp_v"), ("slckT_pool", "cache_slc_k")):
        pt_ = poolT(inp[key])
        for l in range(DEPTH):
            shared["%s%d" % (nm, l)] = pt_[l]
    sv_ = np.asarray(inp["cache_slc_v"], f32).reshape(DEPTH, -1, 128)
    for l in range(DEPTH):
        shared["slcv_pool%d" % l] = np.ascontiguousarray(sv_[l])
    page_table = np.asarray(inp["page_table"], np.int32)
    st_ffn = np.asarray(inp["state_ffn_conv"], f32)
    x_sample = np.asarray(inp["x_sample"], f32)
    st_conv = np.asarray(inp["state_ssm_conv"], f32)
    st_ssm = np.asarray(inp["state_ssm"], f32)
    st_wk = np.asarray(inp["state_win_k"], f32).reshape(DEPTH, 32, 512, 128)
    st_wv = np.asarray(inp["state_win_v"], f32).reshape(DEPTH, 32, 512, 128)
    x_prompt = np.asarray(inp["x_prompt"], f32)
    c_prompt = np.asarray(inp["c_prompt"], f32)
    c_sample = np.asarray(inp["c_sample"], f32)
    in_maps = []
    for i in range(NCORES):
        m = dict(shared)
        m["xT_in"] = np.ascontiguousarray(x_prompt[i].T.reshape(KD, 128, T).transpose(1, 0, 2))
        c5 = np.concatenate([c_prompt[i:i + 1], c_sample[4 * i:4 * i + 4]], axis=0)
        m["cT_in"] = np.ascontiguousarray(c5.T.reshape(KD, 128, 5).transpose(1, 0, 2))
        sb = slice(4 * i, 4 * i + 4)
        m["xsT_in"] = np.ascontiguousarray(x_sample[sb, 0, :].T.reshape(KD, 128, 4).transpose(1, 0, 2))
        m["cst_in"] = np.ascontiguousarray(st_conv[:, sb].reshape(DEPTH, 4, 3, 6, 128).transpose(0, 4, 3, 2, 1))
        m["hs0_in"] = np.ascontiguousarray(st_ssm[:, sb].reshape(DEPTH, 4, 4, 2, 64, 64).transpose(0, 3, 4, 2, 1, 5).reshape(DEPTH, 128, 4, 4, 64))
        m["ptb"] = np.ascontiguousarray(np.broadcast_to(page_table[sb].reshape(1, 256), (128, 256)))
        m["wkT_in"] = np.ascontiguousarray(st_wk[:, sb].transpose(0, 1, 3, 2))
        sf = np.pad(st_ffn[:, sb], ((0, 0), (0, 0), (0, 0), (0, 3072 - D_FF)))
        m["fst_in"] = np.ascontiguousarray(sf.reshape(DEPTH, 4, 2, 24, 128).transpose(0, 4, 3, 2, 1))
        m["wk_in"] = np.ascontiguousarray(st_wk[:, sb])
        m["wv_in"] = np.ascontiguousarray(st_wv[:, sb])
        in_maps.append(m)
    res = run_bass_kernel_spmd(nc, in_maps, core_ids=list(range(NCORES)))
    R = res.results
    B, S = 8, 32
    kv = np.stack([R[i]["kv_out"] for i in range(NCORES)], axis=1)
    kvp = [kv[..., 128 * j:128 * j + 128].reshape(DEPTH, B, T, 2, 64) for j in range(6)]
    yT = np.stack([R[i]["yT_out"] for i in range(NCORES)])
    y_prompt = np.ascontiguousarray(yT.transpose(0, 3, 2, 1).reshape(B, T, D))
    z = lambda *s: np.zeros(s, f32)
    kvn = np.stack([R[i]["kvn_out"] for i in range(NCORES)], axis=1)
    kvn = kvn.transpose(0, 1, 4, 3, 2).reshape(DEPTH, S, 6, 1, 2, 64)
    kvs = [np.ascontiguousarray(kvn[:, :, j]) for j in range(6)]
    wks = np.concatenate([R[i]["wk_out"] for i in range(NCORES)], axis=1).reshape(DEPTH, S, 512, 2, 64)
    wvs = np.concatenate([R[i]["wv_out"] for i in range(NCORES)], axis=1).reshape(DEPTH, S, 512, 2, 64)
    scs = np.stack([R[i]["sconvs_out"] for i in range(NCORES)], axis=1)
    sconv_s = np.ascontiguousarray(scs.transpose(0, 1, 5, 4, 3, 2).reshape(DEPTH, S, 3, 768))
    sss = np.stack([R[i]["ssms_out"] for i in range(NCORES)], axis=1)
    ssm_s = np.ascontiguousarray(sss.reshape(DEPTH, 8, 2, 64, 4, 4, 64).transpose(0, 1, 5, 4, 2, 3, 6).reshape(DEPTH, S, 8, 64, 64))
    fcs = np.stack([R[i]["fcs_out"] for i in range(NCORES)], axis=1)
    fconv_s = np.ascontiguousarray(fcs.transpose(0, 1, 5, 4, 3, 2).reshape(DEPTH, S, 2, 3072)[..., :D_FF])
    ysT = np.stack([R[i]["ysT_out"] for i in range(NCORES)])
    y_sample = np.ascontiguousarray(ysT.transpose(0, 3, 2, 1).reshape(S, 1, D))
    fconv = np.stack([R[i]["fconv_out"] for i in range(NCORES)], axis=1)
    fconv_p = np.ascontiguousarray(fconv.transpose(0, 1, 4, 3, 2).reshape(DEPTH, B, 2, D_FF))
    sconv = np.stack([R[i]["sconv_out"] for i in range(NCORES)], axis=1)
    sconv_p = np.ascontiguousarray(sconv.transpose(0, 1, 4, 3, 2).reshape(DEPTH, B, 3, 768))
    ssm = np.stack([R[i]["ssm_out"] for i in range(NCORES)], axis=1)
    ssm_p = np.ascontiguousarray(ssm.reshape(DEPTH, B, 2, 64, 4, 64).transpose(0, 1, 2, 4, 5, 3).reshape(DEPTH, B, 8, 64, 64))
    outs = (y_prompt, y_sample,
            kvp[0], kvs[0], kvp[1], kvs[1],
            kvp[2], kvs[2], kvp[3], kvs[3],
            np.ascontiguousarray(kvp[4][:, :, T - 512:]), wks,
            np.ascontiguousarray(kvp[5][:, :, T - 512:]), wvs,
            ssm_p, ssm_s,
            sconv_p, sconv_s,
            fconv_p, fconv_s)
    return outs
```

```python
import numpy as np
import concourse.bass as bass
import concourse.mybir as mybir
from concourse.bass_utils import run_bass_kernel_spmd

F32 = mybir.dt.float32
BF16 = mybir.dt.bfloat16
I32 = mybir.dt.int32
ALU = mybir.AluOpType
AF = mybir.ActivationFunctionType
AX = mybir.AxisListType

NCORES = 8
D = 1024
KD = 8
T = 2048
NT = 16
DEPTH = 2
IN_DIM = 2592
D_FF = 2816
NFC = 22
EPS = 1e-6
NEGB = -30000.0
STAGE = 99

SAME_ENGINE_SYNC = True


class Tk:
    __slots__ = ("name", "w", "r", "sem", "cnt", "excl")

    def __init__(self, name, excl=False):
        self.name = name
        self.excl = excl
        self.w = {}
        self.r = {}
        self.sem = None
        self.cnt = 0


class Prog:
    ENG = ("pe", "act", "dve", "pool", "sp")

    def __init__(self, nc):
        self.nc = nc
        self.ops = {e: [] for e in self.ENG}
        self.cnt = {e: 0 for e in self.ENG}
        self.known = {e: {} for e in self.ENG}
        self.esem = {e: nc.alloc_semaphore("sem_" + e) for e in self.ENG}
        self.needed = {e: set() for e in self.ENG}
        self.dsems = []

    def _dsem(self, t):
        if t.sem is None:
            t.sem = self.nc.alloc_semaphore("dsem_%d" % len(self.dsems))
            self.dsems.append(t)
        return t.sem

    def _waits(self, eng, reads, writes, extra=()):
        need = {}
        for t in reads:
            for k, v in t.w.items():
                if need.get(k, 0) < v:
                    need[k] = v
        for t in writes:
            for k, v in t.w.items():
                if need.get(k, 0) < v:
                    need[k] = v
            for k, v in t.r.items():
                if need.get(k, 0) < v:
                    need[k] = v
        for k, v in extra:
            if need.get(k, 0) < v:
                need[k] = v
        out = []
        kn = self.known[eng]
        for k, v in need.items():
            if isinstance(k, str) and k == eng and (eng == "pe" or not SAME_ENGINE_SYNC):
                continue
            if kn.get(k, 0) >= v:
                continue
            kn[k] = v
            if isinstance(k, str):
                self.needed[k].add(v)
            out.append((k, v))
        return out

    def _mark(self, ev, reads, writes, partial):
        k, v = ev
        for t in reads:
            if t.r.get(k, 0) < v:
                t.r[k] = v
        for t in writes:
            if partial:
                if t.w.get(k, 0) < v:
                    t.w[k] = v
            else:
                t.w = {k: v}
                t.r = {}

    def op(self, eng, fn, reads=(), writes=(), partial=False):
        ex = [t for t in reads if t.excl]
        waits = self._waits(eng, reads, list(writes) + ex)
        self.cnt[eng] += 1
        ev = (eng, self.cnt[eng])
        self.ops[eng].append((waits, fn, ("E", eng, self.cnt[eng])))
        self._mark(ev, [t for t in reads if not t.excl], writes, partial)
        if ex:
            self._mark(ev, (), ex, True)

    def dma(self, eng, fn, semt, reads=(), writes=(), n=1, partial=False):
        sem = self._dsem(semt)
        extra = [(sem, semt.cnt)] if semt.cnt else []
        waits = self._waits(eng, reads, writes, extra)
        semt.cnt += 16 * n
        self.ops[eng].append((waits, fn, ("D", sem, 16)))
        self._mark((sem, semt.cnt), reads, writes, partial)

    def barrier(self):
        for e in self.ENG:
            waits = []
            kn = self.known[e]
            for k in self.ENG:
                if k != e and self.cnt[k] > kn.get(k, 0):
                    kn[k] = self.cnt[k]
                    self.needed[k].add(self.cnt[k])
                    waits.append((k, self.cnt[k]))
            for t in self.dsems:
                if t.cnt > kn.get(t.sem, 0):
                    kn[t.sem] = t.cnt
                    waits.append((t.sem, t.cnt))
            if waits:
                self.ops[e].append((waits, None, None))

    def emit(self):
        nc = self.nc
        engobj = {"pe": "tensor", "act": "scalar", "dve": "vector", "pool": "gpsimd", "sp": "sync"}
        rank = {e: {raw: i + 1 for i, raw in enumerate(sorted(self.needed[e]))} for e in self.ENG}
        esem, needed = self.esem, self.needed
        with nc.Block() as block:
            for e in self.ENG:
                ops = self.ops[e]
                if not ops:
                    continue

                def body(eng, ops=ops):
                    for waits, fn, inc in ops:
                        for k, v in waits:
                            if isinstance(k, str):
                                eng.wait_ge(esem[k], rank[k][v])
                            else:
                                eng.wait_ge(k, v)
                        if fn is None:
                            continue
                        ins = fn(eng)
                        if inc[0] == "E":
                            if inc[2] in needed[inc[1]]:
                                ins.then_inc(esem[inc[1]], 1)
                        elif isinstance(ins, (list, tuple)):
                            for i in ins:
                                i.then_inc(inc[1], inc[2])
                        else:
                            ins.then_inc(inc[1], inc[2])
                getattr(block, engobj[e])(body)


class Arena:
    def __init__(self, nc, nbytes):
        self.base = nc.alloc_sbuf_tensor("arena", [128, nbytes // 4], F32).ap()
        self.nbytes = nbytes
        self.off = 0
        self.marks = []

    def alloc(self, shape, dtype, parts=128):
        esz = 4 if dtype in (F32, I32) else 2
        n = 1
        for s in shape:
            n *= s
        nb = (n * esz + 63) // 64 * 64
        assert self.off + nb <= self.nbytes, ("arena overflow", self.off, nb, self.nbytes)
        v = self.base[0:parts, self.off // 4:(self.off + nb) // 4]
        if dtype != F32:
            v = v.bitcast(dtype)
        v = v[:, 0:n]
        self.off += nb
        if len(shape) == 2:
            v = v.rearrange("p (a b) -> p a b", b=shape[1])
        elif len(shape) == 3:
            v = v.rearrange("p (a b c) -> p a b c", b=shape[1], c=shape[2])
        return v

    def mark(self):
        return self.off

    def reset(self, m):
        self.off = m


def _lay_kc(w, kc=KD):
    n = w.shape[1]
    return np.ascontiguousarray(w.reshape(kc, 128, n).transpose(1, 0, 2))


def _col(v, nch):
    return np.ascontiguousarray(v.reshape(nch, 128).T)


def build_program(stage=STAGE):
    nc = bass.Bass("TRN2", target_bir_lowering=False)
    P = Prog(nc)
    dr = {}

    def din(name, shape, dt=F32):
        dr[name] = nc.dram_tensor(name, list(shape), dt, kind="ExternalInput").ap()
        return dr[name]

    def dout(name, shape, dt=F32):
        dr[name] = nc.dram_tensor(name, list(shape), dt, kind="ExternalOutput").ap()
        return dr[name]

    xT_in = din("xT_in", [128, KD, T])
    cT_in = din("cT_in", [128, KD, 5])
    adaw_in = din("adaw", [DEPTH, 6, 128, KD, 1024])
    adab_in = din("adab", [DEPTH, 128, 48])
    n1g_in = din("n1g", [DEPTH, 128, KD])
    n2g_in = din("n2g", [DEPTH, 128, KD])
    fing_in = din("fing", [128, KD])
    wtm_in = din("wtm", [DEPTH, 128, KD, 1312])
    wfm_in = din("wfm", [DEPTH, 20, 128, KD, 128])
    scw_in = din("scw", [DEPTH, 128, 6, 4])
    scb_in = din("scb", [DEPTH, 128, 6])
    ident_in = din("ident", [128, 128])
    dtb_in = din("dtb", [DEPTH, 128, 8])
    alog_in = din("alog", [DEPTH, 128, 8])
    dsk_in = din("dsk", [DEPTH, 128, 8])
    sng_in = din("sng", [DEPTH, 128, 512])
    utri_in = din("utri", [128, 128])
    mtrif_in = din("mtrif", [128, 128])
    ssm_out = dout("ssm_out", [DEPTH, 128, 4, 64])
    NPOOL = 2560
    ptb_in = din("ptb", [128, 256], I32)
    pidx_in = din("pidx", [128, 256], I32)
    cmpkT_pool = [din("cmpkT_pool%d" % l, [NPOOL * 128, 128]) for l in range(DEPTH)]
    cmpvT_pool = [din("cmpvT_pool%d" % l, [NPOOL * 128, 128]) for l in range(DEPTH)]
    slckT_pool = [din("slckT_pool%d" % l, [NPOOL * 128, 128]) for l in range(DEPTH)]
    slcv_pool = [din("slcv_pool%d" % l, [NPOOL * 128, 128]) for l in range(DEPTH)]
    wkT_in = din("wkT_in", [DEPTH, 4, 128, 512])
    augL_in = din("augL", [128, 2, 128])
    augR_in = din("augR", [128, 2, 296])
    mcs_in = din("mcs", [128, 296])
    ovs_in = din("ovs", [128, 4, 130])
    bonus_s_in = din("bonus_s", [128, 129])
    idx_s_in = din("idx_s", [128, 129], I32)
    oneo_in = din("oneo", [128, 2, 128])
    fst_in = din("fst_in", [DEPTH, 128, 24, 2, 4])
    fcs_out = dout("fcs_out", [DEPTH, 128, 24, 2, 4])
    ysT_out = dout("ysT_out", [128, KD, 4])
    xsT_in = din("xsT_in", [128, KD, 4])
    cst_in = din("cst_in", [DEPTH, 128, 6, 3, 4])
    hs0_in = din("hs0_in", [DEPTH, 128, 4, 4, 64])
    hcol_in = din("hcol", [DEPTH, 8, 3])
    dskc_in = din("dskc", [DEPTH, 128, 4])
    sngc_in = din("sngc", [DEPTH, 128, 4])
    ex_in = din("ex", [8, 4, 128])
    wk_in = din("wk_in", [DEPTH, 4, 512, 128])
    wv_in = din("wv_in", [DEPTH, 4, 512, 128])
    wk_out = dout("wk_out", [DEPTH, 4, 512, 128])
    wv_out = dout("wv_out", [DEPTH, 4, 512, 128])
    kvn_out = dout("kvn_out", [DEPTH, 128, 6, 4])
    sconvs_out = dout("sconvs_out", [DEPTH, 128, 6, 3, 4])
    ssms_out = dout("ssms_out", [DEPTH, 128, 4, 4, 64])
    wo_in = din("wo", [DEPTH, 128, KD, D])
    wup_in = din("wup", [DEPTH, 6, 128, KD, 2, 512])
    wdn_in = din("wdn", [DEPTH, 128, 24, D])
    fcw_in = din("fcw", [DEPTH, 128, 24, 3])
    fcb_in = din("fcb", [DEPTH, 128, 24])
    fconv_out = dout("fconv_out", [DEPTH, 128, NFC, 2])
    w1d_in = din("w1d", [DEPTH, 2, 128, 32, 128])
    w2k_in = din("w2k", [DEPTH, 128, 128])
    w2v_in = din("w2v", [DEPTH, 128, 64])
    cb1_in = din("cb1", [DEPTH, 128, 2])
    mtri_in = din("mtri", [128, 128])
    manti_in = din("manti", [128, 128])
    cmpmask_in = din("cmpmask", [128, T])
    eall_in = din("eall", [128, T])
    kaug_in = din("kaug", [128, 32, 128])
    qaug_in = din("qaug", [128, 8, 128])
    ov_in = din("ov", [128, 33])
    bonus_in = din("bonus", [128, NT, 32])
    cap_in = din("cap", [128, NT, 32])
    idxb_in = din("idxb", [128, 32], I32)

    kv_out = dout("kv_out", [DEPTH, T, 768])
    sconv_out = dout("sconv_out", [DEPTH, 128, 6, 3])
    yT_out = dout("yT_out", [128, KD, T])

    banks = [nc.alloc_psum_tensor("bank%d" % i, [128, 512], F32).ap() for i in range(8)]
    tb = [Tk("bank%d" % i, excl=True) for i in range(8)]

    AR = Arena(nc, 190 * 1024)
    identf = AR.alloc([128], F32)
    identb = AR.alloc([128], BF16)
    onesb = AR.alloc([128], BF16)
    t_const = Tk("const")
    P.dma("sp", lambda e: e.dma_start(out=identf, in_=ident_in), t_const, writes=[t_const])
    P.op("dve", lambda e: e.tensor_copy(out=identb, in_=identf), reads=[t_const], writes=[t_const], partial=True)
    P.op("dve", lambda e: e.memset(onesb, 1.0), writes=[t_const], partial=True)

    utri = AR.alloc([128], F32)
    mtrif = AR.alloc([128], F32)
    onesf = AR.alloc([128], F32)
    P.dma("sp", lambda e: e.dma_start(out=utri, in_=utri_in), t_const, writes=[t_const], partial=True)
    P.dma("sp", lambda e: e.dma_start(out=mtrif, in_=mtrif_in), t_const, writes=[t_const], partial=True)
    P.op("dve", lambda e: e.memset(onesf, 1.0), writes=[t_const], partial=True)
    mods = AR.alloc([DEPTH * 48, 5], F32)
    t_mods = Tk("mods")
    smallv = AR.alloc([64], F32)
    t_small = Tk("small")
    n1g = [smallv[:, 0 + 8 * l: 8 + 8 * l] for l in range(DEPTH)]
    n2g = [smallv[:, 16 + 8 * l: 24 + 8 * l] for l in range(DEPTH)]
    fing = smallv[:, 32:40]
    for l in range(DEPTH):
        P.dma("sp", lambda e, l=l: e.dma_start(out=n1g[l], in_=n1g_in[l]), t_small, writes=[t_small], partial=True)
        P.dma("sp", lambda e, l=l: e.dma_start(out=n2g[l], in_=n2g_in[l]), t_small, writes=[t_small], partial=True)
    P.dma("sp", lambda e: e.dma_start(out=fing, in_=fing_in), t_small, writes=[t_small], partial=True)
    AB = AR.alloc([DEPTH * 2 * 2, 8], F32)
    t_AB = Tk("AB")
    base_mark = AR.mark()

    m0 = AR.mark()
    cT = AR.alloc([KD, 5], F32)
    cTb = AR.alloc([KD, 5], BF16)
    adab = AR.alloc([DEPTH, 48], F32)
    t_c = Tk("c")
    t_adab = Tk("adab")
    P.dma("sp", lambda e: e.dma_start(out=cT, in_=cT_in), t_c, writes=[t_c])
    P.op("act", lambda e: e.activation(out=cTb, in_=cT, func=AF.Silu), reads=[t_c], writes=[t_c], partial=True)
    for l in range(DEPTH):
        P.dma("sp", lambda e, l=l: e.dma_start(out=adab[:, l, :], in_=adab_in[l]), t_adab, writes=[t_adab], partial=True)
    wa = [AR.alloc([KD, 1024], BF16) for _ in range(2)]
    t_wa = [Tk("wa0"), Tk("wa1")]
    it = 0
    for l in range(DEPTH):
        bk = l % 2
        for s in range(6):
            sl = it % 2
            it += 1
            P.dma("pool", lambda e, l=l, s=s, sl=sl: e.dma_start(out=wa[sl], in_=adaw_in[l, s]), t_wa[sl], writes=[t_wa[sl]])
            for c8 in range(8):
                c = s * 8 + c8
                for kc in range(KD):
                    P.op("pe", lambda e, bk=bk, c=c, c8=c8, kc=kc, sl=sl: e.matmul(
                        banks[bk][:, c * 5:(c + 1) * 5], lhsT=wa[sl][:, kc, c8 * 128:(c8 + 1) * 128], rhs=cTb[:, kc, :],
                        start=(kc == 0), stop=(kc == KD - 1)), reads=[t_wa[sl], t_c], writes=[tb[bk]], partial=True)
        for r in range(5):
            P.op("dve", lambda e, l=l, r=r, bk=bk: e.tensor_tensor(
                out=mods[:, l * 48:(l + 1) * 48, r], in0=banks[bk][:, 0:240].rearrange("p (c r) -> p c r", r=5)[:, :, r],
                in1=adab[:, l, :], op=ALU.add), reads=[tb[bk], t_adab], writes=[t_mods], partial=True)
        for j, (gv, sci, shi) in enumerate(((n1g[l], 1, 0), (n2g[l], 4, 3))):
            P.op("dve", lambda e, l=l, j=j, gv=gv, sci=sci: e.scalar_tensor_tensor(
                out=AB[:, l * 4 + 2 * j, :], in0=mods[:, l * 48 + sci * 8: l * 48 + sci * 8 + 8, 0], scalar=1.0, in1=gv,
                op0=ALU.add, op1=ALU.mult), reads=[t_mods, t_small], writes=[t_AB], partial=True)
            P.op("dve", lambda e, l=l, j=j, shi=shi: e.tensor_copy(
                out=AB[:, l * 4 + 2 * j + 1, :], in_=mods[:, l * 48 + shi * 8: l * 48 + shi * 8 + 8, 0]),
                reads=[t_mods], writes=[t_AB], partial=True)
    P.barrier()
    AR.reset(m0)

    xsT = AR.alloc([KD, 4], F32)
    hsT = AR.alloc([KD, 4], BF16)
    osT = AR.alloc([KD, 4], BF16)
    kvnT = AR.alloc([6, 4], F32)
    QgT = AR.alloc([4, 4], BF16)
    xbcs = AR.alloc([6, 4], F32)
    zsT = AR.alloc([4, 4], F32)
    dd = AR.alloc([8], F32)
    gsT = AR.alloc([24], F32)
    VN = AR.alloc([4, 2, 130], BF16)
    exm = AR.alloc([4, 128], F32)
    t_xs, t_hs, t_os, t_kvn, t_Qg, t_xbcs, t_zsT, t_dd, t_gs, t_VN, t_ex = (Tk(n) for n in
        ("xs", "hs", "os", "kvn", "Qg", "xbcs", "zsT", "dd", "gs", "VN", "ex"))
    P.dma("sp", lambda e: e.dma_start(out=xsT, in_=xsT_in), t_xs, writes=[t_xs])
    P.dma("sp", lambda e: e.dma_start(out=exm[0:8, :, :], in_=ex_in), t_ex, writes=[t_ex])
    t_win = Tk("win")

    def sample_norm(l, sci, shi, gvec):
        m = AR.mark()
        t1 = AR.alloc([KD, 4], F32)
        As = AR.alloc([KD, 4], F32)
        rs4 = AR.alloc([4], F32)
        t_t1, t_As, t_rs4 = Tk("st1"), Tk("sAs"), Tk("srs")
        P.op("dve", lambda e: e.tensor_tensor(out=t1, in0=xsT, in1=xsT, op=ALU.mult), reads=[t_xs], writes=[t_t1])
        for kc in range(KD):
            P.op("pe", lambda e, kc=kc: e.matmul(banks[0][:, 0:4], lhsT=onesf, rhs=t1[:, kc, :], start=(kc == 0), stop=(kc == KD - 1)),
                 reads=[t_t1, t_const], writes=[tb[0]], partial=(kc > 0))
        P.op("dve", lambda e: e.tensor_scalar(out=rs4, in0=banks[0][:, 0:4], scalar1=1.0 / D, scalar2=EPS, op0=ALU.mult, op1=ALU.add),
             reads=[tb[0]], writes=[t_rs4])
        P.op("act", lambda e: e.activation(out=rs4, in_=rs4, func=AF.Sqrt), reads=[t_rs4], writes=[t_rs4])
        P.op("dve", lambda e: e.reciprocal(out=rs4, in_=rs4), reads=[t_rs4], writes=[t_rs4])
        for r in range(4):
            P.op("dve", lambda e, r=r: e.scalar_tensor_tensor(
                out=As[:, :, r], in0=mods[:, l * 48 + sci * 8: l * 48 + sci * 8 + 8, 1 + r], scalar=1.0, in1=gvec, op0=ALU.add, op1=ALU.mult),
                reads=[t_mods, t_small], writes=[t_As], partial=(r > 0))
        P.op("dve", lambda e: e.tensor_tensor(out=t1, in0=xsT, in1=As, op=ALU.mult), reads=[t_xs, t_As], writes=[t_t1])
        for kc in range(KD):
            P.op("dve", lambda e, kc=kc: e.tensor_tensor(out=t1[:, kc, :], in0=t1[:, kc, :], in1=rs4, op=ALU.mult), reads=[t_t1, t_rs4], writes=[t_t1], partial=True)
        P.op("dve", lambda e: e.tensor_tensor(out=hsT, in0=t1, in1=mods[:, l * 48 + shi * 8: l * 48 + shi * 8 + 8, 1:5], op=ALU.add),
             reads=[t_t1, t_mods], writes=[t_hs])
        AR.reset(m)

    hT = AR.alloc([KD, T], BF16)
    t_h = Tk("hT")
    layer_mark = AR.mark()
    xT = AR.alloc([KD, T], F32)
    t_x = Tk("xT")
    xscr = nc.dram_tensor("xscr", [128, KD, T], F32, kind="Internal").ap()
    t_xscr = Tk("xscr")

    def load_x(src):
        for kc in range(KD):
            P.dma("sp", lambda e, kc=kc, src=src: e.dma_start(out=xT[:, kc, :], in_=src[:, kc, :]), t_x,
                  reads=[t_xscr], writes=[t_x], partial=True)

    def norm_mod(Acol, Bcol, scratch_banks):
        m = AR.mark()
        sq = AR.alloc([KD, 512], BF16)
        rs = AR.alloc([512], F32)
        tmp = AR.alloc([2, 512], F32)
        t_sq, t_rs, t_tmp = Tk("sq"), Tk("rs"), [Tk("tmp0"), Tk("tmp1")]
        for tg in range(4):
            ts = slice(tg * 512, (tg + 1) * 512)
            bk = scratch_banks[tg % len(scratch_banks)]
            P.op("act", lambda e, ts=ts: e.activation(out=sq, in_=xT[:, :, ts], func=AF.Square), reads=[t_x], writes=[t_sq])
            for kc in range(KD):
                P.op("pe", lambda e, kc=kc, bk=bk: e.matmul(banks[bk], lhsT=onesb, rhs=sq[:, kc, :], start=(kc == 0), stop=(kc == KD - 1)),
                     reads=[t_sq, t_const], writes=[tb[bk]], partial=(kc > 0))
            P.op("dve", lambda e, bk=bk: e.tensor_scalar(out=rs, in0=banks[bk], scalar1=1.0 / D, scalar2=EPS, op0=ALU.mult, op1=ALU.add),
                 reads=[tb[bk]], writes=[t_rs])
            P.op("act", lambda e: e.activation(out=rs, in_=rs, func=AF.Sqrt), reads=[t_rs], writes=[t_rs])
            P.op("dve", lambda e: e.reciprocal(out=rs, in_=rs), reads=[t_rs], writes=[t_rs])
            for kc in range(KD):
                tt = kc % 2
                P.op("dve", lambda e, kc=kc, ts=ts, tt=tt: e.scalar_tensor_tensor(
                    out=tmp[:, tt, :], in0=xT[:, kc, ts], scalar=Acol[:, kc:kc + 1], in1=rs, op0=ALU.mult, op1=ALU.mult),
                    reads=[t_x, t_rs, t_AB], writes=[t_tmp[tt]])
                P.op("act", lambda e, kc=kc, ts=ts, tt=tt: e.activation(
                    out=hT[:, kc, ts], in_=tmp[:, tt, :], func=AF.Identity, bias=Bcol[:, kc:kc + 1], scale=1.0),
                    reads=[t_tmp[tt], t_AB], writes=[t_h], partial=True)
        AR.reset(m)

    for l in range(DEPTH):
        AR.reset(layer_mark)
        xsrc = xT_in if l == 0 else xscr
        AR.alloc([KD, T], F32)
        A1, B1, A2, B2 = (AB[:, l * 4 + j, :] for j in range(4))
        load_x(xsrc)
        norm_mod(A1, B1, [0, 1])
        P.barrier()
        sample_norm(l, 1, 0, n1g[l])
        P.barrier()
        AR.reset(layer_mark)
        Vall = AR.alloc([NT, 4, 65], BF16)
        zs = AR.alloc([NT, 512], BF16)
        gat = AR.alloc([NT, 24], F32)
        dtt = AR.alloc([NT, 8], F32)
        aa = AR.alloc([NT, 8], F32)
        lay = AR.alloc([4, 8], F32)
        sng = AR.alloc([512], F32)
        QT = AR.alloc([4, T], BF16)
        kvcT = AR.alloc([2, T], BF16)
        KsT = AR.alloc([2, T], BF16)
        KwT = AR.alloc([2, T], BF16)
        xbcT = AR.alloc([6, T], BF16)
        mA = AR.mark()
        wtm = AR.alloc([KD, 1312], BF16)
        t_wtm = Tk("wtm")
        for kc in range(KD):
            P.dma("pool", lambda e, l=l, kc=kc: e.dma_start(out=wtm[:, kc, :], in_=wtm_in[l, :, kc, :]), t_wtm, writes=[t_wtm], partial=True)
        kvst = [AR.alloc([768], F32) for _ in range(2)]
        t_kvst = [Tk("kvst0"), Tk("kvst1")]
        for tt in range(NT):
            tsl = slice(tt * 128, (tt + 1) * 128)
            sl = tt % 2
            for half in range(2):
                bk = 2 + (tt * 2 + half) % 4
                ncol = 512 if half == 0 else 256
                c0 = half * 512
                for kc in range(KD):
                    P.op("pe", lambda e, bk=bk, kc=kc, tsl=tsl, c0=c0, ncol=ncol: e.matmul(
                        banks[bk][:, 0:ncol], lhsT=hT[:, kc, tsl], rhs=wtm[:, kc, c0:c0 + ncol], start=(kc == 0), stop=(kc == KD - 1)),
                        reads=[t_h, t_wtm], writes=[tb[bk]], partial=(kc > 0))
                P.op("act", lambda e, bk=bk, sl=sl, c0=c0, ncol=ncol: e.copy(out=kvst[sl][:, c0:c0 + ncol], in_=banks[bk][:, 0:ncol]),
                     reads=[tb[bk]], writes=[t_kvst[sl]], partial=(half > 0))
            P.dma("sp", lambda e, l=l, tsl=tsl, sl=sl: e.dma_start(out=kv_out[l, tsl, :], in_=kvst[sl]), t_kvst[sl], reads=[t_kvst[sl]])
        t_V, t_zs, t_gat, t_dt, t_lay, t_Q, t_kvc, t_Ks, t_Kw, t_xbc = (Tk(n) for n in
            ("V", "zs", "gat", "dt", "lay", "Q", "kvc", "Ks", "Kw", "xbc"))
        mA2 = AR.mark()
        P.op("dve", lambda e: e.memset(Vall, 1.0), writes=[t_V])
        P.dma("sp", lambda e, l=l: e.dma_start(out=lay[:, 0, :], in_=dtb_in[l]), t_lay, writes=[t_lay], partial=True)
        P.dma("sp", lambda e, l=l: e.dma_start(out=lay[:, 1, :], in_=alog_in[l]), t_lay, writes=[t_lay], partial=True)
        P.dma("sp", lambda e, l=l: e.dma_start(out=lay[:, 2, :], in_=dsk_in[l]), t_lay, writes=[t_lay], partial=True)
        P.dma("sp", lambda e, l=l: e.dma_start(out=sng, in_=sng_in[l]), t_lay, writes=[t_lay], partial=True)
        P.op("act", lambda e: e.activation(out=lay[:, 1, :], in_=lay[:, 1, :], func=AF.Exp), reads=[t_lay], writes=[t_lay], partial=True)
        P.op("dve", lambda e: e.tensor_scalar(out=lay[:, 1, :], in0=lay[:, 1, :], scalar1=-1.0, scalar2=None, op0=ALU.mult),
             reads=[t_lay], writes=[t_lay], partial=True)
        for tt in range(NT):
            tsl = slice(tt * 128, (tt + 1) * 128)
            b1 = 2 + (tt % 2) * 2
            b2 = b1 + 1
            for kc in range(KD):
                P.op("pe", lambda e, b1=b1, kc=kc, tsl=tsl: e.matmul(
                    banks[b1][:, 0:128], lhsT=hT[:, kc, tsl], rhs=wtm[:, kc, 384:512], start=(kc == 0), stop=(kc == KD - 1)),
                    reads=[t_h, t_wtm], writes=[tb[b1]], partial=(kc > 0))
            for kc in range(KD):
                P.op("pe", lambda e, b1=b1, kc=kc, tsl=tsl: e.matmul(
                    banks[b1][:, 128:256], lhsT=hT[:, kc, tsl], rhs=wtm[:, kc, 640:768], start=(kc == 0), stop=(kc == KD - 1)),
                    reads=[t_h, t_wtm], writes=[tb[b1]], partial=True)
            for kc in range(KD):
                P.op("pe", lambda e, b1=b1, kc=kc, tsl=tsl: e.matmul(
                    banks[b1][:, 256:280], lhsT=hT[:, kc, tsl], rhs=wtm[:, kc, 768:792], start=(kc == 0), stop=(kc == KD - 1)),
                    reads=[t_h, t_wtm], writes=[tb[b1]], partial=True)
            for kc in range(KD):
                P.op("pe", lambda e, b1=b1, kc=kc, tsl=tsl: e.matmul(
                    banks[b1][:, 280:288], lhsT=hT[:, kc, tsl], rhs=wtm[:, kc, 1304:1312], start=(kc == 0), stop=(kc == KD - 1)),
                    reads=[t_h, t_wtm], writes=[tb[b1]], partial=True)
            P.op("act", lambda e, b1=b1, tt=tt: e.copy(out=Vall[:, tt, :, 0:64], in_=banks[b1][:, 0:256].rearrange("p (a b) -> p a b", b=64)),
                 reads=[tb[b1]], writes=[t_V], partial=True)
            P.op("act", lambda e, b1=b1, tt=tt: e.activation(out=gat[:, tt, :], in_=banks[b1][:, 256:280], func=AF.Sigmoid),
                 reads=[tb[b1]], writes=[t_gat], partial=True)
            P.op("dve", lambda e, b1=b1, tt=tt: e.tensor_tensor(out=dtt[:, tt, :], in0=banks[b1][:, 280:288], in1=lay[:, 0, :], op=ALU.add),
                 reads=[tb[b1], t_lay], writes=[t_dt], partial=True)
            for kc in range(KD):
                P.op("pe", lambda e, b2=b2, kc=kc, tsl=tsl: e.matmul(
                    banks[b2], lhsT=hT[:, kc, tsl], rhs=wtm[:, kc, 792:1304], start=(kc == 0), stop=(kc == KD - 1)),
                    reads=[t_h, t_wtm], writes=[tb[b2]], partial=(kc > 0))
            P.op("act", lambda e, b2=b2, tt=tt: e.activation(out=zs[:, tt, :], in_=banks[b2], func=AF.Silu),
                 reads=[tb[b2]], writes=[t_zs], partial=True)
        def smm(bk, prt, cols, lhs_fn, rhs_fn, reads):
            for kc in range(KD):
                P.op("pe", lambda e, kc=kc: e.matmul(banks[bk][prt, cols], lhsT=lhs_fn(kc), rhs=rhs_fn(kc), start=(kc == 0), stop=(kc == KD - 1)),
                     reads=reads, writes=[tb[bk]], partial=True)
        for c6 in range(6):
            smm(0, slice(0, 128), slice(c6 * 4, c6 * 4 + 4), lambda kc, c6=c6: wtm[:, kc, c6 * 128:(c6 + 1) * 128], lambda kc: hsT[:, kc, :], [t_wtm, t_hs])
        for c4 in range(4):
            smm(0, slice(0, 128), slice(24 + c4 * 4, 28 + c4 * 4), lambda kc, c4=c4: wtm[:, kc, 792 + c4 * 128:792 + (c4 + 1) * 128], lambda kc: hsT[:, kc, :], [t_wtm, t_hs])
        smm(0, slice(0, 8), slice(40, 44), lambda kc: wtm[:, kc, 1304:1312], lambda kc: hsT[:, kc, :], [t_wtm, t_hs])
        for bg in range(6):
            smm(0, slice(0, 4), slice(44 + bg * 4, 48 + bg * 4), lambda kc, bg=bg: wtm[:, kc, 768 + bg * 4:772 + bg * 4], lambda kc: hsT[:, kc, :], [t_wtm, t_hs])
        for b in range(4):
            smm(1, slice(0, 1), slice(b * 128, (b + 1) * 128), lambda kc, b=b: hsT[:, kc, b:b + 1], lambda kc: wtm[:, kc, 384:512], [t_wtm, t_hs])
        for b in range(4):
            smm(6, slice(0, 1), slice(b * 128, (b + 1) * 128), lambda kc, b=b: hsT[:, kc, b:b + 1], lambda kc: wtm[:, kc, 640:768], [t_wtm, t_hs])
        P.op("act", lambda e: e.copy(out=kvnT, in_=banks[0][:, 0:24].rearrange("p (a b) -> p a b", b=4)), reads=[tb[0]], writes=[t_kvn])
        P.op("act", lambda e: e.activation(out=zsT, in_=banks[0][:, 24:40].rearrange("p (a b) -> p a b", b=4), func=AF.Silu), reads=[tb[0]], writes=[t_zsT])
        P.op("act", lambda e: e.activation(out=gsT[0:4, :], in_=banks[0][0:4, 44:68], func=AF.Sigmoid), reads=[tb[0]], writes=[t_gs])
        hcol = AR.alloc([3], F32)
        t_hcol = Tk("hcol")
        P.dma("sp", lambda e, l=l: e.dma_start(out=hcol[0:8, :], in_=hcol_in[l]), t_hcol, writes=[t_hcol])
        P.op("act", lambda e: e.activation(out=dd[0:8, 0:4], in_=banks[0][0:8, 40:44], func=AF.Exp, bias=hcol[0:8, 0:1], scale=1.0), reads=[tb[0], t_hcol], writes=[t_dd])
        P.op("dve", lambda e: e.tensor_scalar(out=dd[0:8, 0:4], in0=dd[0:8, 0:4], scalar1=1.0, scalar2=None, op0=ALU.add), reads=[t_dd], writes=[t_dd])
        P.op("act", lambda e: e.activation(out=dd[0:8, 0:4], in_=dd[0:8, 0:4], func=AF.Ln), reads=[t_dd], writes=[t_dd])
        P.op("act", lambda e: e.activation(out=hcol[0:8, 1:2], in_=hcol[0:8, 1:2], func=AF.Exp), reads=[t_hcol], writes=[t_hcol])
        P.op("dve", lambda e: e.tensor_scalar(out=dd[0:8, 4:8], in0=dd[0:8, 0:4], scalar1=hcol[0:8, 1:2], scalar2=-1.0, op0=ALU.mult, op1=ALU.mult), reads=[t_dd, t_hcol], writes=[t_dd])
        P.op("act", lambda e: e.activation(out=dd[0:8, 4:8], in_=dd[0:8, 4:8], func=AF.Exp), reads=[t_dd], writes=[t_dd])
        P.op("dve", lambda e: e.memset(VN[0:1], 1.0), writes=[t_VN])
        P.op("act", lambda e: e.copy(out=VN[0:1, :, 0, :].rearrange("p b (g x) -> p b g x", g=2)[:, :, :, 0:64], in_=banks[1][0:1, 0:512].rearrange("p (b g d) -> p b g d", b=4, g=2)),
             reads=[tb[1]], writes=[t_VN], partial=True)
        P.op("act", lambda e: e.copy(out=VN[0:1, :, 1, :].rearrange("p b (g x) -> p b g x", g=2)[:, :, :, 0:64], in_=banks[6][0:1, 0:512].rearrange("p (b g d) -> p b g d", b=4, g=2)),
             reads=[tb[6]], writes=[t_VN], partial=True)
        P.dma("sp", lambda e, l=l: e.dma_start(out=kvn_out[l], in_=kvnT), t_kvn, reads=[t_kvn])
        for b in range(4):
            P.dma("sp", lambda e, l=l, b=b: e.dma_start(out=wk_out[l, b, 0:511, :], in_=wk_in[l, b, 1:512, :]), t_win, writes=[t_win], partial=True)
            P.dma("sp", lambda e, l=l, b=b: e.dma_start(out=wv_out[l, b, 0:511, :], in_=wv_in[l, b, 1:512, :]), t_win, writes=[t_win], partial=True)
            P.dma("sp", lambda e, l=l, b=b: e.dma_start(out=wk_out[l, b, 511:512, :].rearrange("o f -> f o"), in_=kvnT[:, 4, b:b + 1]), t_kvn, reads=[t_kvn])
            P.dma("sp", lambda e, l=l, b=b: e.dma_start(out=wv_out[l, b, 511:512, :].rearrange("o f -> f o"), in_=kvnT[:, 5, b:b + 1]), t_kvn, reads=[t_kvn])
        P.op("act", lambda e: e.activation(out=dtt, in_=dtt, func=AF.Exp), reads=[t_dt], writes=[t_dt])
        P.op("dve", lambda e: e.tensor_scalar(out=dtt, in0=dtt, scalar1=1.0, scalar2=None, op0=ALU.add), reads=[t_dt], writes=[t_dt])
        P.op("act", lambda e: e.activation(out=dtt, in_=dtt, func=AF.Ln), reads=[t_dt], writes=[t_dt])
        for tt in range(NT):
            P.op("dve", lambda e, tt=tt: e.tensor_tensor(out=aa[:, tt, :], in0=dtt[:, tt, :], in1=lay[:, 1, :], op=ALU.mult),
                 reads=[t_dt, t_lay], writes=[t_dt], partial=True)
        wch = [AR.alloc([KD, 128], BF16) for _ in range(3)]
        t_wch = [Tk("wch%d" % i) for i in range(3)]
        stg = [AR.alloc([515], F32) for _ in range(2)]
        t_stg = [Tk("stg0"), Tk("stg1")]
        cacc = [AR.alloc([512], F32) for _ in range(2)]
        t_cacc = [Tk("cacc0"), Tk("cacc1")]
        scw = AR.alloc([6, 4], F32)
        scb = AR.alloc([6], F32)
        t_sc = Tk("sc")
        P.dma("sp", lambda e, l=l: e.dma_start(out=scw, in_=scw_in[l]), t_sc, writes=[t_sc], partial=True)
        P.dma("sp", lambda e, l=l: e.dma_start(out=scb, in_=scb_in[l]), t_sc, writes=[t_sc], partial=True)
        nb = 0
        for c in range(20):
            sl = c % 3
            P.dma("pool", lambda e, l=l, c=c, sl=sl: e.dma_start(out=wch[sl], in_=wfm_in[l, c]), t_wch[sl], writes=[t_wch[sl]])
            if c >= 10:
                scol = slice((c - 10) * 4, (c - 10) * 4 + 4)
                sbk = 0 if c < 16 else 1
                if c >= 16:
                    scol = slice((c - 16) * 4, (c - 16) * 4 + 4)
                for kc in range(KD):
                    P.op("pe", lambda e, kc=kc, sl=sl, sbk=sbk, scol=scol: e.matmul(banks[sbk][:, scol], lhsT=wch[sl][:, kc, :], rhs=hsT[:, kc, :], start=(kc == 0), stop=(kc == KD - 1)),
                         reads=[t_wch[sl], t_hs], writes=[tb[sbk]], partial=True)
            if c >= 16:
                continue
            for tg in range(4):
                ts = slice(tg * 512, (tg + 1) * 512)
                bk = 4 + nb % 4
                nb += 1
                for kc in range(KD):
                    P.op("pe", lambda e, bk=bk, kc=kc, sl=sl, ts=ts: e.matmul(
                        banks[bk], lhsT=wch[sl][:, kc, :], rhs=hT[:, kc, ts], start=(kc == 0), stop=(kc == KD - 1)),
                        reads=[t_h, t_wch[sl]], writes=[tb[bk]], partial=(kc > 0))
                if c < 4:
                    P.op("act", lambda e, bk=bk, c=c, ts=ts: e.activation(out=QT[:, c, ts], in_=banks[bk], func=AF.Copy, scale=0.125),
                         reads=[tb[bk]], writes=[t_Q], partial=True)
                elif c < 6:
                    P.op("act", lambda e, bk=bk, c=c, ts=ts: e.copy(out=kvcT[:, c - 4, ts], in_=banks[bk]),
                         reads=[tb[bk]], writes=[t_kvc], partial=True)
                elif c < 8:
                    P.op("act", lambda e, bk=bk, c=c, ts=ts: e.copy(out=KsT[:, c - 6, ts], in_=banks[bk]),
                         reads=[tb[bk]], writes=[t_Ks], partial=True)
                elif c < 10:
                    P.op("act", lambda e, bk=bk, c=c, ts=ts: e.copy(out=KwT[:, c - 8, ts], in_=banks[bk]),
                         reads=[tb[bk]], writes=[t_Kw], partial=True)
                else:
                    cc = c - 10
                    si = tg % 2
                    if tg == 0:
                        P.op("dve", lambda e, si=si: e.memset(stg[si][:, 0:3], 0.0), writes=[t_stg[si]], partial=True)
                    P.op("act", lambda e, bk=bk, si=si: e.copy(out=stg[si][:, 3:515], in_=banks[bk]),
                         reads=[tb[bk]], writes=[t_stg[si]], partial=True)
                    P.op("dve", lambda e, si=si, cc=cc: e.tensor_scalar(
                        out=cacc[si], in0=stg[si][:, 3:515], scalar1=scw[:, cc, 3:4], scalar2=scb[:, cc:cc + 1], op0=ALU.mult, op1=ALU.add),
                        reads=[t_stg[si], t_sc], writes=[t_cacc[si]])
                    for k in range(3):
                        P.op("dve", lambda e, si=si, cc=cc, k=k: e.scalar_tensor_tensor(
                            out=cacc[si], in0=stg[si][:, k:k + 512], scalar=scw[:, cc, k:k + 1], in1=cacc[si], op0=ALU.mult, op1=ALU.add),
                            reads=[t_stg[si], t_sc], writes=[t_cacc[si]])
                    P.op("act", lambda e, si=si, cc=cc, ts=ts: e.activation(out=xbcT[:, cc, ts], in_=cacc[si], func=AF.Silu),
                         reads=[t_cacc[si]], writes=[t_xbc], partial=True)
                    if tg < 3:
                        P.op("dve", lambda e, si=si: e.tensor_copy(out=stg[1 - si][:, 0:3], in_=stg[si][:, 512:515]),
                             reads=[t_stg[si]], writes=[t_stg[1 - si]], partial=True)
                    else:
                        P.dma("sp", lambda e, l=l, si=si, cc=cc: e.dma_start(out=sconv_out[l, :, cc, :], in_=stg[si][:, 512:515]),
                              t_stg[si], reads=[t_stg[si]])
        P.op("act", lambda e: e.activation(out=QgT, in_=banks[1][:, 0:16].rearrange("p (a b) -> p a b", b=4), func=AF.Copy, scale=0.125), reads=[tb[1]], writes=[t_Qg])
        cst = AR.alloc([6, 4, 4], F32)
        cst_o = AR.alloc([6, 3, 4], F32)
        cac = AR.alloc([6, 4], F32)
        t_cst, t_cso, t_cac = Tk("cst"), Tk("cso"), Tk("cac")
        P.dma("sp", lambda e, l=l: e.dma_start(out=cst[:, :, 0:3, :], in_=cst_in[l]), t_cst, writes=[t_cst])
        P.op("act", lambda e: e.copy(out=cst[:, :, 3, :], in_=banks[0][:, 0:24].rearrange("p (a b) -> p a b", b=4)), reads=[tb[0]], writes=[t_cst], partial=True)
        P.op("dve", lambda e: e.tensor_copy(out=cst_o, in_=cst[:, :, 1:4, :]), reads=[t_cst], writes=[t_cso])
        P.dma("sp", lambda e, l=l: e.dma_start(out=sconvs_out[l], in_=cst_o), t_cso, reads=[t_cso])
        for cc in range(6):
            P.op("dve", lambda e, cc=cc: e.tensor_scalar(out=cac[:, cc, :], in0=cst[:, cc, 3, :], scalar1=scw[:, cc, 3:4], scalar2=scb[:, cc:cc + 1], op0=ALU.mult, op1=ALU.add),
                 reads=[t_cst, t_sc], writes=[t_cac], partial=(cc > 0))
            for k in range(3):
                P.op("dve", lambda e, cc=cc, k=k: e.scalar_tensor_tensor(out=cac[:, cc, :], in0=cst[:, cc, k, :], scalar=scw[:, cc, k:k + 1], in1=cac[:, cc, :], op0=ALU.mult, op1=ALU.add),
                     reads=[t_cst, t_sc, t_cac], writes=[t_cac], partial=True)
        P.op("act", lambda e: e.activation(out=xbcs, in_=cac, func=AF.Silu), reads=[t_cac], writes=[t_xbcs])
        P.barrier()
        AR.reset(mA)
        if stage <= 2:
            break
        mSS = AR.mark()
        hSs = AR.alloc([4, 4, 64], F32)
        dx = AR.alloc([4, 8], F32)
        BCtm = AR.alloc([256], F32)
        BCbd = AR.alloc([4, 256], F32)
        BC = AR.alloc([4, 2, 128], F32)
        us = AR.alloc([4, 4], F32)
        ys = AR.alloc([4, 4], F32)
        y2s = AR.alloc([4, 4], F32)
        stmp = AR.alloc([2, 64], F32)
        ssg = AR.alloc([2, 4], F32)
        colv = AR.alloc([8], F32)
        t_hSs, t_dx, t_BCtm, t_BCbd, t_BC, t_us, t_ys, t_y2s, t_stmp, t_ssg, t_colv = (Tk(n) for n in
            ("hSs", "dx", "BCtm", "BCbd", "BC", "us", "ys", "y2s", "stmp", "ssg", "colv"))
        P.dma("sp", lambda e, l=l: e.dma_start(out=hSs, in_=hs0_in[l]), t_hSs, writes=[t_hSs])
        P.dma("sp", lambda e, l=l: e.dma_start(out=colv[:, 0:4], in_=dskc_in[l]), t_colv, writes=[t_colv], partial=True)
        P.dma("sp", lambda e, l=l: e.dma_start(out=colv[:, 4:8], in_=sngc_in[l]), t_colv, writes=[t_colv], partial=True)
        for c in range(4):
            P.op("pe", lambda e, c=c: e.matmul(banks[0][:, c * 8:(c + 1) * 8], lhsT=exm[0:8, c, :], rhs=dd[0:8, :], start=True, stop=True),
                 reads=[t_ex, t_dd], writes=[tb[0]], partial=(c > 0))
        P.op("act", lambda e: e.copy(out=dx, in_=banks[0][:, 0:32].rearrange("p (a b) -> p a b", b=8)), reads=[tb[0]], writes=[t_dx])
        for i2 in range(2):
            P.op("pe", lambda e, i2=i2: e.transpose(out=banks[1][0:4, i2 * 128:(i2 + 1) * 128], in_=xbcs[:, 4 + i2, :], identity=identf),
                 reads=[t_xbcs, t_const], writes=[tb[1]], partial=(i2 > 0))
        P.op("act", lambda e: e.copy(out=BCtm[0:4, :], in_=banks[1][0:4, 0:256]), reads=[tb[1]], writes=[t_BCtm])
        for b in range(4):
            P.op("dve", lambda e, b=b: e.tensor_scalar(out=BCbd[0:4, b, :], in0=BCtm[0:4, :], scalar1=identf[0:4, b:b + 1], scalar2=None, op0=ALU.mult),
                 reads=[t_BCtm, t_const], writes=[t_BCbd], partial=(b > 0))
        for b in range(4):
            bk = 2 + b // 2
            P.op("pe", lambda e, b=b, bk=bk: e.matmul(banks[bk][:, (b % 2) * 256:(b % 2 + 1) * 256], lhsT=onesf[0:4, :], rhs=BCbd[0:4, b, :], start=True, stop=True),
                 reads=[t_BCbd, t_const], writes=[tb[bk]], partial=(b % 2 > 0))
        for i2 in range(2):
            P.op("act", lambda e, i2=i2: e.copy(out=BC[:, 2 * i2:2 * i2 + 2, :, :], in_=banks[2 + i2].rearrange("p (b t n) -> p b t n", b=2, t=2)),
                 reads=[tb[2 + i2]], writes=[t_BC], partial=(i2 > 0))
        P.op("dve", lambda e: e.tensor_tensor(out=us, in0=xbcs[:, 0:4, :], in1=dx[:, :, 0:4], op=ALU.mult), reads=[t_xbcs, t_dx], writes=[t_us])
        first = True
        for c in range(4):
            g = c // 2
            for b in range(4):
                P.op("dve", lambda e, c=c, b=b, g=g: e.tensor_scalar(out=stmp[:, 0, :], in0=BC[:, b, 0, g * 64:(g + 1) * 64], scalar1=us[:, c, b:b + 1], scalar2=None, op0=ALU.mult),
                     reads=[t_BC, t_us], writes=[t_stmp])
                P.op("dve", lambda e, c=c, b=b: e.scalar_tensor_tensor(out=hSs[:, c, b, :], in0=hSs[:, c, b, :], scalar=dx[:, c, 4 + b:5 + b], in1=stmp[:, 0, :], op0=ALU.mult, op1=ALU.add),
                     reads=[t_hSs, t_dx, t_stmp], writes=[t_hSs], partial=True)
                P.op("dve", lambda e, c=c, b=b, g=g: e.tensor_tensor(out=stmp[:, 1, :], in0=hSs[:, c, b, :], in1=BC[:, b, 1, g * 64:(g + 1) * 64], op=ALU.mult),
                     reads=[t_hSs, t_BC], writes=[t_stmp], partial=True)
                P.op("dve", lambda e, c=c, b=b: e.reduce_sum(out=ys[:, c, b:b + 1], in_=stmp[:, 1, :], axis=AX.X),
                     reads=[t_stmp], writes=[t_ys], partial=(not first))
                first = False
        P.dma("sp", lambda e, l=l: e.dma_start(out=ssms_out[l], in_=hSs), t_hSs, reads=[t_hSs])
        for c in range(4):
            P.op("dve", lambda e, c=c: e.scalar_tensor_tensor(out=ys[:, c, :], in0=xbcs[:, c, :], scalar=colv[:, c:c + 1], in1=ys[:, c, :], op0=ALU.mult, op1=ALU.add),
                 reads=[t_xbcs, t_colv, t_ys], writes=[t_ys], partial=True)
        P.op("dve", lambda e: e.tensor_tensor(out=ys, in0=ys, in1=zsT, op=ALU.mult), reads=[t_ys, t_zsT], writes=[t_ys])
        P.op("dve", lambda e: e.tensor_tensor(out=y2s, in0=ys, in1=ys, op=ALU.mult), reads=[t_ys], writes=[t_y2s])
        P.op("pe", lambda e: e.matmul(banks[0][:, 0:16], lhsT=onesf, rhs=y2s.rearrange("p a b -> p (a b)"), start=True, stop=True), reads=[t_y2s, t_const], writes=[tb[0]])
        P.op("act", lambda e: e.copy(out=y2s.rearrange("p a b -> p (a b)"), in_=banks[0][:, 0:16]), reads=[tb[0]], writes=[t_y2s])
        s4 = y2s.rearrange("p (g c) b -> p g c b", g=2)
        P.op("dve", lambda e, s4=s4: e.tensor_tensor(out=ssg, in0=s4[:, :, 0, :], in1=s4[:, :, 1, :], op=ALU.add), reads=[t_y2s], writes=[t_ssg])
        P.op("dve", lambda e: e.tensor_scalar(out=ssg, in0=ssg, scalar1=1.0 / 256, scalar2=EPS, op0=ALU.mult, op1=ALU.add), reads=[t_ssg], writes=[t_ssg])
        P.op("act", lambda e: e.activation(out=ssg, in_=ssg, func=AF.Sqrt), reads=[t_ssg], writes=[t_ssg])
        P.op("dve", lambda e: e.reciprocal(out=ssg, in_=ssg), reads=[t_ssg], writes=[t_ssg])
        for c in range(4):
            P.op("dve", lambda e, c=c: e.scalar_tensor_tensor(out=osT[:, 4 + c, :], in0=ys[:, c, :], scalar=colv[:, 4 + c:5 + c], in1=ssg[:, c // 2, :], op0=ALU.mult, op1=ALU.mult),
                 reads=[t_ys, t_colv, t_ssg], writes=[t_os], partial=True)
        P.barrier()
        AR.reset(mSS)
        oT = hT
        t_o = t_h
        mS = AR.mark()
        xB = AR.alloc([NT, 640], BF16)
        t_xB = Tk("xB")
        for tt in range(NT):
            tsl = slice(tt * 128, (tt + 1) * 128)
            bk = tt % 2
            bv = banks[bk].bitcast(BF16)
            for cc in range(5):
                P.op("pe", lambda e, bv=bv, cc=cc, tsl=tsl: e.transpose(out=bv[:, cc * 128:(cc + 1) * 128], in_=xbcT[:, cc, tsl], identity=identb),
                     reads=[t_xbc, t_const], writes=[tb[bk]], partial=(cc > 0))
            P.op("act", lambda e, bv=bv, tt=tt: e.copy(out=xB[:, tt, :], in_=bv[:, 0:640]), reads=[tb[bk]], writes=[t_xB], partial=True)
        hS = AR.alloc([4, 64], F32)
        hSb = AR.alloc([4, 64], BF16)
        t_hS = Tk("hS")
        P.op("dve", lambda e: e.memset(hS, 0.0), writes=[t_hS])
        P.op("dve", lambda e: e.memset(hSb, 0.0), writes=[t_hS], partial=True)
        ncs = AR.alloc([8], F32)
        ecs = AR.alloc([8], F32)
        decc = AR.alloc([8], F32)
        wcol = AR.alloc([8], F32)
        abc = [AR.alloc([128], F32) for _ in range(2)]
        LT = [AR.alloc([128], F32) for _ in range(2)]
        WT = [AR.alloc([128], BF16) for _ in range(2)]
        GT = AR.alloc([2, 128], F32)
        xw = AR.alloc([512], BF16)
        ydsb = AR.alloc([512], F32)
        yy = AR.alloc([512], F32)
        yf = AR.alloc([512], BF16)
        ssq = AR.alloc([4], F32)
        t_ncs, t_ecs, t_dec, t_wcol, t_GT, t_xw, t_yd, t_yy, t_yf, t_ssq = (Tk(n) for n in
            ("ncs", "ecs", "dec", "wcol", "GT", "xw", "yd", "yy", "yf", "ssq"))
        t_abc = [Tk("abc0"), Tk("abc1")]
        t_LT = [Tk("LT0"), Tk("LT1")]
        t_WT = [Tk("WT0"), Tk("WT1")]
        GB = (1, 7)
        for c in range(NT):
            csl = slice(c * 128, (c + 1) * 128)
            P.op("pe", lambda e, c=c: e.matmul(banks[0][:, 0:8], lhsT=utri, rhs=aa[:, c, :], start=True, stop=True),
                 reads=[t_const, t_dt], writes=[tb[0]])
            P.op("dve", lambda e: e.tensor_scalar(out=ncs, in0=banks[0][:, 0:8], scalar1=-1.0, scalar2=None, op0=ALU.mult),
                 reads=[tb[0]], writes=[t_ncs])
            P.op("act", lambda e: e.activation(out=ecs, in_=banks[0][:, 0:8], func=AF.Exp), reads=[tb[0]], writes=[t_ecs])
            for g in range(2):
                ps = slice(g * 64, (g + 1) * 64)
                gb = GB[g]
                P.op("pe", lambda e, g=g, ps=ps, csl=csl, gb=gb: e.matmul(
                    banks[gb][:, 0:128], lhsT=xbcT[ps, 4, csl], rhs=xbcT[ps, 5, csl], start=True, stop=True),
                    reads=[t_xbc], writes=[tb[gb]])
                P.op("act", lambda e, g=g, gb=gb: e.copy(out=GT[:, g, :], in_=banks[gb][:, 0:128]), reads=[tb[gb]], writes=[t_GT], partial=(g > 0))
            for h in range(8):
                g = h // 4
                hb = h % 2
                sb = 2 + (h // 4)
                col = slice((h % 4) * 128, (h % 4 + 1) * 128)
                P.op("dve", lambda e, c=c, h=h, hb=hb: e.tensor_scalar(out=abc[hb], in0=onesf, scalar1=aa[:, c, h:h + 1], scalar2=None, op0=ALU.mult),
                     reads=[t_dt, t_const], writes=[t_abc[hb]])
                P.op("pe", lambda e, sb=sb, col=col, hb=hb: e.matmul(banks[sb][:, col], lhsT=abc[hb], rhs=utri, start=True, stop=False),
                     reads=[t_abc[hb], t_const], writes=[tb[sb]], partial=(h % 4 > 0))
                P.op("pe", lambda e, sb=sb, col=col: e.matmul(banks[sb][:, col], lhsT=identf, rhs=mtrif, start=False, stop=True),
                     reads=[t_const], writes=[tb[sb]], partial=True)
            for h in range(8):
                g = h // 4
                hb = h % 2
                sb = 2 + (h // 4)
                col = slice((h % 4) * 128, (h % 4 + 1) * 128)
                P.op("act", lambda e, sb=sb, col=col, hb=hb, h=h: e.activation(out=LT[hb], in_=banks[sb][:, col], func=AF.Exp, bias=ncs[:, h:h + 1], scale=1.0),
                     reads=[tb[sb], t_ncs], writes=[t_LT[hb]])
                P.op("act", lambda e, sb=sb, h=h: e.activation(out=decc[:, h:h + 1], in_=banks[sb][:, (h % 4) * 128 + 127:(h % 4) * 128 + 128], func=AF.Exp),
                     reads=[tb[sb]], writes=[t_dec], partial=True)
                P.op("dve", lambda e, hb=hb, h=h, g=g, c=c: e.scalar_tensor_tensor(
                    out=WT[hb], in0=LT[hb], scalar=dtt[:, c, h:h + 1], in1=GT[:, g, :], op0=ALU.mult, op1=ALU.mult),
                    reads=[t_LT[hb], t_dt, t_GT], writes=[t_WT[hb]])
                P.op("dve", lambda e, hb=hb, h=h, c=c: e.tensor_tensor(out=wcol[:, h:h + 1], in0=LT[hb][:, 127:128], in1=dtt[:, c, h:h + 1], op=ALU.mult),
                     reads=[t_LT[hb], t_dt], writes=[t_wcol], partial=True)
                P.op("dve", lambda e, h=h, c=c: e.tensor_scalar(out=xw[:, h * 64:(h + 1) * 64], in0=xB[:, c, h * 64:(h + 1) * 64], scalar1=wcol[:, h:h + 1], scalar2=None, op0=ALU.mult),
                     reads=[t_xB, t_wcol], writes=[t_xw], partial=True)
                P.op("pe", lambda e, hb=hb, h=h, c=c: e.matmul(banks[4][:, h * 64:(h + 1) * 64], lhsT=WT[hb], rhs=xB[:, c, h * 64:(h + 1) * 64], start=True, stop=True),
                     reads=[t_WT[hb], t_xB], writes=[tb[4]], partial=(h > 0))
                ps = slice(g * 64, (g + 1) * 64)
                gb = GB[g]
                P.op("pe", lambda e, h=h, ps=ps, csl=csl, gb=gb: e.matmul(banks[gb][:, 128 + (h % 4) * 64:128 + (h % 4 + 1) * 64], lhsT=xbcT[ps, 5, csl], rhs=hSb[ps, h % 4, :], start=True, stop=True),
                     reads=[t_xbc, t_hS], writes=[tb[gb]], partial=True)
            P.op("act", lambda e: e.copy(out=ydsb, in_=banks[4]), reads=[tb[4]], writes=[t_yd])
            for h in range(8):
                hs = slice(h * 64, (h + 1) * 64)
                gb = GB[h // 4]
                P.op("dve", lambda e, h=h, hs=hs, gb=gb: e.scalar_tensor_tensor(out=yy[:, hs], in0=banks[gb][:, 128 + (h % 4) * 64:128 + (h % 4 + 1) * 64], scalar=ecs[:, h:h + 1], in1=ydsb[:, hs], op0=ALU.mult, op1=ALU.add),
                     reads=[tb[gb], t_ecs, t_yd], writes=[t_yy], partial=(h > 0))
            for h in range(8):
                hs = slice(h * 64, (h + 1) * 64)
                P.op("dve", lambda e, h=h, hs=hs, c=c: e.scalar_tensor_tensor(out=yy[:, hs], in0=xB[:, c, hs], scalar=lay[:, 2, h:h + 1], in1=yy[:, hs], op0=ALU.mult, op1=ALU.add),
                     reads=[t_xB, t_lay, t_yy], writes=[t_yy], partial=True)
            P.op("pe", lambda e, c=c: e.matmul(banks[6], lhsT=xB[:, c, 512:640], rhs=xw, start=True, stop=True),
                 reads=[t_xB, t_xw], writes=[tb[6]])
            for h in range(8):
                g = h // 4
                ps = slice(g * 64, (g + 1) * 64)
                P.op("dve", lambda e, h=h, ps=ps: e.scalar_tensor_tensor(out=hS[ps, h % 4, :], in0=hS[ps, h % 4, :], scalar=decc[ps, h:h + 1], in1=banks[6][ps, h * 64:(h + 1) * 64], op0=ALU.mult, op1=ALU.add),
                     reads=[t_hS, t_dec, tb[6]], writes=[t_hS], partial=(h > 0))
            P.op("act", lambda e: e.copy(out=hSb, in_=hS), reads=[t_hS], writes=[t_hS], partial=True)
            P.op("dve", lambda e, c=c: e.tensor_tensor(out=yy, in0=yy, in1=zs[:, c, :], op=ALU.mult), reads=[t_yy, t_zs], writes=[t_yy])
            for g in range(2):
                P.op("act", lambda e, g=g: e.activation(out=ydsb[:, g * 256:(g + 1) * 256], in_=yy[:, g * 256:(g + 1) * 256], func=AF.Square, accum_out=ssq[:, g:g + 1]),
                     reads=[t_yy], writes=[t_ssq, t_yd], partial=(g > 0))
            P.op("dve", lambda e: e.tensor_scalar(out=ssq[:, 2:4], in0=ssq[:, 0:2], scalar1=1.0 / 256, scalar2=EPS, op0=ALU.mult, op1=ALU.add),
                 reads=[t_ssq], writes=[t_ssq], partial=True)
            P.op("act", lambda e: e.activation(out=ssq[:, 2:4], in_=ssq[:, 2:4], func=AF.Sqrt), reads=[t_ssq], writes=[t_ssq], partial=True)
            P.op("dve", lambda e: e.reciprocal(out=ssq[:, 2:4], in_=ssq[:, 2:4]), reads=[t_ssq], writes=[t_ssq], partial=True)
            for g in range(2):
                gs = slice(g * 256, (g + 1) * 256)
                P.op("dve", lambda e, g=g, gs=gs: e.scalar_tensor_tensor(out=yf[:, gs], in0=yy[:, gs], scalar=ssq[:, 2 + g:3 + g], in1=sng[:, gs], op0=ALU.mult, op1=ALU.mult),
                     reads=[t_yy, t_ssq, t_lay], writes=[t_yf], partial=(g > 0))
            bv7 = banks[0].bitcast(BF16)
            for e4 in range(4):
                P.op("pe", lambda e, e4=e4, bv7=bv7: e.transpose(out=bv7[:, e4 * 128:(e4 + 1) * 128], in_=yf[:, e4 * 128:(e4 + 1) * 128], identity=identb),
                     reads=[t_yf, t_const], writes=[tb[0]], partial=(e4 > 0))
            P.op("act", lambda e, bv7=bv7, csl=csl: e.copy(out=oT[:, 4:8, csl], in_=bv7[:, 0:512].rearrange("p (a b) -> p a b", b=128)),
                 reads=[tb[0]], writes=[t_o], partial=True)
        P.dma("sp", lambda e, l=l: e.dma_start(out=ssm_out[l], in_=hS), t_hS, reads=[t_hS])
        P.barrier()
        AR.reset(mS)
        if stage <= 3:
            break
        mT = AR.mark()
        mtri = AR.alloc([128], BF16)
        manti = AR.alloc([128], BF16)
        cmpmask = AR.alloc([T], BF16)
        eall = AR.alloc([T], BF16)
        kaug = AR.alloc([32, 128], BF16)
        qaug = AR.alloc([8, 128], BF16)
        ovc = AR.alloc([33], BF16)
        bonus = AR.alloc([NT, 32], F32)
        cap = AR.alloc([NT, 32], F32)
        idxb = AR.alloc([32], I32)
        t_ac = Tk("attconst")
        for dst, src in ((mtri, mtri_in), (manti, manti_in), (cmpmask, cmpmask_in), (eall, eall_in), (kaug, kaug_in),
                         (qaug, qaug_in), (ovc, ov_in)):
            P.dma("pool", lambda e, dst=dst, src=src: e.dma_start(out=dst, in_=src), t_ac, writes=[t_ac], partial=True)
        for dst, src in ((bonus, bonus_in), (cap, cap_in), (idxb, idxb_in)):
            P.dma("sp", lambda e, dst=dst, src=src: e.dma_start(out=dst, in_=src), t_ac, writes=[t_ac], partial=True)
        kcmpT = AR.alloc([2, 128], BF16)
        VC = AR.alloc([2, 97], BF16)
        t_kcmp, t_VC = Tk("kcmp"), Tk("VC")
        mC = AR.mark()
        w1d = AR.alloc([2, 32, 128], BF16)
        w2k = AR.alloc([128], BF16)
        w2v = AR.alloc([64], BF16)
        cb1 = AR.alloc([2], F32)
        t_cw = Tk("cmpw")
        for kv in range(2):
            P.dma("pool", lambda e, l=l, kv=kv: e.dma_start(out=w1d[:, kv, :, :], in_=w1d_in[l, kv]), t_cw, writes=[t_cw], partial=True)
        P.dma("pool", lambda e, l=l: e.dma_start(out=w2k, in_=w2k_in[l]), t_cw, writes=[t_cw], partial=True)
        P.dma("pool", lambda e, l=l: e.dma_start(out=w2v, in_=w2v_in[l]), t_cw, writes=[t_cw], partial=True)
        P.dma("sp", lambda e, l=l: e.dma_start(out=cb1, in_=cb1_in[l]), t_cw, writes=[t_cw], partial=True)
        P.op("dve", lambda e: e.memset(kcmpT, 0.0), writes=[t_kcmp])
        P.op("dve", lambda e: e.memset(VC, 0.0), writes=[t_VC])
        for g in range(2):
            P.op("dve", lambda e, g=g: e.tensor_copy(out=VC[:, g, 64:97], in_=ovc), reads=[t_ac], writes=[t_VC], partial=True)
        gx = AR.alloc([4, 128], F32)
        gu = AR.alloc([4, 128], F32)
        hid = AR.alloc([4, 128], BF16)
        t_gx, t_gu, t_hid = Tk("gx"), Tk("gu"), Tk("hid")
        RGB = ((0, 1), (2, 3))
        for kv in range(2):
            for g in range(2):
                i4 = kv * 2 + g
                ps = slice(g * 64, (g + 1) * 64)
                bk = RGB[g][kv]
                for j in range(32):
                    P.op("pe", lambda e, kv=kv, ps=ps, j=j, bk=bk: e.matmul(
                        banks[bk][:, 0:127], lhsT=w1d[ps, kv, j, :], rhs=kvcT[ps, kv, j:j + 16 * 126 + 1:16], start=(j == 0), stop=(j == 31)),
                        reads=[t_cw, t_kvc], writes=[tb[bk]], partial=(j > 0))
                n = slice(0, 127)
                P.op("act", lambda e, i4=i4, kv=kv, bk=bk: e.activation(out=gx[:, i4, 0:127], in_=banks[bk][:, 0:127], func=AF.Identity, bias=cb1[:, kv:kv + 1], scale=1.0),
                     reads=[tb[bk], t_cw], writes=[t_gx], partial=True)
        P.op("dve", lambda e: e.memset(gx[:, :, 127:128], 0.0), writes=[t_gx], partial=True)
        P.op("dve", lambda e: e.tensor_tensor(out=gu, in0=gx, in1=gx, op=ALU.mult), reads=[t_gx], writes=[t_gu])
        P.op("dve", lambda e: e.tensor_scalar(out=gu, in0=gu, scalar1=0.044715, scalar2=1.0, op0=ALU.mult, op1=ALU.add), reads=[t_gu], writes=[t_gu])
        P.op("dve", lambda e: e.tensor_tensor(out=gu, in0=gu, in1=gx, op=ALU.mult), reads=[t_gu, t_gx], writes=[t_gu])
        P.op("act", lambda e: e.activation(out=gu, in_=gu, func=AF.Sigmoid, scale=1.5957691216), reads=[t_gu], writes=[t_gu])
        P.op("dve", lambda e: e.tensor_tensor(out=hid, in0=gu, in1=gx, op=ALU.mult), reads=[t_gu, t_gx], writes=[t_hid])
        for g in range(2):
            P.op("pe", lambda e, g=g: e.matmul(banks[4][:, g * 128:g * 128 + 127], lhsT=w2k, rhs=hid[:, g, 0:127], start=True, stop=True),
                 reads=[t_cw, t_hid], writes=[tb[4]], partial=(g > 0))
        P.op("act", lambda e: e.copy(out=kcmpT[:, :, 0:127], in_=banks[4][:, 0:256].rearrange("p (a b) -> p a b", b=128)[:, :, 0:127]),
             reads=[tb[4]], writes=[t_kcmp], partial=True)
        for g in range(2):
            P.op("pe", lambda e, g=g: e.matmul(banks[5][0:127, g * 64:(g + 1) * 64], lhsT=hid[:, 2 + g, 0:127], rhs=w2v, start=True, stop=True),
                 reads=[t_cw, t_hid], writes=[tb[5]], partial=(g > 0))
        P.op("act", lambda e: e.copy(out=VC[0:127, :, 0:64], in_=banks[5][0:127, 0:128].rearrange("p (a b) -> p a b", b=64)),
             reads=[tb[5]], writes=[t_VC], partial=True)
        P.barrier()
        AR.reset(mC)
        PT = [AR.alloc([512], BF16) for _ in range(4)]
        t_PT = [Tk("PT%d" % i) for i in range(4)]
        otile = AR.alloc([512], F32)
        otb = AR.alloc([512], BF16)
        imp = AR.alloc([32], F32)
        sc = AR.alloc([32], F32)
        w2t = AR.alloc([32], F32)
        m8 = AR.alloc([16], F32)
        nsd = AR.alloc([96], F32)
        rr = AR.alloc([8], F32)
        nsT = [AR.alloc([2, 128], BF16) for _ in range(2)]
        t_ot, t_otb, t_imp, t_sc, t_m8, t_nsd, t_rr = (Tk(n) for n in ("ot", "otb", "imp", "sc", "m8", "nsd", "rr"))
        t_nsT = [Tk("nsT0"), Tk("nsT1")]
        P.op("dve", lambda e: e.memset(nsd, 0.0), writes=[t_nsd])
        sbank_next = [0, 0]
        accb = 0
        for qt in range(NT):
            qsl = slice(qt * 128, (qt + 1) * 128)
            nsq = nsT[qt % 2]
            t_nsq = t_nsT[qt % 2]
            for g in range(2):
                for br in (0, 1, 2):
                    ab = 4 + accb % 2
                    accb += 1
                    vw = 97 if br == 0 else 65
                    chunks = []
                    for j in range(4):
                        h = 4 * g + j
                        rg = h % 2
                        if br == 0:
                            blocks = [(0, "c")]
                        elif br == 1:
                            blocks = [(kt, "d" if kt == qt else "s") for kt in range(qt + 1)]
                        else:
                            blocks = []
                            for kt in range(max(0, qt - 4), qt + 1):
                                blocks.append((kt, "d" if kt == qt else ("a" if kt == qt - 4 else "n")))
                        for c0 in range(0, len(blocks), 4):
                            chunks.append((j, h, rg, blocks[c0:c0 + 4], c0 == 0))
                    pend = None

                    def emit_pv(ch, sbk, ab=ab, vw=vw, br=br, g=g):
                        j, h, rg, blks, first = ch
                        for bi, (kt, kind) in enumerate(blks):
                            if br == 0:
                                rhs = VC[:, g, :]
                            elif br == 1:
                                rhs = Vall[:, kt, g, :]
                            else:
                                rhs = Vall[:, kt, 2 + g, :]
                            P.op("pe", lambda e, ab=ab, j=j, vw=vw, sbk=sbk, bi=bi, rhs=rhs, st=(first and bi == 0): e.matmul(
                                banks[ab][:, j * 97:j * 97 + vw], lhsT=PT[sbk][:, bi * 128:(bi + 1) * 128], rhs=rhs, start=st, stop=False,
                                skip_group_check=True),
                                reads=[t_PT[sbk], t_V, t_VC], writes=[tb[ab]], partial=True)
                    for ch in chunks:
                        j, h, rg, blks, first = ch
                        sbk = RGB[rg][sbank_next[rg] % 2]
                        sbank_next[rg] += 1
                        rows = slice(rg * 64, (rg + 1) * 64)
                        arow = slice(rg * 64, rg * 64 + 4)
                        for bi, (kt, kind) in enumerate(blks):
                            cs_ = slice(bi * 128, (bi + 1) * 128)
                            ksl = slice(kt * 128, (kt + 1) * 128)
                            if br == 0:
                                lk = kcmpT[rows, g, :]
                                la = kaug[arow, 16 + qt, :]
                            elif br == 1:
                                lk = KsT[rows, g, ksl]
                                la = kaug[arow, qt - kt, :]
                            else:
                                lk = KwT[rows, g, ksl]
                                la = kaug[arow, qt - kt, :]
                            P.op("pe", lambda e, sbk=sbk, cs_=cs_, lk=lk, rows=rows, h=h, qsl=qsl: e.matmul(
                                banks[sbk][:, cs_], lhsT=lk, rhs=QT[rows, h // 2, qsl], start=True, stop=False, skip_group_check=True),
                                reads=[t_Q, t_Ks, t_Kw, t_kcmp], writes=[tb[sbk]], partial=(bi > 0))
                            extra = []
                            if kind == "c":
                                extra.append((identb, cmpmask[:, qsl], [t_const, t_ac]))
                            if br == 1 and qt >= 8:
                                erow = slice(rg * 64, rg * 64 + 32)
                                extra.append((eall[erow, ksl], nsq[erow, g, :], [t_ac, t_nsq]))
                            if kind == "d":
                                extra.append((identb, mtri, [t_const, t_ac]))
                            if kind == "a":
                                extra.append((identb, manti, [t_const, t_ac]))
                            P.op("pe", lambda e, sbk=sbk, cs_=cs_, la=la, arow=arow, h=h, last=(not extra): e.matmul(
                                banks[sbk][:, cs_], lhsT=la, rhs=qaug[arow, h, :], start=False, stop=last, skip_group_check=True),
                                reads=[t_ac], writes=[tb[sbk]], partial=True)
                            for xi, (lt_, rh_, rd_) in enumerate(extra):
                                P.op("pe", lambda e, sbk=sbk, cs_=cs_, lt_=lt_, rh_=rh_, last=(xi == len(extra) - 1): e.matmul(
                                    banks[sbk][:, cs_], lhsT=lt_, rhs=rh_, start=False, stop=last, skip_group_check=True),
                                    reads=rd_, writes=[tb[sbk]], partial=True)
                        nb_ = len(blks)
                        P.op("act", lambda e, sbk=sbk, nb_=nb_: e.activation(out=PT[sbk][:, 0:nb_ * 128], in_=banks[sbk][:, 0:nb_ * 128], func=AF.Exp),
                             reads=[tb[sbk]], writes=[t_PT[sbk]])
                        if pend is not None:
                            emit_pv(*pend)
                        pend = (ch, sbk)
                    emit_pv(*pend)
                    accv = banks[ab][:, 0:388].rearrange("p (a b) -> p a b", b=97)
                    P.op("dve", lambda e, accv=accv: e.tensor_scalar(out=rr[:, 0:4], in0=accv[:, :, 64], scalar1=1e-30, scalar2=None, op0=ALU.max),
                         reads=[tb[ab]], writes=[t_rr])
                    P.op("dve", lambda e: e.reciprocal(out=rr[:, 0:4], in_=rr[:, 0:4]), reads=[t_rr], writes=[t_rr])
                    P.op("dve", lambda e, qt=qt, br=br, g=g: e.tensor_tensor(out=rr[:, 4:8], in0=rr[:, 0:4], in1=gat[:, qt, br * 8 + g * 4: br * 8 + g * 4 + 4], op=ALU.mult),
                         reads=[t_rr, t_gat], writes=[t_rr], partial=True)
                    for j in range(4):
                        h = 4 * g + j
                        hs = slice(h * 64, (h + 1) * 64)
                        if br == 0:
                            P.op("dve", lambda e, accv=accv, j=j, hs=hs: e.tensor_scalar(out=otile[:, hs], in0=accv[:, j, 0:64], scalar1=rr[:, 4 + j:5 + j], scalar2=None, op0=ALU.mult),
                                 reads=[tb[ab], t_rr], writes=[t_ot], partial=True)
                            if qt < 8:
                                pass
                            elif j == 0:
                                P.op("dve", lambda e, accv=accv, j=j: e.tensor_scalar(out=imp, in0=accv[:, j, 65:97], scalar1=rr[:, j:j + 1], scalar2=None, op0=ALU.mult),
                                     reads=[tb[ab], t_rr], writes=[t_imp])
                            else:
                                P.op("dve", lambda e, accv=accv, j=j: e.scalar_tensor_tensor(out=imp, in0=accv[:, j, 65:97], scalar=rr[:, j:j + 1], in1=imp, op0=ALU.mult, op1=ALU.add),
                                     reads=[tb[ab], t_rr, t_imp], writes=[t_imp])
                        else:
                            P.op("dve", lambda e, accv=accv, j=j, hs=hs: e.scalar_tensor_tensor(out=otile[:, hs], in0=accv[:, j, 0:64], scalar=rr[:, 4 + j:5 + j], in1=otile[:, hs], op0=ALU.mult, op1=ALU.add),
                                 reads=[tb[ab], t_rr, t_ot], writes=[t_ot], partial=True)
                    if br == 0 and qt >= 8:
                        P.op("dve", lambda e: e.tensor_scalar(out=sc, in0=imp, scalar1=float(2.0 ** 64), scalar2=float(2.0 ** -60), op0=ALU.mult, op1=ALU.max),
                             reads=[t_imp], writes=[t_sc])
                        P.op("dve", lambda e, qt=qt: e.tensor_tensor(out=sc, in0=sc, in1=bonus[:, qt, :], op=ALU.add), reads=[t_sc, t_ac], writes=[t_sc])
                        sci = sc.bitcast(I32)
                        P.op("dve", lambda e, sci=sci: e.tensor_single_scalar(out=sci, in_=sci, scalar=-32, op=ALU.bitwise_and), reads=[t_sc], writes=[t_sc])
                        P.op("dve", lambda e, sci=sci: e.tensor_tensor(out=sci, in0=sci, in1=idxb, op=ALU.bitwise_or), reads=[t_sc, t_ac], writes=[t_sc])
                        P.op("dve", lambda e, qt=qt: e.tensor_tensor(out=sc, in0=sc, in1=cap[:, qt, :], op=ALU.min), reads=[t_sc, t_ac], writes=[t_sc])
                        P.op("dve", lambda e: e.max(out=m8[:, 0:8], in_=sc), reads=[t_sc], writes=[t_m8])
                        P.op("dve", lambda e: e.match_replace(out=w2t, in_to_replace=m8[:, 0:8], in_values=sc, imm_value=-3e38), reads=[t_sc, t_m8], writes=[t_m8], partial=True)
                        P.op("dve", lambda e: e.max(out=m8[:, 8:16], in_=w2t), reads=[t_m8], writes=[t_m8], partial=True)
                        P.op("dve", lambda e: e.tensor_scalar(out=w2t, in0=sc, scalar1=m8[:, 15:16], scalar2=None, op0=ALU.is_ge), reads=[t_sc, t_m8], writes=[t_m8], partial=True)
                        P.op("dve", lambda e: e.tensor_scalar(out=nsd[:, 0:32], in0=w2t, scalar1=1.0, scalar2=-NEGB, op0=ALU.subtract, op1=ALU.mult), reads=[t_m8], writes=[t_nsd], partial=True)
                        P.op("dve", lambda e: e.tensor_copy(out=nsd[:, 64:96], in_=nsd[:, 0:32]), reads=[t_nsd], writes=[t_nsd], partial=True)
                        P.op("pe", lambda e: e.transpose(out=banks[6][0:96, 0:128], in_=nsd, identity=identf), reads=[t_nsd, t_const], writes=[tb[6]])
                        P.op("act", lambda e, nsq=nsq, g=g: e.copy(out=nsq[0:96, g, :], in_=banks[6][0:96, 0:128]), reads=[tb[6]], writes=[t_nsq], partial=True)
            P.op("act", lambda e: e.copy(out=otb, in_=otile), reads=[t_ot], writes=[t_otb])
            bv6 = banks[7].bitcast(BF16)
            for e4 in range(4):
                P.op("pe", lambda e, e4=e4, bv6=bv6: e.transpose(out=bv6[:, e4 * 128:(e4 + 1) * 128], in_=otb[:, e4 * 128:(e4 + 1) * 128], identity=identb),
                     reads=[t_otb, t_const], writes=[tb[7]], partial=(e4 > 0))
            P.op("act", lambda e, bv6=bv6, qsl=qsl: e.copy(out=oT[:, 0:4, qsl], in_=bv6[:, 0:512].rearrange("p (a b) -> p a b", b=128)),
                 reads=[tb[7]], writes=[t_o], partial=True)
        P.barrier()
        AR.reset(mT)
        if stage <= 4:
            break
        AR.reset(layer_mark)
        idxall = AR.alloc([256], I32)
        augL = AR.alloc([2, 128], BF16)
        augR = AR.alloc([2, 296], BF16)
        mcs = AR.alloc([296], BF16)
        ovs = AR.alloc([4, 130], BF16)
        bonus_s = AR.alloc([129], F32)
        idx_s = AR.alloc([129], I32)
        oneo = AR.alloc([2, 128], BF16)
        t_sc2 = Tk("sconst")
        P.dma("sp", lambda e: e.dma_start(out=idxall, in_=ptb_in), t_sc2, writes=[t_sc2])
        pidx = AR.alloc([256], I32)
        P.dma("sp", lambda e: e.dma_start(out=pidx, in_=pidx_in), t_sc2, writes=[t_sc2], partial=True)
        P.op("dve", lambda e: e.tensor_single_scalar(out=idxall, in_=idxall, scalar=7, op=ALU.logical_shift_left), reads=[t_sc2], writes=[t_sc2], partial=True)
        P.op("dve", lambda e: e.tensor_tensor(out=idxall, in0=idxall, in1=pidx, op=ALU.bitwise_or), reads=[t_sc2], writes=[t_sc2], partial=True)
        for dst, src in ((augL, augL_in), (augR, augR_in), (mcs, mcs_in), (ovs, ovs_in), (oneo, oneo_in)):
            P.dma("pool", lambda e, dst=dst, src=src: e.dma_start(out=dst, in_=src), t_sc2, writes=[t_sc2], partial=True)
        P.dma("sp", lambda e: e.dma_start(out=bonus_s, in_=bonus_s_in), t_sc2, writes=[t_sc2], partial=True)
        P.dma("sp", lambda e: e.dma_start(out=idx_s, in_=idx_s_in), t_sc2, writes=[t_sc2], partial=True)
        w1ds = AR.alloc([2, 32, 128], BF16)
        w2ks = AR.alloc([128], BF16)
        w2vs = AR.alloc([64], BF16)
        cb1s = AR.alloc([2], F32)
        t_cws = Tk("cmpw_s")
        for kv in range(2):
            P.dma("pool", lambda e, l=l, kv=kv: e.dma_start(out=w1ds[:, kv, :, :], in_=w1d_in[l, kv]), t_cws, writes=[t_cws], partial=True)
        P.dma("pool", lambda e, l=l: e.dma_start(out=w2ks, in_=w2k_in[l]), t_cws, writes=[t_cws], partial=True)
        P.dma("pool", lambda e, l=l: e.dma_start(out=w2vs, in_=w2v_in[l]), t_cws, writes=[t_cws], partial=True)
        P.dma("sp", lambda e, l=l: e.dma_start(out=cb1s, in_=cb1_in[l]), t_cws, writes=[t_cws], partial=True)
        KTa = AR.alloc([8192], BF16)
        KTb = AR.alloc([8192], BF16)
        Vsl = AR.alloc([64, 2, 65], BF16)
        gst = [AR.alloc([8, 128], F32) for _ in range(4)]
        t_KTa, t_KTb, t_Vsl = Tk("KTa"), Tk("KTb"), Tk("Vsl")
        t_gst = [Tk("gst%d" % i) for i in range(4)]
        gxs = AR.alloc([4, 512], F32)
        gus = AR.alloc([4, 512], F32)
        hids = AR.alloc([4, 512], BF16)
        t_gxs, t_gus, t_hids = Tk("gxs"), Tk("gus"), Tk("hids")
        kcs = AR.alloc([512], BF16)
        VCs = AR.alloc([4, 2, 194], BF16)
        KTn = AR.alloc([2, 4, 128], BF16)
        KwTs = AR.alloc([512], BF16)
        kwst = AR.alloc([512], F32)
        Vw = AR.alloc([4, 2, 65], BF16)
        vwst = AR.alloc([4, 128], F32)
        PTc = AR.alloc([16], BF16)
        PT2 = AR.alloc([280], BF16)
        accs = AR.alloc([194], F32)
        r65 = AR.alloc([66], F32)
        rg2 = AR.alloc([4], F32)
        scs = AR.alloc([129], F32)
        wts = AR.alloc([129], F32)
        m8s = AR.alloc([16], F32)
        nss = AR.alloc([130], F32)
        nseo2 = [AR.alloc([2, 64, 4], BF16) for _ in range(2)]
        osum = AR.alloc([8, 128], F32)
        t_kcs, t_VCs, t_KTn, t_KwTs, t_kwst, t_Vw, t_vwst, t_PTc, t_PT2, t_accs, t_r65, t_scs, t_m8s, t_nss, t_nseo, t_osum = (Tk(n) for n in
            ("kcs", "VCs", "KTn", "KwTs", "kwst", "Vw", "vwst", "PTc", "PT2", "accs", "r65", "scs", "m8s", "nss", "nseo", "osum"))
        P.op("dve", lambda e: e.memset(KTn, 0.0), writes=[t_KTn])
        for b in range(4):
            P.op("dve", lambda e, b=b: e.tensor_copy(out=KTn[:, 0, b, 0:1], in_=kvnT[:, 2, b:b + 1]), reads=[t_kvn], writes=[t_KTn], partial=True)
            P.op("dve", lambda e, b=b: e.tensor_copy(out=KTn[:, 1, b, 0:1], in_=kvnT[:, 4, b:b + 1]), reads=[t_kvn], writes=[t_KTn], partial=True)
        P.op("dve", lambda e: e.memset(VCs, 0.0), writes=[t_VCs])
        for g in range(2):
            P.op("dve", lambda e, g=g: e.tensor_copy(out=VCs[:, :, g, 64:194], in_=ovs), reads=[t_sc2], writes=[t_VCs], partial=True)
        P.op("dve", lambda e: e.memset(Vsl, 1.0), writes=[t_Vsl])
        P.op("dve", lambda e: e.memset(Vw, 1.0), writes=[t_Vw])
        P.op("dve", lambda e: e.memset(gxs, 0.0), writes=[t_gxs])
        P.op("dve", lambda e: e.memset(r65, 0.0), writes=[t_r65])
        gi = [0]

        def gather(pool_ap, b, dstfn, t_dst, first_partial):
            for a8 in range(8):
                si = gi[0] % 4
                gi[0] += 1

                def issue(e, a8=a8, si=si):
                    return [e.indirect_dma_start(out=gst[si][:, a, :], out_offset=None, in_=pool_ap,
                                                 in_offset=bass.IndirectOffsetOnAxis(ap=idxall[:, b * 64 + a8 * 8 + a: b * 64 + a8 * 8 + a + 1], axis=0))
                            for a in range(8)]
                P.dma("pool", issue, t_gst[si], reads=[t_sc2], writes=[t_gst[si]], n=8)
                dstfn(a8, si)

        for b in range(4):
            gather(cmpkT_pool[l], b, lambda a8, si: P.op("act", lambda e, a8=a8, si=si: e.copy(out=KTa[:, a8 * 1024:(a8 + 1) * 1024], in_=gst[si].rearrange("p a t -> p (a t)")),
                                                         reads=[t_gst[si]], writes=[t_KTa], partial=True), t_KTa, False)
            gather(cmpvT_pool[l], b, lambda a8, si: P.op("act", lambda e, a8=a8, si=si: e.copy(out=KTb[:, a8 * 1024:(a8 + 1) * 1024], in_=gst[si].rearrange("p a t -> p (a t)")),
                                                         reads=[t_gst[si]], writes=[t_KTb], partial=True), t_KTb, False)
            for kv in range(2):
                KT = KTa if kv == 0 else KTb
                t_KT = t_KTa if kv == 0 else t_KTb
                for g in range(2):
                    i4 = kv * 2 + g
                    ps = slice(g * 64, (g + 1) * 64)
                    bk = RGB[g][kv]
                    for j in range(32):
                        P.op("pe", lambda e, kv=kv, ps=ps, j=j, bk=bk, KT=KT: e.matmul(
                            banks[bk][:, 0:511], lhsT=w1ds[ps, kv, j, :], rhs=KT[ps, j:j + 16 * 510 + 1:16], start=(j == 0), stop=(j == 31)),
                            reads=[t_cws, t_KT], writes=[tb[bk]], partial=(j > 0))
                    P.op("act", lambda e, i4=i4, kv=kv, bk=bk: e.activation(out=gxs[:, i4, 0:511], in_=banks[bk][:, 0:511], func=AF.Identity, bias=cb1s[:, kv:kv + 1], scale=1.0),
                         reads=[tb[bk], t_cws], writes=[t_gxs], partial=True)
            P.op("dve", lambda e: e.tensor_tensor(out=gus, in0=gxs, in1=gxs, op=ALU.mult), reads=[t_gxs], writes=[t_gus])
            P.op("dve", lambda e: e.tensor_scalar(out=gus, in0=gus, scalar1=0.044715, scalar2=1.0, op0=ALU.mult, op1=ALU.add), reads=[t_gus], writes=[t_gus])
            P.op("dve", lambda e: e.tensor_tensor(out=gus, in0=gus, in1=gxs, op=ALU.mult), reads=[t_gus, t_gxs], writes=[t_gus])
            P.op("act", lambda e: e.activation(out=gus, in_=gus, func=AF.Sigmoid, scale=1.5957691216), reads=[t_gus], writes=[t_gus])
            P.op("dve", lambda e: e.tensor_tensor(out=hids, in0=gus, in1=gxs, op=ALU.mult), reads=[t_gus, t_gxs], writes=[t_hids])
            for g in range(2):
                P.op("pe", lambda e, g=g: e.matmul(banks[4 + g][:, 0:512], lhsT=w2ks, rhs=hids[:, g, :], start=True, stop=True), reads=[t_cws, t_hids], writes=[tb[4 + g]])
                ps = slice(g * 64, (g + 1) * 64)
                P.op("act", lambda e, g=g, ps=ps: e.copy(out=kcs[ps, :], in_=banks[4 + g][ps, 0:512]), reads=[tb[4 + g]], writes=[t_kcs], partial=(g > 0))
            for g in range(2):
                for nt in range(4):
                    P.op("pe", lambda e, g=g, nt=nt: e.matmul(banks[6][:, (g * 4 + nt) * 64:(g * 4 + nt + 1) * 64], lhsT=hids[:, 2 + g, nt * 128:(nt + 1) * 128], rhs=w2vs, start=True, stop=True),
                         reads=[t_cws, t_hids], writes=[tb[6]], partial=(g + nt > 0))
            P.op("act", lambda e: e.copy(out=VCs[:, :, :, 0:64].rearrange("p t g d -> p g t d"), in_=banks[6].rearrange("p (g t d) -> p g t d", g=2, t=4)),
                 reads=[tb[6]], writes=[t_VCs], partial=True)
            for g in range(2):
                rows = slice(g * 64, (g + 1) * 64)
                tr = g * 64
                sb1 = RGB[g][0]
                P.op("pe", lambda e, tr=tr, g=g, sb1=sb1: e.matmul(banks[sb1][:, 0:16], lhsT=augL[tr:tr + 3, 1, :], rhs=augR[tr:tr + 3, g, 280:296], start=True, stop=False, skip_group_check=True),
                     reads=[t_sc2], writes=[tb[sb1]])
                P.op("pe", lambda e, sb1=sb1: e.matmul(banks[sb1][:, 0:16], lhsT=identb, rhs=mcs[:, 280:296], start=False, stop=False, skip_group_check=True),
                     reads=[t_sc2, t_const], writes=[tb[sb1]], partial=True)
                for nt in range(4):
                    P.op("pe", lambda e, rows=rows, nt=nt, sb1=sb1, b=b: e.matmul(banks[sb1][:, nt * 4:(nt + 1) * 4], lhsT=kcs[rows, nt * 128:(nt + 1) * 128], rhs=QgT[rows, :, b],
                                                                           start=False, stop=(nt == 3), skip_group_check=True), reads=[t_kcs, t_Qg], writes=[tb[sb1]], partial=True)
                P.op("act", lambda e, sb1=sb1: e.activation(out=PTc, in_=banks[sb1][:, 0:16], func=AF.Exp), reads=[tb[sb1]], writes=[t_PTc])
                for nt in range(4):
                    P.op("pe", lambda e, nt=nt, g=g: e.matmul(banks[7][0:4, 0:194], lhsT=PTc[:, nt * 4:(nt + 1) * 4], rhs=VCs[:, nt, g, :], start=(nt == 0), stop=(nt == 3)),
                         reads=[t_PTc, t_VCs], writes=[tb[7]], partial=(nt > 0))
                P.op("act", lambda e: e.copy(out=accs[0:4, :], in_=banks[7][0:4, 0:194]), reads=[tb[7]], writes=[t_accs])
                P.op("dve", lambda e: e.tensor_scalar(out=r65[0:4, 64:65], in0=accs[0:4, 64:65], scalar1=1e-30, scalar2=None, op0=ALU.max), reads=[t_accs], writes=[t_r65], partial=True)
                P.op("dve", lambda e: e.reciprocal(out=r65[0:4, 64:65], in_=r65[0:4, 64:65]), reads=[t_r65], writes=[t_r65], partial=True)
                P.op("dve", lambda e, g=g, b=b: e.tensor_tensor(out=r65[0:4, 65:66], in0=r65[0:4, 64:65], in1=gsT[0:4, (0 * 2 + g) * 4 + b:(0 * 2 + g) * 4 + b + 1], op=ALU.mult),
                     reads=[t_r65, t_gs], writes=[t_r65], partial=True)
                bg = b * 2 + g
                P.op("dve", lambda e, bg=bg: e.tensor_scalar(out=osum[0:4, bg, 0:64], in0=accs[0:4, 0:64], scalar1=r65[0:4, 65:66], scalar2=None, op0=ALU.mult),
                     reads=[t_accs, t_r65], writes=[t_osum], partial=True)
                lw = r65[0:4, 64:65] if g == 0 else r65[0:4, 0:65]
                npart = 1 if g == 0 else 65
                P.op("pe", lambda e, lw=lw, npart=npart: e.matmul(banks[6][0:npart, 0:129], lhsT=lw, rhs=accs[0:4, 65:194], start=True, stop=True),
                     reads=[t_r65, t_accs], writes=[tb[6]])
                tp = slice(tr, tr + 1)
                P.op("dve", lambda e, tp=tp: e.tensor_scalar(out=scs[tp, :], in0=banks[6][tp, 0:129], scalar1=float(2.0 ** 64), scalar2=float(2.0 ** -60), op0=ALU.mult, op1=ALU.max),
                     reads=[tb[6]], writes=[t_scs])
                P.op("dve", lambda e, tp=tp: e.tensor_tensor(out=scs[tp, :], in0=scs[tp, :], in1=bonus_s[tp, :], op=ALU.add), reads=[t_scs, t_sc2], writes=[t_scs])
                sci2 = scs.bitcast(I32)
                P.op("dve", lambda e, tp=tp, sci2=sci2: e.tensor_single_scalar(out=sci2[tp, :], in_=sci2[tp, :], scalar=-256, op=ALU.bitwise_and), reads=[t_scs], writes=[t_scs])
                P.op("dve", lambda e, tp=tp, sci2=sci2: e.tensor_tensor(out=sci2[tp, :], in0=sci2[tp, :], in1=idx_s[tp, :], op=ALU.bitwise_or), reads=[t_scs, t_sc2], writes=[t_scs])
                P.op("dve", lambda e, tp=tp: e.max(out=m8s[tp, 0:8], in_=scs[tp, :]), reads=[t_scs], writes=[t_m8s])
                P.op("dve", lambda e, tp=tp: e.match_replace(out=wts[tp, :], in_to_replace=m8s[tp, 0:8], in_values=scs[tp, :], imm_value=-3e38), reads=[t_scs, t_m8s], writes=[t_m8s], partial=True)
                P.op("dve", lambda e, tp=tp: e.max(out=m8s[tp, 8:16], in_=wts[tp, :]), reads=[t_m8s], writes=[t_m8s], partial=True)
                P.op("dve", lambda e, tp=tp: e.tensor_scalar(out=wts[tp, :], in0=scs[tp, :], scalar1=m8s[tp, 15:16], scalar2=None, op0=ALU.is_ge), reads=[t_scs, t_m8s], writes=[t_m8s], partial=True)
                P.op("dve", lambda e, tp=tp: e.tensor_scalar(out=nss[tp, 0:129], in0=wts[tp, :], scalar1=1.0, scalar2=-NEGB, op0=ALU.subtract, op1=ALU.mult), reads=[t_m8s], writes=[t_nss])
                nv = nss[tp, 0:128].rearrange("p (k two) -> p k two", two=2)
                for eo in range(2):
                    for j in range(4):
                        P.op("dve", lambda e, tp=tp, eo=eo, j=j, nv=nv, g=g: e.tensor_copy(out=nseo2[g][tp, eo, :, j], in_=nv[:, :, eo]), reads=[t_nss], writes=[t_nseo], partial=True)
                if g == 0:
                    gather(slckT_pool[l], b, lambda a8, si: P.op("act", lambda e, a8=a8, si=si: e.copy(out=KTa[:, a8 * 1024:(a8 + 1) * 1024], in_=gst[si].rearrange("p a t -> p (a t)")),
                                                                 reads=[t_gst[si]], writes=[t_KTa], partial=True), t_KTa, False)
                    gather(slcv_pool[l], b, lambda a8, si: P.op("act", lambda e, a8=a8, si=si: e.copy(out=Vsl[:, a8 * 8:(a8 + 1) * 8, :, 0:64], in_=gst[si].rearrange("p a (g d) -> p a g d", g=2)),
                                                                reads=[t_gst[si]], writes=[t_Vsl], partial=True), t_Vsl, False)
                    P.dma("sp", lambda e, l=l, b=b: e.dma_start(out=kwst, in_=wkT_in[l, b]), t_kwst, writes=[t_kwst])
                    P.op("act", lambda e: e.copy(out=KwTs, in_=kwst), reads=[t_kwst], writes=[t_KwTs])
                    P.dma("sp", lambda e, l=l, b=b: e.dma_start(out=vwst, in_=wv_in[l, b].rearrange("(t p) f -> p t f", p=128)), t_vwst, writes=[t_vwst])
                    P.op("act", lambda e: e.copy(out=Vw[:, :, :, 0:64], in_=vwst.rearrange("p t (g d) -> p t g d", g=2)), reads=[t_vwst], writes=[t_Vw], partial=True)
            for g in range(2):
                rows = slice(g * 64, (g + 1) * 64)
                tr = g * 64
                tp = slice(tr, tr + 1)
                sb2 = RGB[g][1]
                bg = b * 2 + g
                P.op("pe", lambda e, tr=tr, g=g, sb2=sb2: e.matmul(banks[sb2][:, 0:280], lhsT=augL[tr:tr + 3, 0, :], rhs=augR[tr:tr + 3, g, 0:280], start=True, stop=False, skip_group_check=True),
                     reads=[t_sc2], writes=[tb[sb2]])
                for eo in range(2):
                    P.op("pe", lambda e, tp=tp, eo=eo, sb2=sb2, g=g: e.matmul(banks[sb2][:, 0:256], lhsT=oneo[tp, eo, :], rhs=nseo2[g][tp, eo, :, :].rearrange("p k j -> p (k j)"), start=False, stop=False, skip_group_check=True),
                         reads=[t_sc2, t_nseo], writes=[tb[sb2]], partial=True)
                P.op("pe", lambda e, sb2=sb2: e.matmul(banks[sb2][:, 0:280], lhsT=identb, rhs=mcs[:, 0:280], start=False, stop=False, skip_group_check=True),
                     reads=[t_sc2, t_const], writes=[tb[sb2]], partial=True)
                for kt in range(70):
                    if kt < 64:
                        lk, rd = KTa[rows, kt * 128:(kt + 1) * 128], [t_KTa]
                    elif kt == 64:
                        lk, rd = KTn[rows, 0, b, :], [t_KTn]
                    elif kt < 69:
                        lk, rd = KwTs[rows, (kt - 65) * 128:(kt - 64) * 128], [t_KwTs]
                    else:
                        lk, rd = KTn[rows, 1, b, :], [t_KTn]
                    P.op("pe", lambda e, lk=lk, rows=rows, kt=kt, sb2=sb2, b=b: e.matmul(banks[sb2][:, kt * 4:(kt + 1) * 4], lhsT=lk, rhs=QgT[rows, :, b], start=False, stop=(kt == 69), skip_group_check=True),
                         reads=rd + [t_Qg], writes=[tb[sb2]], partial=True)
                P.op("act", lambda e, sb2=sb2: e.activation(out=PT2, in_=banks[sb2][:, 0:280], func=AF.Exp), reads=[tb[sb2]], writes=[t_PT2])
                for kt in range(65):
                    if kt < 64:
                        P.op("pe", lambda e, kt=kt, g=g: e.matmul(banks[7][0:4, 0:65], lhsT=PT2[:, kt * 4:(kt + 1) * 4], rhs=Vsl[:, kt, g, :], start=(kt == 0), stop=False, skip_group_check=True),
                             reads=[t_PT2, t_Vsl], writes=[tb[7]], partial=(kt > 0))
                    else:
                        P.op("pe", lambda e, g=g, b=b: e.matmul(banks[7][0:4, 0:65], lhsT=PT2[0:1, 256:260], rhs=VN[0:1, b, 0, g * 65:(g + 1) * 65], start=False, stop=True, skip_group_check=True),
                             reads=[t_PT2, t_VN], writes=[tb[7]], partial=True)
                for kt in range(5):
                    if kt < 4:
                        P.op("pe", lambda e, kt=kt, g=g: e.matmul(banks[7][0:4, 65:130], lhsT=PT2[:, 260 + kt * 4:264 + kt * 4], rhs=Vw[:, kt, g, :], start=False, stop=False, skip_group_check=True),
                             reads=[t_PT2, t_Vw], writes=[tb[7]], partial=True)
                    else:
                        P.op("pe", lambda e, g=g, b=b: e.matmul(banks[7][0:4, 65:130], lhsT=PT2[0:1, 276:280], rhs=VN[0:1, b, 1, g * 65:(g + 1) * 65], start=False, stop=True, skip_group_check=True),
                             reads=[t_PT2, t_VN], writes=[tb[7]], partial=True)
                P.op("act", lambda e: e.copy(out=accs[0:4, 0:130], in_=banks[7][0:4, 0:130]), reads=[tb[7]], writes=[t_accs])
                for br in (1, 2):
                    c0 = (br - 1) * 65
                    P.op("dve", lambda e, c0=c0: e.tensor_scalar(out=rg2[0:4, 0:1], in0=accs[0:4, c0 + 64:c0 + 65], scalar1=1e-30, scalar2=None, op0=ALU.max), reads=[t_accs], writes=[t_r65])
                    P.op("dve", lambda e: e.reciprocal(out=rg2[0:4, 0:1], in_=rg2[0:4, 0:1]), reads=[t_r65], writes=[t_r65])
                    P.op("dve", lambda e, br=br, g=g, b=b: e.tensor_tensor(out=rg2[0:4, 1:2], in0=rg2[0:4, 0:1], in1=gsT[0:4, (br * 2 + g) * 4 + b:(br * 2 + g) * 4 + b + 1], op=ALU.mult),
                         reads=[t_r65, t_gs], writes=[t_r65])
                    P.op("dve", lambda e, c0=c0, bg=bg: e.scalar_tensor_tensor(out=osum[0:4, bg, 0:64], in0=accs[0:4, c0:c0 + 64], scalar=rg2[0:4, 1:2], in1=osum[0:4, bg, 0:64], op0=ALU.mult, op1=ALU.add),
                         reads=[t_accs, t_r65, t_osum], writes=[t_osum], partial=True)
        P.op("dve", lambda e: e.tensor_copy(out=osum[0:4, :, 64:128], in_=osum[0:4, :, 0:64]), reads=[t_osum], writes=[t_osum], partial=True)
        for bg in range(8):
            b, g = bg // 2, bg % 2
            P.op("pe", lambda e, bg=bg: e.transpose(out=banks[6][:, bg * 4:(bg + 1) * 4], in_=osum[0:4, bg, :], identity=identf[0:4, 0:4]), reads=[t_osum, t_const], writes=[tb[6]], partial=(bg > 0))
        ot4 = banks[6][:, 0:32].rearrange("p (b g a r) -> p b g a r", b=4, g=2, a=2)
        for r in range(2):
            rs_ = slice(r * 64, (r + 1) * 64)
            P.op("act", lambda e, r=r, rs_=rs_, ot4=ot4: e.copy(out=osT[rs_, 0:4, :].rearrange("p (g a) b -> p b g a", g=2), in_=ot4[rs_, :, :, :, r]),
                 reads=[tb[6]], writes=[t_os], partial=True)
        P.barrier()
        AR.reset(layer_mark)
        AR.alloc([KD, T], F32)
        load_x(xsrc)
        mO = AR.mark()
        wo = AR.alloc([KD, D], BF16)
        t_wo = Tk("wo")
        for ec in range(KD):
            P.dma("pool", lambda e, l=l, ec=ec: e.dma_start(out=wo[:, ec, :], in_=wo_in[l, :, ec, :]), t_wo, writes=[t_wo], partial=True)
        nb = 0
        for dc in range(KD):
            for tg in range(4):
                ts = slice(tg * 512, (tg + 1) * 512)
                bk = nb % 4
                nb += 1
                for ec in range(KD):
                    P.op("pe", lambda e, bk=bk, ec=ec, dc=dc, ts=ts: e.matmul(
                        banks[bk], lhsT=wo[:, ec, dc * 128:(dc + 1) * 128], rhs=oT[:, ec, ts], start=(ec == 0), stop=(ec == KD - 1)),
                        reads=[t_wo, t_o], writes=[tb[bk]], partial=(ec > 0))
                P.op("dve", lambda e, bk=bk, dc=dc, ts=ts, l=l: e.scalar_tensor_tensor(
                    out=xT[:, dc, ts], in0=banks[bk], scalar=mods[:, l * 48 + 16 + dc, 0:1], in1=xT[:, dc, ts], op0=ALU.mult, op1=ALU.add),
                    reads=[tb[bk], t_mods, t_x], writes=[t_x], partial=True)
        smx = AR.alloc([4], F32)
        t_smx = Tk("smx")
        for dc in range(KD):
            bk = 4 + dc % 2
            for ec in range(KD):
                P.op("pe", lambda e, bk=bk, ec=ec, dc=dc: e.matmul(banks[bk][:, 0:4], lhsT=wo[:, ec, dc * 128:(dc + 1) * 128], rhs=osT[:, ec, :], start=(ec == 0), stop=(ec == KD - 1)),
                     reads=[t_wo, t_os], writes=[tb[bk]], partial=(ec > 0))
            P.op("dve", lambda e, bk=bk, dc=dc, l=l: e.tensor_tensor(out=smx, in0=banks[bk][:, 0:4], in1=mods[:, l * 48 + 16 + dc, 1:5], op=ALU.mult), reads=[tb[bk], t_mods], writes=[t_smx])
            P.op("dve", lambda e, dc=dc: e.tensor_tensor(out=xsT[:, dc, :], in0=xsT[:, dc, :], in1=smx, op=ALU.add), reads=[t_smx, t_xs], writes=[t_xs], partial=True)
        P.barrier()
        AR.reset(mO)
        norm_mod(A2, B2, [0, 1])
        P.barrier()
        sample_norm(l, 4, 3, n2g[l])
        P.barrier()
        fsp = AR.alloc([24, 3, 4], F32)
        fso = AR.alloc([24, 2, 4], F32)
        sact = AR.alloc([4, 4], BF16)
        sfa = AR.alloc([4], F32)
        t_fsp, t_fso, t_sact, t_sfa = Tk("fsp"), Tk("fso"), Tk("sact"), Tk("sfa")
        P.dma("sp", lambda e, l=l: e.dma_start(out=fsp[:, :, 0:2, :], in_=fst_in[l]), t_fsp, writes=[t_fsp])
        P.op("dve", lambda e: e.memset(fso, 0.0), writes=[t_fso])
        wup = [AR.alloc([KD, 2, 512], BF16) for _ in range(2)]
        wdn = [AR.alloc([4, D], BF16) for _ in range(2)]
        t_wup = [Tk("wup0"), Tk("wup1")]
        t_wdn = [Tk("wdn0"), Tk("wdn1")]
        fcw = AR.alloc([24, 3], F32)
        fcb = AR.alloc([24], F32)
        t_fc = Tk("fc")
        P.dma("sp", lambda e, l=l: e.dma_start(out=fcw, in_=fcw_in[l]), t_fc, writes=[t_fc], partial=True)
        P.dma("sp", lambda e, l=l: e.dma_start(out=fcb, in_=fcb_in[l]), t_fc, writes=[t_fc], partial=True)
        fst = [AR.alloc([514], F32) for _ in range(2)]
        t_fst = [Tk("fst0"), Tk("fst1")]
        hal = AR.alloc([4, 2], F32)
        t_hal = Tk("hal")
        fac = [AR.alloc([512], F32) for _ in range(2)]
        t_fac = [Tk("fac0"), Tk("fac1")]
        actT = [AR.alloc([4, 512], BF16) for _ in range(2)]
        t_act = [Tk("act0"), Tk("act1")]
        nbk = 0
        it = 0
        for fg in range(6):
            nf = 4 if fg < 5 else 2
            ws = fg % 2
            for kc in range(KD):
                P.dma("pool", lambda e, l=l, fg=fg, ws=ws, kc=kc: e.dma_start(out=wup[ws][:, kc, :, :], in_=wup_in[l, fg, :, kc, :, :]),
                      t_wup[ws], writes=[t_wup[ws]], partial=(kc > 0))
            for fcl in range(nf):
                P.dma("pool", lambda e, l=l, fg=fg, ws=ws, fcl=fcl: e.dma_start(out=wdn[ws][:, fcl, :], in_=wdn_in[l, :, fg * 4 + fcl, :]),
                      t_wdn[ws], writes=[t_wdn[ws]], partial=(fcl > 0))
            for tg in range(4):
                ts = slice(tg * 512, (tg + 1) * 512)
                asl = it % 2
                it += 1
                for fcl in range(nf):
                    fc = fg * 4 + fcl
                    bu = 2 + (nbk % 2) * 2
                    bv = bu + 1
                    nbk += 1
                    si = fcl % 2
                    for kc in range(KD):
                        P.op("pe", lambda e, bu=bu, kc=kc, ws=ws, fcl=fcl, ts=ts: e.matmul(
                            banks[bu], lhsT=wup[ws][:, kc, 0, fcl * 128:(fcl + 1) * 128], rhs=hT[:, kc, ts], start=(kc == 0), stop=(kc == KD - 1)),
                            reads=[t_wup[ws], t_h], writes=[tb[bu]], partial=(kc > 0))
                    for kc in range(KD):
                        P.op("pe", lambda e, bv=bv, kc=kc, ws=ws, fcl=fcl, ts=ts: e.matmul(
                            banks[bv], lhsT=wup[ws][:, kc, 1, fcl * 128:(fcl + 1) * 128], rhs=hT[:, kc, ts], start=(kc == 0), stop=(kc == KD - 1)),
                            reads=[t_wup[ws], t_h], writes=[tb[bv]], partial=(kc > 0))
                    if tg == 0:
                        P.op("dve", lambda e, si=si: e.memset(fst[si][:, 0:2], 0.0), writes=[t_fst[si]], partial=True)
                    else:
                        P.op("dve", lambda e, si=si, fcl=fcl: e.tensor_copy(out=fst[si][:, 0:2], in_=hal[:, fcl, :]), reads=[t_hal], writes=[t_fst[si]], partial=True)
                    P.op("act", lambda e, bu=bu, si=si: e.copy(out=fst[si][:, 2:514], in_=banks[bu]), reads=[tb[bu]], writes=[t_fst[si]], partial=True)
                    if tg < 3:
                        P.op("dve", lambda e, si=si, fcl=fcl: e.tensor_copy(out=hal[:, fcl, :], in_=fst[si][:, 512:514]), reads=[t_fst[si]], writes=[t_hal], partial=True)
                    else:
                        P.dma("sp", lambda e, l=l, si=si, fc=fc: e.dma_start(out=fconv_out[l, :, fc, :], in_=fst[si][:, 512:514]), t_fst[si], reads=[t_fst[si]])
                    P.op("dve", lambda e, si=si, fc=fc: e.tensor_scalar(out=fac[si], in0=fst[si][:, 2:514], scalar1=fcw[:, fc, 2:3], scalar2=fcb[:, fc:fc + 1], op0=ALU.mult, op1=ALU.add),
                         reads=[t_fst[si], t_fc], writes=[t_fac[si]])
                    for k in range(2):
                        P.op("dve", lambda e, si=si, fc=fc, k=k: e.scalar_tensor_tensor(out=fac[si], in0=fst[si][:, k:k + 512], scalar=fcw[:, fc, k:k + 1], in1=fac[si], op0=ALU.mult, op1=ALU.add),
                             reads=[t_fst[si], t_fc], writes=[t_fac[si]])
                    P.op("act", lambda e, si=si: e.activation(out=fac[si], in_=fac[si], func=AF.Silu), reads=[t_fac[si]], writes=[t_fac[si]])
                    P.op("dve", lambda e, si=si, bv=bv, asl=asl, fcl=fcl: e.tensor_tensor(out=actT[asl][:, fcl, :], in0=fac[si], in1=banks[bv], op=ALU.mult),
                         reads=[t_fac[si], tb[bv]], writes=[t_act[asl]], partial=(fcl > 0))
                for dc in range(KD):
                    bd = dc % 2
                    for fcl in range(nf):
                        P.op("pe", lambda e, bd=bd, ws=ws, fcl=fcl, dc=dc, asl=asl, nf=nf: e.matmul(
                            banks[bd], lhsT=wdn[ws][:, fcl, dc * 128:(dc + 1) * 128], rhs=actT[asl][:, fcl, :], start=(fcl == 0), stop=(fcl == nf - 1)),
                            reads=[t_wdn[ws], t_act[asl]], writes=[tb[bd]], partial=(fcl > 0))
                    P.op("dve", lambda e, bd=bd, dc=dc, ts=ts, l=l: e.scalar_tensor_tensor(
                        out=xT[:, dc, ts], in0=banks[bd], scalar=mods[:, l * 48 + 40 + dc, 0:1], in1=xT[:, dc, ts], op0=ALU.mult, op1=ALU.add),
                        reads=[tb[bd], t_mods, t_x], writes=[t_x], partial=True)
            for fcl in range(nf):
                fc = fg * 4 + fcl
                for half in range(2):
                    for kc in range(KD):
                        P.op("pe", lambda e, kc=kc, ws=ws, fcl=fcl, half=half: e.matmul(banks[6][:, half * 4:(half + 1) * 4], lhsT=wup[ws][:, kc, half, fcl * 128:(fcl + 1) * 128], rhs=hsT[:, kc, :],
                                                                                    start=(kc == 0), stop=(kc == KD - 1)), reads=[t_wup[ws], t_hs], writes=[tb[6]], partial=(half + kc > 0))
                P.op("act", lambda e, fc=fc: e.copy(out=fsp[:, fc, 2, :], in_=banks[6][:, 0:4]), reads=[tb[6]], writes=[t_fsp], partial=True)
                P.op("dve", lambda e, fc=fc: e.tensor_scalar(out=sfa, in0=fsp[:, fc, 2, :], scalar1=fcw[:, fc, 2:3], scalar2=fcb[:, fc:fc + 1], op0=ALU.mult, op1=ALU.add),
                     reads=[t_fsp, t_fc], writes=[t_sfa])
                for k in range(2):
                    P.op("dve", lambda e, fc=fc, k=k: e.scalar_tensor_tensor(out=sfa, in0=fsp[:, fc, k, :], scalar=fcw[:, fc, k:k + 1], in1=sfa, op0=ALU.mult, op1=ALU.add),
                         reads=[t_fsp, t_fc, t_sfa], writes=[t_sfa])
                P.op("act", lambda e: e.activation(out=sfa, in_=sfa, func=AF.Silu), reads=[t_sfa], writes=[t_sfa])
                P.op("dve", lambda e, fcl=fcl: e.tensor_tensor(out=sact[:, fcl, :], in0=sfa, in1=banks[6][:, 4:8], op=ALU.mult), reads=[t_sfa, tb[6]], writes=[t_sact], partial=(fcl > 0))
            for dc in range(KD):
                for fcl in range(nf):
                    P.op("pe", lambda e, ws=ws, fcl=fcl, dc=dc, nf=nf: e.matmul(banks[7][:, 0:4], lhsT=wdn[ws][:, fcl, dc * 128:(dc + 1) * 128], rhs=sact[:, fcl, :], start=(fcl == 0), stop=(fcl == nf - 1)),
                         reads=[t_wdn[ws], t_sact], writes=[tb[7]], partial=(fcl > 0))
                P.op("dve", lambda e, dc=dc, l=l: e.tensor_tensor(out=sfa, in0=banks[7][:, 0:4], in1=mods[:, l * 48 + 40 + dc, 1:5], op=ALU.mult), reads=[tb[7], t_mods], writes=[t_sfa])
                P.op("dve", lambda e, dc=dc: e.tensor_tensor(out=xsT[:, dc, :], in0=xsT[:, dc, :], in1=sfa, op=ALU.add), reads=[t_sfa, t_xs], writes=[t_xs], partial=True)
        P.barrier()
        P.op("dve", lambda e: e.tensor_copy(out=fso[:, 0:NFC, 0, :], in_=fsp[:, 0:NFC, 1, :]), reads=[t_fsp], writes=[t_fso], partial=True)
        P.op("dve", lambda e: e.tensor_copy(out=fso[:, 0:NFC, 1, :], in_=fsp[:, 0:NFC, 2, :]), reads=[t_fsp], writes=[t_fso], partial=True)
        P.dma("sp", lambda e, l=l: e.dma_start(out=fcs_out[l], in_=fso), t_fso, reads=[t_fso])
        if l < DEPTH - 1:
            for kc in range(KD):
                P.dma("sp", lambda e, kc=kc: e.dma_start(out=xscr[:, kc, :], in_=xT[:, kc, :]), t_x, reads=[t_x], writes=[t_xscr], partial=True)
        P.barrier()
    if stage >= 99:
        AR.reset(layer_mark)
        AR.alloc([KD, T], F32)
        sq = AR.alloc([KD, 512], BF16)
        rs = AR.alloc([512], F32)
        yst = [AR.alloc([512], F32) for _ in range(2)]
        t_sq, t_rs, t_yst = Tk("fsq"), Tk("frs"), [Tk("yst0"), Tk("yst1")]
        for tg in range(4):
            ts = slice(tg * 512, (tg + 1) * 512)
            bk = tg % 2
            P.op("act", lambda e, ts=ts: e.activation(out=sq, in_=xT[:, :, ts], func=AF.Square), reads=[t_x], writes=[t_sq])
            for kc in range(KD):
                P.op("pe", lambda e, kc=kc, bk=bk: e.matmul(banks[bk], lhsT=onesb, rhs=sq[:, kc, :], start=(kc == 0), stop=(kc == KD - 1)),
                     reads=[t_sq, t_const], writes=[tb[bk]], partial=(kc > 0))
            P.op("dve", lambda e, bk=bk: e.tensor_scalar(out=rs, in0=banks[bk], scalar1=1.0 / D, scalar2=EPS, op0=ALU.mult, op1=ALU.add),
                 reads=[tb[bk]], writes=[t_rs])
            P.op("act", lambda e: e.activation(out=rs, in_=rs, func=AF.Sqrt), reads=[t_rs], writes=[t_rs])
            P.op("dve", lambda e: e.reciprocal(out=rs, in_=rs), reads=[t_rs], writes=[t_rs])
            for kc in range(KD):
                yi = kc % 2
                P.op("dve", lambda e, kc=kc, ts=ts, yi=yi: e.scalar_tensor_tensor(
                    out=yst[yi], in0=xT[:, kc, ts], scalar=fing[:, kc:kc + 1], in1=rs, op0=ALU.mult, op1=ALU.mult),
                    reads=[t_x, t_rs, t_small], writes=[t_yst[yi]])
                P.dma("sp", lambda e, kc=kc, ts=ts, yi=yi: e.dma_start(out=yT_out[:, kc, ts], in_=yst[yi]), t_yst[yi], reads=[t_yst[yi]])
        st1 = AR.alloc([KD, 4], F32)
        srs = AR.alloc([4], F32)
        t_st1, t_srs = Tk("fst1"), Tk("fsrs")
        P.op("dve", lambda e: e.tensor_tensor(out=st1, in0=xsT, in1=xsT, op=ALU.mult), reads=[t_xs], writes=[t_st1])
        for kc in range(KD):
            P.op("pe", lambda e, kc=kc: e.matmul(banks[2][:, 0:4], lhsT=onesf, rhs=st1[:, kc, :], start=(kc == 0), stop=(kc == KD - 1)), reads=[t_st1, t_const], writes=[tb[2]], partial=(kc > 0))
        P.op("dve", lambda e: e.tensor_scalar(out=srs, in0=banks[2][:, 0:4], scalar1=1.0 / D, scalar2=EPS, op0=ALU.mult, op1=ALU.add), reads=[tb[2]], writes=[t_srs])
        P.op("act", lambda e: e.activation(out=srs, in_=srs, func=AF.Sqrt), reads=[t_srs], writes=[t_srs])
        P.op("dve", lambda e: e.reciprocal(out=srs, in_=srs), reads=[t_srs], writes=[t_srs])
        for kc in range(KD):
            P.op("dve", lambda e, kc=kc: e.scalar_tensor_tensor(out=st1[:, kc, :], in0=xsT[:, kc, :], scalar=fing[:, kc:kc + 1], in1=srs, op0=ALU.mult, op1=ALU.mult),
                 reads=[t_xs, t_srs, t_small], writes=[t_st1], partial=True)
        P.dma("sp", lambda e: e.dma_start(out=ysT_out, in_=st1), t_st1, reads=[t_st1])
    P.barrier()
    P.emit()
    return nc


_PROG_CACHE = {}


def _host_consts():
    r = np.arange(128)
    f32 = np.float32
    tok = np.arange(T)
    ncmp = np.arange(128)
    ends = 16 * ncmp + 31
    cmpmask = np.where((ends[:, None] > tok[None, :]) | (ncmp[:, None] >= 127), NEGB, 0.0).astype(f32)
    eall = np.zeros((128, T), f32)
    for base in (0, 64):
        eall[base + (tok // 64), tok] = 1.0
    slopes = 2.0 ** (-(np.arange(8) + 1.0))
    kaug = np.zeros((128, 32, 128), f32)
    qaug = np.zeros((128, 8, 128), f32)
    for base in (0, 64):
        for dlt in range(16):
            kaug[base + 0, dlt, :] = -dlt
            kaug[base + 1, dlt, :] = r
            kaug[base + 2, dlt, :] = 1.0
        for qt in range(16):
            kaug[base + 0, 16 + qt, :] = ends // 128 - qt
            kaug[base + 1, 16 + qt, :] = ends % 128
            kaug[base + 2, 16 + qt, :] = 1.0
        for h in range(8):
            qaug[base + 0, h, :] = 128.0 * slopes[h]
            qaug[base + 1, h, :] = slopes[h]
            qaug[base + 2, h, :] = -slopes[h] * r
    ov = np.zeros((128, 33), f32)
    ov[:, 0] = 1.0
    cs_ = 16 * ncmp
    ss_ = 64 * np.arange(32)
    ov[:, 1:] = ((cs_[:, None] < ss_[None, :] + 64) & (cs_[:, None] + 32 > ss_[None, :])).astype(f32)
    ov[127, 1:] = 0.0
    qpos = (np.arange(NT)[None, :] * 128 + r[:, None])
    cur = qpos // 64
    jj = np.arange(32)[None, None, :]
    forced = (jj == 0) | (jj == cur[:, :, None]) | (jj == cur[:, :, None] - 1)
    valid = (jj * 64) <= qpos[:, :, None]
    bonus = np.where(forced, 1e6 * 2.0 ** 64, 0.0).astype(f32)
    cap = np.where(valid, 3e38, -1e30).astype(f32)
    idxb = np.broadcast_to((31 - np.arange(32)).astype(np.int32)[None, :], (128, 32)).copy()
    augL = np.zeros((128, 2, 128), f32)
    augR = np.zeros((128, 2, 296), f32)
    A_t = np.array([kt - 64 for kt in range(64)] + [0] + [kt - 4 for kt in range(4)] + [0], f32)
    for base in (0, 64):
        augL[base + 0, 0, :] = r
        augL[base + 1, 0, :] = 1.0
        augL[base + 2, 0, :] = 1.0
        augL[base + 0, 1, :] = 16.0 * r
        augL[base + 1, 1, :] = 1.0
        augL[base + 2, 1, :] = 1.0
        for g in range(2):
            for j in range(4):
                sj = slopes[4 * g + j]
                augR[base + 0, g, j:280:4] = sj
                augR[base + 1, g, j:280:4] = 128.0 * sj * A_t
                augR[base + 0, g, 280 + j:296:4] = sj
                augR[base + 1, g, 280 + j:296:4] = 31.0 * sj
                augR[base + 2, g, 280 + j:296:4] = 2048.0 * sj * (np.arange(4) - 4)
    mcs = np.zeros((128, 296), f32)
    mcs[1:, 64 * 4:65 * 4] = NEGB
    mcs[1:, 69 * 4:70 * 4] = NEGB
    mcs[127, 280 + 12:296] = NEGB
    ns = np.arange(512)
    jb = np.arange(129)
    ovl = ((16 * ns[:, None] < 64 * jb[None, :] + 64) & (16 * ns[:, None] + 32 > 64 * jb[None, :])).astype(f32)
    ovl[511, :] = 0.0
    ovs = np.zeros((128, 4, 130), f32)
    ovs[:, :, 0] = 1.0
    ovs[:, :, 1:] = ovl.reshape(4, 128, 129).transpose(1, 0, 2)
    bonus_s = np.zeros((128, 129), f32)
    bonus_s[:, [0, 127, 128]] = 1e6 * 2.0 ** 64
    idx_s = np.broadcast_to((255 - jb).astype(np.int32)[None, :], (128, 129)).copy()
    oneo = np.zeros((128, 2, 128), f32)
    oneo[:, 0, :64] = 1.0
    oneo[:, 1, 64:] = 1.0
    pidx = np.broadcast_to(np.arange(128, dtype=np.int32)[:, None], (128, 256)).copy()
    return {"augL": augL, "augR": augR, "mcs": mcs, "ovs": ovs, "bonus_s": bonus_s, "idx_s": idx_s, "oneo": oneo, "pidx": pidx,
            "ident": np.eye(128, dtype=np.float32),
            "mtri": np.where(r[:, None] > r[None, :], NEGB, 0.0).astype(f32),
            "manti": np.where(r[None, :] > r[:, None], NEGB, 0.0).astype(f32),
            "cmpmask": cmpmask, "eall": eall, "kaug": kaug, "qaug": qaug, "ov": ov,
            "bonus": bonus, "cap": cap, "idxb": idxb,
            "utri": (r[:, None] <= r[None, :]).astype(np.float32),
            "mtrif": np.where(r[:, None] > r[None, :], NEGB, 0.0).astype(np.float32)}


def kernel(**inp):
    f32 = np.float32
    nc = _PROG_CACHE.get("nc")
    if nc is None:
        nc = build_program()
        _PROG_CACHE["nc"] = nc
    shared = dict(_host_consts())
    ada_w = np.asarray(inp["ada_w"], f32)
    shared["adaw"] = np.ascontiguousarray(
        ada_w.reshape(DEPTH, KD, 128, 6, 1024).transpose(0, 3, 2, 1, 4))
    shared["adab"] = np.ascontiguousarray(np.asarray(inp["ada_b"], f32).reshape(DEPTH, 48, 128).transpose(0, 2, 1))
    shared["n1g"] = np.stack([_col(np.asarray(inp["norm1_g"], f32)[l], KD) for l in range(DEPTH)])
    shared["n2g"] = np.stack([_col(np.asarray(inp["norm2_g"], f32)[l], KD) for l in range(DEPTH)])
    shared["fing"] = _col(np.asarray(inp["final_g"], f32), KD)
    w_in = np.asarray(inp["w_in"], f32)
    tm_cols = np.concatenate([np.arange(512, 1816), np.arange(2584, 2592)])
    shared["wtm"] = np.stack([_lay_kc(w_in[l][:, tm_cols]) for l in range(DEPTH)])
    fm_chunks = [np.arange(128 * i, 128 * i + 128) for i in range(4)]
    fm_chunks += [np.arange(512, 640), np.arange(640, 768)]
    fm_chunks += [np.concatenate([np.arange(768 + 64 * g, 832 + 64 * g)] * 2) for g in range(2)]
    fm_chunks += [np.concatenate([np.arange(1024 + 64 * g, 1088 + 64 * g)] * 2) for g in range(2)]
    fm_chunks += [np.arange(1816 + 128 * c, 1816 + 128 * c + 128) for c in range(6)]
    fm_chunks += [np.concatenate([np.arange(64 * j, 64 * j + 64), np.arange(256 + 64 * j, 320 + 64 * j)]) for j in range(4)]
    shared["wfm"] = np.stack([np.stack([_lay_kc(w_in[l][:, ch]) for ch in fm_chunks]) for l in range(DEPTH)])
    scw = np.asarray(inp["ssm_conv_w"], f32)
    shared["scw"] = np.ascontiguousarray(scw.reshape(DEPTH, 4, 6, 128).transpose(0, 3, 2, 1))
    rep = lambda v: np.ascontiguousarray(np.broadcast_to(np.asarray(v, f32)[:, None, :], (DEPTH, 128, np.asarray(v).shape[-1])))
    shared["dtb"] = rep(inp["dt_bias"])
    shared["alog"] = rep(inp["a_log"])
    shared["dsk"] = rep(inp["d_skip"])
    shared["sng"] = rep(inp["ssm_norm_g"])
    shared["scb"] = np.ascontiguousarray(np.asarray(inp["ssm_conv_b"], f32).reshape(DEPTH, 6, 128).transpose(0, 2, 1))

    shared["wo"] = np.stack([_lay_kc(np.asarray(inp["w_out"], f32)[l]) for l in range(DEPTH)])
    wu = np.asarray(inp["ffn_w_up"], f32)
    wu = np.pad(wu.reshape(DEPTH, KD, 128, 2, D_FF), ((0, 0), (0, 0), (0, 0), (0, 0), (0, 3072 - D_FF)))
    shared["wup"] = np.ascontiguousarray(wu.reshape(DEPTH, KD, 128, 2, 6, 512).transpose(0, 4, 2, 1, 3, 5))
    wd = np.pad(np.asarray(inp["ffn_w_down"], f32), ((0, 0), (0, 3072 - D_FF), (0, 0)))
    shared["wdn"] = np.ascontiguousarray(wd.reshape(DEPTH, 24, 128, D).transpose(0, 2, 1, 3))
    fw = np.pad(np.asarray(inp["ffn_conv_w"], f32), ((0, 0), (0, 0), (0, 3072 - D_FF)))
    shared["fcw"] = np.ascontiguousarray(fw.reshape(DEPTH, 3, 24, 128).transpose(0, 3, 2, 1))
    fb = np.pad(np.asarray(inp["ffn_conv_b"], f32), ((0, 0), (0, 3072 - D_FF)))
    shared["fcb"] = np.ascontiguousarray(fb.reshape(DEPTH, 24, 128).transpose(0, 2, 1))
    w1 = np.stack([np.asarray(inp["cmpk_w1"], f32), np.asarray(inp["cmpv_w1"], f32)], axis=1)
    w1 = w1.transpose(0, 1, 3, 2, 4)
    shared["w1d"] = np.ascontiguousarray(np.concatenate([w1, w1], axis=2))
    w2k = np.asarray(inp["cmpk_w2"], f32)
    shared["w2k"] = np.ascontiguousarray(np.concatenate([w2k, w2k], axis=2))
    shared["w2v"] = np.ascontiguousarray(np.asarray(inp["cmpv_w2"], f32))
    shared["cb1"] = np.ascontiguousarray(np.stack([np.asarray(inp["cmpk_b1"], f32), np.asarray(inp["cmpv_b1"], f32)], axis=2))
    hc = np.stack([np.asarray(inp["dt_bias"], f32), np.asarray(inp["a_log"], f32), np.asarray(inp["d_skip"], f32)], axis=2)
    shared["hcol"] = np.ascontiguousarray(hc)
    dsk = np.asarray(inp["d_skip"], f32)
    shared["dskc"] = np.ascontiguousarray(np.repeat(dsk.reshape(DEPTH, 4, 2, 1), 64, axis=3).transpose(0, 2, 3, 1).reshape(DEPTH, 128, 4))
    shared["sngc"] = np.ascontiguousarray(np.asarray(inp["ssm_norm_g"], f32).reshape(DEPTH, 4, 128).transpose(0, 2, 1))
    ex = np.zeros((8, 4, 128), f32)
    for c in range(4):
        for hh in range(2):
            ex[2 * c + hh, c, hh * 64:(hh + 1) * 64] = 1.0
    shared["ex"] = ex
    def poolT(a):
        a = np.asarray(a, f32).reshape(DEPTH, -1, 128, 128)
        return np.ascontiguousarray(a.transpose(0, 1, 3, 2)).reshape(DEPTH, -1, 128)
    for nm, key in (("cmpkT_pool", "cache_cmp_k"), ("cmpvT_pool", "cache_cmp_v"), ("slckT_pool", "cache_slc_k")):
        pt_ = poolT(inp[key])
        for l in range(DEPTH):
            shared["%s%d" % (nm, l)] = pt_[l]
    sv_ = np.asarray(inp["cache_slc_v"], f32).reshape(DEPTH, -1, 128)
    for l in range(DEPTH):
        shared["slcv_pool%d" % l] = np.ascontiguousarray(sv_[l])
    page_table = np.asarray(inp["page_table"], np.int32)
    st_ffn = np.asarray(inp["state_ffn_conv"], f32)
    x_sample = np.asarray(inp["x_sample"], f32)
    st_conv = np.asarray(inp["state_ssm_conv"], f32)
    st_ssm = np.asarray(inp["state_ssm"], f32)
    st_wk = np.asarray(inp["state_win_k"], f32).reshape(DEPTH, 32, 512, 128)
    st_wv = np.asarray(inp["state_win_v"], f32).reshape(DEPTH, 32, 512, 128)
    x_prompt = np.asarray(inp["x_prompt"], f32)
    c_prompt = np.asarray(inp["c_prompt"], f32)
    c_sample = np.asarray(inp["c_sample"], f32)
    in_maps = []
    for i in range(NCORES):
        m = dict(shared)
        m["xT_in"] = np.ascontiguousarray(x_prompt[i].T.reshape(KD, 128, T).transpose(1, 0, 2))
        c5 = np.concatenate([c_prompt[i:i + 1], c_sample[4 * i:4 * i + 4]], axis=0)
        m["cT_in"] = np.ascontiguousarray(c5.T.reshape(KD, 128, 5).transpose(1, 0, 2))
        sb = slice(4 * i, 4 * i + 4)
        m["xsT_in"] = np.ascontiguousarray(x_sample[sb, 0, :].T.reshape(KD, 128, 4).transpose(1, 0, 2))
        m["cst_in"] = np.ascontiguousarray(st_conv[:, sb].reshape(DEPTH, 4, 3, 6, 128).transpose(0, 4, 3, 2, 1))
        m["hs0_in"] = np.ascontiguousarray(st_ssm[:, sb].reshape(DEPTH, 4, 4, 2, 64, 64).transpose(0, 3, 4, 2, 1, 5).reshape(DEPTH, 128, 4, 4, 64))
        m["ptb"] = np.ascontiguousarray(np.broadcast_to(page_table[sb].reshape(1, 256), (128, 256)))
        m["wkT_in"] = np.ascontiguousarray(st_wk[:, sb].transpose(0, 1, 3, 2))
        sf = np.pad(st_ffn[:, sb], ((0, 0), (0, 0), (0, 0), (0, 3072 - D_FF)))
        m["fst_in"] = np.ascontiguousarray(sf.reshape(DEPTH, 4, 2, 24, 128).transpose(0, 4, 3, 2, 1))
        m["wk_in"] = np.ascontiguousarray(st_wk[:, sb])
        m["wv_in"] = np.ascontiguousarray(st_wv[:, sb])
        in_maps.append(m)
    res = run_bass_kernel_spmd(nc, in_maps, core_ids=list(range(NCORES)))
    R = res.results
    B, S = 8, 32
    kv = np.stack([R[i]["kv_out"] for i in range(NCORES)], axis=1)
    kvp = [kv[..., 128 * j:128 * j + 128].reshape(DEPTH, B, T, 2, 64) for j in range(6)]
    yT = np.stack([R[i]["yT_out"] for i in range(NCORES)])
    y_prompt = np.ascontiguousarray(yT.transpose(0, 3, 2, 1).reshape(B, T, D))
    z = lambda *s: np.zeros(s, f32)
    kvn = np.stack([R[i]["kvn_out"] for i in range(NCORES)], axis=1)
    kvn = kvn.transpose(0, 1, 4, 3, 2).reshape(DEPTH, S, 6, 1, 2, 64)
    kvs = [np.ascontiguousarray(kvn[:, :, j]) for j in range(6)]
    wks = np.concatenate([R[i]["wk_out"] for i in range(NCORES)], axis=1).reshape(DEPTH, S, 512, 2, 64)
    wvs = np.concatenate([R[i]["wv_out"] for i in range(NCORES)], axis=1).reshape(DEPTH, S, 512, 2, 64)
    scs = np.stack([R[i]["sconvs_out"] for i in range(NCORES)], axis=1)
    sconv_s = np.ascontiguousarray(scs.transpose(0, 1, 5, 4, 3, 2).reshape(DEPTH, S, 3, 768))
    sss = np.stack([R[i]["ssms_out"] for i in range(NCORES)], axis=1)
    ssm_s = np.ascontiguousarray(sss.reshape(DEPTH, 8, 2, 64, 4, 4, 64).transpose(0, 1, 5, 4, 2, 3, 6).reshape(DEPTH, S, 8, 64, 64))
    fcs = np.stack([R[i]["fcs_out"] for i in range(NCORES)], axis=1)
    fconv_s = np.ascontiguousarray(fcs.transpose(0, 1, 5, 4, 3, 2).reshape(DEPTH, S, 2, 3072)[..., :D_FF])
    ysT = np.stack([R[i]["ysT_out"] for i in range(NCORES)])
    y_sample = np.ascontiguousarray(ysT.transpose(0, 3, 2, 1).reshape(S, 1, D))
    fconv = np.stack([R[i]["fconv_out"] for i in range(NCORES)], axis=1)
    fconv_p = np.ascontiguousarray(fconv.transpose(0, 1, 4, 3, 2).reshape(DEPTH, B, 2, D_FF))
    sconv = np.stack([R[i]["sconv_out"] for i in range(NCORES)], axis=1)
    sconv_p = np.ascontiguousarray(sconv.transpose(0, 1, 4, 3, 2).reshape(DEPTH, B, 3, 768))
    ssm = np.stack([R[i]["ssm_out"] for i in range(NCORES)], axis=1)
    ssm_p = np.ascontiguousarray(ssm.reshape(DEPTH, B, 2, 64, 4, 64).transpose(0, 1, 2, 4, 5, 3).reshape(DEPTH, B, 8, 64, 64))
    outs = (y_prompt, y_sample,
            kvp[0], kvs[0], kvp[1], kvs[1],
            kvp[2], kvs[2], kvp[3], kvs[3],
            np.ascontiguousarray(kvp[4][:, :, T - 512:]), wks,
            np.ascontiguousarray(kvp[5][:, :, T - 512:]), wvs,
            ssm_p, ssm_s,
            sconv_p, sconv_s,
            fconv_p, fconv_s)
    return outs
```

```python
import numpy as np
import concourse.bass as bass
import concourse.mybir as mybir
from concourse.bass_utils import run_bass_kernel_spmd

F32 = mybir.dt.float32
BF16 = mybir.dt.bfloat16
I32 = mybir.dt.int32
ALU = mybir.AluOpType
AF = mybir.ActivationFunctionType
AX = mybir.AxisListType

NCORES = 8
D = 1024
KD = 8
T = 2048
NT = 16
DEPTH = 2
IN_DIM = 2592
D_FF = 2816
NFC = 22
EPS = 1e-6
NEGB = -30000.0
STAGE = 99

SAME_ENGINE_SYNC = True


class Tk:
    __slots__ = ("name", "w", "r", "sem", "cnt", "excl")

    def __init__(self, name, excl=False):
        self.name = name
        self.excl = excl
        self.w = {}
        self.r = {}
        self.sem = None
        self.cnt = 0


class Prog:
    ENG = ("pe", "act", "dve", "pool", "sp")

    def __init__(self, nc):
        self.nc = nc
        self.ops = {e: [] for e in self.ENG}
        self.cnt = {e: 0 for e in self.ENG}
        self.known = {e: {} for e in self.ENG}
        self.esem = {e: nc.alloc_semaphore("sem_" + e) for e in self.ENG}
        self.needed = {e: set() for e in self.ENG}
        self.dsems = []

    def _dsem(self, t):
        if t.sem is None:
            t.sem = self.nc.alloc_semaphore("dsem_%d" % len(self.dsems))
            self.dsems.append(t)
        return t.sem

    def _waits(self, eng, reads, writes, extra=()):
        need = {}
        for t in reads:
            for k, v in t.w.items():
                if need.get(k, 0) < v:
                    need[k] = v
        for t in writes:
            for k, v in t.w.items():
                if need.get(k, 0) < v:
                    need[k] = v
            for k, v in t.r.items():
                if need.get(k, 0) < v:
                    need[k] = v
        for k, v in extra:
            if need.get(k, 0) < v:
                need[k] = v
        out = []
        kn = self.known[eng]
        for k, v in need.items():
            if isinstance(k, str) and k == eng and (eng == "pe" or not SAME_ENGINE_SYNC):
                continue
            if kn.get(k, 0) >= v:
                continue
            kn[k] = v
            if isinstance(k, str):
                self.needed[k].add(v)
            out.append((k, v))
        return out

    def _mark(self, ev, reads, writes, partial):
        k, v = ev
        for t in reads:
            if t.r.get(k, 0) < v:
                t.r[k] = v
        for t in writes:
            if partial:
                if t.w.get(k, 0) < v:
                    t.w[k] = v
            else:
                t.w = {k: v}
                t.r = {}

    def op(self, eng, fn, reads=(), writes=(), partial=False):
        ex = [t for t in reads if t.excl]
        waits = self._waits(eng, reads, list(writes) + ex)
        self.cnt[eng] += 1
        ev = (eng, self.cnt[eng])
        self.ops[eng].append((waits, fn, ("E", eng, self.cnt[eng])))
        self._mark(ev, [t for t in reads if not t.excl], writes, partial)
        if ex:
            self._mark(ev, (), ex, True)

    def dma(self, eng, fn, semt, reads=(), writes=(), n=1, partial=False):
        sem = self._dsem(semt)
        extra = [(sem, semt.cnt)] if semt.cnt else []
        waits = self._waits(eng, reads, writes, extra)
        semt.cnt += 16 * n
        self.ops[eng].append((waits, fn, ("D", sem, 16)))
        self._mark((sem, semt.cnt), reads, writes, partial)

    def barrier(self):
        for e in self.ENG:
            waits = []
            kn = self.known[e]
            for k in self.ENG:
                if k != e and self.cnt[k] > kn.get(k, 0):
                    kn[k] = self.cnt[k]
                    self.needed[k].add(self.cnt[k])
                    waits.append((k, self.cnt[k]))
            for t in self.dsems:
                if t.cnt > kn.get(t.sem, 0):
                    kn[t.sem] = t.cnt
                    waits.append((t.sem, t.cnt))
            if waits:
                self.ops[e].append((waits, None, None))

    def emit(self):
        nc = self.nc
        engobj = {"pe": "tensor", "act": "scalar", "dve": "vector", "pool": "gpsimd", "sp": "sync"}
        rank = {e: {raw: i + 1 for i, raw in enumerate(sorted(self.needed[e]))} for e in self.ENG}
        esem, needed = self.esem, self.needed
        with nc.Block() as block:
            for e in self.ENG:
                ops = self.ops[e]
                if not ops:
                    continue

                def body(eng, ops=ops):
                    for waits, fn, inc in ops:
                        for k, v in waits:
                            if isinstance(k, str):
                                eng.wait_ge(esem[k], rank[k][v])
                            else:
                                eng.wait_ge(k, v)
                        if fn is None:
                            continue
                        ins = fn(eng)
                        if inc[0] == "E":
                            if inc[2] in needed[inc[1]]:
                                ins.then_inc(esem[inc[1]], 1)
                        elif isinstance(ins, (list, tuple)):
                            for i in ins:
                                i.then_inc(inc[1], inc[2])
                        else:
                            ins.then_inc(inc[1], inc[2])
                getattr(block, engobj[e])(body)


class Arena:
    def __init__(self, nc, nbytes):
        self.base = nc.alloc_sbuf_tensor("arena", [128, nbytes // 4], F32).ap()
        self.nbytes = nbytes
        self.off = 0
        self.marks = []

    def alloc(self, shape, dtype, parts=128):
        esz = 4 if dtype in (F32, I32) else 2
        n = 1
        for s in shape:
            n *= s
        nb = (n * esz + 63) // 64 * 64
        assert self.off + nb <= self.nbytes, ("arena overflow", self.off, nb, self.nbytes)
        v = self.base[0:parts, self.off // 4:(self.off + nb) // 4]
        if dtype != F32:
            v = v.bitcast(dtype)
        v = v[:, 0:n]
        self.off += nb
        if len(shape) == 2:
            v = v.rearrange("p (a b) -> p a b", b=shape[1])
        elif len(shape) == 3:
            v = v.rearrange("p (a b c) -> p a b c", b=shape[1], c=shape[2])
        return v

    def mark(self):
        return self.off

    def reset(self, m):
        self.off = m


def _lay_kc(w, kc=KD):
    n = w.shape[1]
    return np.ascontiguousarray(w.reshape(kc, 128, n).transpose(1, 0, 2))


def _col(v, nch):
    return np.ascontiguousarray(v.reshape(nch, 128).T)


def build_program(stage=STAGE):
    nc = bass.Bass("TRN2", target_bir_lowering=False)
    P = Prog(nc)
    dr = {}

    def din(name, shape, dt=F32):
        dr[name] = nc.dram_tensor(name, list(shape), dt, kind="ExternalInput").ap()
        return dr[name]

    def dout(name, shape, dt=F32):
        dr[name] = nc.dram_tensor(name, list(shape), dt, kind="ExternalOutput").ap()
        return dr[name]

    xT_in = din("xT_in", [128, KD, T])
    cT_in = din("cT_in", [128, KD, 5])
    adaw_in = din("adaw", [DEPTH, 6, 128, KD, 1024])
    adab_in = din("adab", [DEPTH, 128, 48])
    n1g_in = din("n1g", [DEPTH, 128, KD])
    n2g_in = din("n2g", [DEPTH, 128, KD])
    fing_in = din("fing", [128, KD])
    wtm_in = din("wtm", [DEPTH, 128, KD, 1312])
    wfm_in = din("wfm", [DEPTH, 20, 128, KD, 128])
    scw_in = din("scw", [DEPTH, 128, 6, 4])
    scb_in = din("scb", [DEPTH, 128, 6])
    ident_in = din("ident", [128, 128])
    dtb_in = din("dtb", [DEPTH, 128, 8])
    alog_in = din("alog", [DEPTH, 128, 8])
    dsk_in = din("dsk", [DEPTH, 128, 8])
    sng_in = din("sng", [DEPTH, 128, 512])
    utri_in = din("utri", [128, 128])
    mtrif_in = din("mtrif", [128, 128])
    ssm_out = dout("ssm_out", [DEPTH, 128, 4, 64])
    NPOOL = 2560
    ptb_in = din("ptb", [128, 256], I32)
    pidx_in = din("pidx", [128, 256], I32)
    cmpkT_pool = [din("cmpkT_pool%d" % l, [NPOOL * 128, 128]) for l in range(DEPTH)]
    cmpvT_pool = [din("cmpvT_pool%d" % l, [NPOOL * 128, 128]) for l in range(DEPTH)]
    slckT_pool = [din("slckT_pool%d" % l, [NPOOL * 128, 128]) for l in range(DEPTH)]
    slcv_pool = [din("slcv_pool%d" % l, [NPOOL * 128, 128]) for l in range(DEPTH)]
    wkT_in = din("wkT_in", [DEPTH, 4, 128, 512])
    augL_in = din("augL", [128, 2, 128])
    augR_in = din("augR", [128, 2, 296])
    mcs_in = din("mcs", [128, 296])
    ovs_in = din("ovs", [128, 4, 130])
    bonus_s_in = din("bonus_s", [128, 129])
    idx_s_in = din("idx_s", [128, 129], I32)
    oneo_in = din("oneo", [128, 2, 128])
    fst_in = din("fst_in", [DEPTH, 128, 24, 2, 4])
    fcs_out = dout("fcs_out", [DEPTH, 128, 24, 2, 4])
    ysT_out = dout("ysT_out", [128, KD, 4])
    xsT_in = din("xsT_in", [128, KD, 4])
    cst_in = din("cst_in", [DEPTH, 128, 6, 3, 4])
    hs0_in = din("hs0_in", [DEPTH, 128, 4, 4, 64])
    hcol_in = din("hcol", [DEPTH, 8, 3])
    dskc_in = din("dskc", [DEPTH, 128, 4])
    sngc_in = din("sngc", [DEPTH, 128, 4])
    ex_in = din("ex", [8, 4, 128])
    wk_in = din("wk_in", [DEPTH, 4, 512, 128])
    wv_in = din("wv_in", [DEPTH, 4, 512, 128])
    wk_out = dout("wk_out", [DEPTH, 4, 512, 128])
    wv_out = dout("wv_out", [DEPTH, 4, 512, 128])
    kvn_out = dout("kvn_out", [DEPTH, 128, 6, 4])
    sconvs_out = dout("sconvs_out", [DEPTH, 128, 6, 3, 4])
    ssms_out = dout("ssms_out", [DEPTH, 128, 4, 4, 64])
    wo_in = din("wo", [DEPTH, 128, KD, D])
    wup_in = din("wup", [DEPTH, 6, 128, KD, 2, 512])
    wdn_in = din("wdn", [DEPTH, 128, 24, D])
    fcw_in = din("fcw", [DEPTH, 128, 24, 3])
    fcb_in = din("fcb", [DEPTH, 128, 24])
    fconv_out = dout("fconv_out", [DEPTH, 128, NFC, 2])
    w1d_in = din("w1d", [DEPTH, 2, 128, 32, 128])
    w2k_in = din("w2k", [DEPTH, 128, 128])
    w2v_in = din("w2v", [DEPTH, 128, 64])
    cb1_in = din("cb1", [DEPTH, 128, 2])
    mtri_in = din("mtri", [128, 128])
    manti_in = din("manti", [128, 128])
    cmpmask_in = din("cmpmask", [128, T])
    eall_in = din("eall", [128, T])
    kaug_in = din("kaug", [128, 32, 128])
    qaug_in = din("qaug", [128, 8, 128])
    ov_in = din("ov", [128, 33])
    bonus_in = din("bonus", [128, NT, 32])
    cap_in = din("cap", [128, NT, 32])
    idxb_in = din("idxb", [128, 32], I32)

    kv_out = dout("kv_out", [DEPTH, T, 768])
    sconv_out = dout("sconv_out", [DEPTH, 128, 6, 3])
    yT_out = dout("yT_out", [128, KD, T])

    banks = [nc.alloc_psum_tensor("bank%d" % i, [128, 512], F32).ap() for i in range(8)]
    tb = [Tk("bank%d" % i, excl=True) for i in range(8)]

    AR = Arena(nc, 190 * 1024)
    identf = AR.alloc([128], F32)
    identb = AR.alloc([128], BF16)
    onesb = AR.alloc([128], BF16)
    t_const = Tk("const")
    P.dma("sp", lambda e: e.dma_start(out=identf, in_=ident_in), t_const, writes=[t_const])
    P.op("dve", lambda e: e.tensor_copy(out=identb, in_=identf), reads=[t_const], writes=[t_const], partial=True)
    P.op("dve", lambda e: e.memset(onesb, 1.0), writes=[t_const], partial=True)

    utri = AR.alloc([128], F32)
    mtrif = AR.alloc([128], F32)
    onesf = AR.alloc([128], F32)
    P.dma("sp", lambda e: e.dma_start(out=utri, in_=utri_in), t_const, writes=[t_const], partial=True)
    P.dma("sp", lambda e: e.dma_start(out=mtrif, in_=mtrif_in), t_const, writes=[t_const], partial=True)
    P.op("dve", lambda e: e.memset(onesf, 1.0), writes=[t_const], partial=True)
    mods = AR.alloc([DEPTH * 48, 5], F32)
    t_mods = Tk("mods")
    smallv = AR.alloc([64], F32)
    t_small = Tk("small")
    n1g = [smallv[:, 0 + 8 * l: 8 + 8 * l] for l in range(DEPTH)]
    n2g = [smallv[:, 16 + 8 * l: 24 + 8 * l] for l in range(DEPTH)]
    fing = smallv[:, 32:40]
    for l in range(DEPTH):
        P.dma("sp", lambda e, l=l: e.dma_start(out=n1g[l], in_=n1g_in[l]), t_small, writes=[t_small], partial=True)
        P.dma("sp", lambda e, l=l: e.dma_start(out=n2g[l], in_=n2g_in[l]), t_small, writes=[t_small], partial=True)
    P.dma("sp", lambda e: e.dma_start(out=fing, in_=fing_in), t_small, writes=[t_small], partial=True)
    AB = AR.alloc([DEPTH * 2 * 2, 8], F32)
    t_AB = Tk("AB")
    base_mark = AR.mark()

    m0 = AR.mark()
    cT = AR.alloc([KD, 5], F32)
    cTb = AR.alloc([KD, 5], BF16)
    adab = AR.alloc([DEPTH, 48], F32)
    t_c = Tk("c")
    t_adab = Tk("adab")
    P.dma("sp", lambda e: e.dma_start(out=cT, in_=cT_in), t_c, writes=[t_c])
    P.op("act", lambda e: e.activation(out=cTb, in_=cT, func=AF.Silu), reads=[t_c], writes=[t_c], partial=True)
    for l in range(DEPTH):
        P.dma("sp", lambda e, l=l: e.dma_start(out=adab[:, l, :], in_=adab_in[l]), t_adab, writes=[t_adab], partial=True)
    wa = [AR.alloc([KD, 1024], BF16) for _ in range(2)]
    t_wa = [Tk("wa0"), Tk("wa1")]
    it = 0
    for l in range(DEPTH):
        bk = l % 2
        for s in range(6):
            sl = it % 2
            it += 1
            P.dma("pool", lambda e, l=l, s=s, sl=sl: e.dma_start(out=wa[sl], in_=adaw_in[l, s]), t_wa[sl], writes=[t_wa[sl]])
            for c8 in range(8):
                c = s * 8 + c8
                for kc in range(KD):
                    P.op("pe", lambda e, bk=bk, c=c, c8=c8, kc=kc, sl=sl: e.matmul(
                        banks[bk][:, c * 5:(c + 1) * 5], lhsT=wa[sl][:, kc, c8 * 128:(c8 + 1) * 128], rhs=cTb[:, kc, :],
                        start=(kc == 0), stop=(kc == KD - 1)), reads=[t_wa[sl], t_c], writes=[tb[bk]], partial=True)
        for r in range(5):
            P.op("dve", lambda e, l=l, r=r, bk=bk: e.tensor_tensor(
                out=mods[:, l * 48:(l + 1) * 48, r], in0=banks[bk][:, 0:240].rearrange("p (c r) -> p c r", r=5)[:, :, r],
                in1=adab[:, l, :], op=ALU.add), reads=[tb[bk], t_adab], writes=[t_mods], partial=True)
        for j, (gv, sci, shi) in enumerate(((n1g[l], 1, 0), (n2g[l], 4, 3))):
            P.op("dve", lambda e, l=l, j=j, gv=gv, sci=sci: e.scalar_tensor_tensor(
                out=AB[:, l * 4 + 2 * j, :], in0=mods[:, l * 48 + sci * 8: l * 48 + sci * 8 + 8, 0], scalar=1.0, in1=gv,
                op0=ALU.add, op1=ALU.mult), reads=[t_mods, t_small], writes=[t_AB], partial=True)
            P.op("dve", lambda e, l=l, j=j, shi=shi: e.tensor_copy(
                out=AB[:, l * 4 + 2 * j + 1, :], in_=mods[:, l * 48 + shi * 8: l * 48 + shi * 8 + 8, 0]),
                reads=[t_mods], writes=[t_AB], partial=True)
    P.barrier()
    AR.reset(m0)

    xsT = AR.alloc([KD, 4], F32)
    hsT = AR.alloc([KD, 4], BF16)
    osT = AR.alloc([KD, 4], BF16)
    kvnT = AR.alloc([6, 4], F32)
    QgT = AR.alloc([4, 4], BF16)
    xbcs = AR.alloc([6, 4], F32)
    zsT = AR.alloc([4, 4], F32)
    dd = AR.alloc([8], F32)
    gsT = AR.alloc([24], F32)
    VN = AR.alloc([4, 2, 130], BF16)
    exm = AR.alloc([4, 128], F32)
    t_xs, t_hs, t_os, t_kvn, t_Qg, t_xbcs, t_zsT, t_dd, t_gs, t_VN, t_ex = (Tk(n) for n in
        ("xs", "hs", "os", "kvn", "Qg", "xbcs", "zsT", "dd", "gs", "VN", "ex"))
    P.dma("sp", lambda e: e.dma_start(out=xsT, in_=xsT_in), t_xs, writes=[t_xs])
    P.dma("sp", lambda e: e.dma_start(out=exm[0:8, :, :], in_=ex_in), t_ex, writes=[t_ex])
    t_win = Tk("win")

    def sample_norm(l, sci, shi, gvec):
        m = AR.mark()
        t1 = AR.alloc([KD, 4], F32)
        As = AR.alloc([KD, 4], F32)
        rs4 = AR.alloc([4], F32)
        t_t1, t_As, t_rs4 = Tk("st1"), Tk("sAs"), Tk("srs")
        P.op("dve", lambda e: e.tensor_tensor(out=t1, in0=xsT, in1=xsT, op=ALU.mult), reads=[t_xs], writes=[t_t1])
        for kc in range(KD):
            P.op("pe", lambda e, kc=kc: e.matmul(banks[0][:, 0:4], lhsT=onesf, rhs=t1[:, kc, :], start=(kc == 0), stop=(kc == KD - 1)),
                 reads=[t_t1, t_const], writes=[tb[0]], partial=(kc > 0))
        P.op("dve", lambda e: e.tensor_scalar(out=rs4, in0=banks[0][:, 0:4], scalar1=1.0 / D, scalar2=EPS, op0=ALU.mult, op1=ALU.add),
             reads=[tb[0]], writes=[t_rs4])
        P.op("act", lambda e: e.activation(out=rs4, in_=rs4, func=AF.Sqrt), reads=[t_rs4], writes=[t_rs4])
        P.op("dve", lambda e: e.reciprocal(out=rs4, in_=rs4), reads=[t_rs4], writes=[t_rs4])
        for r in range(4):
            P.op("dve", lambda e, r=r: e.scalar_tensor_tensor(
                out=As[:, :, r], in0=mods[:, l * 48 + sci * 8: l * 48 + sci * 8 + 8, 1 + r], scalar=1.0, in1=gvec, op0=ALU.add, op1=ALU.mult),
                reads=[t_mods, t_small], writes=[t_As], partial=(r > 0))
        P.op("dve", lambda e: e.tensor_tensor(out=t1, in0=xsT, in1=As, op=ALU.mult), reads=[t_xs, t_As], writes=[t_t1])
        for kc in range(KD):
            P.op("dve", lambda e, kc=kc: e.tensor_tensor(out=t1[:, kc, :], in0=t1[:, kc, :], in1=rs4, op=ALU.mult), reads=[t_t1, t_rs4], writes=[t_t1], partial=True)
        P.op("dve", lambda e: e.tensor_tensor(out=hsT, in0=t1, in1=mods[:, l * 48 + shi * 8: l * 48 + shi * 8 + 8, 1:5], op=ALU.add),
             reads=[t_t1, t_mods], writes=[t_hs])
        AR.reset(m)

    hT = AR.alloc([KD, T], BF16)
    t_h = Tk("hT")
    layer_mark = AR.mark()
    xT = AR.alloc([KD, T], F32)
    t_x = Tk("xT")
    xscr = nc.dram_tensor("xscr", [128, KD, T], F32, kind="Internal").ap()
    t_xscr = Tk("xscr")

    def load_x(src):
        for kc in range(KD):
            P.dma("sp", lambda e, kc=kc, src=src: e.dma_start(out=xT[:, kc, :], in_=src[:, kc, :]), t_x,
                  reads=[t_xscr], writes=[t_x], partial=True)

    def norm_mod(Acol, Bcol, scratch_banks):
        m = AR.mark()
        sq = AR.alloc([KD, 512], BF16)
        rs = AR.alloc([512], F32)
        tmp = AR.alloc([2, 512], F32)
        t_sq, t_rs, t_tmp = Tk("sq"), Tk("rs"), [Tk("tmp0"), Tk("tmp1")]
        for tg in range(4):
            ts = slice(tg * 512, (tg + 1) * 512)
            bk = scratch_banks[tg % len(scratch_banks)]
            P.op("act", lambda e, ts=ts: e.activation(out=sq, in_=xT[:, :, ts], func=AF.Square), reads=[t_x], writes=[t_sq])
            for kc in range(KD):
                P.op("pe", lambda e, kc=kc, bk=bk: e.matmul(banks[bk], lhsT=onesb, rhs=sq[:, kc, :], start=(kc == 0), stop=(kc == KD - 1)),
                     reads=[t_sq, t_const], writes=[tb[bk]], partial=(kc > 0))
            P.op("dve", lambda e, bk=bk: e.tensor_scalar(out=rs, in0=banks[bk], scalar1=1.0 / D, scalar2=EPS, op0=ALU.mult, op1=ALU.add),
                 reads=[tb[bk]], writes=[t_rs])
            P.op("act", lambda e: e.activation(out=rs, in_=rs, func=AF.Sqrt), reads=[t_rs], writes=[t_rs])
            P.op("dve", lambda e: e.reciprocal(out=rs, in_=rs), reads=[t_rs], writes=[t_rs])
            for kc in range(KD):
                tt = kc % 2
                P.op("dve", lambda e, kc=kc, ts=ts, tt=tt: e.scalar_tensor_tensor(
                    out=tmp[:, tt, :], in0=xT[:, kc, ts], scalar=Acol[:, kc:kc + 1], in1=rs, op0=ALU.mult, op1=ALU.mult),
                    reads=[t_x, t_rs, t_AB], writes=[t_tmp[tt]])
                P.op("act", lambda e, kc=kc, ts=ts, tt=tt: e.activation(
                    out=hT[:, kc, ts], in_=tmp[:, tt, :], func=AF.Identity, bias=Bcol[:, kc:kc + 1], scale=1.0),
                    reads=[t_tmp[tt], t_AB], writes=[t_h], partial=True)
        AR.reset(m)

    for l in range(DEPTH):
        AR.reset(layer_mark)
        xsrc = xT_in if l == 0 else xscr
        AR.alloc([KD, T], F32)
        A1, B1, A2, B2 = (AB[:, l * 4 + j, :] for j in range(4))
        load_x(xsrc)
        norm_mod(A1, B1, [0, 1])
        P.barrier()
        sample_norm(l, 1, 0, n1g[l])
        P.barrier()
        AR.reset(layer_mark)
        Vall = AR.alloc([NT, 4, 65], BF16)
        zs = AR.alloc([NT, 512], BF16)
        gat = AR.alloc([NT, 24], F32)
        dtt = AR.alloc([NT, 8], F32)
        aa = AR.alloc([NT, 8], F32)
        lay = AR.alloc([4, 8], F32)
        sng = AR.alloc([512], F32)
        QT = AR.alloc([4, T], BF16)
        kvcT = AR.alloc([2, T], BF16)
        KsT = AR.alloc([2, T], BF16)
        KwT = AR.alloc([2, T], BF16)
        xbcT = AR.alloc([6, T], BF16)
        mA = AR.mark()
        wtm = AR.alloc([KD, 1312], BF16)
        t_wtm = Tk("wtm")
        for kc in range(KD):
            P.dma("pool", lambda e, l=l, kc=kc: e.dma_start(out=wtm[:, kc, :], in_=wtm_in[l, :, kc, :]), t_wtm, writes=[t_wtm], partial=True)
        kvst = [AR.alloc([768], F32) for _ in range(2)]
        t_kvst = [Tk("kvst0"), Tk("kvst1")]
        for tt in range(NT):
            tsl = slice(tt * 128, (tt + 1) * 128)
            sl = tt % 2
            for half in range(2):
                bk = 2 + (tt * 2 + half) % 4
                ncol = 512 if half == 0 else 256
                c0 = half * 512
                for kc in range(KD):
                    P.op("pe", lambda e, bk=bk, kc=kc, tsl=tsl, c0=c0, ncol=ncol: e.matmul(
                        banks[bk][:, 0:ncol], lhsT=hT[:, kc, tsl], rhs=wtm[:, kc, c0:c0 + ncol], start=(kc == 0), stop=(kc == KD - 1)),
                        reads=[t_h, t_wtm], writes=[tb[bk]], partial=(kc > 0))
                P.op("act", lambda e, bk=bk, sl=sl, c0=c0, ncol=ncol: e.copy(out=kvst[sl][:, c0:c0 + ncol], in_=banks[bk][:, 0:ncol]),
                     reads=[tb[bk]], writes=[t_kvst[sl]], partial=(half > 0))
            P.dma("sp", lambda e, l=l, tsl=tsl, sl=sl: e.dma_start(out=kv_out[l, tsl, :], in_=kvst[sl]), t_kvst[sl], reads=[t_kvst[sl]])
        t_V, t_zs, t_gat, t_dt, t_lay, t_Q, t_kvc, t_Ks, t_Kw, t_xbc = (Tk(n) for n in
            ("V", "zs", "gat", "dt", "lay", "Q", "kvc", "Ks", "Kw", "xbc"))
        mA2 = AR.mark()
        P.op("dve", lambda e: e.memset(Vall, 1.0), writes=[t_V])
        P.dma("sp", lambda e, l=l: e.dma_start(out=lay[:, 0, :], in_=dtb_in[l]), t_lay, writes=[t_lay], partial=True)
        P.dma("sp", lambda e, l=l: e.dma_start(out=lay[:, 1, :], in_=alog_in[l]), t_lay, writes=[t_lay], partial=True)
        P.dma("sp", lambda e, l=l: e.dma_start(out=lay[:, 2, :], in_=dsk_in[l]), t_lay, writes=[t_lay], partial=True)
        P.dma("sp", lambda e, l=l: e.dma_start(out=sng, in_=sng_in[l]), t_lay, writes=[t_lay], partial=True)
        P.op("act", lambda e: e.activation(out=lay[:, 1, :], in_=lay[:, 1, :], func=AF.Exp), reads=[t_lay], writes=[t_lay], partial=True)
        P.op("dve", lambda e: e.tensor_scalar(out=lay[:, 1, :], in0=lay[:, 1, :], scalar1=-1.0, scalar2=None, op0=ALU.mult),
             reads=[t_lay], writes=[t_lay], partial=True)
        for tt in range(NT):
            tsl = slice(tt * 128, (tt + 1) * 128)
            b1 = 2 + (tt % 2) * 2
            b2 = b1 + 1
            for kc in range(KD):
                P.op("pe", lambda e, b1=b1, kc=kc, tsl=tsl: e.matmul(
                    banks[b1][:, 0:128], lhsT=hT[:, kc, tsl], rhs=wtm[:, kc, 384:512], start=(kc == 0), stop=(kc == KD - 1)),
                    reads=[t_h, t_wtm], writes=[tb[b1]], partial=(kc > 0))
            for kc in range(KD):
                P.op("pe", lambda e, b1=b1, kc=kc, tsl=tsl: e.matmul(
                    banks[b1][:, 128:256], lhsT=hT[:, kc, tsl], rhs=wtm[:, kc, 640:768], start=(kc == 0), stop=(kc == KD - 1)),
                    reads=[t_h, t_wtm], writes=[tb[b1]], partial=True)
            for kc in range(KD):
                P.op("pe", lambda e, b1=b1, kc=kc, tsl=tsl: e.matmul(
                    banks[b1][:, 256:280], lhsT=hT[:, kc, tsl], rhs=wtm[:, kc, 768:792], start=(kc == 0), stop=(kc == KD - 1)),
                    reads=[t_h, t_wtm], writes=[tb[b1]], partial=True)
            for kc in range(KD):
                P.op("pe", lambda e, b1=b1, kc=kc, tsl=tsl: e.matmul(
                    banks[b1][:, 280:288], lhsT=hT[:, kc, tsl], rhs=wtm[:, kc, 1304:1312], start=(kc == 0), stop=(kc == KD - 1)),
                    reads=[t_h, t_wtm], writes=[tb[b1]], partial=True)
            P.op("act", lambda e, b1=b1, tt=tt: e.copy(out=Vall[:, tt, :, 0:64], in_=banks[b1][:, 0:256].rearrange("p (a b) -> p a b", b=64)),
                 reads=[tb[b1]], writes=[t_V], partial=True)
            P.op("act", lambda e, b1=b1, tt=tt: e.activation(out=gat[:, tt, :], in_=banks[b1][:, 256:280], func=AF.Sigmoid),
                 reads=[tb[b1]], writes=[t_gat], partial=True)
            P.op("dve", lambda e, b1=b1, tt=tt: e.tensor_tensor(out=dtt[:, tt, :], in0=banks[b1][:, 280:288], in1=lay[:, 0, :], op=ALU.add),
                 reads=[tb[b1], t_lay], writes=[t_dt], partial=True)
            for kc in range(KD):
                P.op("pe", lambda e, b2=b2, kc=kc, tsl=tsl: e.matmul(
                    banks[b2], lhsT=hT[:, kc, tsl], rhs=wtm[:, kc, 792:1304], start=(kc == 0), stop=(kc == KD - 1)),
                    reads=[t_h, t_wtm], writes=[tb[b2]], partial=(kc > 0))
            P.op("act", lambda e, b2=b2, tt=tt: e.activation(out=zs[:, tt, :], in_=banks[b2], func=AF.Silu),
                 reads=[tb[b2]], writes=[t_zs], partial=True)
        def smm(bk, prt, cols, lhs_fn, rhs_fn, reads):
            for kc in range(KD):
                P.op("pe", lambda e, kc=kc: e.matmul(banks[bk][prt, cols], lhsT=lhs_fn(kc), rhs=rhs_fn(kc), start=(kc == 0), stop=(kc == KD - 1)),
                     reads=reads, writes=[tb[bk]], partial=True)
        for c6 in range(6):
            smm(0, slice(0, 128), slice(c6 * 4, c6 * 4 + 4), lambda kc, c6=c6: wtm[:, kc, c6 * 128:(c6 + 1) * 128], lambda kc: hsT[:, kc, :], [t_wtm, t_hs])
        for c4 in range(4):
            smm(0, slice(0, 128), slice(24 + c4 * 4, 28 + c4 * 4), lambda kc, c4=c4: wtm[:, kc, 792 + c4 * 128:792 + (c4 + 1) * 128], lambda kc: hsT[:, kc, :], [t_wtm, t_hs])
        smm(0, slice(0, 8), slice(40, 44), lambda kc: wtm[:, kc, 1304:1312], lambda kc: hsT[:, kc, :], [t_wtm, t_hs])
        for bg in range(6):
            smm(0, slice(0, 4), slice(44 + bg * 4, 48 + bg * 4), lambda kc, bg=bg: wtm[:, kc, 768 + bg * 4:772 + bg * 4], lambda kc: hsT[:, kc, :], [t_wtm, t_hs])
        for b in range(4):
            smm(1, slice(0, 1), slice(b * 128, (b + 1) * 128), lambda kc, b=b: hsT[:, kc, b:b + 1], lambda kc: wtm[:, kc, 384:512], [t_wtm, t_hs])
        for b in range(4):
            smm(6, slice(0, 1), slice(b * 128, (b + 1) * 128), lambda kc, b=b: hsT[:, kc, b:b + 1], lambda kc: wtm[:, kc, 640:768], [t_wtm, t_hs])
        P.op("act", lambda e: e.copy(out=kvnT, in_=banks[0][:, 0:24].rearrange("p (a b) -> p a b", b=4)), reads=[tb[0]], writes=[t_kvn])
        P.op("act", lambda e: e.activation(out=zsT, in_=banks[0][:, 24:40].rearrange("p (a b) -> p a b", b=4), func=AF.Silu), reads=[tb[0]], writes=[t_zsT])
        P.op("act", lambda e: e.activation(out=gsT[0:4, :], in_=banks[0][0:4, 44:68], func=AF.Sigmoid), reads=[tb[0]], writes=[t_gs])
        hcol = AR.alloc([3], F32)
        t_hcol = Tk("hcol")
        P.dma("sp", lambda e, l=l: e.dma_start(out=hcol[0:8, :], in_=hcol_in[l]), t_hcol, writes=[t_hcol])
        P.op("act", lambda e: e.activation(out=dd[0:8, 0:4], in_=banks[0][0:8, 40:44], func=AF.Exp, bias=hcol[0:8, 0:1], scale=1.0), reads=[tb[0], t_hcol], writes=[t_dd])
        P.op("dve", lambda e: e.tensor_scalar(out=dd[0:8, 0:4], in0=dd[0:8, 0:4], scalar1=1.0, scalar2=None, op0=ALU.add), reads=[t_dd], writes=[t_dd])
        P.op("act", lambda e: e.activation(out=dd[0:8, 0:4], in_=dd[0:8, 0:4], func=AF.Ln), reads=[t_dd], writes=[t_dd])
        P.op("act", lambda e: e.activation(out=hcol[0:8, 1:2], in_=hcol[0:8, 1:2], func=AF.Exp), reads=[t_hcol], writes=[t_hcol])
        P.op("dve", lambda e: e.tensor_scalar(out=dd[0:8, 4:8], in0=dd[0:8, 0:4], scalar1=hcol[0:8, 1:2], scalar2=-1.0, op0=ALU.mult, op1=ALU.mult), reads=[t_dd, t_hcol], writes=[t_dd])
        P.op("act", lambda e: e.activation(out=dd[0:8, 4:8], in_=dd[0:8, 4:8], func=AF.Exp), reads=[t_dd], writes=[t_dd])
        P.op("dve", lambda e: e.memset(VN[0:1], 1.0), writes=[t_VN])
        P.op("act", lambda e: e.copy(out=VN[0:1, :, 0, :].rearrange("p b (g x) -> p b g x", g=2)[:, :, :, 0:64], in_=banks[1][0:1, 0:512].rearrange("p (b g d) -> p b g d", b=4, g=2)),
             reads=[tb[1]], writes=[t_VN], partial=True)
        P.op("act", lambda e: e.copy(out=VN[0:1, :, 1, :].rearrange("p b (g x) -> p b g x", g=2)[:, :, :, 0:64], in_=banks[6][0:1, 0:512].rearrange("p (b g d) -> p b g d", b=4, g=2)),
             reads=[tb[6]], writes=[t_VN], partial=True)
        P.dma("sp", lambda e, l=l: e.dma_start(out=kvn_out[l], in_=kvnT), t_kvn, reads=[t_kvn])
        for b in range(4):
            P.dma("sp", lambda e, l=l, b=b: e.dma_start(out=wk_out[l, b, 0:511, :], in_=wk_in[l, b, 1:512, :]), t_win, writes=[t_win], partial=True)
            P.dma("sp", lambda e, l=l, b=b: e.dma_start(out=wv_out[l, b, 0:511, :], in_=wv_in[l, b, 1:512, :]), t_win, writes=[t_win], partial=True)
            P.dma("sp", lambda e, l=l, b=b: e.dma_start(out=wk_out[l, b, 511:512, :].rearrange("o f -> f o"), in_=kvnT[:, 4, b:b + 1]), t_kvn, reads=[t_kvn])
            P.dma("sp", lambda e, l=l, b=b: e.dma_start(out=wv_out[l, b, 511:512, :].rearrange("o f -> f o"), in_=kvnT[:, 5, b:b + 1]), t_kvn, reads=[t_kvn])
        P.op("act", lambda e: e.activation(out=dtt, in_=dtt, func=AF.Exp), reads=[t_dt], writes=[t_dt])
        P.op("dve", lambda e: e.tensor_scalar(out=dtt, in0=dtt, scalar1=1.0, scalar2=None, op0=ALU.add), reads=[t_dt], writes=[t_dt])
        P.op("act", lambda e: e.activation(out=dtt, in_=dtt, func=AF.Ln), reads=[t_dt], writes=[t_dt])
        for tt in range(NT):
            P.op("dve", lambda e, tt=tt: e.tensor_tensor(out=aa[:, tt, :], in0=dtt[:, tt, :], in1=lay[:, 1, :], op=ALU.mult),
                 reads=[t_dt, t_lay], writes=[t_dt], partial=True)
        wch = [AR.alloc([KD, 128], BF16) for _ in range(3)]
        t_wch = [Tk("wch%d" % i) for i in range(3)]
        stg = [AR.alloc([515], F32) for _ in range(2)]
        t_stg = [Tk("stg0"), Tk("stg1")]
        cacc = [AR.alloc([512], F32) for _ in range(2)]
        t_cacc = [Tk("cacc0"), Tk("cacc1")]
        scw = AR.alloc([6, 4], F32)
        scb = AR.alloc([6], F32)
        t_sc = Tk("sc")
        P.dma("sp", lambda e, l=l: e.dma_start(out=scw, in_=scw_in[l]), t_sc, writes=[t_sc], partial=True)
        P.dma("sp", lambda e, l=l: e.dma_start(out=scb, in_=scb_in[l]), t_sc, writes=[t_sc], partial=True)
        nb = 0
        for c in range(20):
            sl = c % 3
            P.dma("pool", lambda e, l=l, c=c, sl=sl: e.dma_start(out=wch[sl], in_=wfm_in[l, c]), t_wch[sl], writes=[t_wch[sl]])
            if c >= 10:
                scol = slice((c - 10) * 4, (c - 10) * 4 + 4)
                sbk = 0 if c < 16 else 1
                if c >= 16:
                    scol = slice((c - 16) * 4, (c - 16) * 4 + 4)
                for kc in range(KD):
                    P.op("pe", lambda e, kc=kc, sl=sl, sbk=sbk, scol=scol: e.matmul(banks[sbk][:, scol], lhsT=wch[sl][:, kc, :], rhs=hsT[:, kc, :], start=(kc == 0), stop=(kc == KD - 1)),
                         reads=[t_wch[sl], t_hs], writes=[tb[sbk]], partial=True)
            if c >= 16:
                continue
            for tg in range(4):
                ts = slice(tg * 512, (tg + 1) * 512)
                bk = 4 + nb % 4
                nb += 1
                for kc in range(KD):
                    P.op("pe", lambda e, bk=bk, kc=kc, sl=sl, ts=ts: e.matmul(
                        banks[bk], lhsT=wch[sl][:, kc, :], rhs=hT[:, kc, ts], start=(kc == 0), stop=(kc == KD - 1)),
                        reads=[t_h, t_wch[sl]], writes=[tb[bk]], partial=(kc > 0))
                if c < 4:
                    P.op("act", lambda e, bk=bk, c=c, ts=ts: e.activation(out=QT[:, c, ts], in_=banks[bk], func=AF.Copy, scale=0.125),
                         reads=[tb[bk]], writes=[t_Q], partial=True)
                elif c < 6:
                    P.op("act", lambda e, bk=bk, c=c, ts=ts: e.copy(out=kvcT[:, c - 4, ts], in_=banks[bk]),
                         reads=[tb[bk]], writes=[t_kvc], partial=True)
                elif c < 8:
                    P.op("act", lambda e, bk=bk, c=c, ts=ts: e.copy(out=KsT[:, c - 6, ts], in_=banks[bk]),
                         reads=[tb[bk]], writes=[t_Ks], partial=True)
                elif c < 10:
                    P.op("act", lambda e, bk=bk, c=c, ts=ts: e.copy(out=KwT[:, c - 8, ts], in_=banks[bk]),
                         reads=[tb[bk]], writes=[t_Kw], partial=True)
                else:
                    cc = c - 10
                    si = tg % 2
                    if tg == 0:
                        P.op("dve", lambda e, si=si: e.memset(stg[si][:, 0:3], 0.0), writes=[t_stg[si]], partial=True)
                    P.op("act", lambda e, bk=bk, si=si: e.copy(out=stg[si][:, 3:515], in_=banks[bk]),
                         reads=[tb[bk]], writes=[t_stg[si]], partial=True)
                    P.op("dve", lambda e, si=si, cc=cc: e.tensor_scalar(
                        out=cacc[si], in0=stg[si][:, 3:515], scalar1=scw[:, cc, 3:4], scalar2=scb[:, cc:cc + 1], op0=ALU.mult, op1=ALU.add),
                        reads=[t_stg[si], t_sc], writes=[t_cacc[si]])
                    for k in range(3):
                        P.op("dve", lambda e, si=si, cc=cc, k=k: e.scalar_tensor_tensor(
                            out=cacc[si], in0=stg[si][:, k:k + 512], scalar=scw[:, cc, k:k + 1], in1=cacc[si], op0=ALU.mult, op1=ALU.add),
                            reads=[t_stg[si], t_sc], writes=[t_cacc[si]])
                    P.op("act", lambda e, si=si, cc=cc, ts=ts: e.activation(out=xbcT[:, cc, ts], in_=cacc[si], func=AF.Silu),
                         reads=[t_cacc[si]], writes=[t_xbc], partial=True)
                    if tg < 3:
                        P.op("dve", lambda e, si=si: e.tensor_copy(out=stg[1 - si][:, 0:3], in_=stg[si][:, 512:515]),
                             reads=[t_stg[si]], writes=[t_stg[1 - si]], partial=True)
                    else:
                        P.dma("sp", lambda e, l=l, si=si, cc=cc: e.dma_start(out=sconv_out[l, :, cc, :], in_=stg[si][:, 512:515]),
                              t_stg[si], reads=[t_stg[si]])
        P.op("act", lambda e: e.activation(out=QgT, in_=banks[1][:, 0:16].rearrange("p (a b) -> p a b", b=4), func=AF.Copy, scale=0.125), reads=[tb[1]], writes=[t_Qg])
        cst = AR.alloc([6, 4, 4], F32)
        cst_o = AR.alloc([6, 3, 4], F32)
        cac = AR.alloc([6, 4], F32)
        t_cst, t_cso, t_cac = Tk("cst"), Tk("cso"), Tk("cac")
        P.dma("sp", lambda e, l=l: e.dma_start(out=cst[:, :, 0:3, :], in_=cst_in[l]), t_cst, writes=[t_cst])
        P.op("act", lambda e: e.copy(out=cst[:, :, 3, :], in_=banks[0][:, 0:24].rearrange("p (a b) -> p a b", b=4)), reads=[tb[0]], writes=[t_cst], partial=True)
        P.op("dve", lambda e: e.tensor_copy(out=cst_o, in_=cst[:, :, 1:4, :]), reads=[t_cst], writes=[t_cso])
        P.dma("sp", lambda e, l=l: e.dma_start(out=sconvs_out[l], in_=cst_o), t_cso, reads=[t_cso])
        for cc in range(6):
            P.op("dve", lambda e, cc=cc: e.tensor_scalar(out=cac[:, cc, :], in0=cst[:, cc, 3, :], scalar1=scw[:, cc, 3:4], scalar2=scb[:, cc:cc + 1], op0=ALU.mult, op1=ALU.add),
                 reads=[t_cst, t_sc], writes=[t_cac], partial=(cc > 0))
            for k in range(3):
                P.op("dve", lambda e, cc=cc, k=k: e.scalar_tensor_tensor(out=cac[:, cc, :], in0=cst[:, cc, k, :], scalar=scw[:, cc, k:k + 1], in1=cac[:, cc, :], op0=ALU.mult, op1=ALU.add),
                     reads=[t_cst, t_sc, t_cac], writes=[t_cac], partial=True)
        P.op("act", lambda e: e.activation(out=xbcs, in_=cac, func=AF.Silu), reads=[t_cac], writes=[t_xbcs])
        P.barrier()
        AR.reset(mA)
        if stage <= 2:
            break
        mSS = AR.mark()
        hSs = AR.alloc([4, 4, 64], F32)
        dx = AR.alloc([4, 8], F32)
        BCtm = AR.alloc([256], F32)
        BCbd = AR.alloc([4, 256], F32)
        BC = AR.alloc([4, 2, 128], F32)
        us = AR.alloc([4, 4], F32)
        ys = AR.alloc([4, 4], F32)
        y2s = AR.alloc([4, 4], F32)
        stmp = AR.alloc([2, 64], F32)
        ssg = AR.alloc([2, 4], F32)
        colv = AR.alloc([8], F32)
        t_hSs, t_dx, t_BCtm, t_BCbd, t_BC, t_us, t_ys, t_y2s, t_stmp, t_ssg, t_colv = (Tk(n) for n in
            ("hSs", "dx", "BCtm", "BCbd", "BC", "us", "ys", "y2s", "stmp", "ssg", "colv"))
        P.dma("sp", lambda e, l=l: e.dma_start(out=hSs, in_=hs0_in[l]), t_hSs, writes=[t_hSs])
        P.dma("sp", lambda e, l=l: e.dma_start(out=colv[:, 0:4], in_=dskc_in[l]), t_colv, writes=[t_colv], partial=True)
        P.dma("sp", lambda e, l=l: e.dma_start(out=colv[:, 4:8], in_=sngc_in[l]), t_colv, writes=[t_colv], partial=True)
        for c in range(4):
            P.op("pe", lambda e, c=c: e.matmul(banks[0][:, c * 8:(c + 1) * 8], lhsT=exm[0:8, c, :], rhs=dd[0:8, :], start=True, stop=True),
                 reads=[t_ex, t_dd], writes=[tb[0]], partial=(c > 0))
        P.op("act", lambda e: e.copy(out=dx, in_=banks[0][:, 0:32].rearrange("p (a b) -> p a b", b=8)), reads=[tb[0]], writes=[t_dx])
        for i2 in range(2):
            P.op("pe", lambda e, i2=i2: e.transpose(out=banks[1][0:4, i2 * 128:(i2 + 1) * 128], in_=xbcs[:, 4 + i2, :], identity=identf),
                 reads=[t_xbcs, t_const], writes=[tb[1]], partial=(i2 > 0))
        P.op("act", lambda e: e.copy(out=BCtm[0:4, :], in_=banks[1][0:4, 0:256]), reads=[tb[1]], writes=[t_BCtm])
        for b in range(4):
            P.op("dve", lambda e, b=b: e.tensor_scalar(out=BCbd[0:4, b, :], in0=BCtm[0:4, :], scalar1=identf[0:4, b:b + 1], scalar2=None, op0=ALU.mult),
                 reads=[t_BCtm, t_const], writes=[t_BCbd], partial=(b > 0))
        for b in range(4):
            bk = 2 + b // 2
            P.op("pe", lambda e, b=b, bk=bk: e.matmul(banks[bk][:, (b % 2) * 256:(b % 2 + 1) * 256], lhsT=onesf[0:4, :], rhs=BCbd[0:4, b, :], start=True, stop=True),
                 reads=[t_BCbd, t_const], writes=[tb[bk]], partial=(b % 2 > 0))
        for i2 in range(2):
            P.op("act", lambda e, i2=i2: e.copy(out=BC[:, 2 * i2:2 * i2 + 2, :, :], in_=banks[2 + i2].rearrange("p (b t n) -> p b t n", b=2, t=2)),
                 reads=[tb[2 + i2]], writes=[t_BC], partial=(i2 > 0))
        P.op("dve", lambda e: e.tensor_tensor(out=us, in0=xbcs[:, 0:4, :], in1=dx[:, :, 0:4], op=ALU.mult), reads=[t_xbcs, t_dx], writes=[t_us])
        first = True
        for c in range(4):
            g = c // 2
            for b in range(4):
                P.op("dve", lambda e, c=c, b=b, g=g: e.tensor_scalar(out=stmp[:, 0, :], in0=BC[:, b, 0, g * 64:(g + 1) * 64], scalar1=us[:, c, b:b + 1], scalar2=None, op0=ALU.mult),
                     reads=[t_BC, t_us], writes=[t_stmp])
                P.op("dve", lambda e, c=c, b=b: e.scalar_tensor_tensor(out=hSs[:, c, b, :], in0=hSs[:, c, b, :], scalar=dx[:, c, 4 + b:5 + b], in1=stmp[:, 0, :], op0=ALU.mult, op1=ALU.add),
                     reads=[t_hSs, t_dx, t_stmp], writes=[t_hSs], partial=True)
                P.op("dve", lambda e, c=c, b=b, g=g: e.tensor_tensor(out=stmp[:, 1, :], in0=hSs[:, c, b, :], in1=BC[:, b, 1, g * 64:(g + 1) * 64], op=ALU.mult),
                     reads=[t_hSs, t_BC], writes=[t_stmp], partial=True)
                P.op("dve", lambda e, c=c, b=b: e.reduce_sum(out=ys[:, c, b:b + 1], in_=stmp[:, 1, :], axis=AX.X),
                     reads=[t_stmp], writes=[t_ys], partial=(not first))
                first = False
        P.dma("sp", lambda e, l=l: e.dma_start(out=ssms_out[l], in_=hSs), t_hSs, reads=[t_hSs])
        for c in range(4):
            P.op("dve", lambda e, c=c: e.scalar_tensor_tensor(out=ys[:, c, :], in0=xbcs[:, c, :], scalar=colv[:, c:c + 1], in1=ys[:, c, :], op0=ALU.mult, op1=ALU.add),
                 reads=[t_xbcs, t_colv, t_ys], writes=[t_ys], partial=True)
        P.op("dve", lambda e: e.tensor_tensor(out=ys, in0=ys, in1=zsT, op=ALU.mult), reads=[t_ys, t_zsT], writes=[t_ys])
        P.op("dve", lambda e: e.tensor_tensor(out=y2s, in0=ys, in1=ys, op=ALU.mult), reads=[t_ys], writes=[t_y2s])
        P.op("pe", lambda e: e.matmul(banks[0][:, 0:16], lhsT=onesf, rhs=y2s.rearrange("p a b -> p (a b)"), start=True, stop=True), reads=[t_y2s, t_const], writes=[tb[0]])
        P.op("act", lambda e: e.copy(out=y2s.rearrange("p a b -> p (a b)"), in_=banks[0][:, 0:16]), reads=[tb[0]], writes=[t_y2s])
        s4 = y2s.rearrange("p (g c) b -> p g c b", g=2)
        P.op("dve", lambda e, s4=s4: e.tensor_tensor(out=ssg, in0=s4[:, :, 0, :], in1=s4[:, :, 1, :], op=ALU.add), reads=[t_y2s], writes=[t_ssg])
        P.op("dve", lambda e: e.tensor_scalar(out=ssg, in0=ssg, scalar1=1.0 / 256, scalar2=EPS, op0=ALU.mult, op1=ALU.add), reads=[t_ssg], writes=[t_ssg])
        P.op("act", lambda e: e.activation(out=ssg, in_=ssg, func=AF.Sqrt), reads=[t_ssg], writes=[t_ssg])
        P.op("dve", lambda e: e.reciprocal(out=ssg, in_=ssg), reads=[t_ssg], writes=[t_ssg])
        for c in range(4):
            P.op("dve", lambda e, c=c: e.scalar_tensor_tensor(out=osT[:, 4 + c, :], in0=ys[:, c, :], scalar=colv[:, 4 + c:5 + c], in1=ssg[:, c // 2, :], op0=ALU.mult, op1=ALU.mult),
                 reads=[t_ys, t_colv, t_ssg], writes=[t_os], partial=True)
        P.barrier()
        AR.reset(mSS)
        oT = hT
        t_o = t_h
        mS = AR.mark()
        xB = AR.alloc([NT, 640], BF16)
        t_xB = Tk("xB")
        for tt in range(NT):
            tsl = slice(tt * 128, (tt + 1) * 128)
            bk = tt % 2
            bv = banks[bk].bitcast(BF16)
            for cc in range(5):
                P.op("pe", lambda e, bv=bv, cc=cc, tsl=tsl: e.transpose(out=bv[:, cc * 128:(cc + 1) * 128], in_=xbcT[:, cc, tsl], identity=identb),
                     reads=[t_xbc, t_const], writes=[tb[bk]], partial=(cc > 0))
            P.op("act", lambda e, bv=bv, tt=tt: e.copy(out=xB[:, tt, :], in_=bv[:, 0:640]), reads=[tb[bk]], writes=[t_xB], partial=True)
        hS = AR.alloc([4, 64], F32)
        hSb = AR.alloc([4, 64], BF16)
        t_hS = Tk("hS")
        P.op("dve", lambda e: e.memset(hS, 0.0), writes=[t_hS])
        P.op("dve", lambda e: e.memset(hSb, 0.0), writes=[t_hS], partial=True)
        ncs = AR.alloc([8], F32)
        ecs = AR.alloc([8], F32)
        decc = AR.alloc([8], F32)
        wcol = AR.alloc([8], F32)
        abc = [AR.alloc([128], F32) for _ in range(2)]
        LT = [AR.alloc([128], F32) for _ in range(2)]
        WT = [AR.alloc([128], BF16) for _ in range(2)]
        GT = AR.alloc([2, 128], F32)
        xw = AR.alloc([512], BF16)
        ydsb = AR.alloc([512], F32)
        yy = AR.alloc([512], F32)
        yf = AR.alloc([512], BF16)
        ssq = AR.alloc([4], F32)
        t_ncs, t_ecs, t_dec, t_wcol, t_GT, t_xw, t_yd, t_yy, t_yf, t_ssq = (Tk(n) for n in
            ("ncs", "ecs", "dec", "wcol", "GT", "xw", "yd", "yy", "yf", "ssq"))
        t_abc = [Tk("abc0"), Tk("abc1")]
        t_LT = [Tk("LT0"), Tk("LT1")]
        t_WT = [Tk("WT0"), Tk("WT1")]
        GB = (1, 7)
        for c in range(NT):
            csl = slice(c * 128, (c + 1) * 128)
            P.op("pe", lambda e, c=c: e.matmul(banks[0][:, 0:8], lhsT=utri, rhs=aa[:, c, :], start=True, stop=True),
                 reads=[t_const, t_dt], writes=[tb[0]])
            P.op("dve", lambda e: e.tensor_scalar(out=ncs, in0=banks[0][:, 0:8], scalar1=-1.0, scalar2=None, op0=ALU.mult),
                 reads=[tb[0]], writes=[t_ncs])
            P.op("act", lambda e: e.activation(out=ecs, in_=banks[0][:, 0:8], func=AF.Exp), reads=[tb[0]], writes=[t_ecs])
            for g in range(2):
                ps = slice(g * 64, (g + 1) * 64)
                gb = GB[g]
                P.op("pe", lambda e, g=g, ps=ps, csl=csl, gb=gb: e.matmul(
                    banks[gb][:, 0:128], lhsT=xbcT[ps, 4, csl], rhs=xbcT[ps, 5, csl], start=True, stop=True),
                    reads=[t_xbc], writes=[tb[gb]])
                P.op("act", lambda e, g=g, gb=gb: e.copy(out=GT[:, g, :], in_=banks[gb][:, 0:128]), reads=[tb[gb]], writes=[t_GT], partial=(g > 0))
            for h in range(8):
                g = h // 4
                hb = h % 2
                sb = 2 + (h // 4)
                col = slice((h % 4) * 128, (h % 4 + 1) * 128)
                P.op("dve", lambda e, c=c, h=h, hb=hb: e.tensor_scalar(out=abc[hb], in0=onesf, scalar1=aa[:, c, h:h + 1], scalar2=None, op0=ALU.mult),
                     reads=[t_dt, t_const], writes=[t_abc[hb]])
                P.op("pe", lambda e, sb=sb, col=col, hb=hb: e.matmul(banks[sb][:, col], lhsT=abc[hb], rhs=utri, start=True, stop=False),
                     reads=[t_abc[hb], t_const], writes=[tb[sb]], partial=(h % 4 > 0))
                P.op("pe", lambda e, sb=sb, col=col: e.matmul(banks[sb][:, col], lhsT=identf, rhs=mtrif, start=False, stop=True),
                     reads=[t_const], writes=[tb[sb]], partial=True)
            for h in range(8):
                g = h // 4
                hb = h % 2
                sb = 2 + (h // 4)
                col = slice((h % 4) * 128, (h % 4 + 1) * 128)
                P.op("act", lambda e, sb=sb, col=col, hb=hb, h=h: e.activation(out=LT[hb], in_=banks[sb][:, col], func=AF.Exp, bias=ncs[:, h:h + 1], scale=1.0),
                     reads=[tb[sb], t_ncs], writes=[t_LT[hb]])
                P.op("act", lambda e, sb=sb, h=h: e.activation(out=decc[:, h:h + 1], in_=banks[sb][:, (h % 4) * 128 + 127:(h % 4) * 128 + 128], func=AF.Exp),
                     reads=[tb[sb]], writes=[t_dec], partial=True)
                P.op("dve", lambda e, hb=hb, h=h, g=g, c=c: e.scalar_tensor_tensor(
                    out=WT[hb], in0=LT[hb], scalar=dtt[:, c, h:h + 1], in1=GT[:, g, :], op0=ALU.mult, op1=ALU.mult),
                    reads=[t_LT[hb], t_dt, t_GT], writes=[t_WT[hb]])
                P.op("dve", lambda e, hb=hb, h=h, c=c: e.tensor_tensor(out=wcol[:, h:h + 1], in0=LT[hb][:, 127:128], in1=dtt[:, c, h:h + 1], op=ALU.mult),
                     reads=[t_LT[hb], t_dt], writes=[t_wcol], partial=True)
                P.op("dve", lambda e, h=h, c=c: e.tensor_scalar(out=xw[:, h * 64:(h + 1) * 64], in0=xB[:, c, h * 64:(h + 1) * 64], scalar1=wcol[:, h:h + 1], scalar2=None, op0=ALU.mult),
                     reads=[t_xB, t_wcol], writes=[t_xw], partial=True)
                P.op("pe", lambda e, hb=hb, h=h, c=c: e.matmul(banks[4][:, h * 64:(h + 1) * 64], lhsT=WT[hb], rhs=xB[:, c, h * 64:(h + 1) * 64], start=True, stop=True),
                     reads=[t_WT[hb], t_xB], writes=[tb[4]], partial=(h > 0))
                ps = slice(g * 64, (g + 1) * 64)
                gb = GB[g]
                P.op("pe", lambda e, h=h, ps=ps, csl=csl, gb=gb: e.matmul(banks[gb][:, 128 + (h % 4) * 64:128 + (h % 4 + 1) * 64], lhsT=xbcT[ps, 5, csl], rhs=hSb[ps, h % 4, :], start=True, stop=True),
                     reads=[t_xbc, t_hS], writes=[tb[gb]], partial=True)
            P.op("act", lambda e: e.copy(out=ydsb, in_=banks[4]), reads=[tb[4]], writes=[t_yd])
            for h in range(8):
                hs = slice(h * 64, (h + 1) * 64)
                gb = GB[h // 4]
                P.op("dve", lambda e, h=h, hs=hs, gb=gb: e.scalar_tensor_tensor(out=yy[:, hs], in0=banks[gb][:, 128 + (h % 4) * 64:128 + (h % 4 + 1) * 64], scalar=ecs[:, h:h + 1], in1=ydsb[:, hs], op0=ALU.mult, op1=ALU.add),
                     reads=[tb[gb], t_ecs, t_yd], writes=[t_yy], partial=(h > 0))
            for h in range(8):
                hs = slice(h * 64, (h + 1) * 64)
                P.op("dve", lambda e, h=h, hs=hs, c=c: e.scalar_tensor_tensor(out=yy[:, hs], in0=xB[:, c, hs], scalar=lay[:, 2, h:h + 1], in1=yy[:, hs], op0=ALU.mult, op1=ALU.add),
                     reads=[t_xB, t_lay, t_yy], writes=[t_yy], partial=True)
            P.op("pe", lambda e, c=c: e.matmul(banks[6], lhsT=xB[:, c, 512:640], rhs=xw, start=True, stop=True),
                 reads=[t_xB, t_xw], writes=[tb[6]])
            for h in range(8):
                g = h // 4
                ps = slice(g * 64, (g + 1) * 64)
                P.op("dve", lambda e, h=h, ps=ps: e.scalar_tensor_tensor(out=hS[ps, h % 4, :], in0=hS[ps, h % 4, :], scalar=decc[ps, h:h + 1], in1=banks[6][ps, h * 64:(h + 1) * 64], op0=ALU.mult, op1=ALU.add),
                     reads=[t_hS, t_dec, tb[6]], writes=[t_hS], partial=(h > 0))
            P.op("act", lambda e: e.copy(out=hSb, in_=hS), reads=[t_hS], writes=[t_hS], partial=True)
            P.op("dve", lambda e, c=c: e.tensor_tensor(out=yy, in0=yy, in1=zs[:, c, :], op=ALU.mult), reads=[t_yy, t_zs], writes=[t_yy])
            for g in range(2):
                P.op("act", lambda e, g=g: e.activation(out=ydsb[:, g * 256:(g + 1) * 256], in_=yy[:, g * 256:(g + 1) * 256], func=AF.Square, accum_out=ssq[:, g:g + 1]),
                     reads=[t_yy], writes=[t_ssq, t_yd], partial=(g > 0))
            P.op("dve", lambda e: e.tensor_scalar(out=ssq[:, 2:4], in0=ssq[:, 0:2], scalar1=1.0 / 256, scalar2=EPS, op0=ALU.mult, op1=ALU.add),
                 reads=[t_ssq], writes=[t_ssq], partial=True)
            P.op("act", lambda e: e.activation(out=ssq[:, 2:4], in_=ssq[:, 2:4], func=AF.Sqrt), reads=[t_ssq], writes=[t_ssq], partial=True)
            P.op("dve", lambda e: e.reciprocal(out=ssq[:, 2:4], in_=ssq[:, 2:4]), reads=[t_ssq], writes=[t_ssq], partial=True)
            for g in range(2):
                gs = slice(g * 256, (g + 1) * 256)
                P.op("dve", lambda e, g=g, gs=gs: e.scalar_tensor_tensor(out=yf[:, gs], in0=yy[:, gs], scalar=ssq[:, 2 + g:3 + g], in1=sng[:, gs], op0=ALU.mult, op1=ALU.mult),
                     reads=[t_yy, t_ssq, t_lay], writes=[t_yf], partial=(g > 0))
            bv7 = banks[0].bitcast(BF16)
            for e4 in range(4):
                P.op("pe", lambda e, e4=e4, bv7=bv7: e.transpose(out=bv7[:, e4 * 128:(e4 + 1) * 128], in_=yf[:, e4 * 128:(e4 + 1) * 128], identity=identb),
                     reads=[t_yf, t_const], writes=[tb[0]], partial=(e4 > 0))
            P.op("act", lambda e, bv7=bv7, csl=csl: e.copy(out=oT[:, 4:8, csl], in_=bv7[:, 0:512].rearrange("p (a b) -> p a b", b=128)),
                 reads=[tb[0]], writes=[t_o], partial=True)
        P.dma("sp", lambda e, l=l: e.dma_start(out=ssm_out[l], in_=hS), t_hS, reads=[t_hS])
        P.barrier()
        AR.reset(mS)
        if stage <= 3:
            break
        mT = AR.mark()
        mtri = AR.alloc([128], BF16)
        manti = AR.alloc([128], BF16)
        cmpmask = AR.alloc([T], BF16)
        eall = AR.alloc([T], BF16)
        kaug = AR.alloc([32, 128], BF16)
        qaug = AR.alloc([8, 128], BF16)
        ovc = AR.alloc([33], BF16)
        bonus = AR.alloc([NT, 32], F32)
        cap = AR.alloc([NT, 32], F32)
        idxb = AR.alloc([32], I32)
        t_ac = Tk("attconst")
        for dst, src in ((mtri, mtri_in), (manti, manti_in), (cmpmask, cmpmask_in), (eall, eall_in), (kaug, kaug_in),
                         (qaug, qaug_in), (ovc, ov_in)):
            P.dma("pool", lambda e, dst=dst, src=src: e.dma_start(out=dst, in_=src), t_ac, writes=[t_ac], partial=True)
        for dst, src in ((bonus, bonus_in), (cap, cap_in), (idxb, idxb_in)):
            P.dma("sp", lambda e, dst=dst, src=src: e.dma_start(out=dst, in_=src), t_ac, writes=[t_ac], partial=True)
        kcmpT = AR.alloc([2, 128], BF16)
        VC = AR.alloc([2, 97], BF16)
        t_kcmp, t_VC = Tk("kcmp"), Tk("VC")
        mC = AR.mark()
        w1d = AR.alloc([2, 32, 128], BF16)
        w2k = AR.alloc([128], BF16)
        w2v = AR.alloc([64], BF16)
        cb1 = AR.alloc([2], F32)
        t_cw = Tk("cmpw")
        for kv in range(2):
            P.dma("pool", lambda e, l=l, kv=kv: e.dma_start(out=w1d[:, kv, :, :], in_=w1d_in[l, kv]), t_cw, writes=[t_cw], partial=True)
        P.dma("pool", lambda e, l=l: e.dma_start(out=w2k, in_=w2k_in[l]), t_cw, writes=[t_cw], partial=True)
        P.dma("pool", lambda e, l=l: e.dma_start(out=w2v, in_=w2v_in[l]), t_cw, writes=[t_cw], partial=True)
        P.dma("sp", lambda e, l=l: e.dma_start(out=cb1, in_=cb1_in[l]), t_cw, writes=[t_cw], partial=True)
        P.op("dve", lambda e: e.memset(kcmpT, 0.0), writes=[t_kcmp])
        P.op("dve", lambda e: e.memset(VC, 0.0), writes=[t_VC])
        for g in range(2):
            P.op("dve", lambda e, g=g: e.tensor_copy(out=VC[:, g, 64:97], in_=ovc), reads=[t_ac], writes=[t_VC], partial=True)
        gx = AR.alloc([4, 128], F32)
        gu = AR.alloc([4, 128], F32)
        hid = AR.alloc([4, 128], BF16)
        t_gx, t_gu, t_hid = Tk("gx"), Tk("gu"), Tk("hid")
        RGB = ((0, 1), (2, 3))
        for kv in range(2):
            for g in range(2):
                i4 = kv * 2 + g
                ps = slice(g * 64, (g + 1) * 64)
                bk = RGB[g][kv]
                for j in range(32):
                    P.op("pe", lambda e, kv=kv, ps=ps, j=j, bk=bk: e.matmul(
                        banks[bk][:, 0:127], lhsT=w1d[ps, kv, j, :], rhs=kvcT[ps, kv, j:j + 16 * 126 + 1:16], start=(j == 0), stop=(j == 31)),
                        reads=[t_cw, t_kvc], writes=[tb[bk]], partial=(j > 0))
                n = slice(0, 127)
                P.op("act", lambda e, i4=i4, kv=kv, bk=bk: e.activation(out=gx[:, i4, 0:127], in_=banks[bk][:, 0:127], func=AF.Identity, bias=cb1[:, kv:kv + 1], scale=1.0),
                     reads=[tb[bk], t_cw], writes=[t_gx], partial=True)
        P.op("dve", lambda e: e.memset(gx[:, :, 127:128], 0.0), writes=[t_gx], partial=True)
        P.op("dve", lambda e: e.tensor_tensor(out=gu, in0=gx, in1=gx, op=ALU.mult), reads=[t_gx], writes=[t_gu])
        P.op("dve", lambda e: e.tensor_scalar(out=gu, in0=gu, scalar1=0.044715, scalar2=1.0, op0=ALU.mult, op1=ALU.add), reads=[t_gu], writes=[t_gu])
        P.op("dve", lambda e: e.tensor_tensor(out=gu, in0=gu, in1=gx, op=ALU.mult), reads=[t_gu, t_gx], writes=[t_gu])
        P.op("act", lambda e: e.activation(out=gu, in_=gu, func=AF.Sigmoid, scale=1.5957691216), reads=[t_gu], writes=[t_gu])
        P.op("dve", lambda e: e.tensor_tensor(out=hid, in0=gu, in1=gx, op=ALU.mult), reads=[t_gu, t_gx], writes=[t_hid])
        for g in range(2):
            P.op("pe", lambda e, g=g: e.matmul(banks[4][:, g * 128:g * 128 + 127], lhsT=w2k, rhs=hid[:, g, 0:127], start=True, stop=True),
                 reads=[t_cw, t_hid], writes=[tb[4]], partial=(g > 0))
        P.op("act", lambda e: e.copy(out=kcmpT[:, :, 0:127], in_=banks[4][:, 0:256].rearrange("p (a b) -> p a b", b=128)[:, :, 0:127]),
             reads=[tb[4]], writes=[t_kcmp], partial=True)
        for g in range(2):
            P.op("pe", lambda e, g=g: e.matmul(banks[5][0:127, g * 64:(g + 1) * 64], lhsT=hid[:, 2 + g, 0:127], rhs=w2v, start=True, stop=True),
                 reads=[t_cw, t_hid], writes=[tb[5]], partial=(g > 0))
        P.op("act", lambda e: e.copy(out=VC[0:127, :, 0:64], in_=banks[5][0:127, 0:128].rearrange("p (a b) -> p a b", b=64)),
             reads=[tb[5]], writes=[t_VC], partial=True)
        P.barrier()
        AR.reset(mC)
        PT = [AR.alloc([512], BF16) for _ in range(4)]
        t_PT = [Tk("PT%d" % i) for i in range(4)]
        otile = AR.alloc([512], F32)
        otb = AR.alloc([512], BF16)
        imp = AR.alloc([32], F32)
        sc = AR.alloc([32], F32)
        w2t = AR.alloc([32], F32)
        m8 = AR.alloc([16], F32)
        nsd = AR.alloc([96], F32)
        rr = AR.alloc([8], F32)
        nsT = [AR.alloc([2, 128], BF16) for _ in range(2)]
        t_ot, t_otb, t_imp, t_sc, t_m8, t_nsd, t_rr = (Tk(n) for n in ("ot", "otb", "imp", "sc", "m8", "nsd", "rr"))
        t_nsT = [Tk("nsT0"), Tk("nsT1")]
        P.op("dve", lambda e: e.memset(nsd, 0.0), writes=[t_nsd])
        sbank_next = [0, 0]
        accb = 0
        for qt in range(NT):
            qsl = slice(qt * 128, (qt + 1) * 128)
            nsq = nsT[qt % 2]
            t_nsq = t_nsT[qt % 2]
            for br in (0, 2, 1):
                for g in range(2):
                    ab = 4 + accb % 2
                    accb += 1
                    vw = 97 if br == 0 else 65
                    chunks = []
                    for j in range(4):
                        h = 4 * g + j
                        rg = h % 2
                        if br == 0:
                            blocks = [(0, "c")]
                        elif br == 1:
                            blocks = [(kt, "d" if kt == qt else "s") for kt in range(qt + 1)]
                        else:
                            blocks = []
                            for kt in range(max(0, qt - 4), qt + 1):
                                blocks.append((kt, "d" if kt == qt else ("a" if kt == qt - 4 else "n")))
                        for c0 in range(0, len(blocks), 4):
                            chunks.append((j, h, rg, blocks[c0:c0 + 4], c0 == 0))
                    pend = None

                    def emit_pv(ch, sbk, ab=ab, vw=vw, br=br, g=g):
                        j, h, rg, blks, first = ch
                        for bi, (kt, kind) in enumerate(blks):
                            if br == 0:
                                rhs = VC[:, g, :]
                            elif br == 1:
                                rhs = Vall[:, kt, g, :]
                            else:
                                rhs = Vall[:, kt, 2 + g, :]
                            P.op("pe", lambda e, ab=ab, j=j, vw=vw, sbk=sbk, bi=bi, rhs=rhs, st=(first and bi == 0): e.matmul(
                                banks[ab][:, j * 97:j * 97 + vw], lhsT=PT[sbk][:, bi * 128:(bi + 1) * 128], rhs=rhs, start=st, stop=False,
                                skip_group_check=True),
                                reads=[t_PT[sbk], t_V, t_VC], writes=[tb[ab]], partial=True)
                    for ch in chunks:
                        j, h, rg, blks, first = ch
                        sbk = RGB[rg][sbank_next[rg] % 2]
                        sbank_next[rg] += 1
                        rows = slice(rg * 64, (rg + 1) * 64)
                        arow = slice(rg * 64, rg * 64 + 4)
                        for bi, (kt, kind) in enumerate(blks):
                            cs_ = slice(bi * 128, (bi + 1) * 128)
                            ksl = slice(kt * 128, (kt + 1) * 128)
                            if br == 0:
                                lk = kcmpT[rows, g, :]
                                la = kaug[arow, 16 + qt, :]
                            elif br == 1:
                                lk = KsT[rows, g, ksl]
                                la = kaug[arow, qt - kt, :]
                            else:
                                lk = KwT[rows, g, ksl]
                                la = kaug[arow, qt - kt, :]
                            P.op("pe", lambda e, sbk=sbk, cs_=cs_, lk=lk, rows=rows, h=h, qsl=qsl: e.matmul(
                                banks[sbk][:, cs_], lhsT=lk, rhs=QT[rows, h // 2, qsl], start=True, stop=False, skip_group_check=True),
                                reads=[t_Q, t_Ks, t_Kw, t_kcmp], writes=[tb[sbk]], partial=(bi > 0))
                            extra = []
                            if kind == "c":
                                extra.append((identb, cmpmask[:, qsl], [t_const, t_ac]))
                            if br == 1 and qt >= 8:
                                erow = slice(rg * 64, rg * 64 + 32)
                                extra.append((eall[erow, ksl], nsq[erow, g, :], [t_ac, t_nsq]))
                            if kind == "d":
                                extra.append((identb, mtri, [t_const, t_ac]))
                            if kind == "a":
                                extra.append((identb, manti, [t_const, t_ac]))
                            P.op("pe", lambda e, sbk=sbk, cs_=cs_, la=la, arow=arow, h=h, last=(not extra): e.matmul(
                                banks[sbk][:, cs_], lhsT=la, rhs=qaug[arow, h, :], start=False, stop=last, skip_group_check=True),
                                reads=[t_ac], writes=[tb[sbk]], partial=True)
                            for xi, (lt_, rh_, rd_) in enumerate(extra):
                                P.op("pe", lambda e, sbk=sbk, cs_=cs_, lt_=lt_, rh_=rh_, last=(xi == len(extra) - 1): e.matmul(
                                    banks[sbk][:, cs_], lhsT=lt_, rhs=rh_, start=False, stop=last, skip_group_check=True),
                                    reads=rd_, writes=[tb[sbk]], partial=True)
                        nb_ = len(blks)
                        P.op("act", lambda e, sbk=sbk, nb_=nb_: e.activation(out=PT[sbk][:, 0:nb_ * 128], in_=banks[sbk][:, 0:nb_ * 128], func=AF.Exp),
                             reads=[tb[sbk]], writes=[t_PT[sbk]])
                        if pend is not None:
                            emit_pv(*pend)
                        pend = (ch, sbk)
                    emit_pv(*pend)
                    accv = banks[ab][:, 0:388].rearrange("p (a b) -> p a b", b=97)
                    P.op("dve", lambda e, accv=accv: e.tensor_scalar(out=rr[:, 0:4], in0=accv[:, :, 64], scalar1=1e-30, scalar2=None, op0=ALU.max),
                         reads=[tb[ab]], writes=[t_rr])
                    P.op("dve", lambda e: e.reciprocal(out=rr[:, 0:4], in_=rr[:, 0:4]), reads=[t_rr], writes=[t_rr])
                    P.op("dve", lambda e, qt=qt, br=br, g=g: e.tensor_tensor(out=rr[:, 4:8], in0=rr[:, 0:4], in1=gat[:, qt, br * 8 + g * 4: br * 8 + g * 4 + 4], op=ALU.mult),
                         reads=[t_rr, t_gat], writes=[t_rr], partial=True)
                    for j in range(4):
                        h = 4 * g + j
                        hs = slice(h * 64, (h + 1) * 64)
                        if br == 0:
                            P.op("dve", lambda e, accv=accv, j=j, hs=hs: e.tensor_scalar(out=otile[:, hs], in0=accv[:, j, 0:64], scalar1=rr[:, 4 + j:5 + j], scalar2=None, op0=ALU.mult),
                                 reads=[tb[ab], t_rr], writes=[t_ot], partial=True)
                            if qt < 8:
                                pass
                            elif j == 0:
                                P.op("dve", lambda e, accv=accv, j=j: e.tensor_scalar(out=imp, in0=accv[:, j, 65:97], scalar1=rr[:, j:j + 1], scalar2=None, op0=ALU.mult),
                                     reads=[tb[ab], t_rr], writes=[t_imp])
                            else:
                                P.op("dve", lambda e, accv=accv, j=j: e.scalar_tensor_tensor(out=imp, in0=accv[:, j, 65:97], scalar=rr[:, j:j + 1], in1=imp, op0=ALU.mult, op1=ALU.add),
                                     reads=[tb[ab], t_rr, t_imp], writes=[t_imp])
                        else:
                            P.op("dve", lambda e, accv=accv, j=j, hs=hs: e.scalar_tensor_tensor(out=otile[:, hs], in0=accv[:, j, 0:64], scalar=rr[:, 4 + j:5 + j], in1=otile[:, hs], op0=ALU.mult, op1=ALU.add),
                                 reads=[tb[ab], t_rr, t_ot], writes=[t_ot], partial=True)
                    if br == 0 and qt >= 8:
                        P.op("dve", lambda e: e.tensor_scalar(out=sc, in0=imp, scalar1=float(2.0 ** 64), scalar2=float(2.0 ** -60), op0=ALU.mult, op1=ALU.max),
                             reads=[t_imp], writes=[t_sc])
                        P.op("dve", lambda e, qt=qt: e.tensor_tensor(out=sc, in0=sc, in1=bonus[:, qt, :], op=ALU.add), reads=[t_sc, t_ac], writes=[t_sc])
                        sci = sc.bitcast(I32)
                        P.op("dve", lambda e, sci=sci: e.tensor_single_scalar(out=sci, in_=sci, scalar=-32, op=ALU.bitwise_and), reads=[t_sc], writes=[t_sc])
                        P.op("dve", lambda e, sci=sci: e.tensor_tensor(out=sci, in0=sci, in1=idxb, op=ALU.bitwise_or), reads=[t_sc, t_ac], writes=[t_sc])
                        P.op("dve", lambda e, qt=qt: e.tensor_tensor(out=sc, in0=sc, in1=cap[:, qt, :], op=ALU.min), reads=[t_sc, t_ac], writes=[t_sc])
                        P.op("dve", lambda e: e.max(out=m8[:, 0:8], in_=sc), reads=[t_sc], writes=[t_m8])
                        P.op("dve", lambda e: e.match_replace(out=w2t, in_to_replace=m8[:, 0:8], in_values=sc, imm_value=-3e38), reads=[t_sc, t_m8], writes=[t_m8], partial=True)
                        P.op("dve", lambda e: e.max(out=m8[:, 8:16], in_=w2t), reads=[t_m8], writes=[t_m8], partial=True)
                        P.op("dve", lambda e: e.tensor_scalar(out=w2t, in0=sc, scalar1=m8[:, 15:16], scalar2=None, op0=ALU.is_ge), reads=[t_sc, t_m8], writes=[t_m8], partial=True)
                        P.op("dve", lambda e: e.tensor_scalar(out=nsd[:, 0:32], in0=w2t, scalar1=1.0, scalar2=-NEGB, op0=ALU.subtract, op1=ALU.mult), reads=[t_m8], writes=[t_nsd], partial=True)
                        P.op("dve", lambda e: e.tensor_copy(out=nsd[:, 64:96], in_=nsd[:, 0:32]), reads=[t_nsd], writes=[t_nsd], partial=True)
                        P.op("pe", lambda e: e.transpose(out=banks[6][0:96, 0:128], in_=nsd, identity=identf), reads=[t_nsd, t_const], writes=[tb[6]])
                        P.op("act", lambda e, nsq=nsq, g=g: e.copy(out=nsq[0:96, g, :], in_=banks[6][0:96, 0:128]), reads=[tb[6]], writes=[t_nsq], partial=True)
            P.op("act", lambda e: e.copy(out=otb, in_=otile), reads=[t_ot], writes=[t_otb])
            bv6 = banks[7].bitcast(BF16)
            for e4 in range(4):
                P.op("pe", lambda e, e4=e4, bv6=bv6: e.transpose(out=bv6[:, e4 * 128:(e4 + 1) * 128], in_=otb[:, e4 * 128:(e4 + 1) * 128], identity=identb),
                     reads=[t_otb, t_const], writes=[tb[7]], partial=(e4 > 0))
            P.op("act", lambda e, bv6=bv6, qsl=qsl: e.copy(out=oT[:, 0:4, qsl], in_=bv6[:, 0:512].rearrange("p (a b) -> p a b", b=128)),
                 reads=[tb[7]], writes=[t_o], partial=True)
        P.barrier()
        AR.reset(mT)
        if stage <= 4:
            break
        AR.reset(layer_mark)
        idxall = AR.alloc([256], I32)
        augL = AR.alloc([2, 128], BF16)
        augR = AR.alloc([2, 296], BF16)
        mcs = AR.alloc([296], BF16)
        ovs = AR.alloc([4, 130], BF16)
        bonus_s = AR.alloc([129], F32)
        idx_s = AR.alloc([129], I32)
        oneo = AR.alloc([2, 128], BF16)
        t_sc2 = Tk("sconst")
        P.dma("sp", lambda e: e.dma_start(out=idxall, in_=ptb_in), t_sc2, writes=[t_sc2])
        pidx = AR.alloc([256], I32)
        P.dma("sp", lambda e: e.dma_start(out=pidx, in_=pidx_in), t_sc2, writes=[t_sc2], partial=True)
        P.op("dve", lambda e: e.tensor_single_scalar(out=idxall, in_=idxall, scalar=7, op=ALU.logical_shift_left), reads=[t_sc2], writes=[t_sc2], partial=True)
        P.op("dve", lambda e: e.tensor_tensor(out=idxall, in0=idxall, in1=pidx, op=ALU.bitwise_or), reads=[t_sc2], writes=[t_sc2], partial=True)
        for dst, src in ((augL, augL_in), (augR, augR_in), (mcs, mcs_in), (ovs, ovs_in), (oneo, oneo_in)):
            P.dma("pool", lambda e, dst=dst, src=src: e.dma_start(out=dst, in_=src), t_sc2, writes=[t_sc2], partial=True)
        P.dma("sp", lambda e: e.dma_start(out=bonus_s, in_=bonus_s_in), t_sc2, writes=[t_sc2], partial=True)
        P.dma("sp", lambda e: e.dma_start(out=idx_s, in_=idx_s_in), t_sc2, writes=[t_sc2], partial=True)
        w1ds = AR.alloc([2, 32, 128], BF16)
        w2ks = AR.alloc([128], BF16)
        w2vs = AR.alloc([64], BF16)
        cb1s = AR.alloc([2], F32)
        t_cws = Tk("cmpw_s")
        for kv in range(2):
            P.dma("pool", lambda e, l=l, kv=kv: e.dma_start(out=w1ds[:, kv, :, :], in_=w1d_in[l, kv]), t_cws, writes=[t_cws], partial=True)
        P.dma("pool", lambda e, l=l: e.dma_start(out=w2ks, in_=w2k_in[l]), t_cws, writes=[t_cws], partial=True)
        P.dma("pool", lambda e, l=l: e.dma_start(out=w2vs, in_=w2v_in[l]), t_cws, writes=[t_cws], partial=True)
        P.dma("sp", lambda e, l=l: e.dma_start(out=cb1s, in_=cb1_in[l]), t_cws, writes=[t_cws], partial=True)
        KTa = AR.alloc([8192], BF16)
        KTb = AR.alloc([8192], BF16)
        Vsl = AR.alloc([64, 2, 65], BF16)
        gst = [AR.alloc([8, 128], F32) for _ in range(4)]
        t_KTa, t_KTb, t_Vsl = Tk("KTa"), Tk("KTb"), Tk("Vsl")
        t_gst = [Tk("gst%d" % i) for i in range(4)]
        gxs = AR.alloc([4, 512], F32)
        gus = AR.alloc([4, 512], F32)
        hids = AR.alloc([4, 512], BF16)
        t_gxs, t_gus, t_hids = Tk("gxs"), Tk("gus"), Tk("hids")
        kcs = AR.alloc([512], BF16)
        VCs = AR.alloc([4, 2, 194], BF16)
        KTn = AR.alloc([2, 4, 128], BF16)
        KwTs = AR.alloc([512], BF16)
        kwst = AR.alloc([512], F32)
        Vw = AR.alloc([4, 2, 65], BF16)
        vwst = AR.alloc([4, 128], F32)
        PTc = AR.alloc([16], BF16)
        PT2 = AR.alloc([280], BF16)
        accs = AR.alloc([194], F32)
        r65 = AR.alloc([66], F32)
        rg2 = AR.alloc([4], F32)
        scs = AR.alloc([129], F32)
        wts = AR.alloc([129], F32)
        m8s = AR.alloc([16], F32)
        nss = AR.alloc([130], F32)
        nseo2 = [AR.alloc([2, 64, 4], BF16) for _ in range(2)]
        osum = AR.alloc([8, 128], F32)
        t_kcs, t_VCs, t_KTn, t_KwTs, t_kwst, t_Vw, t_vwst, t_PTc, t_PT2, t_accs, t_r65, t_scs, t_m8s, t_nss, t_nseo, t_osum = (Tk(n) for n in
            ("kcs", "VCs", "KTn", "KwTs", "kwst", "Vw", "vwst", "PTc", "PT2", "accs", "r65", "scs", "m8s", "nss", "nseo", "osum"))
        P.op("dve", lambda e: e.memset(KTn, 0.0), writes=[t_KTn])
        for b in range(4):
            P.op("dve", lambda e, b=b: e.tensor_copy(out=KTn[:, 0, b, 0:1], in_=kvnT[:, 2, b:b + 1]), reads=[t_kvn], writes=[t_KTn], partial=True)
            P.op("dve", lambda e, b=b: e.tensor_copy(out=KTn[:, 1, b, 0:1], in_=kvnT[:, 4, b:b + 1]), reads=[t_kvn], writes=[t_KTn], partial=True)
        P.op("dve", lambda e: e.memset(VCs, 0.0), writes=[t_VCs])
        for g in range(2):
            P.op("dve", lambda e, g=g: e.tensor_copy(out=VCs[:, :, g, 64:194], in_=ovs), reads=[t_sc2], writes=[t_VCs], partial=True)
        P.op("dve", lambda e: e.memset(Vsl, 1.0), writes=[t_Vsl])
        P.op("dve", lambda e: e.memset(Vw, 1.0), writes=[t_Vw])
        P.op("dve", lambda e: e.memset(gxs, 0.0), writes=[t_gxs])
        P.op("dve", lambda e: e.memset(r65, 0.0), writes=[t_r65])
        gi = [0]

        def gather(pool_ap, b, dstfn, t_dst, first_partial):
            for a8 in range(8):
                si = gi[0] % 4
                gi[0] += 1

                def issue(e, a8=a8, si=si):
                    return [e.indirect_dma_start(out=gst[si][:, a, :], out_offset=None, in_=pool_ap,
                                                 in_offset=bass.IndirectOffsetOnAxis(ap=idxall[:, b * 64 + a8 * 8 + a: b * 64 + a8 * 8 + a + 1], axis=0))
                            for a in range(8)]
                P.dma("pool", issue, t_gst[si], reads=[t_sc2], writes=[t_gst[si]], n=8)
                dstfn(a8, si)

        for b in range(4):
            gather(cmpkT_pool[l], b, lambda a8, si: P.op("act", lambda e, a8=a8, si=si: e.copy(out=KTa[:, a8 * 1024:(a8 + 1) * 1024], in_=gst[si].rearrange("p a t -> p (a t)")),
                                                         reads=[t_gst[si]], writes=[t_KTa], partial=True), t_KTa, False)
            gather(cmpvT_pool[l], b, lambda a8, si: P.op("act", lambda e, a8=a8, si=si: e.copy(out=KTb[:, a8 * 1024:(a8 + 1) * 1024], in_=gst[si].rearrange("p a t -> p (a t)")),
                                                         reads=[t_gst[si]], writes=[t_KTb], partial=True), t_KTb, False)
            for kv in range(2):
                KT = KTa if kv == 0 else KTb
                t_KT = t_KTa if kv == 0 else t_KTb
                for g in range(2):
                    i4 = kv * 2 + g
                    ps = slice(g * 64, (g + 1) * 64)
                    bk = RGB[g][kv]
                    for j in range(32):
                        P.op("pe", lambda e, kv=kv, ps=ps, j=j, bk=bk, KT=KT: e.matmul(
                            banks[bk][:, 0:511], lhsT=w1ds[ps, kv, j, :], rhs=KT[ps, j:j + 16 * 510 + 1:16], start=(j == 0), stop=(j == 31)),
                            reads=[t_cws, t_KT], writes=[tb[bk]], partial=(j > 0))
                    P.op("act", lambda e, i4=i4, kv=kv, bk=bk: e.activation(out=gxs[:, i4, 0:511], in_=banks[bk][:, 0:511], func=AF.Identity, bias=cb1s[:, kv:kv + 1], scale=1.0),
                         reads=[tb[bk], t_cws], writes=[t_gxs], partial=True)
            P.op("dve", lambda e: e.tensor_tensor(out=gus, in0=gxs, in1=gxs, op=ALU.mult), reads=[t_gxs], writes=[t_gus])
            P.op("dve", lambda e: e.tensor_scalar(out=gus, in0=gus, scalar1=0.044715, scalar2=1.0, op0=ALU.mult, op1=ALU.add), reads=[t_gus], writes=[t_gus])
            P.op("dve", lambda e: e.tensor_tensor(out=gus, in0=gus, in1=gxs, op=ALU.mult), reads=[t_gus, t_gxs], writes=[t_gus])
            P.op("act", lambda e: e.activation(out=gus, in_=gus, func=AF.Sigmoid, scale=1.5957691216), reads=[t_gus], writes=[t_gus])
            P.op("dve", lambda e: e.tensor_tensor(out=hids, in0=gus, in1=gxs, op=ALU.mult), reads=[t_gus, t_gxs], writes=[t_hids])
            for g in range(2):
                P.op("pe", lambda e, g=g: e.matmul(banks[4 + g][:, 0:512], lhsT=w2ks, rhs=hids[:, g, :], start=True, stop=True), reads=[t_cws, t_hids], writes=[tb[4 + g]])
                ps = slice(g * 64, (g + 1) * 64)
                P.op("act", lambda e, g=g, ps=ps: e.copy(out=kcs[ps, :], in_=banks[4 + g][ps, 0:512]), reads=[tb[4 + g]], writes=[t_kcs], partial=(g > 0))
            for g in range(2):
                for nt in range(4):
                    P.op("pe", lambda e, g=g, nt=nt: e.matmul(banks[6][:, (g * 4 + nt) * 64:(g * 4 + nt + 1) * 64], lhsT=hids[:, 2 + g, nt * 128:(nt + 1) * 128], rhs=w2vs, start=True, stop=True),
                         reads=[t_cws, t_hids], writes=[tb[6]], partial=(g + nt > 0))
            P.op("act", lambda e: e.copy(out=VCs[:, :, :, 0:64].rearrange("p t g d -> p g t d"), in_=banks[6].rearrange("p (g t d) -> p g t d", g=2, t=4)),
                 reads=[tb[6]], writes=[t_VCs], partial=True)
            for g in range(2):
                rows = slice(g * 64, (g + 1) * 64)
                tr = g * 64
                sb1 = RGB[g][0]
                P.op("pe", lambda e, tr=tr, g=g, sb1=sb1: e.matmul(banks[sb1][:, 0:16], lhsT=augL[tr:tr + 3, 1, :], rhs=augR[tr:tr + 3, g, 280:296], start=True, stop=False, skip_group_check=True),
                     reads=[t_sc2], writes=[tb[sb1]])
                P.op("pe", lambda e, sb1=sb1: e.matmul(banks[sb1][:, 0:16], lhsT=identb, rhs=mcs[:, 280:296], start=False, stop=False, skip_group_check=True),
                     reads=[t_sc2, t_const], writes=[tb[sb1]], partial=True)
                for nt in range(4):
                    P.op("pe", lambda e, rows=rows, nt=nt, sb1=sb1, b=b: e.matmul(banks[sb1][:, nt * 4:(nt + 1) * 4], lhsT=kcs[rows, nt * 128:(nt + 1) * 128], rhs=QgT[rows, :, b],
                                                                           start=False, stop=(nt == 3), skip_group_check=True), reads=[t_kcs, t_Qg], writes=[tb[sb1]], partial=True)
                P.op("act", lambda e, sb1=sb1: e.activation(out=PTc, in_=banks[sb1][:, 0:16], func=AF.Exp), reads=[tb[sb1]], writes=[t_PTc])
                for nt in range(4):
                    P.op("pe", lambda e, nt=nt, g=g: e.matmul(banks[7][0:4, 0:194], lhsT=PTc[:, nt * 4:(nt + 1) * 4], rhs=VCs[:, nt, g, :], start=(nt == 0), stop=(nt == 3)),
                         reads=[t_PTc, t_VCs], writes=[tb[7]], partial=(nt > 0))
                P.op("act", lambda e: e.copy(out=accs[0:4, :], in_=banks[7][0:4, 0:194]), reads=[tb[7]], writes=[t_accs])
                P.op("dve", lambda e: e.tensor_scalar(out=r65[0:4, 64:65], in0=accs[0:4, 64:65], scalar1=1e-30, scalar2=None, op0=ALU.max), reads=[t_accs], writes=[t_r65], partial=True)
                P.op("dve", lambda e: e.reciprocal(out=r65[0:4, 64:65], in_=r65[0:4, 64:65]), reads=[t_r65], writes=[t_r65], partial=True)
                P.op("dve", lambda e, g=g, b=b: e.tensor_tensor(out=r65[0:4, 65:66], in0=r65[0:4, 64:65], in1=gsT[0:4, (0 * 2 + g) * 4 + b:(0 * 2 + g) * 4 + b + 1], op=ALU.mult),
                     reads=[t_r65, t_gs], writes=[t_r65], partial=True)
                bg = b * 2 + g
                P.op("dve", lambda e, bg=bg: e.tensor_scalar(out=osum[0:4, bg, 0:64], in0=accs[0:4, 0:64], scalar1=r65[0:4, 65:66], scalar2=None, op0=ALU.mult),
                     reads=[t_accs, t_r65], writes=[t_osum], partial=True)
                lw = r65[0:4, 64:65] if g == 0 else r65[0:4, 0:65]
                npart = 1 if g == 0 else 65
                P.op("pe", lambda e, lw=lw, npart=npart: e.matmul(banks[6][0:npart, 0:129], lhsT=lw, rhs=accs[0:4, 65:194], start=True, stop=True),
                     reads=[t_r65, t_accs], writes=[tb[6]])
                tp = slice(tr, tr + 1)
                P.op("dve", lambda e, tp=tp: e.tensor_scalar(out=scs[tp, :], in0=banks[6][tp, 0:129], scalar1=float(2.0 ** 64), scalar2=float(2.0 ** -60), op0=ALU.mult, op1=ALU.max),
                     reads=[tb[6]], writes=[t_scs])
                P.op("dve", lambda e, tp=tp: e.tensor_tensor(out=scs[tp, :], in0=scs[tp, :], in1=bonus_s[tp, :], op=ALU.add), reads=[t_scs, t_sc2], writes=[t_scs])
                sci2 = scs.bitcast(I32)
                P.op("dve", lambda e, tp=tp, sci2=sci2: e.tensor_single_scalar(out=sci2[tp, :], in_=sci2[tp, :], scalar=-256, op=ALU.bitwise_and), reads=[t_scs], writes=[t_scs])
                P.op("dve", lambda e, tp=tp, sci2=sci2: e.tensor_tensor(out=sci2[tp, :], in0=sci2[tp, :], in1=idx_s[tp, :], op=ALU.bitwise_or), reads=[t_scs, t_sc2], writes=[t_scs])
                P.op("dve", lambda e, tp=tp: e.max(out=m8s[tp, 0:8], in_=scs[tp, :]), reads=[t_scs], writes=[t_m8s])
                P.op("dve", lambda e, tp=tp: e.match_replace(out=wts[tp, :], in_to_replace=m8s[tp, 0:8], in_values=scs[tp, :], imm_value=-3e38), reads=[t_scs, t_m8s], writes=[t_m8s], partial=True)
                P.op("dve", lambda e, tp=tp: e.max(out=m8s[tp, 8:16], in_=wts[tp, :]), reads=[t_m8s], writes=[t_m8s], partial=True)
                P.op("dve", lambda e, tp=tp: e.tensor_scalar(out=wts[tp, :], in0=scs[tp, :], scalar1=m8s[tp, 15:16], scalar2=None, op0=ALU.is_ge), reads=[t_scs, t_m8s], writes=[t_m8s], partial=True)
                P.op("dve", lambda e, tp=tp: e.tensor_scalar(out=nss[tp, 0:129], in0=wts[tp, :], scalar1=1.0, scalar2=-NEGB, op0=ALU.subtract, op1=ALU.mult), reads=[t_m8s], writes=[t_nss])
                nv = nss[tp, 0:128].rearrange("p (k two) -> p k two", two=2)
                for eo in range(2):
                    for j in range(4):
                        P.op("dve", lambda e, tp=tp, eo=eo, j=j, nv=nv, g=g: e.tensor_copy(out=nseo2[g][tp, eo, :, j], in_=nv[:, :, eo]), reads=[t_nss], writes=[t_nseo], partial=True)
                if g == 0:
                    gather(slckT_pool[l], b, lambda a8, si: P.op("act", lambda e, a8=a8, si=si: e.copy(out=KTa[:, a8 * 1024:(a8 + 1) * 1024], in_=gst[si].rearrange("p a t -> p (a t)")),
                                                                 reads=[t_gst[si]], writes=[t_KTa], partial=True), t_KTa, False)
                    gather(slcv_pool[l], b, lambda a8, si: P.op("act", lambda e, a8=a8, si=si: e.copy(out=Vsl[:, a8 * 8:(a8 + 1) * 8, :, 0:64], in_=gst[si].rearrange("p a (g d) -> p a g d", g=2)),
                                                                reads=[t_gst[si]], writes=[t_Vsl], partial=True), t_Vsl, False)
                    P.dma("sp", lambda e, l=l, b=b: e.dma_start(out=kwst, in_=wkT_in[l, b]), t_kwst, writes=[t_kwst])
                    P.op("act", lambda e: e.copy(out=KwTs, in_=kwst), reads=[t_kwst], writes=[t_KwTs])
                    P.dma("sp", lambda e, l=l, b=b: e.dma_start(out=vwst, in_=wv_in[l, b].rearrange("(t p) f -> p t f", p=128)), t_vwst, writes=[t_vwst])
                    P.op("act", lambda e: e.copy(out=Vw[:, :, :, 0:64], in_=vwst.rearrange("p t (g d) -> p t g d", g=2)), reads=[t_vwst], writes=[t_Vw], partial=True)
            for g in range(2):
                rows = slice(g * 64, (g + 1) * 64)
                tr = g * 64
                tp = slice(tr, tr + 1)
                sb2 = RGB[g][1]
                bg = b * 2 + g
                P.op("pe", lambda e, tr=tr, g=g, sb2=sb2: e.matmul(banks[sb2][:, 0:280], lhsT=augL[tr:tr + 3, 0, :], rhs=augR[tr:tr + 3, g, 0:280], start=True, stop=False, skip_group_check=True),
                     reads=[t_sc2], writes=[tb[sb2]])
                for eo in range(2):
                    P.op("pe", lambda e, tp=tp, eo=eo, sb2=sb2, g=g: e.matmul(banks[sb2][:, 0:256], lhsT=oneo[tp, eo, :], rhs=nseo2[g][tp, eo, :, :].rearrange("p k j -> p (k j)"), start=False, stop=False, skip_group_check=True),
                         reads=[t_sc2, t_nseo], writes=[tb[sb2]], partial=True)
                P.op("pe", lambda e, sb2=sb2: e.matmul(banks[sb2][:, 0:280], lhsT=identb, rhs=mcs[:, 0:280], start=False, stop=False, skip_group_check=True),
                     reads=[t_sc2, t_const], writes=[tb[sb2]], partial=True)
                for kt in range(70):
                    if kt < 64:
                        lk, rd = KTa[rows, kt * 128:(kt + 1) * 128], [t_KTa]
                    elif kt == 64:
                        lk, rd = KTn[rows, 0, b, :], [t_KTn]
                    elif kt < 69:
                        lk, rd = KwTs[rows, (kt - 65) * 128:(kt - 64) * 128], [t_KwTs]
                    else:
                        lk, rd = KTn[rows, 1, b, :], [t_KTn]
                    P.op("pe", lambda e, lk=lk, rows=rows, kt=kt, sb2=sb2, b=b: e.matmul(banks[sb2][:, kt * 4:(kt + 1) * 4], lhsT=lk, rhs=QgT[rows, :, b], start=False, stop=(kt == 69), skip_group_check=True),
                         reads=rd + [t_Qg], writes=[tb[sb2]], partial=True)
                P.op("act", lambda e, sb2=sb2: e.activation(out=PT2, in_=banks[sb2][:, 0:280], func=AF.Exp), reads=[tb[sb2]], writes=[t_PT2])
                for kt in range(65):
                    if kt < 64:
                        P.op("pe", lambda e, kt=kt, g=g: e.matmul(banks[7][0:4, 0:65], lhsT=PT2[:, kt * 4:(kt + 1) * 4], rhs=Vsl[:, kt, g, :], start=(kt == 0), stop=False, skip_group_check=True),
                             reads=[t_PT2, t_Vsl], writes=[tb[7]], partial=(kt > 0))
                    else:
                        P.op("pe", lambda e, g=g, b=b: e.matmul(banks[7][0:4, 0:65], lhsT=PT2[0:1, 256:260], rhs=VN[0:1, b, 0, g * 65:(g + 1) * 65], start=False, stop=True, skip_group_check=True),
                             reads=[t_PT2, t_VN], writes=[tb[7]], partial=True)
                for kt in range(5):
                    if kt < 4:
                        P.op("pe", lambda e, kt=kt, g=g: e.matmul(banks[7][0:4, 65:130], lhsT=PT2[:, 260 + kt * 4:264 + kt * 4], rhs=Vw[:, kt, g, :], start=False, stop=False, skip_group_check=True),
                             reads=[t_PT2, t_Vw], writes=[tb[7]], partial=True)
                    else:
                        P.op("pe", lambda e, g=g, b=b: e.matmul(banks[7][0:4, 65:130], lhsT=PT2[0:1, 276:280], rhs=VN[0:1, b, 1, g * 65:(g + 1) * 65], start=False, stop=True, skip_group_check=True),
                             reads=[t_PT2, t_VN], writes=[tb[7]], partial=True)
                P.op("act", lambda e: e.copy(out=accs[0:4, 0:130], in_=banks[7][0:4, 0:130]), reads=[tb[7]], writes=[t_accs])
                for br in (1, 2):
                    c0 = (br - 1) * 65
                    P.op("dve", lambda e, c0=c0: e.tensor_scalar(out=rg2[0:4, 0:1], in0=accs[0:4, c0 + 64:c0 + 65], scalar1=1e-30, scalar2=None, op0=ALU.max), reads=[t_accs], writes=[t_r65])
                    P.op("dve", lambda e: e.reciprocal(out=rg2[0:4, 0:1], in_=rg2[0:4, 0:1]), reads=[t_r65], writes=[t_r65])
                    P.op("dve", lambda e, br=br, g=g, b=b: e.tensor_tensor(out=rg2[0:4, 1:2], in0=rg2[0:4, 0:1], in1=gsT[0:4, (br * 2 + g) * 4 + b:(br * 2 + g) * 4 + b + 1], op=ALU.mult),
                         reads=[t_r65, t_gs], writes=[t_r65])
                    P.op("dve", lambda e, c0=c0, bg=bg: e.scalar_tensor_tensor(out=osum[0:4, bg, 0:64], in0=accs[0:4, c0:c0 + 64], scalar=rg2[0:4, 1:2], in1=osum[0:4, bg, 0:64], op0=ALU.mult, op1=ALU.add),
                         reads=[t_accs, t_r65, t_osum], writes=[t_osum], partial=True)
        P.op("dve", lambda e: e.tensor_copy(out=osum[0:4, :, 64:128], in_=osum[0:4, :, 0:64]), reads=[t_osum], writes=[t_osum], partial=True)
        for bg in range(8):
            b, g = bg // 2, bg % 2
            P.op("pe", lambda e, bg=bg: e.transpose(out=banks[6][:, bg * 4:(bg + 1) * 4], in_=osum[0:4, bg, :], identity=identf[0:4, 0:4]), reads=[t_osum, t_const], writes=[tb[6]], partial=(bg > 0))
        ot4 = banks[6][:, 0:32].rearrange("p (b g a r) -> p b g a r", b=4, g=2, a=2)
        for r in range(2):
            rs_ = slice(r * 64, (r + 1) * 64)
            P.op("act", lambda e, r=r, rs_=rs_, ot4=ot4: e.copy(out=osT[rs_, 0:4, :].rearrange("p (g a) b -> p b g a", g=2), in_=ot4[rs_, :, :, :, r]),
                 reads=[tb[6]], writes=[t_os], partial=True)
        P.barrier()
        AR.reset(layer_mark)
        AR.alloc([KD, T], F32)
        load_x(xsrc)
        mO = AR.mark()
        wo = AR.alloc([KD, D], BF16)
        t_wo = Tk("wo")
        for ec in range(KD):
            P.dma("pool", lambda e, l=l, ec=ec: e.dma_start(out=wo[:, ec, :], in_=wo_in[l, :, ec, :]), t_wo, writes=[t_wo], partial=True)
        nb = 0
        for dc in range(KD):
            for tg in range(4):
                ts = slice(tg * 512, (tg + 1) * 512)
                bk = nb % 4
                nb += 1
                for ec in range(KD):
                    P.op("pe", lambda e, bk=bk, ec=ec, dc=dc, ts=ts: e.matmul(
                        banks[bk], lhsT=wo[:, ec, dc * 128:(dc + 1) * 128], rhs=oT[:, ec, ts], start=(ec == 0), stop=(ec == KD - 1)),
                        reads=[t_wo, t_o], writes=[tb[bk]], partial=(ec > 0))
                P.op("dve", lambda e, bk=bk, dc=dc, ts=ts, l=l: e.scalar_tensor_tensor(
                    out=xT[:, dc, ts], in0=banks[bk], scalar=mods[:, l * 48 + 16 + dc, 0:1], in1=xT[:, dc, ts], op0=ALU.mult, op1=ALU.add),
                    reads=[tb[bk], t_mods, t_x], writes=[t_x], partial=True)
        smx = AR.alloc([4], F32)
        t_smx = Tk("smx")
        for dc in range(KD):
            bk = 4 + dc % 2
            for ec in range(KD):
                P.op("pe", lambda e, bk=bk, ec=ec, dc=dc: e.matmul(banks[bk][:, 0:4], lhsT=wo[:, ec, dc * 128:(dc + 1) * 128], rhs=osT[:, ec, :], start=(ec == 0), stop=(ec == KD - 1)),
                     reads=[t_wo, t_os], writes=[tb[bk]], partial=(ec > 0))
            P.op("dve", lambda e, bk=bk, dc=dc, l=l: e.tensor_tensor(out=smx, in0=banks[bk][:, 0:4], in1=mods[:, l * 48 + 16 + dc, 1:5], op=ALU.mult), reads=[tb[bk], t_mods], writes=[t_smx])
            P.op("dve", lambda e, dc=dc: e.tensor_tensor(out=xsT[:, dc, :], in0=xsT[:, dc, :], in1=smx, op=ALU.add), reads=[t_smx, t_xs], writes=[t_xs], partial=True)
        P.barrier()
        AR.reset(mO)
        norm_mod(A2, B2, [0, 1])
        P.barrier()
        sample_norm(l, 4, 3, n2g[l])
        P.barrier()
        fsp = AR.alloc([24, 3, 4], F32)
        fso = AR.alloc([24, 2, 4], F32)
        sact = AR.alloc([4, 4], BF16)
        sfa = AR.alloc([4], F32)
        t_fsp, t_fso, t_sact, t_sfa = Tk("fsp"), Tk("fso"), Tk("sact"), Tk("sfa")
        P.dma("sp", lambda e, l=l: e.dma_start(out=fsp[:, :, 0:2, :], in_=fst_in[l]), t_fsp, writes=[t_fsp])
        P.op("dve", lambda e: e.memset(fso, 0.0), writes=[t_fso])
        wup = [AR.alloc([KD, 2, 512], BF16) for _ in range(2)]
        wdn = [AR.alloc([4, D], BF16) for _ in range(2)]
        t_wup = [Tk("wup0"), Tk("wup1")]
        t_wdn = [Tk("wdn0"), Tk("wdn1")]
        fcw = AR.alloc([24, 3], F32)
        fcb = AR.alloc([24], F32)
        t_fc = Tk("fc")
        P.dma("sp", lambda e, l=l: e.dma_start(out=fcw, in_=fcw_in[l]), t_fc, writes=[t_fc], partial=True)
        P.dma("sp", lambda e, l=l: e.dma_start(out=fcb, in_=fcb_in[l]), t_fc, writes=[t_fc], partial=True)
        fst = [AR.alloc([514], F32) for _ in range(2)]
        t_fst = [Tk("fst0"), Tk("fst1")]
        hal = AR.alloc([4, 2], F32)
        t_hal = Tk("hal")
        fac = [AR.alloc([512], F32) for _ in range(2)]
        t_fac = [Tk("fac0"), Tk("fac1")]
        actT = [AR.alloc([4, 512], BF16) for _ in range(2)]
        t_act = [Tk("act0"), Tk("act1")]
        nbk = 0
        it = 0
        for fg in range(6):
            nf = 4 if fg < 5 else 2
            ws = fg % 2
            for kc in range(KD):
                P.dma("pool", lambda e, l=l, fg=fg, ws=ws, kc=kc: e.dma_start(out=wup[ws][:, kc, :, :], in_=wup_in[l, fg, :, kc, :, :]),
                      t_wup[ws], writes=[t_wup[ws]], partial=(kc > 0))
            for fcl in range(nf):
                P.dma("pool", lambda e, l=l, fg=fg, ws=ws, fcl=fcl: e.dma_start(out=wdn[ws][:, fcl, :], in_=wdn_in[l, :, fg * 4 + fcl, :]),
                      t_wdn[ws], writes=[t_wdn[ws]], partial=(fcl > 0))
            for tg in range(4):
                ts = slice(tg * 512, (tg + 1) * 512)
                asl = it % 2
                it += 1
                for fcl in range(nf):
                    fc = fg * 4 + fcl
                    bu = 2 + (nbk % 2) * 2
                    bv = bu + 1
                    nbk += 1
                    si = fcl % 2
                    for kc in range(KD):
                        P.op("pe", lambda e, bu=bu, kc=kc, ws=ws, fcl=fcl, ts=ts: e.matmul(
                            banks[bu], lhsT=wup[ws][:, kc, 0, fcl * 128:(fcl + 1) * 128], rhs=hT[:, kc, ts], start=(kc == 0), stop=(kc == KD - 1)),
                            reads=[t_wup[ws], t_h], writes=[tb[bu]], partial=(kc > 0))
                    for kc in range(KD):
                        P.op("pe", lambda e, bv=bv, kc=kc, ws=ws, fcl=fcl, ts=ts: e.matmul(
                            banks[bv], lhsT=wup[ws][:, kc, 1, fcl * 128:(fcl + 1) * 128], rhs=hT[:, kc, ts], start=(kc == 0), stop=(kc == KD - 1)),
                            reads=[t_wup[ws], t_h], writes=[tb[bv]], partial=(kc > 0))
                    if tg == 0:
                        P.op("dve", lambda e, si=si: e.memset(fst[si][:, 0:2], 0.0), writes=[t_fst[si]], partial=True)
                    else:
                        P.op("dve", lambda e, si=si, fcl=fcl: e.tensor_copy(out=fst[si][:, 0:2], in_=hal[:, fcl, :]), reads=[t_hal], writes=[t_fst[si]], partial=True)
                    P.op("act", lambda e, bu=bu, si=si: e.copy(out=fst[si][:, 2:514], in_=banks[bu]), reads=[tb[bu]], writes=[t_fst[si]], partial=True)
                    if tg < 3:
                        P.op("dve", lambda e, si=si, fcl=fcl: e.tensor_copy(out=hal[:, fcl, :], in_=fst[si][:, 512:514]), reads=[t_fst[si]], writes=[t_hal], partial=True)
                    else:
                        P.dma("sp", lambda e, l=l, si=si, fc=fc: e.dma_start(out=fconv_out[l, :, fc, :], in_=fst[si][:, 512:514]), t_fst[si], reads=[t_fst[si]])
                    P.op("dve", lambda e, si=si, fc=fc: e.tensor_scalar(out=fac[si], in0=fst[si][:, 2:514], scalar1=fcw[:, fc, 2:3], scalar2=fcb[:, fc:fc + 1], op0=ALU.mult, op1=ALU.add),
                         reads=[t_fst[si], t_fc], writes=[t_fac[si]])
                    for k in range(2):
                        P.op("dve", lambda e, si=si, fc=fc, k=k: e.scalar_tensor_tensor(out=fac[si], in0=fst[si][:, k:k + 512], scalar=fcw[:, fc, k:k + 1], in1=fac[si], op0=ALU.mult, op1=ALU.add),
                             reads=[t_fst[si], t_fc], writes=[t_fac[si]])
                    P.op("act", lambda e, si=si: e.activation(out=fac[si], in_=fac[si], func=AF.Silu), reads=[t_fac[si]], writes=[t_fac[si]])
                    P.op("dve", lambda e, si=si, bv=bv, asl=asl, fcl=fcl: e.tensor_tensor(out=actT[asl][:, fcl, :], in0=fac[si], in1=banks[bv], op=ALU.mult),
                         reads=[t_fac[si], tb[bv]], writes=[t_act[asl]], partial=(fcl > 0))
                for dc in range(KD):
                    bd = dc % 2
                    for fcl in range(nf):
                        P.op("pe", lambda e, bd=bd, ws=ws, fcl=fcl, dc=dc, asl=asl, nf=nf: e.matmul(
                            banks[bd], lhsT=wdn[ws][:, fcl, dc * 128:(dc + 1) * 128], rhs=actT[asl][:, fcl, :], start=(fcl == 0), stop=(fcl == nf - 1)),
                            reads=[t_wdn[ws], t_act[asl]], writes=[tb[bd]], partial=(fcl > 0))
                    P.op("dve", lambda e, bd=bd, dc=dc, ts=ts, l=l: e.scalar_tensor_tensor(
                        out=xT[:, dc, ts], in0=banks[bd], scalar=mods[:, l * 48 + 40 + dc, 0:1], in1=xT[:, dc, ts], op0=ALU.mult, op1=ALU.add),
                        reads=[tb[bd], t_mods, t_x], writes=[t_x], partial=True)
            for fcl in range(nf):
                fc = fg * 4 + fcl
                for half in range(2):
                    for kc in range(KD):
                        P.op("pe", lambda e, kc=kc, ws=ws, fcl=fcl, half=half: e.matmul(banks[6][:, half * 4:(half + 1) * 4], lhsT=wup[ws][:, kc, half, fcl * 128:(fcl + 1) * 128], rhs=hsT[:, kc, :],
                                                                                    start=(kc == 0), stop=(kc == KD - 1)), reads=[t_wup[ws], t_hs], writes=[tb[6]], partial=(half + kc > 0))
                P.op("act", lambda e, fc=fc: e.copy(out=fsp[:, fc, 2, :], in_=banks[6][:, 0:4]), reads=[tb[6]], writes=[t_fsp], partial=True)
                P.op("dve", lambda e, fc=fc: e.tensor_scalar(out=sfa, in0=fsp[:, fc, 2, :], scalar1=fcw[:, fc, 2:3], scalar2=fcb[:, fc:fc + 1], op0=ALU.mult, op1=ALU.add),
                     reads=[t_fsp, t_fc], writes=[t_sfa])
                for k in range(2):
                    P.op("dve", lambda e, fc=fc, k=k: e.scalar_tensor_tensor(out=sfa, in0=fsp[:, fc, k, :], scalar=fcw[:, fc, k:k + 1], in1=sfa, op0=ALU.mult, op1=ALU.add),
                         reads=[t_fsp, t_fc, t_sfa], writes=[t_sfa])
                P.op("act", lambda e: e.activation(out=sfa, in_=sfa, func=AF.Silu), reads=[t_sfa], writes=[t_sfa])
                P.op("dve", lambda e, fcl=fcl: e.tensor_tensor(out=sact[:, fcl, :], in0=sfa, in1=banks[6][:, 4:8], op=ALU.mult), reads=[t_sfa, tb[6]], writes=[t_sact], partial=(fcl > 0))
            for dc in range(KD):
                for fcl in range(nf):
                    P.op("pe", lambda e, ws=ws, fcl=fcl, dc=dc, nf=nf: e.matmul(banks[7][:, 0:4], lhsT=wdn[ws][:, fcl, dc * 128:(dc + 1) * 128], rhs=sact[:, fcl, :], start=(fcl == 0), stop=(fcl == nf - 1)),
                         reads=[t_wdn[ws], t_sact], writes=[tb[7]], partial=(fcl > 0))
                P.op("dve", lambda e, dc=dc, l=l: e.tensor_tensor(out=sfa, in0=banks[7][:, 0:4], in1=mods[:, l * 48 + 40 + dc, 1:5], op=ALU.mult), reads=[tb[7], t_mods], writes=[t_sfa])
                P.op("dve", lambda e, dc=dc: e.tensor_tensor(out=xsT[:, dc, :], in0=xsT[:, dc, :], in1=sfa, op=ALU.add), reads=[t_sfa, t_xs], writes=[t_xs], partial=True)
        P.barrier()
        P.op("dve", lambda e: e.tensor_copy(out=fso[:, 0:NFC, 0, :], in_=fsp[:, 0:NFC, 1, :]), reads=[t_fsp], writes=[t_fso], partial=True)
        P.op("dve", lambda e: e.tensor_copy(out=fso[:, 0:NFC, 1, :], in_=fsp[:, 0:NFC, 2, :]), reads=[t_fsp], writes=[t_fso], partial=True)
        P.dma("sp", lambda e, l=l: e.dma_start(out=fcs_out[l], in_=fso), t_fso, reads=[t_fso])
        if l < DEPTH - 1:
            for kc in range(KD):
                P.dma("sp", lambda e, kc=kc: e.dma_start(out=xscr[:, kc, :], in_=xT[:, kc, :]), t_x, reads=[t_x], writes=[t_xscr], partial=True)
        P.barrier()
    if stage >= 99:
        AR.reset(layer_mark)
        AR.alloc([KD, T], F32)
        sq = AR.alloc([KD, 512], BF16)
        rs = AR.alloc([512], F32)
        yst = [AR.alloc([512], F32) for _ in range(2)]
        t_sq, t_rs, t_yst = Tk("fsq"), Tk("frs"), [Tk("yst0"), Tk("yst1")]
        for tg in range(4):
            ts = slice(tg * 512, (tg + 1) * 512)
            bk = tg % 2
            P.op("act", lambda e, ts=ts: e.activation(out=sq, in_=xT[:, :, ts], func=AF.Square), reads=[t_x], writes=[t_sq])
            for kc in range(KD):
                P.op("pe", lambda e, kc=kc, bk=bk: e.matmul(banks[bk], lhsT=onesb, rhs=sq[:, kc, :], start=(kc == 0), stop=(kc == KD - 1)),
                     reads=[t_sq, t_const], writes=[tb[bk]], partial=(kc > 0))
            P.op("dve", lambda e, bk=bk: e.tensor_scalar(out=rs, in0=banks[bk], scalar1=1.0 / D, scalar2=EPS, op0=ALU.mult, op1=ALU.add),
                 reads=[tb[bk]], writes=[t_rs])
            P.op("act", lambda e: e.activation(out=rs, in_=rs, func=AF.Sqrt), reads=[t_rs], writes=[t_rs])
            P.op("dve", lambda e: e.reciprocal(out=rs, in_=rs), reads=[t_rs], writes=[t_rs])
            for kc in range(KD):
                yi = kc % 2
                P.op("dve", lambda e, kc=kc, ts=ts, yi=yi: e.scalar_tensor_tensor(
                    out=yst[yi], in0=xT[:, kc, ts], scalar=fing[:, kc:kc + 1], in1=rs, op0=ALU.mult, op1=ALU.mult),
                    reads=[t_x, t_rs, t_small], writes=[t_yst[yi]])
                P.dma("sp", lambda e, kc=kc, ts=ts, yi=yi: e.dma_start(out=yT_out[:, kc, ts], in_=yst[yi]), t_yst[yi], reads=[t_yst[yi]])
        st1 = AR.alloc([KD, 4], F32)
        srs = AR.alloc([4], F32)
        t_st1, t_srs = Tk("fst1"), Tk("fsrs")
        P.op("dve", lambda e: e.tensor_tensor(out=st1, in0=xsT, in1=xsT, op=ALU.mult), reads=[t_xs], writes=[t_st1])
        for kc in range(KD):
            P.op("pe", lambda e, kc=kc: e.matmul(banks[2][:, 0:4], lhsT=onesf, rhs=st1[:, kc, :], start=(kc == 0), stop=(kc == KD - 1)), reads=[t_st1, t_const], writes=[tb[2]], partial=(kc > 0))
        P.op("dve", lambda e: e.tensor_scalar(out=srs, in0=banks[2][:, 0:4], scalar1=1.0 / D, scalar2=EPS, op0=ALU.mult, op1=ALU.add), reads=[tb[2]], writes=[t_srs])
        P.op("act", lambda e: e.activation(out=srs, in_=srs, func=AF.Sqrt), reads=[t_srs], writes=[t_srs])
        P.op("dve", lambda e: e.reciprocal(out=srs, in_=srs), reads=[t_srs], writes=[t_srs])
        for kc in range(KD):
            P.op("dve", lambda e, kc=kc: e.scalar_tensor_tensor(out=st1[:, kc, :], in0=xsT[:, kc, :], scalar=fing[:, kc:kc + 1], in1=srs, op0=ALU.mult, op1=ALU.mult),
                 reads=[t_xs, t_srs, t_small], writes=[t_st1], partial=True)
        P.dma("sp", lambda e: e.dma_start(out=ysT_out, in_=st1), t_st1, reads=[t_st1])
    P.barrier()
    P.emit()
    return nc


_PROG_CACHE = {}


def _host_consts():
    r = np.arange(128)
    f32 = np.float32
    tok = np.arange(T)
    ncmp = np.arange(128)
    ends = 16 * ncmp + 31
    cmpmask = np.where((ends[:, None] > tok[None, :]) | (ncmp[:, None] >= 127), NEGB, 0.0).astype(f32)
    eall = np.zeros((128, T), f32)
    for base in (0, 64):
        eall[base + (tok // 64), tok] = 1.0
    slopes = 2.0 ** (-(np.arange(8) + 1.0))
    kaug = np.zeros((128, 32, 128), f32)
    qaug = np.zeros((128, 8, 128), f32)
    for base in (0, 64):
        for dlt in range(16):
            kaug[base + 0, dlt, :] = -dlt
            kaug[base + 1, dlt, :] = r
            kaug[base + 2, dlt, :] = 1.0
        for qt in range(16):
            kaug[base + 0, 16 + qt, :] = ends // 128 - qt
            kaug[base + 1, 16 + qt, :] = ends % 128
            kaug[base + 2, 16 + qt, :] = 1.0
        for h in range(8):
            qaug[base + 0, h, :] = 128.0 * slopes[h]
            qaug[base + 1, h, :] = slopes[h]
            qaug[base + 2, h, :] = -slopes[h] * r
    ov = np.zeros((128, 33), f32)
    ov[:, 0] = 1.0
    cs_ = 16 * ncmp
    ss_ = 64 * np.arange(32)
    ov[:, 1:] = ((cs_[:, None] < ss_[None, :] + 64) & (cs_[:, None] + 32 > ss_[None, :])).astype(f32)
    ov[127, 1:] = 0.0
    qpos = (np.arange(NT)[None, :] * 128 + r[:, None])
    cur = qpos // 64
    jj = np.arange(32)[None, None, :]
    forced = (jj == 0) | (jj == cur[:, :, None]) | (jj == cur[:, :, None] - 1)
    valid = (jj * 64) <= qpos[:, :, None]
    bonus = np.where(forced, 1e6 * 2.0 ** 64, 0.0).astype(f32)
    cap = np.where(valid, 3e38, -1e30).astype(f32)
    idxb = np.broadcast_to((31 - np.arange(32)).astype(np.int32)[None, :], (128, 32)).copy()
    augL = np.zeros((128, 2, 128), f32)
    augR = np.zeros((128, 2, 296), f32)
    A_t = np.array([kt - 64 for kt in range(64)] + [0] + [kt - 4 for kt in range(4)] + [0], f32)
    for base in (0, 64):
        augL[base + 0, 0, :] = r
        augL[base + 1, 0, :] = 1.0
        augL[base + 2, 0, :] = 1.0
        augL[base + 0, 1, :] = 16.0 * r
        augL[base + 1, 1, :] = 1.0
        augL[base + 2, 1, :] = 1.0
        for g in range(2):
            for j in range(4):
                sj = slopes[4 * g + j]
                augR[base + 0, g, j:280:4] = sj
                augR[base + 1, g, j:280:4] = 128.0 * sj * A_t
                augR[base + 0, g, 280 + j:296:4] = sj
                augR[base + 1, g, 280 + j:296:4] = 31.0 * sj
                augR[base + 2, g, 280 + j:296:4] = 2048.0 * sj * (np.arange(4) - 4)
    mcs = np.zeros((128, 296), f32)
    mcs[1:, 64 * 4:65 * 4] = NEGB
    mcs[1:, 69 * 4:70 * 4] = NEGB
    mcs[127, 280 + 12:296] = NEGB
    ns = np.arange(512)
    jb = np.arange(129)
    ovl = ((16 * ns[:, None] < 64 * jb[None, :] + 64) & (16 * ns[:, None] + 32 > 64 * jb[None, :])).astype(f32)
    ovl[511, :] = 0.0
    ovs = np.zeros((128, 4, 130), f32)
    ovs[:, :, 0] = 1.0
    ovs[:, :, 1:] = ovl.reshape(4, 128, 129).transpose(1, 0, 2)
    bonus_s = np.zeros((128, 129), f32)
    bonus_s[:, [0, 127, 128]] = 1e6 * 2.0 ** 64
    idx_s = np.broadcast_to((255 - jb).astype(np.int32)[None, :], (128, 129)).copy()
    oneo = np.zeros((128, 2, 128), f32)
    oneo[:, 0, :64] = 1.0
    oneo[:, 1, 64:] = 1.0
    pidx = np.broadcast_to(np.arange(128, dtype=np.int32)[:, None], (128, 256)).copy()
    return {"augL": augL, "augR": augR, "mcs": mcs, "ovs": ovs, "bonus_s": bonus_s, "idx_s": idx_s, "oneo": oneo, "pidx": pidx,
            "ident": np.eye(128, dtype=np.float32),
            "mtri": np.where(r[:, None] > r[None, :], NEGB, 0.0).astype(f32),
            "manti": np.where(r[None, :] > r[:, None], NEGB, 0.0).astype(f32),
            "cmpmask": cmpmask, "eall": eall, "kaug": kaug, "qaug": qaug, "ov": ov,
            "bonus": bonus, "cap": cap, "idxb": idxb,
            "utri": (r[:, None] <= r[None, :]).astype(np.float32),
            "mtrif": np.where(r[:, None] > r[None, :], NEGB, 0.0).astype(np.float32)}


def kernel(**inp):
    f32 = np.float32
    nc = _PROG_CACHE.get("nc")
    if nc is None:
        nc = build_program()
        _PROG_CACHE["nc"] = nc
    shared = dict(_host_consts())
    ada_w = np.asarray(inp["ada_w"], f32)
    shared["adaw"] = np.ascontiguousarray(
        ada_w.reshape(DEPTH, KD, 128, 6, 1024).transpose(0, 3, 2, 1, 4))
    shared["adab"] = np.ascontiguousarray(np.asarray(inp["ada_b"], f32).reshape(DEPTH, 48, 128).transpose(0, 2, 1))
    shared["n1g"] = np.stack([_col(np.asarray(inp["norm1_g"], f32)[l], KD) for l in range(DEPTH)])
    shared["n2g"] = np.stack([_col(np.asarray(inp["norm2_g"], f32)[l], KD) for l in range(DEPTH)])
    shared["fing"] = _col(np.asarray(inp["final_g"], f32), KD)
    w_in = np.asarray(inp["w_in"], f32)
    tm_cols = np.concatenate([np.arange(512, 1816), np.arange(2584, 2592)])
    shared["wtm"] = np.stack([_lay_kc(w_in[l][:, tm_cols]) for l in range(DEPTH)])
    fm_chunks = [np.arange(128 * i, 128 * i + 128) for i in range(4)]
    fm_chunks += [np.arange(512, 640), np.arange(640, 768)]
    fm_chunks += [np.concatenate([np.arange(768 + 64 * g, 832 + 64 * g)] * 2) for g in range(2)]
    fm_chunks += [np.concatenate([np.arange(1024 + 64 * g, 1088 + 64 * g)] * 2) for g in range(2)]
    fm_chunks += [np.arange(1816 + 128 * c, 1816 + 128 * c + 128) for c in range(6)]
    fm_chunks += [np.concatenate([np.arange(64 * j, 64 * j + 64), np.arange(256 + 64 * j, 320 + 64 * j)]) for j in range(4)]
    shared["wfm"] = np.stack([np.stack([_lay_kc(w_in[l][:, ch]) for ch in fm_chunks]) for l in range(DEPTH)])
    scw = np.asarray(inp["ssm_conv_w"], f32)
    shared["scw"] = np.ascontiguousarray(scw.reshape(DEPTH, 4, 6, 128).transpose(0, 3, 2, 1))
    rep = lambda v: np.ascontiguousarray(np.broadcast_to(np.asarray(v, f32)[:, None, :], (DEPTH, 128, np.asarray(v).shape[-1])))
    shared["dtb"] = rep(inp["dt_bias"])
    shared["alog"] = rep(inp["a_log"])
    shared["dsk"] = rep(inp["d_skip"])
    shared["sng"] = rep(inp["ssm_norm_g"])
    shared["scb"] = np.ascontiguousarray(np.asarray(inp["ssm_conv_b"], f32).reshape(DEPTH, 6, 128).transpose(0, 2, 1))

    shared["wo"] = np.stack([_lay_kc(np.asarray(inp["w_out"], f32)[l]) for l in range(DEPTH)])
    wu = np.asarray(inp["ffn_w_up"], f32)
    wu = np.pad(wu.reshape(DEPTH, KD, 128, 2, D_FF), ((0, 0), (0, 0), (0, 0), (0, 0), (0, 3072 - D_FF)))
    shared["wup"] = np.ascontiguousarray(wu.reshape(DEPTH, KD, 128, 2, 6, 512).transpose(0, 4, 2, 1, 3, 5))
    wd = np.pad(np.asarray(inp["ffn_w_down"], f32), ((0, 0), (0, 3072 - D_FF), (0, 0)))
    shared["wdn"] = np.ascontiguousarray(wd.reshape(DEPTH, 24, 128, D).transpose(0, 2, 1, 3))
    fw = np.pad(np.asarray(inp["ffn_conv_w"], f32), ((0, 0), (0, 0), (0, 3072 - D_FF)))
    shared["fcw"] = np.ascontiguousarray(fw.reshape(DEPTH, 3, 24, 128).transpose(0, 3, 2, 1))
    fb = np.pad(np.asarray(inp["ffn_conv_b"], f32), ((0, 0), (0, 3072 - D_FF)))
    shared["fcb"] = np.ascontiguousarray(fb.reshape(DEPTH, 24, 128).transpose(0, 2, 1))
    w1 = np.stack([np.asarray(inp["cmpk_w1"], f32), np.asarray(inp["cmpv_w1"], f32)], axis=1)
    w1 = w1.transpose(0, 1, 3, 2, 4)
    shared["w1d"] = np.ascontiguousarray(np.concatenate([w1, w1], axis=2))
    w2k = np.asarray(inp["cmpk_w2"], f32)
    shared["w2k"] = np.ascontiguousarray(np.concatenate([w2k, w2k], axis=2))
    shared["w2v"] = np.ascontiguousarray(np.asarray(inp["cmpv_w2"], f32))
    shared["cb1"] = np.ascontiguousarray(np.stack([np.asarray(inp["cmpk_b1"], f32), np.asarray(inp["cmpv_b1"], f32)], axis=2))
    hc = np.stack([np.asarray(inp["dt_bias"], f32), np.asarray(inp["a_log"], f32), np.asarray(inp["d_skip"], f32)], axis=2)
    shared["hcol"] = np.ascontiguousarray(hc)
    dsk = np.asarray(inp["d_skip"], f32)
    shared["dskc"] = np.ascontiguousarray(np.repeat(dsk.reshape(DEPTH, 4, 2, 1), 64, axis=3).transpose(0, 2, 3, 1).reshape(DEPTH, 128, 4))
    shared["sngc"] = np.ascontiguousarray(np.asarray(inp["ssm_norm_g"], f32).reshape(DEPTH, 4, 128).transpose(0, 2, 1))
    ex = np.zeros((8, 4, 128), f32)
    for c in range(4):
        for hh in range(2):
            ex[2 * c + hh, c, hh * 64:(hh + 1) * 64] = 1.0
    shared["ex"] = ex
    def poolT(a):
        a = np.asarray(a, f32).reshape(DEPTH, -1, 128, 128)
        return np.ascontiguousarray(a.transpose(0, 1, 3, 2)).reshape(DEPTH, -1, 128)
    for nm, key in (("cmpkT_pool", "cache_cmp_k"), ("cmpvT_pool", "cache_cmp_v"), ("slckT_pool", "cache_slc_k")):
        pt_ = poolT(inp[key])
        for l in range(DEPTH):
            shared["%s%d" % (nm, l)] = pt_[l]
    sv_ = np.asarray(inp["cache_slc_v"], f32).reshape(DEPTH, -1, 128)
    for l in range(DEPTH):
        shared["slcv_pool%d" % l] = np.ascontiguousarray(sv_[l])
    page_table = np.asarray(inp["page_table"], np.int32)
    st_ffn = np.asarray(inp["state_ffn_conv"], f32)
    x_sample = np.asarray(inp["x_sample"], f32)
    st_conv = np.asarray(inp["state_ssm_conv"], f32)
    st_ssm = np.asarray(inp["state_ssm"], f32)
    st_wk = np.asarray(inp["state_win_k"], f32).reshape(DEPTH, 32, 512, 128)
    st_wv = np.asarray(inp["state_win_v"], f32).reshape(DEPTH, 32, 512, 128)
    x_prompt = np.asarray(inp["x_prompt"], f32)
    c_prompt = np.asarray(inp["c_prompt"], f32)
    c_sample = np.asarray(inp["c_sample"], f32)
    in_maps = []
    for i in range(NCORES):
        m = dict(shared)
        m["xT_in"] = np.ascontiguousarray(x_prompt[i].T.reshape(KD, 128, T).transpose(1, 0, 2))
        c5 = np.concatenate([c_prompt[i:i + 1], c_sample[4 * i:4 * i + 4]], axis=0)
        m["cT_in"] = np.ascontiguousarray(c5.T.reshape(KD, 128, 5).transpose(1, 0, 2))
        sb = slice(4 * i, 4 * i + 4)
        m["xsT_in"] = np.ascontiguousarray(x_sample[sb, 0, :].T.reshape(KD, 128, 4).transpose(1, 0, 2))
        m["cst_in"] = np.ascontiguousarray(st_conv[:, sb].reshape(DEPTH, 4, 3, 6, 128).transpose(0, 4, 3, 2, 1))
        m["hs0_in"] = np.ascontiguousarray(st_ssm[:, sb].reshape(DEPTH, 4, 4, 2, 64, 64).transpose(0, 3, 4, 2, 1, 5).reshape(DEPTH, 128, 4, 4, 64))
        m["ptb"] = np.ascontiguousarray(np.broadcast_to(page_table[sb].reshape(1, 256), (128, 256)))
        m["wkT_in"] = np.ascontiguousarray(st_wk[:, sb].transpose(0, 1, 3, 2))
        sf = np.pad(st_ffn[:, sb], ((0, 0), (0, 0), (0, 0), (0, 3072 - D_FF)))
        m["fst_in"] = np.ascontiguousarray(sf.reshape(DEPTH, 4, 2, 24, 128).transpose(0, 4, 3, 2, 1))
        m["wk_in"] = np.ascontiguousarray(st_wk[:, sb])
        m["wv_in"] = np.ascontiguousarray(st_wv[:, sb])
        in_maps.append(m)
    res = run_bass_kernel_spmd(nc, in_maps, core_ids=list(range(NCORES)))
    R = res.results
    B, S = 8, 32
    kv = np.stack([R[i]["kv_out"] for i in range(NCORES)], axis=1)
    kvp = [kv[..., 128 * j:128 * j + 128].reshape(DEPTH, B, T, 2, 64) for j in range(6)]
    yT = np.stack([R[i]["yT_out"] for i in range(NCORES)])
    y_prompt = np.ascontiguousarray(yT.transpose(0, 3, 2, 1).reshape(B, T, D))
    z = lambda *s: np.zeros(s, f32)
    kvn = np.stack([R[i]["kvn_out"] for i in range(NCORES)], axis=1)
    kvn = kvn.transpose(0, 1, 4, 3, 2).reshape(DEPTH, S, 6, 1, 2, 64)
    kvs = [np.ascontiguousarray(kvn[:, :, j]) for j in range(6)]
    wks = np.concatenate([R[i]["wk_out"] for i in range(NCORES)], axis=1).reshape(DEPTH, S, 512, 2, 64)
    wvs = np.concatenate([R[i]["wv_out"] for i in range(NCORES)], axis=1).reshape(DEPTH, S, 512, 2, 64)
    scs = np.stack([R[i]["sconvs_out"] for i in range(NCORES)], axis=1)
    sconv_s = np.ascontiguousarray(scs.transpose(0, 1, 5, 4, 3, 2).reshape(DEPTH, S, 3, 768))
    sss = np.stack([R[i]["ssms_out"] for i in range(NCORES)], axis=1)
    ssm_s = np.ascontiguousarray(sss.reshape(DEPTH, 8, 2, 64, 4, 4, 64).transpose(0, 1, 5, 4, 2, 3, 6).reshape(DEPTH, S, 8, 64, 64))
    fcs = np.stack([R[i]["fcs_out"] for i in range(NCORES)], axis=1)
    fconv_s = np.ascontiguousarray(fcs.transpose(0, 1, 5, 4, 3, 2).reshape(DEPTH, S, 2, 3072)[..., :D_FF])
    ysT = np.stack([R[i]["ysT_out"] for i in range(NCORES)])
    y_sample = np.ascontiguousarray(ysT.transpose(0, 3, 2, 1).reshape(S, 1, D))
    fconv = np.stack([R[i]["fconv_out"] for i in range(NCORES)], axis=1)
    fconv_p = np.ascontiguousarray(fconv.transpose(0, 1, 4, 3, 2).reshape(DEPTH, B, 2, D_FF))
    sconv = np.stack([R[i]["sconv_out"] for i in range(NCORES)], axis=1)
    sconv_p = np.ascontiguousarray(sconv.transpose(0, 1, 4, 3, 2).reshape(DEPTH, B, 3, 768))
    ssm = np.stack([R[i]["ssm_out"] for i in range(NCORES)], axis=1)
    ssm_p = np.ascontiguousarray(ssm.reshape(DEPTH, B, 2, 64, 4, 64).transpose(0, 1, 2, 4, 5, 3).reshape(DEPTH, B, 8, 64, 64))
    outs = (y_prompt, y_sample,
            kvp[0], kvs[0], kvp[1], kvs[1],
            kvp[2], kvs[2], kvp[3], kvs[3],
            np.ascontiguousarray(kvp[4][:, :, T - 512:]), wks,
            np.ascontiguousarray(kvp[5][:, :, T - 512:]), wvs,
            ssm_p, ssm_s,
            sconv_p, sconv_s,
            fconv_p, fconv_s)
    return outs
```

```python
import numpy as np
import concourse.bass as bass
import concourse.mybir as mybir
from concourse.bass_utils import run_bass_kernel_spmd

F32 = mybir.dt.float32
BF16 = mybir.dt.bfloat16
I32 = mybir.dt.int32
ALU = mybir.AluOpType
AF = mybir.ActivationFunctionType
AX = mybir.AxisListType

NCORES = 8
D = 1024
KD = 8
T = 2048
NT = 16
DEPTH = 2
IN_DIM = 2592
D_FF = 2816
NFC = 22
EPS = 1e-6
NEGB = -30000.0
STAGE = 99

SAME_ENGINE_SYNC = True


class Tk:
    __slots__ = ("name", "w", "r", "sem", "cnt", "excl")

    def __init__(self, name, excl=False):
        self.name = name
        self.excl = excl
        self.w = {}
        self.r = {}
        self.sem = None
        self.cnt = 0


class Prog:
    ENG = ("pe", "act", "dve", "pool", "sp")

    def __init__(self, nc):
        self.nc = nc
        self.ops = {e: [] for e in self.ENG}
        self.cnt = {e: 0 for e in self.ENG}
        self.known = {e: {} for e in self.ENG}
        self.esem = {e: nc.alloc_semaphore("sem_" + e) for e in self.ENG}
        self.needed = {e: set() for e in self.ENG}
        self.dsems = []

    def _dsem(self, t):
        if t.sem is None:
            t.sem = self.nc.alloc_semaphore("dsem_%d" % len(self.dsems))
            self.dsems.append(t)
        return t.sem

    def _waits(self, eng, reads, writes, extra=()):
        need = {}
        for t in reads:
            for k, v in t.w.items():
                if need.get(k, 0) < v:
                    need[k] = v
        for t in writes:
            for k, v in t.w.items():
                if need.get(k, 0) < v:
                    need[k] = v
            for k, v in t.r.items():
                if need.get(k, 0) < v:
                    need[k] = v
        for k, v in extra:
            if need.get(k, 0) < v:
                need[k] = v
        out = []
        kn = self.known[eng]
        for k, v in need.items():
            if isinstance(k, str) and k == eng and (eng == "pe" or not SAME_ENGINE_SYNC):
                continue
            if kn.get(k, 0) >= v:
                continue
            kn[k] = v
            if isinstance(k, str):
                self.needed[k].add(v)
            out.append((k, v))
        return out

    def _mark(self, ev, reads, writes, partial):
        k, v = ev
        for t in reads:
            if t.r.get(k, 0) < v:
                t.r[k] = v
        for t in writes:
            if partial:
                if t.w.get(k, 0) < v:
                    t.w[k] = v
            else:
                t.w = {k: v}
                t.r = {}

    def op(self, eng, fn, reads=(), writes=(), partial=False):
        ex = [t for t in reads if t.excl]
        waits = self._waits(eng, reads, list(writes) + ex)
        self.cnt[eng] += 1
        ev = (eng, self.cnt[eng])
        self.ops[eng].append((waits, fn, ("E", eng, self.cnt[eng])))
        self._mark(ev, [t for t in reads if not t.excl], writes, partial)
        if ex:
            self._mark(ev, (), ex, True)

    def dma(self, eng, fn, semt, reads=(), writes=(), n=1, partial=False):
        sem = self._dsem(semt)
        extra = [(sem, semt.cnt)] if semt.cnt else []
        waits = self._waits(eng, reads, writes, extra)
        semt.cnt += 16 * n
        self.ops[eng].append((waits, fn, ("D", sem, 16)))
        self._mark((sem, semt.cnt), reads, writes, partial)

    def barrier(self):
        for e in self.ENG:
            waits = []
            kn = self.known[e]
            for k in self.ENG:
                if k != e and self.cnt[k] > kn.get(k, 0):
                    kn[k] = self.cnt[k]
                    self.needed[k].add(self.cnt[k])
                    waits.append((k, self.cnt[k]))
            for t in self.dsems:
                if t.cnt > kn.get(t.sem, 0):
                    kn[t.sem] = t.cnt
                    waits.append((t.sem, t.cnt))
            if waits:
                self.ops[e].append((waits, None, None))

    def emit(self):
        nc = self.nc
        engobj = {"pe": "tensor", "act": "scalar", "dve": "vector", "pool": "gpsimd", "sp": "sync"}
        rank = {e: {raw: i + 1 for i, raw in enumerate(sorted(self.needed[e]))} for e in self.ENG}
        esem, needed = self.esem, self.needed
        with nc.Block() as block:
            for e in self.ENG:
                ops = self.ops[e]
                if not ops:
                    continue

                def body(eng, ops=ops):
                    for waits, fn, inc in ops:
                        for k, v in waits:
                            if isinstance(k, str):
                                eng.wait_ge(esem[k], rank[k][v])
                            else:
                                eng.wait_ge(k, v)
                        if fn is None:
                            continue
                        ins = fn(eng)
                        if inc[0] == "E":
                            if inc[2] in needed[inc[1]]:
                                ins.then_inc(esem[inc[1]], 1)
                        elif isinstance(ins, (list, tuple)):
                            for i in ins:
                                i.then_inc(inc[1], inc[2])
                        else:
                            ins.then_inc(inc[1], inc[2])
                getattr(block, engobj[e])(body)


class Arena:
    def __init__(self, nc, nbytes):
        self.base = nc.alloc_sbuf_tensor("arena", [128, nbytes // 4], F32).ap()
        self.nbytes = nbytes
        self.off = 0
        self.marks = []

    def alloc(self, shape, dtype, parts=128):
        esz = 4 if dtype in (F32, I32) else 2
        n = 1
        for s in shape:
            n *= s
        nb = (n * esz + 63) // 64 * 64
        assert self.off + nb <= self.nbytes, ("arena overflow", self.off, nb, self.nbytes)
        v = self.base[0:parts, self.off // 4:(self.off + nb) // 4]
        if dtype != F32:
            v = v.bitcast(dtype)
        v = v[:, 0:n]
        self.off += nb
        if len(shape) == 2:
            v = v.rearrange("p (a b) -> p a b", b=shape[1])
        elif len(shape) == 3:
            v = v.rearrange("p (a b c) -> p a b c", b=shape[1], c=shape[2])
        return v

    def mark(self):
        return self.off

    def reset(self, m):
        self.off = m


def _lay_kc(w, kc=KD):
    n = w.shape[1]
    return np.ascontiguousarray(w.reshape(kc, 128, n).transpose(1, 0, 2))


def _col(v, nch):
    return np.ascontiguousarray(v.reshape(nch, 128).T)


def build_program(stage=STAGE):
    nc = bass.Bass("TRN2", target_bir_lowering=False)
    P = Prog(nc)
    dr = {}

    def din(name, shape, dt=F32):
        dr[name] = nc.dram_tensor(name, list(shape), dt, kind="ExternalInput").ap()
        return dr[name]

    def dout(name, shape, dt=F32):
        dr[name] = nc.dram_tensor(name, list(shape), dt, kind="ExternalOutput").ap()
        return dr[name]

    xT_in = din("xT_in", [128, KD, T])
    cT_in = din("cT_in", [128, KD, 5])
    adaw_in = din("adaw", [DEPTH, 6, 128, KD, 1024])
    adab_in = din("adab", [DEPTH, 128, 48])
    n1g_in = din("n1g", [DEPTH, 128, KD])
    n2g_in = din("n2g", [DEPTH, 128, KD])
    fing_in = din("fing", [128, KD])
    wtm_in = din("wtm", [DEPTH, 128, KD, 1312])
    wfm_in = din("wfm", [DEPTH, 20, 128, KD, 128])
    scw_in = din("scw", [DEPTH, 128, 6, 4])
    scb_in = din("scb", [DEPTH, 128, 6])
    ident_in = din("ident", [128, 128])
    dtb_in = din("dtb", [DEPTH, 128, 8])
    alog_in = din("alog", [DEPTH, 128, 8])
    dsk_in = din("dsk", [DEPTH, 128, 8])
    sng_in = din("sng", [DEPTH, 128, 512])
    utri_in = din("utri", [128, 128])
    mtrif_in = din("mtrif", [128, 128])
    ssm_out = dout("ssm_out", [DEPTH, 128, 4, 64])
    NPOOL = 2560
    ptb_in = din("ptb", [128, 256], I32)
    pidx_in = din("pidx", [128, 256], I32)
    cmpkT_pool = [din("cmpkT_pool%d" % l, [NPOOL * 128, 128]) for l in range(DEPTH)]
    cmpvT_pool = [din("cmpvT_pool%d" % l, [NPOOL * 128, 128]) for l in range(DEPTH)]
    slckT_pool = [din("slckT_pool%d" % l, [NPOOL * 128, 128]) for l in range(DEPTH)]
    slcv_pool = [din("slcv_pool%d" % l, [NPOOL * 128, 128]) for l in range(DEPTH)]
    wkT_in = din("wkT_in", [DEPTH, 4, 128, 512])
    augL_in = din("augL", [128, 2, 128])
    augR_in = din("augR", [128, 2, 296])
    mcs_in = din("mcs", [128, 296])
    ovs_in = din("ovs", [128, 4, 130])
    bonus_s_in = din("bonus_s", [128, 129])
    idx_s_in = din("idx_s", [128, 129], I32)
    oneo_in = din("oneo", [128, 2, 128])
    fst_in = din("fst_in", [DEPTH, 128, 24, 2, 4])
    fcs_out = dout("fcs_out", [DEPTH, 128, 24, 2, 4])
    ysT_out = dout("ysT_out", [128, KD, 4])
    xsT_in = din("xsT_in", [128, KD, 4])
    cst_in = din("cst_in", [DEPTH, 128, 6, 3, 4])
    hs0_in = din("hs0_in", [DEPTH, 128, 4, 4, 64])
    hcol_in = din("hcol", [DEPTH, 8, 3])
    dskc_in = din("dskc", [DEPTH, 128, 4])
    sngc_in = din("sngc", [DEPTH, 128, 4])
    ex_in = din("ex", [8, 4, 128])
    wk_in = din("wk_in", [DEPTH, 4, 512, 128])
    wv_in = din("wv_in", [DEPTH, 4, 512, 128])
    wk_out = dout("wk_out", [DEPTH, 4, 512, 128])
    wv_out = dout("wv_out", [DEPTH, 4, 512, 128])
    kvn_out = dout("kvn_out", [DEPTH, 128, 6, 4])
    sconvs_out = dout("sconvs_out", [DEPTH, 128, 6, 3, 4])
    ssms_out = dout("ssms_out", [DEPTH, 128, 4, 4, 64])
    wo_in = din("wo", [DEPTH, 128, KD, D])
    wup_in = din("wup", [DEPTH, 6, 128, KD, 2, 512])
    wdn_in = din("wdn", [DEPTH, 128, 24, D])
    fcw_in = din("fcw", [DEPTH, 128, 24, 3])
    fcb_in = din("fcb", [DEPTH, 128, 24])
    fconv_out = dout("fconv_out", [DEPTH, 128, NFC, 2])
    w1d_in = din("w1d", [DEPTH, 2, 128, 32, 128])
    w2k_in = din("w2k", [DEPTH, 128, 128])
    w2v_in = din("w2v", [DEPTH, 128, 64])
    cb1_in = din("cb1", [DEPTH, 128, 2])
    mtri_in = din("mtri", [128, 128])
    manti_in = din("manti", [128, 128])
    cmpmask_in = din("cmpmask", [128, T])
    eall_in = din("eall", [128, T])
    kaug_in = din("kaug", [128, 32, 128])
    qaug_in = din("qaug", [128, 8, 128])
    ov_in = din("ov", [128, 33])
    bonus_in = din("bonus", [128, NT, 32])
    cap_in = din("cap", [128, NT, 32])
    idxb_in = din("idxb", [128, 32], I32)

    kv_out = dout("kv_out", [DEPTH, T, 768])
    sconv_out = dout("sconv_out", [DEPTH, 128, 6, 3])
    yT_out = dout("yT_out", [128, KD, T])

    banks = [nc.alloc_psum_tensor("bank%d" % i, [128, 512], F32).ap() for i in range(8)]
    tb = [Tk("bank%d" % i, excl=True) for i in range(8)]

    AR = Arena(nc, 190 * 1024)
    identf = AR.alloc([128], F32)
    identb = AR.alloc([128], BF16)
    onesb = AR.alloc([128], BF16)
    t_const = Tk("const")
    P.dma("sp", lambda e: e.dma_start(out=identf, in_=ident_in), t_const, writes=[t_const])
    P.op("dve", lambda e: e.tensor_copy(out=identb, in_=identf), reads=[t_const], writes=[t_const], partial=True)
    P.op("dve", lambda e: e.memset(onesb, 1.0), writes=[t_const], partial=True)

    utri = AR.alloc([128], F32)
    mtrif = AR.alloc([128], F32)
    onesf = AR.alloc([128], F32)
    P.dma("sp", lambda e: e.dma_start(out=utri, in_=utri_in), t_const, writes=[t_const], partial=True)
    P.dma("sp", lambda e: e.dma_start(out=mtrif, in_=mtrif_in), t_const, writes=[t_const], partial=True)
    P.op("dve", lambda e: e.memset(onesf, 1.0), writes=[t_const], partial=True)
    mods = AR.alloc([DEPTH * 48, 5], F32)
    t_mods = Tk("mods")
    smallv = AR.alloc([64], F32)
    t_small = Tk("small")
    n1g = [smallv[:, 0 + 8 * l: 8 + 8 * l] for l in range(DEPTH)]
    n2g = [smallv[:, 16 + 8 * l: 24 + 8 * l] for l in range(DEPTH)]
    fing = smallv[:, 32:40]
    for l in range(DEPTH):
        P.dma("sp", lambda e, l=l: e.dma_start(out=n1g[l], in_=n1g_in[l]), t_small, writes=[t_small], partial=True)
        P.dma("sp", lambda e, l=l: e.dma_start(out=n2g[l], in_=n2g_in[l]), t_small, writes=[t_small], partial=True)
    P.dma("sp", lambda e: e.dma_start(out=fing, in_=fing_in), t_small, writes=[t_small], partial=True)
    AB = AR.alloc([DEPTH * 2 * 2, 8], F32)
    t_AB = Tk("AB")
    base_mark = AR.mark()

    m0 = AR.mark()
    cT = AR.alloc([KD, 5], F32)
    cTb = AR.alloc([KD, 5], BF16)
    adab = AR.alloc([DEPTH, 48], F32)
    t_c = Tk("c")
    t_adab = Tk("adab")
    P.dma("sp", lambda e: e.dma_start(out=cT, in_=cT_in), t_c, writes=[t_c])
    P.op("act", lambda e: e.activation(out=cTb, in_=cT, func=AF.Silu), reads=[t_c], writes=[t_c], partial=True)
    for l in range(DEPTH):
        P.dma("sp", lambda e, l=l: e.dma_start(out=adab[:, l, :], in_=adab_in[l]), t_adab, writes=[t_adab], partial=True)
    wa = [AR.alloc([KD, 1024], BF16) for _ in range(2)]
    t_wa = [Tk("wa0"), Tk("wa1")]
    it = 0
    for l in range(DEPTH):
        bk = l % 2
        for s in range(6):
            sl = it % 2
            it += 1
            P.dma("pool", lambda e, l=l, s=s, sl=sl: e.dma_start(out=wa[sl], in_=adaw_in[l, s]), t_wa[sl], writes=[t_wa[sl]])
            for c8 in range(8):
                c = s * 8 + c8
                for kc in range(KD):
                    P.op("pe", lambda e, bk=bk, c=c, c8=c8, kc=kc, sl=sl: e.matmul(
                        banks[bk][:, c * 5:(c + 1) * 5], lhsT=wa[sl][:, kc, c8 * 128:(c8 + 1) * 128], rhs=cTb[:, kc, :],
                        start=(kc == 0), stop=(kc == KD - 1)), reads=[t_wa[sl], t_c], writes=[tb[bk]], partial=True)
        for r in range(5):
            P.op("dve", lambda e, l=l, r=r, bk=bk: e.tensor_tensor(
                out=mods[:, l * 48:(l + 1) * 48, r], in0=banks[bk][:, 0:240].rearrange("p (c r) -> p c r", r=5)[:, :, r],
                in1=adab[:, l, :], op=ALU.add), reads=[tb[bk], t_adab], writes=[t_mods], partial=True)
        for j, (gv, sci, shi) in enumerate(((n1g[l], 1, 0), (n2g[l], 4, 3))):
            P.op("dve", lambda e, l=l, j=j, gv=gv, sci=sci: e.scalar_tensor_tensor(
                out=AB[:, l * 4 + 2 * j, :], in0=mods[:, l * 48 + sci * 8: l * 48 + sci * 8 + 8, 0], scalar=1.0, in1=gv,
                op0=ALU.add, op1=ALU.mult), reads=[t_mods, t_small], writes=[t_AB], partial=True)
            P.op("dve", lambda e, l=l, j=j, shi=shi: e.tensor_copy(
                out=AB[:, l * 4 + 2 * j + 1, :], in_=mods[:, l * 48 + shi * 8: l * 48 + shi * 8 + 8, 0]),
                reads=[t_mods], writes=[t_AB], partial=True)
    P.barrier()
    AR.reset(m0)

    xsT = AR.alloc([KD, 4], F32)
    hsT = AR.alloc([KD, 4], BF16)
    osT = AR.alloc([KD, 4], BF16)
    kvnT = AR.alloc([6, 4], F32)
    QgT = AR.alloc([4, 4], BF16)
    xbcs = AR.alloc([6, 4], F32)
    zsT = AR.alloc([4, 4], F32)
    dd = AR.alloc([8], F32)
    gsT = AR.alloc([24], F32)
    VN = AR.alloc([4, 2, 130], BF16)
    exm = AR.alloc([4, 128], F32)
    t_xs, t_hs, t_os, t_kvn, t_Qg, t_xbcs, t_zsT, t_dd, t_gs, t_VN, t_ex = (Tk(n) for n in
        ("xs", "hs", "os", "kvn", "Qg", "xbcs", "zsT", "dd", "gs", "VN", "ex"))
    P.dma("sp", lambda e: e.dma_start(out=xsT, in_=xsT_in), t_xs, writes=[t_xs])
    P.dma("sp", lambda e: e.dma_start(out=exm[0:8, :, :], in_=ex_in), t_ex, writes=[t_ex])
    t_win = Tk("win")

    def sample_norm(l, sci, shi, gvec):
        m = AR.mark()
        t1 = AR.alloc([KD, 4], F32)
        As = AR.alloc([KD, 4], F32)
        rs4 = AR.alloc([4], F32)
        t_t1, t_As, t_rs4 = Tk("st1"), Tk("sAs"), Tk("srs")
        P.op("dve", lambda e: e.tensor_tensor(out=t1, in0=xsT, in1=xsT, op=ALU.mult), reads=[t_xs], writes=[t_t1])
        for kc in range(KD):
            P.op("pe", lambda e, kc=kc: e.matmul(banks[0][:, 0:4], lhsT=onesf, rhs=t1[:, kc, :], start=(kc == 0), stop=(kc == KD - 1)),
                 reads=[t_t1, t_const], writes=[tb[0]], partial=(kc > 0))
        P.op("dve", lambda e: e.tensor_scalar(out=rs4, in0=banks[0][:, 0:4], scalar1=1.0 / D, scalar2=EPS, op0=ALU.mult, op1=ALU.add),
             reads=[tb[0]], writes=[t_rs4])
        P.op("act", lambda e: e.activation(out=rs4, in_=rs4, func=AF.Sqrt), reads=[t_rs4], writes=[t_rs4])
        P.op("dve", lambda e: e.reciprocal(out=rs4, in_=rs4), reads=[t_rs4], writes=[t_rs4])
        for r in range(4):
            P.op("dve", lambda e, r=r: e.scalar_tensor_tensor(
                out=As[:, :, r], in0=mods[:, l * 48 + sci * 8: l * 48 + sci * 8 + 8, 1 + r], scalar=1.0, in1=gvec, op0=ALU.add, op1=ALU.mult),
                reads=[t_mods, t_small], writes=[t_As], partial=(r > 0))
        P.op("dve", lambda e: e.tensor_tensor(out=t1, in0=xsT, in1=As, op=ALU.mult), reads=[t_xs, t_As], writes=[t_t1])
        for kc in range(KD):
            P.op("dve", lambda e, kc=kc: e.tensor_tensor(out=t1[:, kc, :], in0=t1[:, kc, :], in1=rs4, op=ALU.mult), reads=[t_t1, t_rs4], writes=[t_t1], partial=True)
        P.op("dve", lambda e: e.tensor_tensor(out=hsT, in0=t1, in1=mods[:, l * 48 + shi * 8: l * 48 + shi * 8 + 8, 1:5], op=ALU.add),
             reads=[t_t1, t_mods], writes=[t_hs])
        AR.reset(m)

    hT = AR.alloc([KD, T], BF16)
    t_h = Tk("hT")
    layer_mark = AR.mark()
    xT = AR.alloc([KD, T], F32)
    t_x = Tk("xT")
    xscr = nc.dram_tensor("xscr", [128, KD, T], F32, kind="Internal").ap()
    t_xscr = Tk("xscr")

    def load_x(src):
        for kc in range(KD):
            P.dma("sp", lambda e, kc=kc, src=src: e.dma_start(out=xT[:, kc, :], in_=src[:, kc, :]), t_x,
                  reads=[t_xscr], writes=[t_x], partial=True)

    def norm_mod(Acol, Bcol, scratch_banks):
        m = AR.mark()
        sq = AR.alloc([KD, 512], BF16)
        rs = AR.alloc([512], F32)
        tmp = AR.alloc([2, 512], F32)
        t_sq, t_rs, t_tmp = Tk("sq"), Tk("rs"), [Tk("tmp0"), Tk("tmp1")]
        for tg in range(4):
            ts = slice(tg * 512, (tg + 1) * 512)
            bk = scratch_banks[tg % len(scratch_banks)]
            P.op("act", lambda e, ts=ts: e.activation(out=sq, in_=xT[:, :, ts], func=AF.Square), reads=[t_x], writes=[t_sq])
            for kc in range(KD):
                P.op("pe", lambda e, kc=kc, bk=bk: e.matmul(banks[bk], lhsT=onesb, rhs=sq[:, kc, :], start=(kc == 0), stop=(kc == KD - 1)),
                     reads=[t_sq, t_const], writes=[tb[bk]], partial=(kc > 0))
            P.op("dve", lambda e, bk=bk: e.tensor_scalar(out=rs, in0=banks[bk], scalar1=1.0 / D, scalar2=EPS, op0=ALU.mult, op1=ALU.add),
                 reads=[tb[bk]], writes=[t_rs])
            P.op("act", lambda e: e.activation(out=rs, in_=rs, func=AF.Sqrt), reads=[t_rs], writes=[t_rs])
            P.op("dve", lambda e: e.reciprocal(out=rs, in_=rs), reads=[t_rs], writes=[t_rs])
            for kc in range(KD):
                tt = kc % 2
                P.op("dve", lambda e, kc=kc, ts=ts, tt=tt: e.scalar_tensor_tensor(
                    out=tmp[:, tt, :], in0=xT[:, kc, ts], scalar=Acol[:, kc:kc + 1], in1=rs, op0=ALU.mult, op1=ALU.mult),
                    reads=[t_x, t_rs, t_AB], writes=[t_tmp[tt]])
                P.op("act", lambda e, kc=kc, ts=ts, tt=tt: e.activation(
                    out=hT[:, kc, ts], in_=tmp[:, tt, :], func=AF.Identity, bias=Bcol[:, kc:kc + 1], scale=1.0),
                    reads=[t_tmp[tt], t_AB], writes=[t_h], partial=True)
        AR.reset(m)

    for l in range(DEPTH):
        AR.reset(layer_mark)
        xsrc = xT_in if l == 0 else xscr
        AR.alloc([KD, T], F32)
        A1, B1, A2, B2 = (AB[:, l * 4 + j, :] for j in range(4))
        load_x(xsrc)
        norm_mod(A1, B1, [0, 1])
        P.barrier()
        sample_norm(l, 1, 0, n1g[l])
        P.barrier()
        AR.reset(layer_mark)
        Vall = AR.alloc([NT, 4, 65], BF16)
        zs = AR.alloc([NT, 512], BF16)
        gat = AR.alloc([NT, 24], F32)
        dtt = AR.alloc([NT, 8], F32)
        aa = AR.alloc([NT, 8], F32)
        lay = AR.alloc([4, 8], F32)
        sng = AR.alloc([512], F32)
        QT = AR.alloc([4, T], BF16)
        kvcT = AR.alloc([2, T], BF16)
        KsT = AR.alloc([2, T], BF16)
        KwT = AR.alloc([2, T], BF16)
        xbcT = AR.alloc([6, T], BF16)
        mA = AR.mark()
        wtm = AR.alloc([KD, 1312], BF16)
        t_wtm = Tk("wtm")
        for kc in range(KD):
            P.dma("pool", lambda e, l=l, kc=kc: e.dma_start(out=wtm[:, kc, :], in_=wtm_in[l, :, kc, :]), t_wtm, writes=[t_wtm], partial=True)
        kvst = [AR.alloc([768], F32) for _ in range(2)]
        t_kvst = [Tk("kvst0"), Tk("kvst1")]
        for tt in range(NT):
            tsl = slice(tt * 128, (tt + 1) * 128)
            sl = tt % 2
            for half in range(2):
                bk = 2 + (tt * 2 + half) % 4
                ncol = 512 if half == 0 else 256
                c0 = half * 512
                for kc in range(KD):
                    P.op("pe", lambda e, bk=bk, kc=kc, tsl=tsl, c0=c0, ncol=ncol: e.matmul(
                        banks[bk][:, 0:ncol], lhsT=hT[:, kc, tsl], rhs=wtm[:, kc, c0:c0 + ncol], start=(kc == 0), stop=(kc == KD - 1)),
                        reads=[t_h, t_wtm], writes=[tb[bk]], partial=(kc > 0))
                P.op("act", lambda e, bk=bk, sl=sl, c0=c0, ncol=ncol: e.copy(out=kvst[sl][:, c0:c0 + ncol], in_=banks[bk][:, 0:ncol]),
                     reads=[tb[bk]], writes=[t_kvst[sl]], partial=(half > 0))
            P.dma("sp", lambda e, l=l, tsl=tsl, sl=sl: e.dma_start(out=kv_out[l, tsl, :], in_=kvst[sl]), t_kvst[sl], reads=[t_kvst[sl]])
        t_V, t_zs, t_gat, t_dt, t_lay, t_Q, t_kvc, t_Ks, t_Kw, t_xbc = (Tk(n) for n in
            ("V", "zs", "gat", "dt", "lay", "Q", "kvc", "Ks", "Kw", "xbc"))
        mA2 = AR.mark()
        P.op("dve", lambda e: e.memset(Vall, 1.0), writes=[t_V])
        P.dma("sp", lambda e, l=l: e.dma_start(out=lay[:, 0, :], in_=dtb_in[l]), t_lay, writes=[t_lay], partial=True)
        P.dma("sp", lambda e, l=l: e.dma_start(out=lay[:, 1, :], in_=alog_in[l]), t_lay, writes=[t_lay], partial=True)
        P.dma("sp", lambda e, l=l: e.dma_start(out=lay[:, 2, :], in_=dsk_in[l]), t_lay, writes=[t_lay], partial=True)
        P.dma("sp", lambda e, l=l: e.dma_start(out=sng, in_=sng_in[l]), t_lay, writes=[t_lay], partial=True)
        P.op("act", lambda e: e.activation(out=lay[:, 1, :], in_=lay[:, 1, :], func=AF.Exp), reads=[t_lay], writes=[t_lay], partial=True)
        P.op("dve", lambda e: e.tensor_scalar(out=lay[:, 1, :], in0=lay[:, 1, :], scalar1=-1.0, scalar2=None, op0=ALU.mult),
             reads=[t_lay], writes=[t_lay], partial=True)
        for tt in range(NT):
            tsl = slice(tt * 128, (tt + 1) * 128)
            b1 = 2 + (tt % 2) * 2
            b2 = b1 + 1
            for kc in range(KD):
                P.op("pe", lambda e, b1=b1, kc=kc, tsl=tsl: e.matmul(
                    banks[b1][:, 0:128], lhsT=hT[:, kc, tsl], rhs=wtm[:, kc, 384:512], start=(kc == 0), stop=(kc == KD - 1)),
                    reads=[t_h, t_wtm], writes=[tb[b1]], partial=(kc > 0))
            for kc in range(KD):
                P.op("pe", lambda e, b1=b1, kc=kc, tsl=tsl: e.matmul(
                    banks[b1][:, 128:256], lhsT=hT[:, kc, tsl], rhs=wtm[:, kc, 640:768], start=(kc == 0), stop=(kc == KD - 1)),
                    reads=[t_h, t_wtm], writes=[tb[b1]], partial=True)
            for kc in range(KD):
                P.op("pe", lambda e, b1=b1, kc=kc, tsl=tsl: e.matmul(
                    banks[b1][:, 256:280], lhsT=hT[:, kc, tsl], rhs=wtm[:, kc, 768:792], start=(kc == 0), stop=(kc == KD - 1)),
                    reads=[t_h, t_wtm], writes=[tb[b1]], partial=True)
            for kc in range(KD):
                P.op("pe", lambda e, b1=b1, kc=kc, tsl=tsl: e.matmul(
                    banks[b1][:, 280:288], lhsT=hT[:, kc, tsl], rhs=wtm[:, kc, 1304:1312], start=(kc == 0), stop=(kc == KD - 1)),
                    reads=[t_h, t_wtm], writes=[tb[b1]], partial=True)
            P.op("act", lambda e, b1=b1, tt=tt: e.copy(out=Vall[:, tt, :, 0:64], in_=banks[b1][:, 0:256].rearrange("p (a b) -> p a b", b=64)),
                 reads=[tb[b1]], writes=[t_V], partial=True)
            P.op("act", lambda e, b1=b1, tt=tt: e.activation(out=gat[:, tt, :], in_=banks[b1][:, 256:280], func=AF.Sigmoid),
                 reads=[tb[b1]], writes=[t_gat], partial=True)
            P.op("dve", lambda e, b1=b1, tt=tt: e.tensor_tensor(out=dtt[:, tt, :], in0=banks[b1][:, 280:288], in1=lay[:, 0, :], op=ALU.add),
                 reads=[tb[b1], t_lay], writes=[t_dt], partial=True)
            for kc in range(KD):
                P.op("pe", lambda e, b2=b2, kc=kc, tsl=tsl: e.matmul(
                    banks[b2], lhsT=hT[:, kc, tsl], rhs=wtm[:, kc, 792:1304], start=(kc == 0), stop=(kc == KD - 1)),
                    reads=[t_h, t_wtm], writes=[tb[b2]], partial=(kc > 0))
            P.op("act", lambda e, b2=b2, tt=tt: e.activation(out=zs[:, tt, :], in_=banks[b2], func=AF.Silu),
                 reads=[tb[b2]], writes=[t_zs], partial=True)
        def smm(bk, prt, cols, lhs_fn, rhs_fn, reads):
            for kc in range(KD):
                P.op("pe", lambda e, kc=kc: e.matmul(banks[bk][prt, cols], lhsT=lhs_fn(kc), rhs=rhs_fn(kc), start=(kc == 0), stop=(kc == KD - 1)),
                     reads=reads, writes=[tb[bk]], partial=True)
        for c6 in range(6):
            smm(0, slice(0, 128), slice(c6 * 4, c6 * 4 + 4), lambda kc, c6=c6: wtm[:, kc, c6 * 128:(c6 + 1) * 128], lambda kc: hsT[:, kc, :], [t_wtm, t_hs])
        for c4 in range(4):
            smm(0, slice(0, 128), slice(24 + c4 * 4, 28 + c4 * 4), lambda kc, c4=c4: wtm[:, kc, 792 + c4 * 128:792 + (c4 + 1) * 128], lambda kc: hsT[:, kc, :], [t_wtm, t_hs])
        smm(0, slice(0, 8), slice(40, 44), lambda kc: wtm[:, kc, 1304:1312], lambda kc: hsT[:, kc, :], [t_wtm, t_hs])
        for bg in range(6):
            smm(0, slice(0, 4), slice(44 + bg * 4, 48 + bg * 4), lambda kc, bg=bg: wtm[:, kc, 768 + bg * 4:772 + bg * 4], lambda kc: hsT[:, kc, :], [t_wtm, t_hs])
        for b in range(4):
            smm(1, slice(0, 1), slice(b * 128, (b + 1) * 128), lambda kc, b=b: hsT[:, kc, b:b + 1], lambda kc: wtm[:, kc, 384:512], [t_wtm, t_hs])
        for b in range(4):
            smm(6, slice(0, 1), slice(b * 128, (b + 1) * 128), lambda kc, b=b: hsT[:, kc, b:b + 1], lambda kc: wtm[:, kc, 640:768], [t_wtm, t_hs])
        P.op("act", lambda e: e.copy(out=kvnT, in_=banks[0][:, 0:24].rearrange("p (a b) -> p a b", b=4)), reads=[tb[0]], writes=[t_kvn])
        P.op("act", lambda e: e.activation(out=zsT, in_=banks[0][:, 24:40].rearrange("p (a b) -> p a b", b=4), func=AF.Silu), reads=[tb[0]], writes=[t_zsT])
        P.op("act", lambda e: e.activation(out=gsT[0:4, :], in_=banks[0][0:4, 44:68], func=AF.Sigmoid), reads=[tb[0]], writes=[t_gs])
        hcol = AR.alloc([3], F32)
        t_hcol = Tk("hcol")
        P.dma("sp", lambda e, l=l: e.dma_start(out=hcol[0:8, :], in_=hcol_in[l]), t_hcol, writes=[t_hcol])
        P.op("act", lambda e: e.activation(out=dd[0:8, 0:4], in_=banks[0][0:8, 40:44], func=AF.Exp, bias=hcol[0:8, 0:1], scale=1.0), reads=[tb[0], t_hcol], writes=[t_dd])
        P.op("dve", lambda e: e.tensor_scalar(out=dd[0:8, 0:4], in0=dd[0:8, 0:4], scalar1=1.0, scalar2=None, op0=ALU.add), reads=[t_dd], writes=[t_dd])
        P.op("act", lambda e: e.activation(out=dd[0:8, 0:4], in_=dd[0:8, 0:4], func=AF.Ln), reads=[t_dd], writes=[t_dd])
        P.op("act", lambda e: e.activation(out=hcol[0:8, 1:2], in_=hcol[0:8, 1:2], func=AF.Exp), reads=[t_hcol], writes=[t_hcol])
        P.op("dve", lambda e: e.tensor_scalar(out=dd[0:8, 4:8], in0=dd[0:8, 0:4], scalar1=hcol[0:8, 1:2], scalar2=-1.0, op0=ALU.mult, op1=ALU.mult), reads=[t_dd, t_hcol], writes=[t_dd])
        P.op("act", lambda e: e.activation(out=dd[0:8, 4:8], in_=dd[0:8, 4:8], func=AF.Exp), reads=[t_dd], writes=[t_dd])
        P.op("dve", lambda e: e.memset(VN[0:1], 1.0), writes=[t_VN])
        P.op("act", lambda e: e.copy(out=VN[0:1, :, 0, :].rearrange("p b (g x) -> p b g x", g=2)[:, :, :, 0:64], in_=banks[1][0:1, 0:512].rearrange("p (b g d) -> p b g d", b=4, g=2)),
             reads=[tb[1]], writes=[t_VN], partial=True)
        P.op("act", lambda e: e.copy(out=VN[0:1, :, 1, :].rearrange("p b (g x) -> p b g x", g=2)[:, :, :, 0:64], in_=banks[6][0:1, 0:512].rearrange("p (b g d) -> p b g d", b=4, g=2)),
             reads=[tb[6]], writes=[t_VN], partial=True)
        P.dma("sp", lambda e, l=l: e.dma_start(out=kvn_out[l], in_=kvnT), t_kvn, reads=[t_kvn])
        for b in range(4):
            P.dma("sp", lambda e, l=l, b=b: e.dma_start(out=wk_out[l, b, 0:511, :], in_=wk_in[l, b, 1:512, :]), t_win, writes=[t_win], partial=True)
            P.dma("sp", lambda e, l=l, b=b: e.dma_start(out=wv_out[l, b, 0:511, :], in_=wv_in[l, b, 1:512, :]), t_win, writes=[t_win], partial=True)
            P.dma("sp", lambda e, l=l, b=b: e.dma_start(out=wk_out[l, b, 511:512, :].rearrange("o f -> f o"), in_=kvnT[:, 4, b:b + 1]), t_kvn, reads=[t_kvn])
            P.dma("sp", lambda e, l=l, b=b: e.dma_start(out=wv_out[l, b, 511:512, :].rearrange("o f -> f o"), in_=kvnT[:, 5, b:b + 1]), t_kvn, reads=[t_kvn])
        P.op("act", lambda e: e.activation(out=dtt, in_=dtt, func=AF.Exp), reads=[t_dt], writes=[t_dt])
        P.op("dve", lambda e: e.tensor_scalar(out=dtt, in0=dtt, scalar1=1.0, scalar2=None, op0=ALU.add), reads=[t_dt], writes=[t_dt])
        P.op("act", lambda e: e.activation(out=dtt, in_=dtt, func=AF.Ln), reads=[t_dt], writes=[t_dt])
        for tt in range(NT):
            P.op("dve", lambda e, tt=tt: e.tensor_tensor(out=aa[:, tt, :], in0=dtt[:, tt, :], in1=lay[:, 1, :], op=ALU.mult),
                 reads=[t_dt, t_lay], writes=[t_dt], partial=True)
        wch = [AR.alloc([KD, 128], BF16) for _ in range(3)]
        t_wch = [Tk("wch%d" % i) for i in range(3)]
        stg = [AR.alloc([515], F32) for _ in range(2)]
        t_stg = [Tk("stg0"), Tk("stg1")]
        cacc = [AR.alloc([512], F32) for _ in range(2)]
        t_cacc = [Tk("cacc0"), Tk("cacc1")]
        scw = AR.alloc([6, 4], F32)
        scb = AR.alloc([6], F32)
        t_sc = Tk("sc")
        P.dma("sp", lambda e, l=l: e.dma_start(out=scw, in_=scw_in[l]), t_sc, writes=[t_sc], partial=True)
        P.dma("sp", lambda e, l=l: e.dma_start(out=scb, in_=scb_in[l]), t_sc, writes=[t_sc], partial=True)
        nb = 0
        for c in range(20):
            sl = c % 3
            P.dma("pool", lambda e, l=l, c=c, sl=sl: e.dma_start(out=wch[sl], in_=wfm_in[l, c]), t_wch[sl], writes=[t_wch[sl]])
            if c >= 10:
                scol = slice((c - 10) * 4, (c - 10) * 4 + 4)
                sbk = 0 if c < 16 else 1
                if c >= 16:
                    scol = slice((c - 16) * 4, (c - 16) * 4 + 4)
                for kc in range(KD):
                    P.op("pe", lambda e, kc=kc, sl=sl, sbk=sbk, scol=scol: e.matmul(banks[sbk][:, scol], lhsT=wch[sl][:, kc, :], rhs=hsT[:, kc, :], start=(kc == 0), stop=(kc == KD - 1)),
                         reads=[t_wch[sl], t_hs], writes=[tb[sbk]], partial=True)
            if c >= 16:
                continue
            for tg in range(4):
                ts = slice(tg * 512, (tg + 1) * 512)
                bk = 4 + nb % 4
                nb += 1
                for kc in range(KD):
                    P.op("pe", lambda e, bk=bk, kc=kc, sl=sl, ts=ts: e.matmul(
                        banks[bk], lhsT=wch[sl][:, kc, :], rhs=hT[:, kc, ts], start=(kc == 0), stop=(kc == KD - 1)),
                        reads=[t_h, t_wch[sl]], writes=[tb[bk]], partial=(kc > 0))
                if c < 4:
                    P.op("act", lambda e, bk=bk, c=c, ts=ts: e.activation(out=QT[:, c, ts], in_=banks[bk], func=AF.Copy, scale=0.125),
                         reads=[tb[bk]], writes=[t_Q], partial=True)
                elif c < 6:
                    P.op("act", lambda e, bk=bk, c=c, ts=ts: e.copy(out=kvcT[:, c - 4, ts], in_=banks[bk]),
                         reads=[tb[bk]], writes=[t_kvc], partial=True)
                elif c < 8:
                    P.op("act", lambda e, bk=bk, c=c, ts=ts: e.copy(out=KsT[:, c - 6, ts], in_=banks[bk]),
                         reads=[tb[bk]], writes=[t_Ks], partial=True)
                elif c < 10:
                    P.op("act", lambda e, bk=bk, c=c, ts=ts: e.copy(out=KwT[:, c - 8, ts], in_=banks[bk]),
                         reads=[tb[bk]], writes=[t_Kw], partial=True)
                else:
                    cc = c - 10
                    si = tg % 2
                    if tg == 0:
                        P.op("dve", lambda e, si=si: e.memset(stg[si][:, 0:3], 0.0), writes=[t_stg[si]], partial=True)
                    P.op("act", lambda e, bk=bk, si=si: e.copy(out=stg[si][:, 3:515], in_=banks[bk]),
                         reads=[tb[bk]], writes=[t_stg[si]], partial=True)
                    P.op("dve", lambda e, si=si, cc=cc: e.tensor_scalar(
                        out=cacc[si], in0=stg[si][:, 3:515], scalar1=scw[:, cc, 3:4], scalar2=scb[:, cc:cc + 1], op0=ALU.mult, op1=ALU.add),
                        reads=[t_stg[si], t_sc], writes=[t_cacc[si]])
                    for k in range(3):
                        P.op("dve", lambda e, si=si, cc=cc, k=k: e.scalar_tensor_tensor(
                            out=cacc[si], in0=stg[si][:, k:k + 512], scalar=scw[:, cc, k:k + 1], in1=cacc[si], op0=ALU.mult, op1=ALU.add),
                            reads=[t_stg[si], t_sc], writes=[t_cacc[si]])
                    P.op("act", lambda e, si=si, cc=cc, ts=ts: e.activation(out=xbcT[:, cc, ts], in_=cacc[si], func=AF.Silu),
                         reads=[t_cacc[si]], writes=[t_xbc], partial=True)
                    if tg < 3:
                        P.op("dve", lambda e, si=si: e.tensor_copy(out=stg[1 - si][:, 0:3], in_=stg[si][:, 512:515]),
                             reads=[t_stg[si]], writes=[t_stg[1 - si]], partial=True)
                    else:
                        P.dma("sp", lambda e, l=l, si=si, cc=cc: e.dma_start(out=sconv_out[l, :, cc, :], in_=stg[si][:, 512:515]),
                              t_stg[si], reads=[t_stg[si]])
        P.op("act", lambda e: e.activation(out=QgT, in_=banks[1][:, 0:16].rearrange("p (a b) -> p a b", b=4), func=AF.Copy, scale=0.125), reads=[tb[1]], writes=[t_Qg])
        cst = AR.alloc([6, 4, 4], F32)
        cst_o = AR.alloc([6, 3, 4], F32)
        cac = AR.alloc([6, 4], F32)
        t_cst, t_cso, t_cac = Tk("cst"), Tk("cso"), Tk("cac")
        P.dma("sp", lambda e, l=l: e.dma_start(out=cst[:, :, 0:3, :], in_=cst_in[l]), t_cst, writes=[t_cst])
        P.op("act", lambda e: e.copy(out=cst[:, :, 3, :], in_=banks[0][:, 0:24].rearrange("p (a b) -> p a b", b=4)), reads=[tb[0]], writes=[t_cst], partial=True)
        P.op("dve", lambda e: e.tensor_copy(out=cst_o, in_=cst[:, :, 1:4, :]), reads=[t_cst], writes=[t_cso])
        P.dma("sp", lambda e, l=l: e.dma_start(out=sconvs_out[l], in_=cst_o), t_cso, reads=[t_cso])
        for cc in range(6):
            P.op("dve", lambda e, cc=cc: e.tensor_scalar(out=cac[:, cc, :], in0=cst[:, cc, 3, :], scalar1=scw[:, cc, 3:4], scalar2=scb[:, cc:cc + 1], op0=ALU.mult, op1=ALU.add),
                 reads=[t_cst, t_sc], writes=[t_cac], partial=(cc > 0))
            for k in range(3):
                P.op("dve", lambda e, cc=cc, k=k: e.scalar_tensor_tensor(out=cac[:, cc, :], in0=cst[:, cc, k, :], scalar=scw[:, cc, k:k + 1], in1=cac[:, cc, :], op0=ALU.mult, op1=ALU.add),
                     reads=[t_cst, t_sc, t_cac], writes=[t_cac], partial=True)
        P.op("act", lambda e: e.activation(out=xbcs, in_=cac, func=AF.Silu), reads=[t_cac], writes=[t_xbcs])
        P.barrier()
        AR.reset(mA)
        if stage <= 2:
            break
        mSS = AR.mark()
        hSs = AR.alloc([4, 4, 64], F32)
        dx = AR.alloc([4, 8], F32)
        BCtm = AR.alloc([256], F32)
        BCbd = AR.alloc([4, 256], F32)
        BC = AR.alloc([4, 2, 128], F32)
        us = AR.alloc([4, 4], F32)
        ys = AR.alloc([4, 4], F32)
        y2s = AR.alloc([4, 4], F32)
        stmp = AR.alloc([2, 64], F32)
        ssg = AR.alloc([2, 4], F32)
        colv = AR.alloc([8], F32)
        t_hSs, t_dx, t_BCtm, t_BCbd, t_BC, t_us, t_ys, t_y2s, t_stmp, t_ssg, t_colv = (Tk(n) for n in
            ("hSs", "dx", "BCtm", "BCbd", "BC", "us", "ys", "y2s", "stmp", "ssg", "colv"))
        P.dma("sp", lambda e, l=l: e.dma_start(out=hSs, in_=hs0_in[l]), t_hSs, writes=[t_hSs])
        P.dma("sp", lambda e, l=l: e.dma_start(out=colv[:, 0:4], in_=dskc_in[l]), t_colv, writes=[t_colv], partial=True)
        P.dma("sp", lambda e, l=l: e.dma_start(out=colv[:, 4:8], in_=sngc_in[l]), t_colv, writes=[t_colv], partial=True)
        for c in range(4):
            P.op("pe", lambda e, c=c: e.matmul(banks[0][:, c * 8:(c + 1) * 8], lhsT=exm[0:8, c, :], rhs=dd[0:8, :], start=True, stop=True),
                 reads=[t_ex, t_dd], writes=[tb[0]], partial=(c > 0))
        P.op("act", lambda e: e.copy(out=dx, in_=banks[0][:, 0:32].rearrange("p (a b) -> p a b", b=8)), reads=[tb[0]], writes=[t_dx])
        for i2 in range(2):
            P.op("pe", lambda e, i2=i2: e.transpose(out=banks[1][0:4, i2 * 128:(i2 + 1) * 128], in_=xbcs[:, 4 + i2, :], identity=identf),
                 reads=[t_xbcs, t_const], writes=[tb[1]], partial=(i2 > 0))
        P.op("act", lambda e: e.copy(out=BCtm[0:4, :], in_=banks[1][0:4, 0:256]), reads=[tb[1]], writes=[t_BCtm])
        for b in range(4):
            P.op("dve", lambda e, b=b: e.tensor_scalar(out=BCbd[0:4, b, :], in0=BCtm[0:4, :], scalar1=identf[0:4, b:b + 1], scalar2=None, op0=ALU.mult),
                 reads=[t_BCtm, t_const], writes=[t_BCbd], partial=(b > 0))
        for b in range(4):
            bk = 2 + b // 2
            P.op("pe", lambda e, b=b, bk=bk: e.matmul(banks[bk][:, (b % 2) * 256:(b % 2 + 1) * 256], lhsT=onesf[0:4, :], rhs=BCbd[0:4, b, :], start=True, stop=True),
                 reads=[t_BCbd, t_const], writes=[tb[bk]], partial=(b % 2 > 0))
        for i2 in range(2):
            P.op("act", lambda e, i2=i2: e.copy(out=BC[:, 2 * i2:2 * i2 + 2, :, :], in_=banks[2 + i2].rearrange("p (b t n) -> p b t n", b=2, t=2)),
                 reads=[tb[2 + i2]], writes=[t_BC], partial=(i2 > 0))
        P.op("dve", lambda e: e.tensor_tensor(out=us, in0=xbcs[:, 0:4, :], in1=dx[:, :, 0:4], op=ALU.mult), reads=[t_xbcs, t_dx], writes=[t_us])
        first = True
        for c in range(4):
            g = c // 2
            for b in range(4):
                P.op("dve", lambda e, c=c, b=b, g=g: e.tensor_scalar(out=stmp[:, 0, :], in0=BC[:, b, 0, g * 64:(g + 1) * 64], scalar1=us[:, c, b:b + 1], scalar2=None, op0=ALU.mult),
                     reads=[t_BC, t_us], writes=[t_stmp])
                P.op("dve", lambda e, c=c, b=b: e.scalar_tensor_tensor(out=hSs[:, c, b, :], in0=hSs[:, c, b, :], scalar=dx[:, c, 4 + b:5 + b], in1=stmp[:, 0, :], op0=ALU.mult, op1=ALU.add),
                     reads=[t_hSs, t_dx, t_stmp], writes=[t_hSs], partial=True)
                P.op("dve", lambda e, c=c, b=b, g=g: e.tensor_tensor(out=stmp[:, 1, :], in0=hSs[:, c, b, :], in1=BC[:, b, 1, g * 64:(g + 1) * 64], op=ALU.mult),
                     reads=[t_hSs, t_BC], writes=[t_stmp], partial=True)
                P.op("dve", lambda e, c=c, b=b: e.reduce_sum(out=ys[:, c, b:b + 1], in_=stmp[:, 1, :], axis=AX.X),
                     reads=[t_stmp], writes=[t_ys], partial=(not first))
                first = False
        P.dma("sp", lambda e, l=l: e.dma_start(out=ssms_out[l], in_=hSs), t_hSs, reads=[t_hSs])
        for c in range(4):
            P.op("dve", lambda e, c=c: e.scalar_tensor_tensor(out=ys[:, c, :], in0=xbcs[:, c, :], scalar=colv[:, c:c + 1], in1=ys[:, c, :], op0=ALU.mult, op1=ALU.add),
                 reads=[t_xbcs, t_colv, t_ys], writes=[t_ys], partial=True)
        P.op("dve", lambda e: e.tensor_tensor(out=ys, in0=ys, in1=zsT, op=ALU.mult), reads=[t_ys, t_zsT], writes=[t_ys])
        P.op("dve", lambda e: e.tensor_tensor(out=y2s, in0=ys, in1=ys, op=ALU.mult), reads=[t_ys], writes=[t_y2s])
        P.op("pe", lambda e: e.matmul(banks[0][:, 0:16], lhsT=onesf, rhs=y2s.rearrange("p a b -> p (a b)"), start=True, stop=True), reads=[t_y2s, t_const], writes=[tb[0]])
        P.op("act", lambda e: e.copy(out=y2s.rearrange("p a b -> p (a b)"), in_=banks[0][:, 0:16]), reads=[tb[0]], writes=[t_y2s])
        s4 = y2s.rearrange("p (g c) b -> p g c b", g=2)
        P.op("dve", lambda e, s4=s4: e.tensor_tensor(out=ssg, in0=s4[:, :, 0, :], in1=s4[:, :, 1, :], op=ALU.add), reads=[t_y2s], writes=[t_ssg])
        P.op("dve", lambda e: e.tensor_scalar(out=ssg, in0=ssg, scalar1=1.0 / 256, scalar2=EPS, op0=ALU.mult, op1=ALU.add), reads=[t_ssg], writes=[t_ssg])
        P.op("act", lambda e: e.activation(out=ssg, in_=ssg, func=AF.Sqrt), reads=[t_ssg], writes=[t_ssg])
        P.op("dve", lambda e: e.reciprocal(out=ssg, in_=ssg), reads=[t_ssg], writes=[t_ssg])
        for c in range(4):
            P.op("dve", lambda e, c=c: e.scalar_tensor_tensor(out=osT[:, 4 + c, :], in0=ys[:, c, :], scalar=colv[:, 4 + c:5 + c], in1=ssg[:, c // 2, :], op0=ALU.mult, op1=ALU.mult),
                 reads=[t_ys, t_colv, t_ssg], writes=[t_os], partial=True)
        P.barrier()
        AR.reset(mSS)
        oT = hT
        t_o = t_h
        mS = AR.mark()
        xB = AR.alloc([NT, 640], BF16)
        t_xB = Tk("xB")
        for tt in range(NT):
            tsl = slice(tt * 128, (tt + 1) * 128)
            bk = tt % 2
            bv = banks[bk].bitcast(BF16)
            for cc in range(5):
                P.op("pe", lambda e, bv=bv, cc=cc, tsl=tsl: e.transpose(out=bv[:, cc * 128:(cc + 1) * 128], in_=xbcT[:, cc, tsl], identity=identb),
                     reads=[t_xbc, t_const], writes=[tb[bk]], partial=(cc > 0))
            P.op("act", lambda e, bv=bv, tt=tt: e.copy(out=xB[:, tt, :], in_=bv[:, 0:640]), reads=[tb[bk]], writes=[t_xB], partial=True)
        hS = AR.alloc([4, 64], F32)
        hSb = AR.alloc([4, 64], BF16)
        t_hS = Tk("hS")
        P.op("dve", lambda e: e.memset(hS, 0.0), writes=[t_hS])
        P.op("dve", lambda e: e.memset(hSb, 0.0), writes=[t_hS], partial=True)
        ncs = AR.alloc([8], F32)
        ecs = AR.alloc([8], F32)
        decc = AR.alloc([8], F32)
        wcol = AR.alloc([8], F32)
        abc = [AR.alloc([128], F32) for _ in range(2)]
        LT = [AR.alloc([128], F32) for _ in range(2)]
        WT = [AR.alloc([128], BF16) for _ in range(2)]
        GT = AR.alloc([2, 128], F32)
        xw = AR.alloc([512], BF16)
        ydsb = AR.alloc([512], F32)
        yy = AR.alloc([512], F32)
        yf = AR.alloc([512], BF16)
        ssq = AR.alloc([4], F32)
        t_ncs, t_ecs, t_dec, t_wcol, t_GT, t_xw, t_yd, t_yy, t_yf, t_ssq = (Tk(n) for n in
            ("ncs", "ecs", "dec", "wcol", "GT", "xw", "yd", "yy", "yf", "ssq"))
        t_abc = [Tk("abc0"), Tk("abc1")]
        t_LT = [Tk("LT0"), Tk("LT1")]
        t_WT = [Tk("WT0"), Tk("WT1")]
        GB = (1, 7)
        for c in range(NT):
            csl = slice(c * 128, (c + 1) * 128)
            P.op("pe", lambda e, c=c: e.matmul(banks[0][:, 0:8], lhsT=utri, rhs=aa[:, c, :], start=True, stop=True),
                 reads=[t_const, t_dt], writes=[tb[0]])
            P.op("dve", lambda e: e.tensor_scalar(out=ncs, in0=banks[0][:, 0:8], scalar1=-1.0, scalar2=None, op0=ALU.mult),
                 reads=[tb[0]], writes=[t_ncs])
            P.op("act", lambda e: e.activation(out=ecs, in_=banks[0][:, 0:8], func=AF.Exp), reads=[tb[0]], writes=[t_ecs])
            for g in range(2):
                ps = slice(g * 64, (g + 1) * 64)
                gb = GB[g]
                P.op("pe", lambda e, g=g, ps=ps, csl=csl, gb=gb: e.matmul(
                    banks[gb][:, 0:128], lhsT=xbcT[ps, 4, csl], rhs=xbcT[ps, 5, csl], start=True, stop=True),
                    reads=[t_xbc], writes=[tb[gb]])
                P.op("act", lambda e, g=g, gb=gb: e.copy(out=GT[:, g, :], in_=banks[gb][:, 0:128]), reads=[tb[gb]], writes=[t_GT], partial=(g > 0))
            for h in range(8):
                g = h // 4
                hb = h % 2
                sb = 2 + (h // 4)
                col = slice((h % 4) * 128, (h % 4 + 1) * 128)
                P.op("dve", lambda e, c=c, h=h, hb=hb: e.tensor_scalar(out=abc[hb], in0=onesf, scalar1=aa[:, c, h:h + 1], scalar2=None, op0=ALU.mult),
                     reads=[t_dt, t_const], writes=[t_abc[hb]])
                P.op("pe", lambda e, sb=sb, col=col, hb=hb: e.matmul(banks[sb][:, col], lhsT=abc[hb], rhs=utri, start=True, stop=False),
                     reads=[t_abc[hb], t_const], writes=[tb[sb]], partial=(h % 4 > 0))
                P.op("pe", lambda e, sb=sb, col=col: e.matmul(banks[sb][:, col], lhsT=identf, rhs=mtrif, start=False, stop=True),
                     reads=[t_const], writes=[tb[sb]], partial=True)
            for h in range(8):
                g = h // 4
                hb = h % 2
                sb = 2 + (h // 4)
                col = slice((h % 4) * 128, (h % 4 + 1) * 128)
                P.op("act", lambda e, sb=sb, col=col, hb=hb, h=h: e.activation(out=LT[hb], in_=banks[sb][:, col], func=AF.Exp, bias=ncs[:, h:h + 1], scale=1.0),
                     reads=[tb[sb], t_ncs], writes=[t_LT[hb]])
                P.op("act", lambda e, sb=sb, h=h: e.activation(out=decc[:, h:h + 1], in_=banks[sb][:, (h % 4) * 128 + 127:(h % 4) * 128 + 128], func=AF.Exp),
                     reads=[tb[sb]], writes=[t_dec], partial=True)
                P.op("dve", lambda e, hb=hb, h=h, g=g, c=c: e.scalar_tensor_tensor(
                    out=WT[hb], in0=LT[hb], scalar=dtt[:, c, h:h + 1], in1=GT[:, g, :], op0=ALU.mult, op1=ALU.mult),
                    reads=[t_LT[hb], t_dt, t_GT], writes=[t_WT[hb]])
                P.op("dve", lambda e, hb=hb, h=h, c=c: e.tensor_tensor(out=wcol[:, h:h + 1], in0=LT[hb][:, 127:128], in1=dtt[:, c, h:h + 1], op=ALU.mult),
                     reads=[t_LT[hb], t_dt], writes=[t_wcol], partial=True)
                P.op("dve", lambda e, h=h, c=c: e.tensor_scalar(out=xw[:, h * 64:(h + 1) * 64], in0=xB[:, c, h * 64:(h + 1) * 64], scalar1=wcol[:, h:h + 1], scalar2=None, op0=ALU.mult),
                     reads=[t_xB, t_wcol], writes=[t_xw], partial=True)
                P.op("pe", lambda e, hb=hb, h=h, c=c: e.matmul(banks[4][:, h * 64:(h + 1) * 64], lhsT=WT[hb], rhs=xB[:, c, h * 64:(h + 1) * 64], start=True, stop=True),
                     reads=[t_WT[hb], t_xB], writes=[tb[4]], partial=(h > 0))
                ps = slice(g * 64, (g + 1) * 64)
                gb = GB[g]
                P.op("pe", lambda e, h=h, ps=ps, csl=csl, gb=gb: e.matmul(banks[gb][:, 128 + (h % 4) * 64:128 + (h % 4 + 1) * 64], lhsT=xbcT[ps, 5, csl], rhs=hSb[ps, h % 4, :], start=True, stop=True),
                     reads=[t_xbc, t_hS], writes=[tb[gb]], partial=True)
            P.op("act", lambda e: e.copy(out=ydsb, in_=banks[4]), reads=[tb[4]], writes=[t_yd])
            for h in range(8):
                hs = slice(h * 64, (h + 1) * 64)
                gb = GB[h // 4]
                P.op("dve", lambda e, h=h, hs=hs, gb=gb: e.scalar_tensor_tensor(out=yy[:, hs], in0=banks[gb][:, 128 + (h % 4) * 64:128 + (h % 4 + 1) * 64], scalar=ecs[:, h:h + 1], in1=ydsb[:, hs], op0=ALU.mult, op1=ALU.add),
                     reads=[tb[gb], t_ecs, t_yd], writes=[t_yy], partial=(h > 0))
            for h in range(8):
                hs = slice(h * 64, (h + 1) * 64)
                P.op("dve", lambda e, h=h, hs=hs, c=c: e.scalar_tensor_tensor(out=yy[:, hs], in0=xB[:, c, hs], scalar=lay[:, 2, h:h + 1], in1=yy[:, hs], op0=ALU.mult, op1=ALU.add),
                     reads=[t_xB, t_lay, t_yy], writes=[t_yy], partial=True)
            P.op("pe", lambda e, c=c: e.matmul(banks[6], lhsT=xB[:, c, 512:640], rhs=xw, start=True, stop=True),
                 reads=[t_xB, t_xw], writes=[tb[6]])
            for h in range(8):
                g = h // 4
                ps = slice(g * 64, (g + 1) * 64)
                P.op("dve", lambda e, h=h, ps=ps: e.scalar_tensor_tensor(out=hS[ps, h % 4, :], in0=hS[ps, h % 4, :], scalar=decc[ps, h:h + 1], in1=banks[6][ps, h * 64:(h + 1) * 64], op0=ALU.mult, op1=ALU.add),
                     reads=[t_hS, t_dec, tb[6]], writes=[t_hS], partial=(h > 0))
            P.op("act", lambda e: e.copy(out=hSb, in_=hS), reads=[t_hS], writes=[t_hS], partial=True)
            P.op("dve", lambda e, c=c: e.tensor_tensor(out=yy, in0=yy, in1=zs[:, c, :], op=ALU.mult), reads=[t_yy, t_zs], writes=[t_yy])
            for g in range(2):
                P.op("act", lambda e, g=g: e.activation(out=ydsb[:, g * 256:(g + 1) * 256], in_=yy[:, g * 256:(g + 1) * 256], func=AF.Square, accum_out=ssq[:, g:g + 1]),
                     reads=[t_yy], writes=[t_ssq, t_yd], partial=(g > 0))
            P.op("dve", lambda e: e.tensor_scalar(out=ssq[:, 2:4], in0=ssq[:, 0:2], scalar1=1.0 / 256, scalar2=EPS, op0=ALU.mult, op1=ALU.add),
                 reads=[t_ssq], writes=[t_ssq], partial=True)
            P.op("act", lambda e: e.activation(out=ssq[:, 2:4], in_=ssq[:, 2:4], func=AF.Sqrt), reads=[t_ssq], writes=[t_ssq], partial=True)
            P.op("dve", lambda e: e.reciprocal(out=ssq[:, 2:4], in_=ssq[:, 2:4]), reads=[t_ssq], writes=[t_ssq], partial=True)
            for g in range(2):
                gs = slice(g * 256, (g + 1) * 256)
                P.op("dve", lambda e, g=g, gs=gs: e.scalar_tensor_tensor(out=yf[:, gs], in0=yy[:, gs], scalar=ssq[:, 2 + g:3 + g], in1=sng[:, gs], op0=ALU.mult, op1=ALU.mult),
                     reads=[t_yy, t_ssq, t_lay], writes=[t_yf], partial=(g > 0))
            bv7 = banks[0].bitcast(BF16)
            for e4 in range(4):
                P.op("pe", lambda e, e4=e4, bv7=bv7: e.transpose(out=bv7[:, e4 * 128:(e4 + 1) * 128], in_=yf[:, e4 * 128:(e4 + 1) * 128], identity=identb),
                     reads=[t_yf, t_const], writes=[tb[0]], partial=(e4 > 0))
            P.op("act", lambda e, bv7=bv7, csl=csl: e.copy(out=oT[:, 4:8, csl], in_=bv7[:, 0:512].rearrange("p (a b) -> p a b", b=128)),
                 reads=[tb[0]], writes=[t_o], partial=True)
        P.dma("sp", lambda e, l=l: e.dma_start(out=ssm_out[l], in_=hS), t_hS, reads=[t_hS])
        P.barrier()
        AR.reset(mS)
        if stage <= 3:
            break
        mT = AR.mark()
        mtri = AR.alloc([128], BF16)
        manti = AR.alloc([128], BF16)
        cmpmask = AR.alloc([T], BF16)
        eall = AR.alloc([T], BF16)
        kaug = AR.alloc([32, 128], BF16)
        qaug = AR.alloc([8, 128], BF16)
        ovc = AR.alloc([33], BF16)
        bonus = AR.alloc([NT, 32], F32)
        cap = AR.alloc([NT, 32], F32)
        idxb = AR.alloc([32], I32)
        t_ac = Tk("attconst")
        for dst, src in ((mtri, mtri_in), (manti, manti_in), (cmpmask, cmpmask_in), (eall, eall_in), (kaug, kaug_in),
                         (qaug, qaug_in), (ovc, ov_in)):
            P.dma("pool", lambda e, dst=dst, src=src: e.dma_start(out=dst, in_=src), t_ac, writes=[t_ac], partial=True)
        for dst, src in ((bonus, bonus_in), (cap, cap_in), (idxb, idxb_in)):
            P.dma("sp", lambda e, dst=dst, src=src: e.dma_start(out=dst, in_=src), t_ac, writes=[t_ac], partial=True)
        kcmpT = AR.alloc([2, 128], BF16)
        VC = AR.alloc([2, 97], BF16)
        t_kcmp, t_VC = Tk("kcmp"), Tk("VC")
        mC = AR.mark()
        w1d = AR.alloc([2, 32, 128], BF16)
        w2k = AR.alloc([128], BF16)
        w2v = AR.alloc([64], BF16)
        cb1 = AR.alloc([2], F32)
        t_cw = Tk("cmpw")
        for kv in range(2):
            P.dma("pool", lambda e, l=l, kv=kv: e.dma_start(out=w1d[:, kv, :, :], in_=w1d_in[l, kv]), t_cw, writes=[t_cw], partial=True)
        P.dma("pool", lambda e, l=l: e.dma_start(out=w2k, in_=w2k_in[l]), t_cw, writes=[t_cw], partial=True)
        P.dma("pool", lambda e, l=l: e.dma_start(out=w2v, in_=w2v_in[l]), t_cw, writes=[t_cw], partial=True)
        P.dma("sp", lambda e, l=l: e.dma_start(out=cb1, in_=cb1_in[l]), t_cw, writes=[t_cw], partial=True)
        P.op("dve", lambda e: e.memset(kcmpT, 0.0), writes=[t_kcmp])
        P.op("dve", lambda e: e.memset(VC, 0.0), writes=[t_VC])
        for g in range(2):
            P.op("dve", lambda e, g=g: e.tensor_copy(out=VC[:, g, 64:97], in_=ovc), reads=[t_ac], writes=[t_VC], partial=True)
        gx = AR.alloc([4, 128], F32)
        gu = AR.alloc([4, 128], F32)
        hid = AR.alloc([4, 128], BF16)
        t_gx, t_gu, t_hid = Tk("gx"), Tk("gu"), Tk("hid")
        RGB = ((0, 1), (2, 3))
        for kv in range(2):
            for g in range(2):
                i4 = kv * 2 + g
                ps = slice(g * 64, (g + 1) * 64)
                bk = RGB[g][kv]
                for j in range(32):
                    P.op("pe", lambda e, kv=kv, ps=ps, j=j, bk=bk: e.matmul(
                        banks[bk][:, 0:127], lhsT=w1d[ps, kv, j, :], rhs=kvcT[ps, kv, j:j + 16 * 126 + 1:16], start=(j == 0), stop=(j == 31)),
                        reads=[t_cw, t_kvc], writes=[tb[bk]], partial=(j > 0))
                n = slice(0, 127)
                P.op("act", lambda e, i4=i4, kv=kv, bk=bk: e.activation(out=gx[:, i4, 0:127], in_=banks[bk][:, 0:127], func=AF.Identity, bias=cb1[:, kv:kv + 1], scale=1.0),
                     reads=[tb[bk], t_cw], writes=[t_gx], partial=True)
        P.op("dve", lambda e: e.memset(gx[:, :, 127:128], 0.0), writes=[t_gx], partial=True)
        P.op("dve", lambda e: e.tensor_tensor(out=gu, in0=gx, in1=gx, op=ALU.mult), reads=[t_gx], writes=[t_gu])
        P.op("dve", lambda e: e.tensor_scalar(out=gu, in0=gu, scalar1=0.044715, scalar2=1.0, op0=ALU.mult, op1=ALU.add), reads=[t_gu], writes=[t_gu])
        P.op("dve", lambda e: e.tensor_tensor(out=gu, in0=gu, in1=gx, op=ALU.mult), reads=[t_gu, t_gx], writes=[t_gu])
        P.op("act", lambda e: e.activation(out=gu, in_=gu, func=AF.Sigmoid, scale=1.5957691216), reads=[t_gu], writes=[t_gu])
        P.op("dve", lambda e: e.tensor_tensor(out=hid, in0=gu, in1=gx, op=ALU.mult), reads=[t_gu, t_gx], writes=[t_hid])
        for g in range(2):
            P.op("pe", lambda e, g=g: e.matmul(banks[4][:, g * 128:g * 128 + 127], lhsT=w2k, rhs=hid[:, g, 0:127], start=True, stop=True),
                 reads=[t_cw, t_hid], writes=[tb[4]], partial=(g > 0))
        P.op("act", lambda e: e.copy(out=kcmpT[:, :, 0:127], in_=banks[4][:, 0:256].rearrange("p (a b) -> p a b", b=128)[:, :, 0:127]),
             reads=[tb[4]], writes=[t_kcmp], partial=True)
        for g in range(2):
            P.op("pe", lambda e, g=g: e.matmul(banks[5][0:127, g * 64:(g + 1) * 64], lhsT=hid[:, 2 + g, 0:127], rhs=w2v, start=True, stop=True),
                 reads=[t_cw, t_hid], writes=[tb[5]], partial=(g > 0))
        P.op("act", lambda e: e.copy(out=VC[0:127, :, 0:64], in_=banks[5][0:127, 0:128].rearrange("p (a b) -> p a b", b=64)),
             reads=[tb[5]], writes=[t_VC], partial=True)
        P.barrier()
        AR.reset(mC)
        PT = [AR.alloc([512], BF16) for _ in range(4)]
        t_PT = [Tk("PT%d" % i) for i in range(4)]
        otile = AR.alloc([512], F32)
        otb = AR.alloc([512], BF16)
        imp = AR.alloc([32], F32)
        sc = AR.alloc([32], F32)
        w2t = AR.alloc([32], F32)
        m8 = AR.alloc([16], F32)
        nsd = AR.alloc([96], F32)
        rr = AR.alloc([8], F32)
        nsT = [AR.alloc([2, 128], BF16) for _ in range(2)]
        t_ot, t_otb, t_imp, t_sc, t_m8, t_nsd, t_rr = (Tk(n) for n in ("ot", "otb", "imp", "sc", "m8", "nsd", "rr"))
        t_nsT = [Tk("nsT0"), Tk("nsT1")]
        P.op("dve", lambda e: e.memset(nsd, 0.0), writes=[t_nsd])
        sbank_next = [0, 0]
        accb = 0
        for qt in range(NT):
            qsl = slice(qt * 128, (qt + 1) * 128)
            nsq = nsT[qt % 2]
            t_nsq = t_nsT[qt % 2]
            for br in (0, 2, 1):
                for g in range(2):
                    ab = 4 + accb % 2
                    accb += 1
                    vw = 97 if br == 0 else 65
                    chunks = []
                    for j in range(4):
                        h = 4 * g + j
                        rg = h % 2
                        if br == 0:
                            blocks = [(0, "c")]
                        elif br == 1:
                            blocks = [(kt, "d" if kt == qt else "s") for kt in range(qt + 1)]
                        else:
                            blocks = []
                            for kt in range(max(0, qt - 4), qt + 1):
                                blocks.append((kt, "d" if kt == qt else ("a" if kt == qt - 4 else "n")))
                        for c0 in range(0, len(blocks), 4):
                            chunks.append((j, h, rg, blocks[c0:c0 + 4], c0 == 0))
                    pend = None

                    def emit_pv(ch, sbk, ab=ab, vw=vw, br=br, g=g):
                        j, h, rg, blks, first = ch
                        for bi, (kt, kind) in enumerate(blks):
                            if br == 0:
                                rhs = VC[:, g, :]
                            elif br == 1:
                                rhs = Vall[:, kt, g, :]
                            else:
                                rhs = Vall[:, kt, 2 + g, :]
                            P.op("pe", lambda e, ab=ab, j=j, vw=vw, sbk=sbk, bi=bi, rhs=rhs, st=(first and bi == 0): e.matmul(
                                banks[ab][:, j * 97:j * 97 + vw], lhsT=PT[sbk][:, bi * 128:(bi + 1) * 128], rhs=rhs, start=st, stop=False,
                                skip_group_check=True),
                                reads=[t_PT[sbk], t_V, t_VC], writes=[tb[ab]], partial=True)
                    for ch in chunks:
                        j, h, rg, blks, first = ch
                        sbk = RGB[rg][sbank_next[rg] % 2]
                        sbank_next[rg] += 1
                        rows = slice(rg * 64, (rg + 1) * 64)
                        arow = slice(rg * 64, rg * 64 + 4)
                        for bi, (kt, kind) in enumerate(blks):
                            cs_ = slice(bi * 128, (bi + 1) * 128)
                            ksl = slice(kt * 128, (kt + 1) * 128)
                            if br == 0:
                                lk = kcmpT[rows, g, :]
                                la = kaug[arow, 16 + qt, :]
                            elif br == 1:
                                lk = KsT[rows, g, ksl]
                                la = kaug[arow, qt - kt, :]
                            else:
                                lk = KwT[rows, g, ksl]
                                la = kaug[arow, qt - kt, :]
                            P.op("pe", lambda e, sbk=sbk, cs_=cs_, lk=lk, rows=rows, h=h, qsl=qsl: e.matmul(
                                banks[sbk][:, cs_], lhsT=lk, rhs=QT[rows, h // 2, qsl], start=True, stop=False, skip_group_check=True),
                                reads=[t_Q, t_Ks, t_Kw, t_kcmp], writes=[tb[sbk]], partial=(bi > 0))
                            extra = []
                            if kind == "c":
                                extra.append((identb, cmpmask[:, qsl], [t_const, t_ac]))
                            if br == 1 and qt >= 8:
                                erow = slice(rg * 64, rg * 64 + 32)
                                extra.append((eall[erow, ksl], nsq[erow, g, :], [t_ac, t_nsq]))
                            if kind == "d":
                                extra.append((identb, mtri, [t_const, t_ac]))
                            if kind == "a":
                                extra.append((identb, manti, [t_const, t_ac]))
                            P.op("pe", lambda e, sbk=sbk, cs_=cs_, la=la, arow=arow, h=h, last=(not extra): e.matmul(
                                banks[sbk][:, cs_], lhsT=la, rhs=qaug[arow, h, :], start=False, stop=last, skip_group_check=True),
                                reads=[t_ac], writes=[tb[sbk]], partial=True)
                            for xi, (lt_, rh_, rd_) in enumerate(extra):
                                P.op("pe", lambda e, sbk=sbk, cs_=cs_, lt_=lt_, rh_=rh_, last=(xi == len(extra) - 1): e.matmul(
                                    banks[sbk][:, cs_], lhsT=lt_, rhs=rh_, start=False, stop=last, skip_group_check=True),
                                    reads=rd_, writes=[tb[sbk]], partial=True)
                        nb_ = len(blks)
                        P.op("act", lambda e, sbk=sbk, nb_=nb_: e.activation(out=PT[sbk][:, 0:nb_ * 128], in_=banks[sbk][:, 0:nb_ * 128], func=AF.Exp),
                             reads=[tb[sbk]], writes=[t_PT[sbk]])
                        if pend is not None:
                            emit_pv(*pend)
                        pend = (ch, sbk)
                    emit_pv(*pend)
                    accv = banks[ab][:, 0:388].rearrange("p (a b) -> p a b", b=97)
                    P.op("dve", lambda e, accv=accv: e.tensor_scalar(out=rr[:, 0:4], in0=accv[:, :, 64], scalar1=1e-30, scalar2=None, op0=ALU.max),
                         reads=[tb[ab]], writes=[t_rr])
                    P.op("dve", lambda e: e.reciprocal(out=rr[:, 0:4], in_=rr[:, 0:4]), reads=[t_rr], writes=[t_rr])
                    P.op("dve", lambda e, qt=qt, br=br, g=g: e.tensor_tensor(out=rr[:, 4:8], in0=rr[:, 0:4], in1=gat[:, qt, br * 8 + g * 4: br * 8 + g * 4 + 4], op=ALU.mult),
                         reads=[t_rr, t_gat], writes=[t_rr], partial=True)
                    for j in range(4):
                        h = 4 * g + j
                        hs = slice(h * 64, (h + 1) * 64)
                        if br == 0:
                            P.op("dve", lambda e, accv=accv, j=j, hs=hs: e.tensor_scalar(out=otile[:, hs], in0=accv[:, j, 0:64], scalar1=rr[:, 4 + j:5 + j], scalar2=None, op0=ALU.mult),
                                 reads=[tb[ab], t_rr], writes=[t_ot], partial=True)
                            if qt < 8:
                                pass
                            elif j == 0:
                                P.op("dve", lambda e, accv=accv, j=j: e.tensor_scalar(out=imp, in0=accv[:, j, 65:97], scalar1=rr[:, j:j + 1], scalar2=None, op0=ALU.mult),
                                     reads=[tb[ab], t_rr], writes=[t_imp])
                            else:
                                P.op("dve", lambda e, accv=accv, j=j: e.scalar_tensor_tensor(out=imp, in0=accv[:, j, 65:97], scalar=rr[:, j:j + 1], in1=imp, op0=ALU.mult, op1=ALU.add),
                                     reads=[tb[ab], t_rr, t_imp], writes=[t_imp])
                        else:
                            P.op("dve", lambda e, accv=accv, j=j, hs=hs: e.scalar_tensor_tensor(out=otile[:, hs], in0=accv[:, j, 0:64], scalar=rr[:, 4 + j:5 + j], in1=otile[:, hs], op0=ALU.mult, op1=ALU.add),
                                 reads=[tb[ab], t_rr, t_ot], writes=[t_ot], partial=True)
                    if br == 0 and qt >= 8:
                        P.op("dve", lambda e: e.tensor_scalar(out=sc, in0=imp, scalar1=float(2.0 ** 64), scalar2=float(2.0 ** -60), op0=ALU.mult, op1=ALU.max),
                             reads=[t_imp], writes=[t_sc])
                        P.op("dve", lambda e, qt=qt: e.tensor_tensor(out=sc, in0=sc, in1=bonus[:, qt, :], op=ALU.add), reads=[t_sc, t_ac], writes=[t_sc])
                        sci = sc.bitcast(I32)
                        P.op("dve", lambda e, sci=sci: e.tensor_single_scalar(out=sci, in_=sci, scalar=-32, op=ALU.bitwise_and), reads=[t_sc], writes=[t_sc])
                        P.op("dve", lambda e, sci=sci: e.tensor_tensor(out=sci, in0=sci, in1=idxb, op=ALU.bitwise_or), reads=[t_sc, t_ac], writes=[t_sc])
                        P.op("dve", lambda e, qt=qt: e.tensor_tensor(out=sc, in0=sc, in1=cap[:, qt, :], op=ALU.min), reads=[t_sc, t_ac], writes=[t_sc])
                        P.op("dve", lambda e: e.max(out=m8[:, 0:8], in_=sc), reads=[t_sc], writes=[t_m8])
                        P.op("dve", lambda e: e.match_replace(out=w2t, in_to_replace=m8[:, 0:8], in_values=sc, imm_value=-3e38), reads=[t_sc, t_m8], writes=[t_m8], partial=True)
                        P.op("dve", lambda e: e.max(out=m8[:, 8:16], in_=w2t), reads=[t_m8], writes=[t_m8], partial=True)
                        P.op("dve", lambda e: e.tensor_scalar(out=w2t, in0=sc, scalar1=m8[:, 15:16], scalar2=None, op0=ALU.is_ge), reads=[t_sc, t_m8], writes=[t_m8], partial=True)
                        P.op("dve", lambda e: e.tensor_scalar(out=nsd[:, 0:32], in0=w2t, scalar1=1.0, scalar2=-NEGB, op0=ALU.subtract, op1=ALU.mult), reads=[t_m8], writes=[t_nsd], partial=True)
                        P.op("dve", lambda e: e.tensor_copy(out=nsd[:, 64:96], in_=nsd[:, 0:32]), reads=[t_nsd], writes=[t_nsd], partial=True)
                        P.op("pe", lambda e: e.transpose(out=banks[6][0:96, 0:128], in_=nsd, identity=identf), reads=[t_nsd, t_const], writes=[tb[6]])
                        P.op("act", lambda e, nsq=nsq, g=g: e.copy(out=nsq[0:96, g, :], in_=banks[6][0:96, 0:128]), reads=[tb[6]], writes=[t_nsq], partial=True)
            P.op("act", lambda e: e.copy(out=otb, in_=otile), reads=[t_ot], writes=[t_otb])
            bv6 = banks[7].bitcast(BF16)
            for e4 in range(4):
                P.op("pe", lambda e, e4=e4, bv6=bv6: e.transpose(out=bv6[:, e4 * 128:(e4 + 1) * 128], in_=otb[:, e4 * 128:(e4 + 1) * 128], identity=identb),
                     reads=[t_otb, t_const], writes=[tb[7]], partial=(e4 > 0))
            P.op("act", lambda e, bv6=bv6, qsl=qsl: e.copy(out=oT[:, 0:4, qsl], in_=bv6[:, 0:512].rearrange("p (a b) -> p a b", b=128)),
                 reads=[tb[7]], writes=[t_o], partial=True)
        P.barrier()
        AR.reset(mT)
        if stage <= 4:
            break
        AR.reset(layer_mark)
        idxall = AR.alloc([256], I32)
        augL = AR.alloc([2, 128], BF16)
        augR = AR.alloc([2, 296], BF16)
        mcs = AR.alloc([296], BF16)
        ovs = AR.alloc([4, 130], BF16)
        bonus_s = AR.alloc([129], F32)
        idx_s = AR.alloc([129], I32)
        oneo = AR.alloc([2, 128], BF16)
        t_sc2 = Tk("sconst")
        P.dma("sp", lambda e: e.dma_start(out=idxall, in_=ptb_in), t_sc2, writes=[t_sc2])
        pidx = AR.alloc([256], I32)
        P.dma("sp", lambda e: e.dma_start(out=pidx, in_=pidx_in), t_sc2, writes=[t_sc2], partial=True)
        P.op("dve", lambda e: e.tensor_single_scalar(out=idxall, in_=idxall, scalar=7, op=ALU.logical_shift_left), reads=[t_sc2], writes=[t_sc2], partial=True)
        P.op("dve", lambda e: e.tensor_tensor(out=idxall, in0=idxall, in1=pidx, op=ALU.bitwise_or), reads=[t_sc2], writes=[t_sc2], partial=True)
        for dst, src in ((augL, augL_in), (augR, augR_in), (mcs, mcs_in), (ovs, ovs_in), (oneo, oneo_in)):
            P.dma("pool", lambda e, dst=dst, src=src: e.dma_start(out=dst, in_=src), t_sc2, writes=[t_sc2], partial=True)
        P.dma("sp", lambda e: e.dma_start(out=bonus_s, in_=bonus_s_in), t_sc2, writes=[t_sc2], partial=True)
        P.dma("sp", lambda e: e.dma_start(out=idx_s, in_=idx_s_in), t_sc2, writes=[t_sc2], partial=True)
        w1ds = AR.alloc([2, 32, 128], BF16)
        w2ks = AR.alloc([128], BF16)
        w2vs = AR.alloc([64], BF16)
        cb1s = AR.alloc([2], F32)
        t_cws = Tk("cmpw_s")
        for kv in range(2):
            P.dma("pool", lambda e, l=l, kv=kv: e.dma_start(out=w1ds[:, kv, :, :], in_=w1d_in[l, kv]), t_cws, writes=[t_cws], partial=True)
        P.dma("pool", lambda e, l=l: e.dma_start(out=w2ks, in_=w2k_in[l]), t_cws, writes=[t_cws], partial=True)
        P.dma("pool", lambda e, l=l: e.dma_start(out=w2vs, in_=w2v_in[l]), t_cws, writes=[t_cws], partial=True)
        P.dma("sp", lambda e, l=l: e.dma_start(out=cb1s, in_=cb1_in[l]), t_cws, writes=[t_cws], partial=True)
        KTa = AR.alloc([8192], BF16)
        KTb = AR.alloc([8192], BF16)
        KTc = AR.alloc([8192], BF16)
        t_KTc = Tk("KTc")
        Vsl = AR.alloc([64, 2, 65], BF16)
        gst = [AR.alloc([8, 128], F32) for _ in range(4)]
        t_KTa, t_KTb, t_Vsl = Tk("KTa"), Tk("KTb"), Tk("Vsl")
        t_gst = [Tk("gst%d" % i) for i in range(4)]
        gxs = AR.alloc([4, 512], F32)
        gus = AR.alloc([4, 512], F32)
        hids = AR.alloc([4, 512], BF16)
        t_gxs, t_gus, t_hids = Tk("gxs"), Tk("gus"), Tk("hids")
        kcs = AR.alloc([512], BF16)
        VCs = AR.alloc([4, 2, 194], BF16)
        KTn = AR.alloc([2, 4, 128], BF16)
        KwTs = AR.alloc([512], BF16)
        kwst = AR.alloc([512], F32)
        Vw = AR.alloc([4, 2, 65], BF16)
        vwst = AR.alloc([4, 128], F32)
        PTc = AR.alloc([16], BF16)
        PT2 = AR.alloc([280], BF16)
        accs = AR.alloc([194], F32)
        r65 = AR.alloc([66], F32)
        rg2 = AR.alloc([4], F32)
        scs = AR.alloc([129], F32)
        wts = AR.alloc([129], F32)
        m8s = AR.alloc([16], F32)
        nss = AR.alloc([130], F32)
        nseo2 = [AR.alloc([2, 64, 4], BF16) for _ in range(2)]
        osum = AR.alloc([8, 128], F32)
        t_kcs, t_VCs, t_KTn, t_KwTs, t_kwst, t_Vw, t_vwst, t_PTc, t_PT2, t_accs, t_r65, t_scs, t_m8s, t_nss, t_nseo, t_osum = (Tk(n) for n in
            ("kcs", "VCs", "KTn", "KwTs", "kwst", "Vw", "vwst", "PTc", "PT2", "accs", "r65", "scs", "m8s", "nss", "nseo", "osum"))
        P.op("dve", lambda e: e.memset(KTn, 0.0), writes=[t_KTn])
        for b in range(4):
            P.op("dve", lambda e, b=b: e.tensor_copy(out=KTn[:, 0, b, 0:1], in_=kvnT[:, 2, b:b + 1]), reads=[t_kvn], writes=[t_KTn], partial=True)
            P.op("dve", lambda e, b=b: e.tensor_copy(out=KTn[:, 1, b, 0:1], in_=kvnT[:, 4, b:b + 1]), reads=[t_kvn], writes=[t_KTn], partial=True)
        P.op("dve", lambda e: e.memset(VCs, 0.0), writes=[t_VCs])
        for g in range(2):
            P.op("dve", lambda e, g=g: e.tensor_copy(out=VCs[:, :, g, 64:194], in_=ovs), reads=[t_sc2], writes=[t_VCs], partial=True)
        P.op("dve", lambda e: e.memset(Vsl, 1.0), writes=[t_Vsl])
        P.op("dve", lambda e: e.memset(Vw, 1.0), writes=[t_Vw])
        P.op("dve", lambda e: e.memset(gxs, 0.0), writes=[t_gxs])
        P.op("dve", lambda e: e.memset(r65, 0.0), writes=[t_r65])
        gi = [0]

        def gather(pool_ap, b, dstfn, t_dst, first_partial):
            for a8 in range(8):
                si = gi[0] % 4
                gi[0] += 1

                def issue(e, a8=a8, si=si):
                    return [e.indirect_dma_start(out=gst[si][:, a, :], out_offset=None, in_=pool_ap,
                                                 in_offset=bass.IndirectOffsetOnAxis(ap=idxall[:, b * 64 + a8 * 8 + a: b * 64 + a8 * 8 + a + 1], axis=0))
                            for a in range(8)]
                P.dma("pool", issue, t_gst[si], reads=[t_sc2], writes=[t_gst[si]], n=8)
                dstfn(a8, si)

        for b in range(4):
            gather(cmpkT_pool[l], b, lambda a8, si: P.op("act", lambda e, a8=a8, si=si: e.copy(out=KTa[:, a8 * 1024:(a8 + 1) * 1024], in_=gst[si].rearrange("p a t -> p (a t)")),
                                                         reads=[t_gst[si]], writes=[t_KTa], partial=True), t_KTa, False)
            gather(cmpvT_pool[l], b, lambda a8, si: P.op("act", lambda e, a8=a8, si=si: e.copy(out=KTb[:, a8 * 1024:(a8 + 1) * 1024], in_=gst[si].rearrange("p a t -> p (a t)")),
                                                         reads=[t_gst[si]], writes=[t_KTb], partial=True), t_KTb, False)
            for kv in range(2):
                KT = KTa if kv == 0 else KTb
                t_KT = t_KTa if kv == 0 else t_KTb
                for g in range(2):
                    i4 = kv * 2 + g
                    ps = slice(g * 64, (g + 1) * 64)
                    bk = RGB[g][kv]
                    for j in range(32):
                        P.op("pe", lambda e, kv=kv, ps=ps, j=j, bk=bk, KT=KT: e.matmul(
                            banks[bk][:, 0:511], lhsT=w1ds[ps, kv, j, :], rhs=KT[ps, j:j + 16 * 510 + 1:16], start=(j == 0), stop=(j == 31)),
                            reads=[t_cws, t_KT], writes=[tb[bk]], partial=(j > 0))
                    P.op("act", lambda e, i4=i4, kv=kv, bk=bk: e.activation(out=gxs[:, i4, 0:511], in_=banks[bk][:, 0:511], func=AF.Identity, bias=cb1s[:, kv:kv + 1], scale=1.0),
                         reads=[tb[bk], t_cws], writes=[t_gxs], partial=True)
            P.op("dve", lambda e: e.tensor_tensor(out=gus, in0=gxs, in1=gxs, op=ALU.mult), reads=[t_gxs], writes=[t_gus])
            P.op("dve", lambda e: e.tensor_scalar(out=gus, in0=gus, scalar1=0.044715, scalar2=1.0, op0=ALU.mult, op1=ALU.add), reads=[t_gus], writes=[t_gus])
            P.op("dve", lambda e: e.tensor_tensor(out=gus, in0=gus, in1=gxs, op=ALU.mult), reads=[t_gus, t_gxs], writes=[t_gus])
            P.op("act", lambda e: e.activation(out=gus, in_=gus, func=AF.Sigmoid, scale=1.5957691216), reads=[t_gus], writes=[t_gus])
            P.op("dve", lambda e: e.tensor_tensor(out=hids, in0=gus, in1=gxs, op=ALU.mult), reads=[t_gus, t_gxs], writes=[t_hids])
            for g in range(2):
                P.op("pe", lambda e, g=g: e.matmul(banks[4 + g][:, 0:512], lhsT=w2ks, rhs=hids[:, g, :], start=True, stop=True), reads=[t_cws, t_hids], writes=[tb[4 + g]])
                ps = slice(g * 64, (g + 1) * 64)
                P.op("act", lambda e, g=g, ps=ps: e.copy(out=kcs[ps, :], in_=banks[4 + g][ps, 0:512]), reads=[tb[4 + g]], writes=[t_kcs], partial=(g > 0))
            for g in range(2):
                for nt in range(4):
                    P.op("pe", lambda e, g=g, nt=nt: e.matmul(banks[6][:, (g * 4 + nt) * 64:(g * 4 + nt + 1) * 64], lhsT=hids[:, 2 + g, nt * 128:(nt + 1) * 128], rhs=w2vs, start=True, stop=True),
                         reads=[t_cws, t_hids], writes=[tb[6]], partial=(g + nt > 0))
            P.op("act", lambda e: e.copy(out=VCs[:, :, :, 0:64].rearrange("p t g d -> p g t d"), in_=banks[6].rearrange("p (g t d) -> p g t d", g=2, t=4)),
                 reads=[tb[6]], writes=[t_VCs], partial=True)
            for g in range(2):
                rows = slice(g * 64, (g + 1) * 64)
                tr = g * 64
                sb1 = RGB[g][0]
                P.op("pe", lambda e, tr=tr, g=g, sb1=sb1: e.matmul(banks[sb1][:, 0:16], lhsT=augL[tr:tr + 3, 1, :], rhs=augR[tr:tr + 3, g, 280:296], start=True, stop=False, skip_group_check=True),
                     reads=[t_sc2], writes=[tb[sb1]])
                P.op("pe", lambda e, sb1=sb1: e.matmul(banks[sb1][:, 0:16], lhsT=identb, rhs=mcs[:, 280:296], start=False, stop=False, skip_group_check=True),
                     reads=[t_sc2, t_const], writes=[tb[sb1]], partial=True)
                for nt in range(4):
                    P.op("pe", lambda e, rows=rows, nt=nt, sb1=sb1, b=b: e.matmul(banks[sb1][:, nt * 4:(nt + 1) * 4], lhsT=kcs[rows, nt * 128:(nt + 1) * 128], rhs=QgT[rows, :, b],
                                                                           start=False, stop=(nt == 3), skip_group_check=True), reads=[t_kcs, t_Qg], writes=[tb[sb1]], partial=True)
                P.op("act", lambda e, sb1=sb1: e.activation(out=PTc, in_=banks[sb1][:, 0:16], func=AF.Exp), reads=[tb[sb1]], writes=[t_PTc])
                for nt in range(4):
                    P.op("pe", lambda e, nt=nt, g=g: e.matmul(banks[7][0:4, 0:194], lhsT=PTc[:, nt * 4:(nt + 1) * 4], rhs=VCs[:, nt, g, :], start=(nt == 0), stop=(nt == 3)),
                         reads=[t_PTc, t_VCs], writes=[tb[7]], partial=(nt > 0))
                P.op("act", lambda e: e.copy(out=accs[0:4, :], in_=banks[7][0:4, 0:194]), reads=[tb[7]], writes=[t_accs])
                P.op("dve", lambda e: e.tensor_scalar(out=r65[0:4, 64:65], in0=accs[0:4, 64:65], scalar1=1e-30, scalar2=None, op0=ALU.max), reads=[t_accs], writes=[t_r65], partial=True)
                P.op("dve", lambda e: e.reciprocal(out=r65[0:4, 64:65], in_=r65[0:4, 64:65]), reads=[t_r65], writes=[t_r65], partial=True)
                P.op("dve", lambda e, g=g, b=b: e.tensor_tensor(out=r65[0:4, 65:66], in0=r65[0:4, 64:65], in1=gsT[0:4, (0 * 2 + g) * 4 + b:(0 * 2 + g) * 4 + b + 1], op=ALU.mult),
                     reads=[t_r65, t_gs], writes=[t_r65], partial=True)
                bg = b * 2 + g
                P.op("dve", lambda e, bg=bg: e.tensor_scalar(out=osum[0:4, bg, 0:64], in0=accs[0:4, 0:64], scalar1=r65[0:4, 65:66], scalar2=None, op0=ALU.mult),
                     reads=[t_accs, t_r65], writes=[t_osum], partial=True)
                lw = r65[0:4, 64:65] if g == 0 else r65[0:4, 0:65]
                npart = 1 if g == 0 else 65
                P.op("pe", lambda e, lw=lw, npart=npart: e.matmul(banks[6][0:npart, 0:129], lhsT=lw, rhs=accs[0:4, 65:194], start=True, stop=True),
                     reads=[t_r65, t_accs], writes=[tb[6]])
                tp = slice(tr, tr + 1)
                P.op("dve", lambda e, tp=tp: e.tensor_scalar(out=scs[tp, :], in0=banks[6][tp, 0:129], scalar1=float(2.0 ** 64), scalar2=float(2.0 ** -60), op0=ALU.mult, op1=ALU.max),
                     reads=[tb[6]], writes=[t_scs])
                P.op("dve", lambda e, tp=tp: e.tensor_tensor(out=scs[tp, :], in0=scs[tp, :], in1=bonus_s[tp, :], op=ALU.add), reads=[t_scs, t_sc2], writes=[t_scs])
                sci2 = scs.bitcast(I32)
                P.op("dve", lambda e, tp=tp, sci2=sci2: e.tensor_single_scalar(out=sci2[tp, :], in_=sci2[tp, :], scalar=-256, op=ALU.bitwise_and), reads=[t_scs], writes=[t_scs])
                P.op("dve", lambda e, tp=tp, sci2=sci2: e.tensor_tensor(out=sci2[tp, :], in0=sci2[tp, :], in1=idx_s[tp, :], op=ALU.bitwise_or), reads=[t_scs, t_sc2], writes=[t_scs])
                P.op("dve", lambda e, tp=tp: e.max(out=m8s[tp, 0:8], in_=scs[tp, :]), reads=[t_scs], writes=[t_m8s])
                P.op("dve", lambda e, tp=tp: e.match_replace(out=wts[tp, :], in_to_replace=m8s[tp, 0:8], in_values=scs[tp, :], imm_value=-3e38), reads=[t_scs, t_m8s], writes=[t_m8s], partial=True)
                P.op("dve", lambda e, tp=tp: e.max(out=m8s[tp, 8:16], in_=wts[tp, :]), reads=[t_m8s], writes=[t_m8s], partial=True)
                P.op("dve", lambda e, tp=tp: e.tensor_scalar(out=wts[tp, :], in0=scs[tp, :], scalar1=m8s[tp, 15:16], scalar2=None, op0=ALU.is_ge), reads=[t_scs, t_m8s], writes=[t_m8s], partial=True)
                P.op("dve", lambda e, tp=tp: e.tensor_scalar(out=nss[tp, 0:129], in0=wts[tp, :], scalar1=1.0, scalar2=-NEGB, op0=ALU.subtract, op1=ALU.mult), reads=[t_m8s], writes=[t_nss])
                nv = nss[tp, 0:128].rearrange("p (k two) -> p k two", two=2)
                for eo in range(2):
                    for j in range(4):
                        P.op("dve", lambda e, tp=tp, eo=eo, j=j, nv=nv, g=g: e.tensor_copy(out=nseo2[g][tp, eo, :, j], in_=nv[:, :, eo]), reads=[t_nss], writes=[t_nseo], partial=True)
                if g == 0:
                    gather(slckT_pool[l], b, lambda a8, si: P.op("act", lambda e, a8=a8, si=si: e.copy(out=KTc[:, a8 * 1024:(a8 + 1) * 1024], in_=gst[si].rearrange("p a t -> p (a t)")),
                                                                 reads=[t_gst[si]], writes=[t_KTc], partial=True), t_KTc, False)
                    gather(slcv_pool[l], b, lambda a8, si: P.op("act", lambda e, a8=a8, si=si: e.copy(out=Vsl[:, a8 * 8:(a8 + 1) * 8, :, 0:64], in_=gst[si].rearrange("p a (g d) -> p a g d", g=2)),
                                                                reads=[t_gst[si]], writes=[t_Vsl], partial=True), t_Vsl, False)
                    P.dma("sp", lambda e, l=l, b=b: e.dma_start(out=kwst, in_=wkT_in[l, b]), t_kwst, writes=[t_kwst])
                    P.op("act", lambda e: e.copy(out=KwTs, in_=kwst), reads=[t_kwst], writes=[t_KwTs])
                    P.dma("sp", lambda e, l=l, b=b: e.dma_start(out=vwst, in_=wv_in[l, b].rearrange("(t p) f -> p t f", p=128)), t_vwst, writes=[t_vwst])
                    P.op("act", lambda e: e.copy(out=Vw[:, :, :, 0:64], in_=vwst.rearrange("p t (g d) -> p t g d", g=2)), reads=[t_vwst], writes=[t_Vw], partial=True)
            for g in range(2):
                rows = slice(g * 64, (g + 1) * 64)
                tr = g * 64
                tp = slice(tr, tr + 1)
                sb2 = RGB[g][1]
                bg = b * 2 + g
                P.op("pe", lambda e, tr=tr, g=g, sb2=sb2: e.matmul(banks[sb2][:, 0:280], lhsT=augL[tr:tr + 3, 0, :], rhs=augR[tr:tr + 3, g, 0:280], start=True, stop=False, skip_group_check=True),
                     reads=[t_sc2], writes=[tb[sb2]])
                for eo in range(2):
                    P.op("pe", lambda e, tp=tp, eo=eo, sb2=sb2, g=g: e.matmul(banks[sb2][:, 0:256], lhsT=oneo[tp, eo, :], rhs=nseo2[g][tp, eo, :, :].rearrange("p k j -> p (k j)"), start=False, stop=False, skip_group_check=True),
                         reads=[t_sc2, t_nseo], writes=[tb[sb2]], partial=True)
                P.op("pe", lambda e, sb2=sb2: e.matmul(banks[sb2][:, 0:280], lhsT=identb, rhs=mcs[:, 0:280], start=False, stop=False, skip_group_check=True),
                     reads=[t_sc2, t_const], writes=[tb[sb2]], partial=True)
                for kt in range(70):
                    if kt < 64:
                        lk, rd = KTc[rows, kt * 128:(kt + 1) * 128], [t_KTc]
                    elif kt == 64:
                        lk, rd = KTn[rows, 0, b, :], [t_KTn]
                    elif kt < 69:
                        lk, rd = KwTs[rows, (kt - 65) * 128:(kt - 64) * 128], [t_KwTs]
                    else:
                        lk, rd = KTn[rows, 1, b, :], [t_KTn]
                    P.op("pe", lambda e, lk=lk, rows=rows, kt=kt, sb2=sb2, b=b: e.matmul(banks[sb2][:, kt * 4:(kt + 1) * 4], lhsT=lk, rhs=QgT[rows, :, b], start=False, stop=(kt == 69), skip_group_check=True),
                         reads=rd + [t_Qg], writes=[tb[sb2]], partial=True)
                P.op("act", lambda e, sb2=sb2: e.activation(out=PT2, in_=banks[sb2][:, 0:280], func=AF.Exp), reads=[tb[sb2]], writes=[t_PT2])
                for kt in range(65):
                    if kt < 64:
                        P.op("pe", lambda e, kt=kt, g=g: e.matmul(banks[7][0:4, 0:65], lhsT=PT2[:, kt * 4:(kt + 1) * 4], rhs=Vsl[:, kt, g, :], start=(kt == 0), stop=False, skip_group_check=True),
                             reads=[t_PT2, t_Vsl], writes=[tb[7]], partial=(kt > 0))
                    else:
                        P.op("pe", lambda e, g=g, b=b: e.matmul(banks[7][0:4, 0:65], lhsT=PT2[0:1, 256:260], rhs=VN[0:1, b, 0, g * 65:(g + 1) * 65], start=False, stop=True, skip_group_check=True),
                             reads=[t_PT2, t_VN], writes=[tb[7]], partial=True)
                for kt in range(5):
                    if kt < 4:
                        P.op("pe", lambda e, kt=kt, g=g: e.matmul(banks[7][0:4, 65:130], lhsT=PT2[:, 260 + kt * 4:264 + kt * 4], rhs=Vw[:, kt, g, :], start=False, stop=False, skip_group_check=True),
                             reads=[t_PT2, t_Vw], writes=[tb[7]], partial=True)
                    else:
                        P.op("pe", lambda e, g=g, b=b: e.matmul(banks[7][0:4, 65:130], lhsT=PT2[0:1, 276:280], rhs=VN[0:1, b, 1, g * 65:(g + 1) * 65], start=False, stop=True, skip_group_check=True),
                             reads=[t_PT2, t_VN], writes=[tb[7]], partial=True)
                P.op("act", lambda e: e.copy(out=accs[0:4, 0:130], in_=banks[7][0:4, 0:130]), reads=[tb[7]], writes=[t_accs])
                for br in (1, 2):
                    c0 = (br - 1) * 65
                    P.op("dve", lambda e, c0=c0: e.tensor_scalar(out=rg2[0:4, 0:1], in0=accs[0:4, c0 + 64:c0 + 65], scalar1=1e-30, scalar2=None, op0=ALU.max), reads=[t_accs], writes=[t_r65])
                    P.op("dve", lambda e: e.reciprocal(out=rg2[0:4, 0:1], in_=rg2[0:4, 0:1]), reads=[t_r65], writes=[t_r65])
                    P.op("dve", lambda e, br=br, g=g, b=b: e.tensor_tensor(out=rg2[0:4, 1:2], in0=rg2[0:4, 0:1], in1=gsT[0:4, (br * 2 + g) * 4 + b:(br * 2 + g) * 4 + b + 1], op=ALU.mult),
                         reads=[t_r65, t_gs], writes=[t_r65])
                    P.op("dve", lambda e, c0=c0, bg=bg: e.scalar_tensor_tensor(out=osum[0:4, bg, 0:64], in0=accs[0:4, c0:c0 + 64], scalar=rg2[0:4, 1:2], in1=osum[0:4, bg, 0:64], op0=ALU.mult, op1=ALU.add),
                         reads=[t_accs, t_r65, t_osum], writes=[t_osum], partial=True)
        P.op("dve", lambda e: e.tensor_copy(out=osum[0:4, :, 64:128], in_=osum[0:4, :, 0:64]), reads=[t_osum], writes=[t_osum], partial=True)
        for bg in range(8):
            b, g = bg // 2, bg % 2
            P.op("pe", lambda e, bg=bg: e.transpose(out=banks[6][:, bg * 4:(bg + 1) * 4], in_=osum[0:4, bg, :], identity=identf[0:4, 0:4]), reads=[t_osum, t_const], writes=[tb[6]], partial=(bg > 0))
        ot4 = banks[6][:, 0:32].rearrange("p (b g a r) -> p b g a r", b=4, g=2, a=2)
        for r in range(2):
            rs_ = slice(r * 64, (r + 1) * 64)
            P.op("act", lambda e, r=r, rs_=rs_, ot4=ot4: e.copy(out=osT[rs_, 0:4, :].rearrange("p (g a) b -> p b g a", g=2), in_=ot4[rs_, :, :, :, r]),
                 reads=[tb[6]], writes=[t_os], partial=True)
        P.barrier()
        AR.reset(layer_mark)
        AR.alloc([KD, T], F32)
        load_x(xsrc)
        mO = AR.mark()
        wo = AR.alloc([KD, D], BF16)
        t_wo = Tk("wo")
        for ec in range(KD):
            P.dma("pool", lambda e, l=l, ec=ec: e.dma_start(out=wo[:, ec, :], in_=wo_in[l, :, ec, :]), t_wo, writes=[t_wo], partial=True)
        nb = 0
        for dc in range(KD):
            for tg in range(4):
                ts = slice(tg * 512, (tg + 1) * 512)
                bk = nb % 4
                nb += 1
                for ec in range(KD):
                    P.op("pe", lambda e, bk=bk, ec=ec, dc=dc, ts=ts: e.matmul(
                        banks[bk], lhsT=wo[:, ec, dc * 128:(dc + 1) * 128], rhs=oT[:, ec, ts], start=(ec == 0), stop=(ec == KD - 1)),
                        reads=[t_wo, t_o], writes=[tb[bk]], partial=(ec > 0))
                P.op("dve", lambda e, bk=bk, dc=dc, ts=ts, l=l: e.scalar_tensor_tensor(
                    out=xT[:, dc, ts], in0=banks[bk], scalar=mods[:, l * 48 + 16 + dc, 0:1], in1=xT[:, dc, ts], op0=ALU.mult, op1=ALU.add),
                    reads=[tb[bk], t_mods, t_x], writes=[t_x], partial=True)
        smx = AR.alloc([4], F32)
        t_smx = Tk("smx")
        for dc in range(KD):
            bk = 4 + dc % 2
            for ec in range(KD):
                P.op("pe", lambda e, bk=bk, ec=ec, dc=dc: e.matmul(banks[bk][:, 0:4], lhsT=wo[:, ec, dc * 128:(dc + 1) * 128], rhs=osT[:, ec, :], start=(ec == 0), stop=(ec == KD - 1)),
                     reads=[t_wo, t_os], writes=[tb[bk]], partial=(ec > 0))
            P.op("dve", lambda e, bk=bk, dc=dc, l=l: e.tensor_tensor(out=smx, in0=banks[bk][:, 0:4], in1=mods[:, l * 48 + 16 + dc, 1:5], op=ALU.mult), reads=[tb[bk], t_mods], writes=[t_smx])
            P.op("dve", lambda e, dc=dc: e.tensor_tensor(out=xsT[:, dc, :], in0=xsT[:, dc, :], in1=smx, op=ALU.add), reads=[t_smx, t_xs], writes=[t_xs], partial=True)
        P.barrier()
        AR.reset(mO)
        norm_mod(A2, B2, [0, 1])
        P.barrier()
        sample_norm(l, 4, 3, n2g[l])
        P.barrier()
        fsp = AR.alloc([24, 3, 4], F32)
        fso = AR.alloc([24, 2, 4], F32)
        sact = AR.alloc([4, 4], BF16)
        sfa = AR.alloc([4], F32)
        t_fsp, t_fso, t_sact, t_sfa = Tk("fsp"), Tk("fso"), Tk("sact"), Tk("sfa")
        P.dma("sp", lambda e, l=l: e.dma_start(out=fsp[:, :, 0:2, :], in_=fst_in[l]), t_fsp, writes=[t_fsp])
        P.op("dve", lambda e: e.memset(fso, 0.0), writes=[t_fso])
        wup = [AR.alloc([KD, 2, 512], BF16) for _ in range(2)]
        wdn = [AR.alloc([4, D], BF16) for _ in range(2)]
        t_wup = [Tk("wup0"), Tk("wup1")]
        t_wdn = [Tk("wdn0"), Tk("wdn1")]
        fcw = AR.alloc([24, 3], F32)
        fcb = AR.alloc([24], F32)
        t_fc = Tk("fc")
        P.dma("sp", lambda e, l=l: e.dma_start(out=fcw, in_=fcw_in[l]), t_fc, writes=[t_fc], partial=True)
        P.dma("sp", lambda e, l=l: e.dma_start(out=fcb, in_=fcb_in[l]), t_fc, writes=[t_fc], partial=True)
        fst = [AR.alloc([514], F32) for _ in range(2)]
        t_fst = [Tk("fst0"), Tk("fst1")]
        hal = AR.alloc([4, 2], F32)
        t_hal = Tk("hal")
        fac = [AR.alloc([512], F32) for _ in range(2)]
        t_fac = [Tk("fac0"), Tk("fac1")]
        actT = [AR.alloc([4, 512], BF16) for _ in range(2)]
        t_act = [Tk("act0"), Tk("act1")]
        nbk = 0
        it = 0
        for fg in range(6):
            nf = 4 if fg < 5 else 2
            ws = fg % 2
            for kc in range(KD):
                P.dma("pool", lambda e, l=l, fg=fg, ws=ws, kc=kc: e.dma_start(out=wup[ws][:, kc, :, :], in_=wup_in[l, fg, :, kc, :, :]),
                      t_wup[ws], writes=[t_wup[ws]], partial=(kc > 0))
            for fcl in range(nf):
                P.dma("pool", lambda e, l=l, fg=fg, ws=ws, fcl=fcl: e.dma_start(out=wdn[ws][:, fcl, :], in_=wdn_in[l, :, fg * 4 + fcl, :]),
                      t_wdn[ws], writes=[t_wdn[ws]], partial=(fcl > 0))
            for tg in range(4):
                ts = slice(tg * 512, (tg + 1) * 512)
                asl = it % 2
                it += 1
                for fcl in range(nf):
                    fc = fg * 4 + fcl
                    bu = 2 + (nbk % 2) * 2
                    bv = bu + 1
                    nbk += 1
                    si = fcl % 2
                    for kc in range(KD):
                        P.op("pe", lambda e, bu=bu, kc=kc, ws=ws, fcl=fcl, ts=ts: e.matmul(
                            banks[bu], lhsT=wup[ws][:, kc, 0, fcl * 128:(fcl + 1) * 128], rhs=hT[:, kc, ts], start=(kc == 0), stop=(kc == KD - 1)),
                            reads=[t_wup[ws], t_h], writes=[tb[bu]], partial=(kc > 0))
                    for kc in range(KD):
                        P.op("pe", lambda e, bv=bv, kc=kc, ws=ws, fcl=fcl, ts=ts: e.matmul(
                            banks[bv], lhsT=wup[ws][:, kc, 1, fcl * 128:(fcl + 1) * 128], rhs=hT[:, kc, ts], start=(kc == 0), stop=(kc == KD - 1)),
                            reads=[t_wup[ws], t_h], writes=[tb[bv]], partial=(kc > 0))
                    if tg == 0:
                        P.op("dve", lambda e, si=si: e.memset(fst[si][:, 0:2], 0.0), writes=[t_fst[si]], partial=True)
                    else:
                        P.op("dve", lambda e, si=si, fcl=fcl: e.tensor_copy(out=fst[si][:, 0:2], in_=hal[:, fcl, :]), reads=[t_hal], writes=[t_fst[si]], partial=True)
                    P.op("act", lambda e, bu=bu, si=si: e.copy(out=fst[si][:, 2:514], in_=banks[bu]), reads=[tb[bu]], writes=[t_fst[si]], partial=True)
                    if tg < 3:
                        P.op("dve", lambda e, si=si, fcl=fcl: e.tensor_copy(out=hal[:, fcl, :], in_=fst[si][:, 512:514]), reads=[t_fst[si]], writes=[t_hal], partial=True)
                    else:
                        P.dma("sp", lambda e, l=l, si=si, fc=fc: e.dma_start(out=fconv_out[l, :, fc, :], in_=fst[si][:, 512:514]), t_fst[si], reads=[t_fst[si]])
                    P.op("dve", lambda e, si=si, fc=fc: e.tensor_scalar(out=fac[si], in0=fst[si][:, 2:514], scalar1=fcw[:, fc, 2:3], scalar2=fcb[:, fc:fc + 1], op0=ALU.mult, op1=ALU.add),
                         reads=[t_fst[si], t_fc], writes=[t_fac[si]])
                    for k in range(2):
                        P.op("dve", lambda e, si=si, fc=fc, k=k: e.scalar_tensor_tensor(out=fac[si], in0=fst[si][:, k:k + 512], scalar=fcw[:, fc, k:k + 1], in1=fac[si], op0=ALU.mult, op1=ALU.add),
                             reads=[t_fst[si], t_fc], writes=[t_fac[si]])
                    P.op("act", lambda e, si=si: e.activation(out=fac[si], in_=fac[si], func=AF.Silu), reads=[t_fac[si]], writes=[t_fac[si]])
                    P.op("dve", lambda e, si=si, bv=bv, asl=asl, fcl=fcl: e.tensor_tensor(out=actT[asl][:, fcl, :], in0=fac[si], in1=banks[bv], op=ALU.mult),
                         reads=[t_fac[si], tb[bv]], writes=[t_act[asl]], partial=(fcl > 0))
                for dc in range(KD):
                    bd = dc % 2
                    for fcl in range(nf):
                        P.op("pe", lambda e, bd=bd, ws=ws, fcl=fcl, dc=dc, asl=asl, nf=nf: e.matmul(
                            banks[bd], lhsT=wdn[ws][:, fcl, dc * 128:(dc + 1) * 128], rhs=actT[asl][:, fcl, :], start=(fcl == 0), stop=(fcl == nf - 1)),
                            reads=[t_wdn[ws], t_act[asl]], writes=[tb[bd]], partial=(fcl > 0))
                    P.op("dve", lambda e, bd=bd, dc=dc, ts=ts, l=l: e.scalar_tensor_tensor(
                        out=xT[:, dc, ts], in0=banks[bd], scalar=mods[:, l * 48 + 40 + dc, 0:1], in1=xT[:, dc, ts], op0=ALU.mult, op1=ALU.add),
                        reads=[tb[bd], t_mods, t_x], writes=[t_x], partial=True)
            for fcl in range(nf):
                fc = fg * 4 + fcl
                for half in range(2):
                    for kc in range(KD):
                        P.op("pe", lambda e, kc=kc, ws=ws, fcl=fcl, half=half: e.matmul(banks[6][:, half * 4:(half + 1) * 4], lhsT=wup[ws][:, kc, half, fcl * 128:(fcl + 1) * 128], rhs=hsT[:, kc, :],
                                                                                    start=(kc == 0), stop=(kc == KD - 1)), reads=[t_wup[ws], t_hs], writes=[tb[6]], partial=(half + kc > 0))
                P.op("act", lambda e, fc=fc: e.copy(out=fsp[:, fc, 2, :], in_=banks[6][:, 0:4]), reads=[tb[6]], writes=[t_fsp], partial=True)
                P.op("dve", lambda e, fc=fc: e.tensor_scalar(out=sfa, in0=fsp[:, fc, 2, :], scalar1=fcw[:, fc, 2:3], scalar2=fcb[:, fc:fc + 1], op0=ALU.mult, op1=ALU.add),
                     reads=[t_fsp, t_fc], writes=[t_sfa])
                for k in range(2):
                    P.op("dve", lambda e, fc=fc, k=k: e.scalar_tensor_tensor(out=sfa, in0=fsp[:, fc, k, :], scalar=fcw[:, fc, k:k + 1], in1=sfa, op0=ALU.mult, op1=ALU.add),
                         reads=[t_fsp, t_fc, t_sfa], writes=[t_sfa])
                P.op("act", lambda e: e.activation(out=sfa, in_=sfa, func=AF.Silu), reads=[t_sfa], writes=[t_sfa])
                P.op("dve", lambda e, fcl=fcl: e.tensor_tensor(out=sact[:, fcl, :], in0=sfa, in1=banks[6][:, 4:8], op=ALU.mult), reads=[t_sfa, tb[6]], writes=[t_sact], partial=(fcl > 0))
            for dc in range(KD):
                for fcl in range(nf):
                    P.op("pe", lambda e, ws=ws, fcl=fcl, dc=dc, nf=nf: e.matmul(banks[7][:, 0:4], lhsT=wdn[ws][:, fcl, dc * 128:(dc + 1) * 128], rhs=sact[:, fcl, :], start=(fcl == 0), stop=(fcl == nf - 1)),
                         reads=[t_wdn[ws], t_sact], writes=[tb[7]], partial=(fcl > 0))
                P.op("dve", lambda e, dc=dc, l=l: e.tensor_tensor(out=sfa, in0=banks[7][:, 0:4], in1=mods[:, l * 48 + 40 + dc, 1:5], op=ALU.mult), reads=[tb[7], t_mods], writes=[t_sfa])
                P.op("dve", lambda e, dc=dc: e.tensor_tensor(out=xsT[:, dc, :], in0=xsT[:, dc, :], in1=sfa, op=ALU.add), reads=[t_sfa, t_xs], writes=[t_xs], partial=True)
        P.barrier()
        P.op("dve", lambda e: e.tensor_copy(out=fso[:, 0:NFC, 0, :], in_=fsp[:, 0:NFC, 1, :]), reads=[t_fsp], writes=[t_fso], partial=True)
        P.op("dve", lambda e: e.tensor_copy(out=fso[:, 0:NFC, 1, :], in_=fsp[:, 0:NFC, 2, :]), reads=[t_fsp], writes=[t_fso], partial=True)
        P.dma("sp", lambda e, l=l: e.dma_start(out=fcs_out[l], in_=fso), t_fso, reads=[t_fso])
        if l < DEPTH - 1:
            for kc in range(KD):
                P.dma("sp", lambda e, kc=kc: e.dma_start(out=xscr[:, kc, :], in_=xT[:, kc, :]), t_x, reads=[t_x], writes=[t_xscr], partial=True)
        P.barrier()
    if stage >= 99:
        AR.reset(layer_mark)
        AR.alloc([KD, T], F32)
        sq = AR.alloc([KD, 512], BF16)
        rs = AR.alloc([512], F32)
        yst = [AR.alloc([512], F32) for _ in range(2)]
        t_sq, t_rs, t_yst = Tk("fsq"), Tk("frs"), [Tk("yst0"), Tk("yst1")]
        for tg in range(4):
            ts = slice(tg * 512, (tg + 1) * 512)
            bk = tg % 2
            P.op("act", lambda e, ts=ts: e.activation(out=sq, in_=xT[:, :, ts], func=AF.Square), reads=[t_x], writes=[t_sq])
            for kc in range(KD):
                P.op("pe", lambda e, kc=kc, bk=bk: e.matmul(banks[bk], lhsT=onesb, rhs=sq[:, kc, :], start=(kc == 0), stop=(kc == KD - 1)),
                     reads=[t_sq, t_const], writes=[tb[bk]], partial=(kc > 0))
            P.op("dve", lambda e, bk=bk: e.tensor_scalar(out=rs, in0=banks[bk], scalar1=1.0 / D, scalar2=EPS, op0=ALU.mult, op1=ALU.add),
                 reads=[tb[bk]], writes=[t_rs])
            P.op("act", lambda e: e.activation(out=rs, in_=rs, func=AF.Sqrt), reads=[t_rs], writes=[t_rs])
            P.op("dve", lambda e: e.reciprocal(out=rs, in_=rs), reads=[t_rs], writes=[t_rs])
            for kc in range(KD):
                yi = kc % 2
                P.op("dve", lambda e, kc=kc, ts=ts, yi=yi: e.scalar_tensor_tensor(
                    out=yst[yi], in0=xT[:, kc, ts], scalar=fing[:, kc:kc + 1], in1=rs, op0=ALU.mult, op1=ALU.mult),
                    reads=[t_x, t_rs, t_small], writes=[t_yst[yi]])
                P.dma("sp", lambda e, kc=kc, ts=ts, yi=yi: e.dma_start(out=yT_out[:, kc, ts], in_=yst[yi]), t_yst[yi], reads=[t_yst[yi]])
        st1 = AR.alloc([KD, 4], F32)
        srs = AR.alloc([4], F32)
        t_st1, t_srs = Tk("fst1"), Tk("fsrs")
        P.op("dve", lambda e: e.tensor_tensor(out=st1, in0=xsT, in1=xsT, op=ALU.mult), reads=[t_xs], writes=[t_st1])
        for kc in range(KD):
            P.op("pe", lambda e, kc=kc: e.matmul(banks[2][:, 0:4], lhsT=onesf, rhs=st1[:, kc, :], start=(kc == 0), stop=(kc == KD - 1)), reads=[t_st1, t_const], writes=[tb[2]], partial=(kc > 0))
        P.op("dve", lambda e: e.tensor_scalar(out=srs, in0=banks[2][:, 0:4], scalar1=1.0 / D, scalar2=EPS, op0=ALU.mult, op1=ALU.add), reads=[tb[2]], writes=[t_srs])
        P.op("act", lambda e: e.activation(out=srs, in_=srs, func=AF.Sqrt), reads=[t_srs], writes=[t_srs])
        P.op("dve", lambda e: e.reciprocal(out=srs, in_=srs), reads=[t_srs], writes=[t_srs])
        for kc in range(KD):
            P.op("dve", lambda e, kc=kc: e.scalar_tensor_tensor(out=st1[:, kc, :], in0=xsT[:, kc, :], scalar=fing[:, kc:kc + 1], in1=srs, op0=ALU.mult, op1=ALU.mult),
                 reads=[t_xs, t_srs, t_small], writes=[t_st1], partial=True)
        P.dma("sp", lambda e: e.dma_start(out=ysT_out, in_=st1), t_st1, reads=[t_st1])
    P.barrier()
    P.emit()
    return nc


_PROG_CACHE = {}


def _host_consts():
    r = np.arange(128)
    f32 = np.float32
    tok = np.arange(T)
    ncmp = np.arange(128)
    ends = 16 * ncmp + 31
    cmpmask = np.where((ends[:, None] > tok[None, :]) | (ncmp[:, None] >= 127), NEGB, 0.0).astype(f32)
    eall = np.zeros((128, T), f32)
    for base in (0, 64):
        eall[base + (tok // 64), tok] = 1.0
    slopes = 2.0 ** (-(np.arange(8) + 1.0))
    kaug = np.zeros((128, 32, 128), f32)
    qaug = np.zeros((128, 8, 128), f32)
    for base in (0, 64):
        for dlt in range(16):
            kaug[base + 0, dlt, :] = -dlt
            kaug[base + 1, dlt, :] = r
            kaug[base + 2, dlt, :] = 1.0
        for qt in range(16):
            kaug[base + 0, 16 + qt, :] = ends // 128 - qt
            kaug[base + 1, 16 + qt, :] = ends % 128
            kaug[base + 2, 16 + qt, :] = 1.0
        for h in range(8):
            qaug[base + 0, h, :] = 128.0 * slopes[h]
            qaug[base + 1, h, :] = slopes[h]
            qaug[base + 2, h, :] = -slopes[h] * r
    ov = np.zeros((128, 33), f32)
    ov[:, 0] = 1.0
    cs_ = 16 * ncmp
    ss_ = 64 * np.arange(32)
    ov[:, 1:] = ((cs_[:, None] < ss_[None, :] + 64) & (cs_[:, None] + 32 > ss_[None, :])).astype(f32)
    ov[127, 1:] = 0.0
    qpos = (np.arange(NT)[None, :] * 128 + r[:, None])
    cur = qpos // 64
    jj = np.arange(32)[None, None, :]
    forced = (jj == 0) | (jj == cur[:, :, None]) | (jj == cur[:, :, None] - 1)
    valid = (jj * 64) <= qpos[:, :, None]
    bonus = np.where(forced, 1e6 * 2.0 ** 64, 0.0).astype(f32)
    cap = np.where(valid, 3e38, -1e30).astype(f32)
    idxb = np.broadcast_to((31 - np.arange(32)).astype(np.int32)[None, :], (128, 32)).copy()
    augL = np.zeros((128, 2, 128), f32)
    augR = np.zeros((128, 2, 296), f32)
    A_t = np.array([kt - 64 for kt in range(64)] + [0] + [kt - 4 for kt in range(4)] + [0], f32)
    for base in (0, 64):
        augL[base + 0, 0, :] = r
        augL[base + 1, 0, :] = 1.0
        augL[base + 2, 0, :] = 1.0
        augL[base + 0, 1, :] = 16.0 * r
        augL[base + 1, 1, :] = 1.0
        augL[base + 2, 1, :] = 1.0
        for g in range(2):
            for j in range(4):
                sj = slopes[4 * g + j]
                augR[base + 0, g, j:280:4] = sj
                augR[base + 1, g, j:280:4] = 128.0 * sj * A_t
                augR[base + 0, g, 280 + j:296:4] = sj
                augR[base + 1, g, 280 + j:296:4] = 31.0 * sj
                augR[base + 2, g, 280 + j:296:4] = 2048.0 * sj * (np.arange(4) - 4)
    mcs = np.zeros((128, 296), f32)
    mcs[1:, 64 * 4:65 * 4] = NEGB
    mcs[1:, 69 * 4:70 * 4] = NEGB
    mcs[127, 280 + 12:296] = NEGB
    ns = np.arange(512)
    jb = np.arange(129)
    ovl = ((16 * ns[:, None] < 64 * jb[None, :] + 64) & (16 * ns[:, None] + 32 > 64 * jb[None, :])).astype(f32)
    ovl[511, :] = 0.0
    ovs = np.zeros((128, 4, 130), f32)
    ovs[:, :, 0] = 1.0
    ovs[:, :, 1:] = ovl.reshape(4, 128, 129).transpose(1, 0, 2)
    bonus_s = np.zeros((128, 129), f32)
    bonus_s[:, [0, 127, 128]] = 1e6 * 2.0 ** 64
    idx_s = np.broadcast_to((255 - jb).astype(np.int32)[None, :], (128, 129)).copy()
    oneo = np.zeros((128, 2, 128), f32)
    oneo[:, 0, :64] = 1.0
    oneo[:, 1, 64:] = 1.0
    pidx = np.broadcast_to(np.arange(128, dtype=np.int32)[:, None], (128, 256)).copy()
    return {"augL": augL, "augR": augR, "mcs": mcs, "ovs": ovs, "bonus_s": bonus_s, "idx_s": idx_s, "oneo": oneo, "pidx": pidx,
            "ident": np.eye(128, dtype=np.float32),
            "mtri": np.where(r[:, None] > r[None, :], NEGB, 0.0).astype(f32),
            "manti": np.where(r[None, :] > r[:, None], NEGB, 0.0).astype(f32),
            "cmpmask": cmpmask, "eall": eall, "kaug": kaug, "qaug": qaug, "ov": ov,
            "bonus": bonus, "cap": cap, "idxb": idxb,
            "utri": (r[:, None] <= r[None, :]).astype(np.float32),
            "mtrif": np.where(r[:, None] > r[None, :], NEGB, 0.0).astype(np.float32)}


def kernel(**inp):
    f32 = np.float32
    nc = _PROG_CACHE.get("nc")
    if nc is None:
        nc = build_program()
        _PROG_CACHE["nc"] = nc
    shared = dict(_host_consts())
    ada_w = np.asarray(inp["ada_w"], f32)
    shared["adaw"] = np.ascontiguousarray(
        ada_w.reshape(DEPTH, KD, 128, 6, 1024).transpose(0, 3, 2, 1, 4))
    shared["adab"] = np.ascontiguousarray(np.asarray(inp["ada_b"], f32).reshape(DEPTH, 48, 128).transpose(0, 2, 1))
    shared["n1g"] = np.stack([_col(np.asarray(inp["norm1_g"], f32)[l], KD) for l in range(DEPTH)])
    shared["n2g"] = np.stack([_col(np.asarray(inp["norm2_g"], f32)[l], KD) for l in range(DEPTH)])
    shared["fing"] = _col(np.asarray(inp["final_g"], f32), KD)
    w_in = np.asarray(inp["w_in"], f32)
    tm_cols = np.concatenate([np.arange(512, 1816), np.arange(2584, 2592)])
    shared["wtm"] = np.stack([_lay_kc(w_in[l][:, tm_cols]) for l in range(DEPTH)])
    fm_chunks = [np.arange(128 * i, 128 * i + 128) for i in range(4)]
    fm_chunks += [np.arange(512, 640), np.arange(640, 768)]
    fm_chunks += [np.concatenate([np.arange(768 + 64 * g, 832 + 64 * g)] * 2) for g in range(2)]
    fm_chunks += [np.concatenate([np.arange(1024 + 64 * g, 1088 + 64 * g)] * 2) for g in range(2)]
    fm_chunks += [np.arange(1816 + 128 * c, 1816 + 128 * c + 128) for c in range(6)]
    fm_chunks += [np.concatenate([np.arange(64 * j, 64 * j + 64), np.arange(256 + 64 * j, 320 + 64 * j)]) for j in range(4)]
    shared["wfm"] = np.stack([np.stack([_lay_kc(w_in[l][:, ch]) for ch in fm_chunks]) for l in range(DEPTH)])
    scw = np.asarray(inp["ssm_conv_w"], f32)
    shared["scw"] = np.ascontiguousarray(scw.reshape(DEPTH, 4, 6, 128).transpose(0, 3, 2, 1))
    rep = lambda v: np.ascontiguousarray(np.broadcast_to(np.asarray(v, f32)[:, None, :], (DEPTH, 128, np.asarray(v).shape[-1])))
    shared["dtb"] = rep(inp["dt_bias"])
    shared["alog"] = rep(inp["a_log"])
    shared["dsk"] = rep(inp["d_skip"])
    shared["sng"] = rep(inp["ssm_norm_g"])
    shared["scb"] = np.ascontiguousarray(np.asarray(inp["ssm_conv_b"], f32).reshape(DEPTH, 6, 128).transpose(0, 2, 1))

    shared["wo"] = np.stack([_lay_kc(np.asarray(inp["w_out"], f32)[l]) for l in range(DEPTH)])
    wu = np.asarray(inp["ffn_w_up"], f32)
    wu = np.pad(wu.reshape(DEPTH, KD, 128, 2, D_FF), ((0, 0), (0, 0), (0, 0), (0, 0), (0, 3072 - D_FF)))
    shared["wup"] = np.ascontiguousarray(wu.reshape(DEPTH, KD, 128, 2, 6, 512).transpose(0, 4, 2, 1, 3, 5))
    wd = np.pad(np.asarray(inp["ffn_w_down"], f32), ((0, 0), (0, 3072 - D_FF), (0, 0)))
    shared["wdn"] = np.ascontiguousarray(wd.reshape(DEPTH, 24, 128, D).transpose(0, 2, 1, 3))
    fw = np.pad(np.asarray(inp["ffn_conv_w"], f32), ((0, 0), (0, 0), (0, 3072 - D_FF)))
    shared["fcw"] = np.ascontiguousarray(fw.reshape(DEPTH, 3, 24, 128).transpose(0, 3, 2, 1))
    fb = np.pad(np.asarray(inp["ffn_conv_b"], f32), ((0, 0), (0, 3072 - D_FF)))
    shared["fcb"] = np.ascontiguousarray(fb.reshape(DEPTH, 24, 128).transpose(0, 2, 1))
    w1 = np.stack([np.asarray(inp["cmpk_w1"], f32), np.asarray(inp["cmpv_w1"], f32)], axis=1)
    w1 = w1.transpose(0, 1, 3, 2, 4)
    shared["w1d"] = np.ascontiguousarray(np.concatenate([w1, w1], axis=2))
    w2k = np.asarray(inp["cmpk_w2"], f32)
    shared["w2k"] = np.ascontiguousarray(np.concatenate([w2k, w2k], axis=2))
    shared["w2v"] = np.ascontiguousarray(np.asarray(inp["cmpv_w2"], f32))
    shared["cb1"] = np.ascontiguousarray(np.stack([np.asarray(inp["cmpk_b1"], f32), np.asarray(inp["cmpv_b1"], f32)], axis=2))
    hc = np.stack([np.asarray(inp["dt_bias"], f32), np.asarray(inp["a_log"], f32), np.asarray(inp["d_skip"], f32)], axis=2)
    shared["hcol"] = np.ascontiguousarray(hc)
    dsk = np.asarray(inp["d_skip"], f32)
    shared["dskc"] = np.ascontiguousarray(np.repeat(dsk.reshape(DEPTH, 4, 2, 1), 64, axis=3).transpose(0, 2, 3, 1).reshape(DEPTH, 128, 4))
    shared["sngc"] = np.ascontiguousarray(np.asarray(inp["ssm_norm_g"], f32).reshape(DEPTH, 4, 128).transpose(0, 2, 1))
    ex = np.zeros((8, 4, 128), f32)
    for c in range(4):
        for hh in range(2):
            ex[2 * c + hh, c, hh * 64:(hh + 1) * 64] = 1.0
    shared["ex"] = ex
    def poolT(a):
        a = np.asarray(a, f32).reshape(DEPTH, -1, 128, 128)
        return np.ascontiguousarray(a.transpose(0, 1, 3, 2)).reshape(DEPTH, -1, 128)
    for nm, key in (("cmpkT_pool", "cache_cmp_k"), ("cmpvT_pool", "cache_cmp_v"), ("slckT_pool", "cache_slc_k")):
        pt_ = poolT(inp[key])
        for l in range(DEPTH):
            shared["%s%d" % (nm, l)] = pt_[l]
    sv_ = np.asarray(inp["cache_slc_v"], f32).reshape(DEPTH, -1, 128)
    for l in range(DEPTH):
        shared["slcv_pool%d" % l] = np.ascontiguousarray(sv_[l])
    page_table = np.asarray(inp["page_table"], np.int32)
    st_ffn = np.asarray(inp["state_ffn_conv"], f32)
    x_sample = np.asarray(inp["x_sample"], f32)
    st_conv = np.asarray(inp["state_ssm_conv"], f32)
    st_ssm = np.asarray(inp["state_ssm"], f32)
    st_wk = np.asarray(inp["state_win_k"], f32).reshape(DEPTH, 32, 512, 128)
    st_wv = np.asarray(inp["state_win_v"], f32).reshape(DEPTH, 32, 512, 128)
    x_prompt = np.asarray(inp["x_prompt"], f32)
    c_prompt = np.asarray(inp["c_prompt"], f32)
    c_sample = np.asarray(inp["c_sample"], f32)
    in_maps = []
    for i in range(NCORES):
        m = dict(shared)
        m["xT_in"] = np.ascontiguousarray(x_prompt[i].T.reshape(KD, 128, T).transpose(1, 0, 2))
        c5 = np.concatenate([c_prompt[i:i + 1], c_sample[4 * i:4 * i + 4]], axis=0)
        m["cT_in"] = np.ascontiguousarray(c5.T.reshape(KD, 128, 5).transpose(1, 0, 2))
        sb = slice(4 * i, 4 * i + 4)
        m["xsT_in"] = np.ascontiguousarray(x_sample[sb, 0, :].T.reshape(KD, 128, 4).transpose(1, 0, 2))
        m["cst_in"] = np.ascontiguousarray(st_conv[:, sb].reshape(DEPTH, 4, 3, 6, 128).transpose(0, 4, 3, 2, 1))
        m["hs0_in"] = np.ascontiguousarray(st_ssm[:, sb].reshape(DEPTH, 4, 4, 2, 64, 64).transpose(0, 3, 4, 2, 1, 5).reshape(DEPTH, 128, 4, 4, 64))
        m["ptb"] = np.ascontiguousarray(np.broadcast_to(page_table[sb].reshape(1, 256), (128, 256)))
        m["wkT_in"] = np.ascontiguousarray(st_wk[:, sb].transpose(0, 1, 3, 2))
        sf = np.pad(st_ffn[:, sb], ((0, 0), (0, 0), (0, 0), (0, 3072 - D_FF)))
        m["fst_in"] = np.ascontiguousarray(sf.reshape(DEPTH, 4, 2, 24, 128).transpose(0, 4, 3, 2, 1))
        m["wk_in"] = np.ascontiguousarray(st_wk[:, sb])
        m["wv_in"] = np.ascontiguousarray(st_wv[:, sb])
        in_maps.append(m)
    res = run_bass_kernel_spmd(nc, in_maps, core_ids=list(range(NCORES)))
    R = res.results
    B, S = 8, 32
    kv = np.stack([R[i]["kv_out"] for i in range(NCORES)], axis=1)
    kvp = [kv[..., 128 * j:128 * j + 128].reshape(DEPTH, B, T, 2, 64) for j in range(6)]
    yT = np.stack([R[i]["yT_out"] for i in range(NCORES)])
    y_prompt = np.ascontiguousarray(yT.transpose(0, 3, 2, 1).reshape(B, T, D))
    z = lambda *s: np.zeros(s, f32)
    kvn = np.stack([R[i]["kvn_out"] for i in range(NCORES)], axis=1)
    kvn = kvn.transpose(0, 1, 4, 3, 2).reshape(DEPTH, S, 6, 1, 2, 64)
    kvs = [np.ascontiguousarray(kvn[:, :, j]) for j in range(6)]
    wks = np.concatenate([R[i]["wk_out"] for i in range(NCORES)], axis=1).reshape(DEPTH, S, 512, 2, 64)
    wvs = np.concatenate([R[i]["wv_out"] for i in range(NCORES)], axis=1).reshape(DEPTH, S, 512, 2, 64)
    scs = np.stack([R[i]["sconvs_out"] for i in range(NCORES)], axis=1)
    sconv_s = np.ascontiguousarray(scs.transpose(0, 1, 5, 4, 3, 2).reshape(DEPTH, S, 3, 768))
    sss = np.stack([R[i]["ssms_out"] for i in range(NCORES)], axis=1)
    ssm_s = np.ascontiguousarray(sss.reshape(DEPTH, 8, 2, 64, 4, 4, 64).transpose(0, 1, 5, 4, 2, 3, 6).reshape(DEPTH, S, 8, 64, 64))
    fcs = np.stack([R[i]["fcs_out"] for i in range(NCORES)], axis=1)
    fconv_s = np.ascontiguousarray(fcs.transpose(0, 1, 5, 4, 3, 2).reshape(DEPTH, S, 2, 3072)[..., :D_FF])
    ysT = np.stack([R[i]["ysT_out"] for i in range(NCORES)])
    y_sample = np.ascontiguousarray(ysT.transpose(0, 3, 2, 1).reshape(S, 1, D))
    fconv = np.stack([R[i]["fconv_out"] for i in range(NCORES)], axis=1)
    fconv_p = np.ascontiguousarray(fconv.transpose(0, 1, 4, 3, 2).reshape(DEPTH, B, 2, D_FF))
    sconv = np.stack([R[i]["sconv_out"] for i in range(NCORES)], axis=1)
    sconv_p = np.ascontiguousarray(sconv.transpose(0, 1, 4, 3, 2).reshape(DEPTH, B, 3, 768))
    ssm = np.stack([R[i]["ssm_out"] for i in range(NCORES)], axis=1)
    ssm_p = np.ascontiguousarray(ssm.reshape(DEPTH, B, 2, 64, 4, 64).transpose(0, 1, 2, 4, 5, 3).reshape(DEPTH, B, 8, 64, 64))
    outs = (y_prompt, y_sample,
            kvp[0], kvs[0], kvp[1], kvs[1],
            kvp[2], kvs[2], kvp[3], kvs[3],
            np.ascontiguousarray(kvp[4][:, :, T - 512:]), wks,
            np.ascontiguousarray(kvp[5][:, :, T - 512:]), wvs,
            ssm_p, ssm_s,
            sconv_p, sconv_s,
            fconv_p, fconv_s)
    return outs
```
